# Optimizing a Trainium2 kernel written in Bass

```python
import math
import jax, jax.numpy as jnp
from jax import lax
import numpy as np

D_MODEL = 1024
BATCH = 2
SEQ = 8192
DEPTH = 2
DEC_BATCH = 128
DEC_SEQ = 1
PAST_LEN = 16384
PAGE_SIZE = 128

N_A_LAYERS = DEPTH // 2
N_B_LAYERS = DEPTH - N_A_LAYERS
MIX_WIDTH = D_MODEL
MEM_LEN = 256
MEM_HEADS = 4
MEM_HEAD_DIM = MIX_WIDTH // 4 // MEM_HEADS
MEM_Q_WIDTH = MEM_HEADS * MEM_HEAD_DIM
GLA_HEADS = 4
GLA_V_WIDTH = MIX_WIDTH - MEM_Q_WIDTH
GLA_K_WIDTH = GLA_V_WIDTH // 2
GLA_DK = GLA_K_WIDTH // GLA_HEADS
GLA_DV = GLA_V_WIDTH // GLA_HEADS
GLA_GATE_RANK = 16
GLA_GATE_NORM = 16.0
GLA_CHUNK = 64
SWA_HEAD_DIM = 64
SWA_HEADS = GLA_V_WIDTH // SWA_HEAD_DIM
SWA_KV_HEADS = 4
SWA_GROUP = SWA_HEADS // SWA_KV_HEADS
SWA_Q_WIDTH = SWA_HEADS * SWA_HEAD_DIM
SWA_KV_WIDTH = SWA_KV_HEADS * SWA_HEAD_DIM
WINDOW = 128
N_BUCKETS = 32
MAX_DISTANCE = 128
D_FF = 4 * D_MODEL
EPS = 1e-6
IN_A_SIZES = (GLA_K_WIDTH, GLA_K_WIDTH, GLA_V_WIDTH, GLA_V_WIDTH, GLA_GATE_RANK, MEM_Q_WIDTH)
IN_A = sum(IN_A_SIZES)
IN_B_SIZES = (SWA_Q_WIDTH, MEM_Q_WIDTH)
IN_B = sum(IN_B_SIZES)

kernel_name = "yoco_gla_swa_sink_memory_decoder_step"


def rmsnorm(x, g):
    xf = x.astype(jnp.float32)
    y = xf * lax.rsqrt(jnp.mean(xf * xf, axis=-1, keepdims=True) + EPS)
    return (y * g.astype(jnp.float32)).astype(x.dtype)


def split_cols(x, sizes):
    idx = np.cumsum(np.array(sizes))[:-1].tolist()
    return jnp.split(x, idx, axis=-1)


def squared_relu_mlp(h, w_up, w_down):
    return jnp.square(jax.nn.relu(h @ w_up)) @ w_down


def mem_kv(mem, g, w):
    B = mem.shape[0]
    k, v = jnp.split(rmsnorm(mem, g) @ w, 2, axis=-1)
    shp = (B, MEM_LEN, MEM_HEADS, MEM_HEAD_DIM)
    return k.reshape(shp), v.reshape(shp)


def mem_attend(qm, mem_k, mem_v):
    B, L, _ = qm.shape
    q = qm.reshape(B, L, MEM_HEADS, MEM_HEAD_DIM)
    s = jnp.einsum('blhd,bmhd->bhlm', q, mem_k).astype(jnp.float32) * (MEM_HEAD_DIM ** -0.5)
    p = jax.nn.softmax(s, axis=-1).astype(mem_v.dtype)
    return jnp.einsum('bhlm,bmhd->blhd', p, mem_v).reshape(B, L, MEM_Q_WIDTH)


def gla_chunk_step(S, inp):
    q, k, v, g = inp
    C = q.shape[1]
    b = jnp.cumsum(g, axis=1)
    causal = jnp.tril(jnp.ones((C, C), dtype=bool))
    diff = b[:, :, None] - b[:, None, :]
    decay = jnp.exp(jnp.where(causal[None, :, :, None, None], diff, -jnp.inf))
    attn = jnp.einsum('bthd,bshd,btshd->bhts', q, k, decay)
    o = jnp.einsum('bhts,bshv->bthv', attn, v) + jnp.einsum('bthd,bhdv->bthv', q * jnp.exp(b), S)
    b_last = b[:, -1]
    S_new = jnp.exp(b_last)[..., None] * S + jnp.einsum(
        'bshd,bshv->bhdv', k * jnp.exp(b_last[:, None] - b), v)
    return S_new, o


def gla_recurrence(q, k, v, g, S0):
    B, L, H, _ = q.shape
    C = math.gcd(L, GLA_CHUNK)
    n = L // C

    def to_chunks(t):
        return jnp.moveaxis(t.astype(jnp.float32).reshape((B, n, C) + t.shape[2:]), 1, 0)

    S, o = lax.scan(gla_chunk_step, S0.astype(jnp.float32),
                    (to_chunks(q), to_chunks(k), to_chunks(v), to_chunks(g)))
    o = jnp.moveaxis(o, 0, 1).reshape(B, L, H, GLA_DV)
    return o, S


def rel_bucket(dist):
    max_exact = N_BUCKETS // 2
    n = jnp.maximum(dist, 0)
    nf = jnp.maximum(n, 1).astype(jnp.float32)
    large = max_exact + (jnp.log(nf / max_exact) / math.log(MAX_DISTANCE / max_exact)
                         * (N_BUCKETS - max_exact)).astype(jnp.int32)
    large = jnp.minimum(large, N_BUCKETS - 1)
    return jnp.where(n < max_exact, n, large)


def t5_bias(dist, rel_bias):
    b = jnp.transpose(rel_bias[rel_bucket(dist)], (2, 0, 1)).astype(jnp.float32)
    return b.reshape((SWA_KV_HEADS, SWA_GROUP) + dist.shape)


def sink_attention(q, k, v, bias, valid, sinks):
    s = jnp.einsum('...qhgd,...khd->...hgqk', q, k).astype(jnp.float32) * (SWA_HEAD_DIM ** -0.5) + bias
    s = jnp.where(valid, s, -jnp.inf)
    sink = sinks.astype(jnp.float32).reshape(SWA_KV_HEADS, SWA_GROUP)[:, :, None, None]
    m = jnp.maximum(jnp.max(s, axis=-1, keepdims=True), sink)
    p = jnp.exp(s - m)
    p = (p / (jnp.sum(p, axis=-1, keepdims=True) + jnp.exp(sink - m))).astype(v.dtype)
    return jnp.einsum('...hgqk,...khd->...qhgd', p, v)


def swa_prompt(q, k, v, rel_bias, sinks):
    B, L = q.shape[:2]
    nb = L // WINDOW
    qb = q.reshape(B, nb, WINDOW, SWA_KV_HEADS, SWA_GROUP, SWA_HEAD_DIM)

    def band_keys(t):
        tb = t.reshape(B, nb, WINDOW, SWA_KV_HEADS, SWA_HEAD_DIM)
        prev = jnp.pad(tb, ((0, 0), (1, 0), (0, 0), (0, 0), (0, 0)))[:, :-1]
        return jnp.concatenate([prev, tb], axis=2)

    kk, vv = band_keys(k), band_keys(v)
    qi = jnp.arange(WINDOW)[:, None] + WINDOW
    kj = jnp.arange(2 * WINDOW)[None, :]
    dist = qi - kj
    band = (dist >= 0) & (dist < WINDOW)
    blk = jnp.arange(nb)[:, None, None]
    valid = band[None] & ((blk > 0) | (kj >= WINDOW)[None])
    o = sink_attention(qb, kk, vv, t5_bias(dist, rel_bias), valid[:, None, None], sinks)
    return o.reshape(B, L, SWA_Q_WIDTH)


def swa_sample(q, k_all, v_all, rel_bias, sinks):
    B, Lq = q.shape[:2]
    qq = q.reshape(B, Lq, SWA_KV_HEADS, SWA_GROUP, SWA_HEAD_DIM)
    dist = (jnp.arange(Lq)[:, None] + WINDOW) - jnp.arange(WINDOW + Lq)[None, :]
    valid = (dist >= 0) & (dist < WINDOW)
    o = sink_attention(qq, k_all, v_all, t5_bias(dist, rel_bias), valid, sinks)
    return o.reshape(B, Lq, SWA_Q_WIDTH)


def run_trunk(x, mem_k, mem_v, gla_state, buf_k, buf_v,
              norm_mix_pre, norm_mix_post, norm_ffn_pre, norm_ffn_post,
              w_in_a, w_gate_up, b_gate, gla_norm, w_in_b, sinks,
              norm_kv, w_kv, rel_bias, w_out, w_ffn_up, w_ffn_down):
    B, L, _ = x.shape
    new_gla = []
    k_all = v_all = new_k = new_v = None
    for l in range(DEPTH):
        if l == N_A_LAYERS:
            ks, vs = jnp.split(rmsnorm(x, norm_kv) @ w_kv, 2, axis=-1)
            ks = ks.reshape(B, L, SWA_KV_HEADS, SWA_HEAD_DIM)
            vs = vs.reshape(B, L, SWA_KV_HEADS, SWA_HEAD_DIM)
            if buf_k is None:
                k_all, v_all = ks, vs
            else:
                k_all = jnp.concatenate([buf_k, ks], axis=1)
                v_all = jnp.concatenate([buf_v, vs], axis=1)
            new_k, new_v = k_all[:, -WINDOW:], v_all[:, -WINDOW:]
        h = rmsnorm(x, norm_mix_pre[l])
        if l < N_A_LAYERS:
            a = l
            q, k, v, r, glr, qm = split_cols(h @ w_in_a[a], IN_A_SIZES)
            q = q.reshape(B, L, GLA_HEADS, GLA_DK) * (GLA_DK ** -0.5)
            k = k.reshape(B, L, GLA_HEADS, GLA_DK)
            v = v.reshape(B, L, GLA_HEADS, GLA_DV)
            g = jax.nn.log_sigmoid((glr @ w_gate_up[a] + b_gate[a]).astype(jnp.float32)) / GLA_GATE_NORM
            g = g.reshape(B, L, GLA_HEADS, GLA_DK)
            o, S = gla_recurrence(q, k, v, g, gla_state[a])
            new_gla.append(S.astype(gla_state.dtype))
            o_main = rmsnorm(o.astype(h.dtype), gla_norm[a]).reshape(B, L, GLA_V_WIDTH) * jax.nn.silu(r)
        else:
            bl = l - N_A_LAYERS
            q, qm = split_cols(h @ w_in_b[bl], IN_B_SIZES)
            if buf_k is None:
                o_main = swa_prompt(q, k_all, v_all, rel_bias, sinks[bl])
            else:
                o_main = swa_sample(q, k_all, v_all, rel_bias, sinks[bl])
        o_mem = mem_attend(qm, mem_k[l], mem_v[l])
        mix = jnp.concatenate([o_main, o_mem], axis=-1) @ w_out[l]
        x = x + rmsnorm(mix, norm_mix_post[l])
        f = squared_relu_mlp(rmsnorm(x, norm_ffn_pre[l]), w_ffn_up[l], w_ffn_down[l])
        x = x + rmsnorm(f, norm_ffn_post[l])
    return x, jnp.stack(new_gla), new_k, new_v


def setup_inputs(seed: int = 0) -> dict:
    key = jax.random.key(seed)
    ks = iter(jax.random.split(key, 40))

    def nrm(shape, scale):
        return scale * jax.random.normal(next(ks), shape, jnp.float32)

    def gain(shape):
        return 1.0 + 0.05 * jax.random.normal(next(ks), shape, jnp.float32)

    return {
        "x_prompt": nrm((BATCH, SEQ, D_MODEL), 1.0),
        "x_sample": nrm((DEC_BATCH, DEC_SEQ, D_MODEL), 1.0),
        "state_gla": nrm((N_A_LAYERS, DEC_BATCH, GLA_HEADS, GLA_DK, GLA_DV), 0.5),
        "cache_swa_k": nrm((DEC_BATCH, WINDOW, SWA_KV_HEADS, SWA_HEAD_DIM), 1.0),
        "cache_swa_v": nrm((DEC_BATCH, WINDOW, SWA_KV_HEADS, SWA_HEAD_DIM), 1.0),
        "cache_mem_k": nrm((DEPTH, DEC_BATCH, MEM_LEN, MEM_HEADS, MEM_HEAD_DIM), 1.0),
        "cache_mem_v": nrm((DEPTH, DEC_BATCH, MEM_LEN, MEM_HEADS, MEM_HEAD_DIM), 1.0),
        "mem_prompt": nrm((BATCH, MEM_LEN, D_MODEL), 1.0),
        "norm_mix_pre": gain((DEPTH, D_MODEL)),
        "norm_mix_post": gain((DEPTH, D_MODEL)),
        "norm_ffn_pre": gain((DEPTH, D_MODEL)),
        "norm_ffn_post": gain((DEPTH, D_MODEL)),
        "norm_mem": gain((DEPTH, D_MODEL)),
        "w_mem_kv": nrm((DEPTH, D_MODEL, 2 * MEM_Q_WIDTH), D_MODEL ** -0.5),
        "w_in_a": nrm((N_A_LAYERS, D_MODEL, IN_A), D_MODEL ** -0.5),
        "w_gate_up": nrm((N_A_LAYERS, GLA_GATE_RANK, GLA_K_WIDTH), GLA_GATE_RANK ** -0.5),
        "b_gate": nrm((N_A_LAYERS, GLA_K_WIDTH), 0.1),
        "gla_norm": gain((N_A_LAYERS, GLA_DV)),
        "w_in_b": nrm((N_B_LAYERS, D_MODEL, IN_B), D_MODEL ** -0.5),
        "sinks": nrm((N_B_LAYERS, SWA_HEADS), 1.0),
        "norm_kv": gain((D_MODEL,)),
        "w_kv": nrm((D_MODEL, 2 * SWA_KV_WIDTH), D_MODEL ** -0.5),
        "rel_bias": nrm((N_BUCKETS, SWA_HEADS), 0.5),
        "w_out": nrm((DEPTH, MIX_WIDTH, D_MODEL), MIX_WIDTH ** -0.5),
        "w_ffn_up": nrm((DEPTH, D_MODEL, D_FF), D_MODEL ** -0.5),
        "w_ffn_down": nrm((DEPTH, D_FF, D_MODEL), D_FF ** -0.5),
    }


def reference(x_prompt, x_sample, state_gla, cache_swa_k, cache_swa_v, cache_mem_k, cache_mem_v,
              mem_prompt, norm_mix_pre, norm_mix_post, norm_ffn_pre, norm_ffn_post, norm_mem,
              w_mem_kv, w_in_a, w_gate_up, b_gate, gla_norm, w_in_b, sinks, norm_kv, w_kv,
              rel_bias, w_out, w_ffn_up, w_ffn_down):
    weights = (norm_mix_pre, norm_mix_post, norm_ffn_pre, norm_ffn_post,
               w_in_a, w_gate_up, b_gate, gla_norm, w_in_b, sinks,
               norm_kv, w_kv, rel_bias, w_out, w_ffn_up, w_ffn_down)
    mk, mv = [], []
    for l in range(DEPTH):
        k_l, v_l = mem_kv(mem_prompt, norm_mem[l], w_mem_kv[l])
        mk.append(k_l)
        mv.append(v_l)
    cache_mem_k_prompt = jnp.stack(mk)
    cache_mem_v_prompt = jnp.stack(mv)
    gla0 = jnp.zeros((N_A_LAYERS, BATCH, GLA_HEADS, GLA_DK, GLA_DV), x_prompt.dtype)
    y_prompt, state_gla_prompt, cache_swa_k_prompt, cache_swa_v_prompt = run_trunk(
        x_prompt, cache_mem_k_prompt, cache_mem_v_prompt, gla0, None, None, *weights)
    y_sample, state_gla_sample, cache_swa_k_sample, cache_swa_v_sample = run_trunk(
        x_sample, cache_mem_k, cache_mem_v, state_gla, cache_swa_k, cache_swa_v, *weights)
    return (y_prompt, y_sample, state_gla_prompt, state_gla_sample,
            cache_swa_k_prompt, cache_swa_v_prompt, cache_swa_k_sample, cache_swa_v_sample,
            cache_mem_k_prompt, cache_mem_v_prompt)
```

```python
import math
import numpy as np
from contextlib import ExitStack
import concourse.bass as bass
import concourse.mybir as mybir
from concourse.bass_utils import run_bass_kernel_spmd

F32 = mybir.dt.float32
BF16 = mybir.dt.bfloat16
ALU = mybir.AluOpType
AF = mybir.ActivationFunctionType
AX = mybir.AxisListType

D = 1024
DK, DV, H = 96, 192, 4
EPS = 1e-6
NCORES = 8


class Buf:
    __slots__ = ("name", "w", "r", "excl")

    def __init__(self, name="", excl=False):
        self.name = name
        self.w = None
        self.r = {}
        self.excl = excl


class StopBuild(Exception):
    pass


class TB:
    def __init__(self, t, name=""):
        self.t = t
        self.b = Buf(name)

    def __getitem__(self, k):
        return self.t[k]


class Sched:
    ENGS = ("pe", "act", "dve", "pool", "sp")
    BLK = {"pe": "tensor", "act": "scalar", "dve": "vector", "pool": "gpsimd", "sp": "sync"}

    def __init__(self, nc, es, n_dma=(28, 14, 8)):
        self.nc = nc
        self.sem = {e: es.enter_context(nc.semaphore("s_" + e)) for e in self.ENGS}
        self.cnt = {e: 0 for e in self.ENGS}
        self.known = {e: {e2: 0 for e2 in self.ENGS} for e in self.ENGS}
        self.dq = {}
        k = 0
        for q, n in zip(("sp", "pool", "act"), n_dma):
            self.dq[q] = list(range(k, k + n))
            k += n
        self.dsem = [es.enter_context(nc.semaphore("d%d" % i)) for i in range(k)]
        self.dcnt = [0] * k
        self.drr = {q: 0 for q in self.dq}
        self.kdma = {e: [0] * k for e in self.ENGS}
        self.prog = {e: [] for e in self.ENGS}

    @staticmethod
    def _b(x):
        return getattr(x, "b", x)

    def _deps(self, reads, writes):
        deps = []
        for b in reads:
            b = self._b(b)
            if b.w is not None:
                deps.append(b.w)
        for b in writes:
            b = self._b(b)
            if b.w is not None:
                deps.append(b.w)
            deps.extend(b.r.values())
        return deps

    def _resolve(self, eng, deps):
        need = {}
        for ev in deps:
            if ev[0] == "c":
                _, e2, k = ev
                if e2 == eng and eng == "pe":
                    continue
                if self.known[eng][e2] >= k:
                    continue
                need[("c", e2)] = max(need.get(("c", e2), 0), k)
            else:
                _, idx, v = ev
                if self.kdma[eng][idx] >= v:
                    continue
                need[("d", idx)] = max(need.get(("d", idx), 0), v)
        waits = []
        for (kind, key), v in need.items():
            if kind == "c":
                self.known[eng][key] = v
                waits.append((self.sem[key], v))
            else:
                self.kdma[eng][key] = v
                waits.append((self.dsem[key], v))
        return waits

    def _mark(self, ev, key, reads, writes):
        for b in reads:
            self._b(b).r[key] = ev
        for b in writes:
            b = self._b(b)
            b.w = ev
            b.r = {}

    def op(self, eng, fn, reads=(), writes=()):
        ex = [b for b in reads if self._b(b).excl]
        if ex:
            reads = [b for b in reads if not self._b(b).excl]
            writes = list(writes) + ex
        waits = self._resolve(eng, self._deps(reads, writes))
        self.cnt[eng] += 1
        ev = ("c", eng, self.cnt[eng])
        self.prog[eng].append((waits, fn, None))
        self._mark(ev, eng, reads, writes)
        return ev

    def dma(self, q, out, in_, reads=(), writes=(), **kw):
        waits = self._resolve(q, self._deps(reads, writes))
        pool = self.dq[q]
        idx = pool[self.drr[q] % len(pool)]
        self.drr[q] += 1
        v0 = 16 * self.dcnt[idx]
        if v0 > 0 and self.kdma[q][idx] < v0:
            self.kdma[q][idx] = v0
            waits.append((self.dsem[idx], v0))
        self.dcnt[idx] += 1
        ev = ("d", idx, 16 * self.dcnt[idx])
        self.prog[q].append((waits, lambda h: h.dma_start(out=out, in_=in_, **kw), idx))
        self._mark(ev, ("d", idx), reads, writes)
        return ev

    def flush(self):
        nc = self.nc
        tails = {}
        for q, pool in self.dq.items():
            tl = []
            for idx in pool:
                v = 16 * self.dcnt[idx]
                if v > 0 and self.kdma[q][idx] < v:
                    tl.append((self.dsem[idx], v))
            tails[q] = tl
        with nc.Block() as block:
            for e in self.ENGS:
                items = self.prog[e]
                tl = tails.get(e, [])
                if not items and not tl:
                    continue

                def body(h, items=items, tl=tl, e=e):
                    for waits, fn, didx in items:
                        for s, v in waits:
                            h.wait_ge(s, v)
                        ins = fn(h)
                        if didx is None:
                            ins.then_inc(self.sem[e], 1)
                        else:
                            ins.then_inc(self.dsem[didx], 16)
                    for s, v in tl:
                        h.wait_ge(s, v)

                getattr(block, self.BLK[e])(body)
        self.prog = {e: [] for e in self.ENGS}
        for e in self.ENGS:
            for e2 in self.ENGS:
                self.known[e][e2] = self.cnt[e2]
            for i in range(len(self.dcnt)):
                self.kdma[e][i] = 16 * self.dcnt[i]


def bc(ap, dim, n):
    u = ap.unsqueeze(dim)
    shp = list(u.shape)
    shp[dim] = n
    return u.broadcast_to(shp)


def qcol(h):
    s, r = divmod(h, 6)
    half, j = divmod(r, 3)
    return (3 * s + j) * 128 + half * 64


def build(NT=16, NPRE=47, NS=16, stop=99):
    try:
        return _build(NT, NPRE, NS, stop)
    except StopBuild as e:
        return e.args[0]


def _build(NT=16, NPRE=47, NS=16, stop=99):
    nc = bass.Bass("TRN2", target_bir_lowering=False)
    NTI = NPRE + 1 + NT
    NT1 = NT + 1

    def din(name, shape):
        return nc.dram_tensor(name, list(shape), F32, kind="ExternalInput").ap()

    def dout(name, shape):
        return nc.dram_tensor(name, list(shape), F32, kind="ExternalOutput").ap()

    xin = din("xin", [NTI * 128, D])
    xs_in = din("xs_in", [NS, D])
    st_in = din("st_in", [NS, H, DK, DV])
    swk_in = din("swk_in", [NS, 128, 256])
    swv_in = din("swv_in", [NS, 128, 256])
    mk_in = din("mk_in", [2, NS, 256, 256])
    mv_in = din("mv_in", [2, NS, 256, 256])
    mem_in = din("mem_in", [256, D])
    g_mix_pre = din("norm_mix_pre", [2, D])
    g_mix_post = din("norm_mix_post", [2, D])
    g_ffn_pre = din("norm_ffn_pre", [2, D])
    g_ffn_post = din("norm_ffn_post", [2, D])
    g_mem = din("norm_mem", [2, D])
    w_mem_kv = din("w_mem_kv", [2, D, 512])
    w_in_a = din("w_in_a", [D, 2576])
    w_gate_up = din("w_gate_up", [16, 384])
    b_gate = din("b_gate", [1, 384])
    gla_norm = din("gla_norm", [1, DV])
    w_in_b = din("w_in_b", [D, 1024])
    sinks = din("sinks", [1, 12])
    g_kv = din("norm_kv", [1, D])
    w_kv = din("w_kv", [D, 512])
    rel_bias = din("rel_bias", [32, 12])
    w_out = din("w_out", [2, D, D])
    w_up = din("w_ffn_up", [2, D, 4096])
    w_down = din("w_ffn_down", [2, 4096, D])
    c_ident = din("c_ident", [128, 128])
    c_cmask = din("c_cmask", [128, 128])
    c_ohu = din("c_ohu", [32, 384])
    c_valid = din("c_valid", [128, 384])
    c_id16r = din("c_id16r", [128, 256])
    c_flag = din("c_flag", [128, 1])

    y = dout("y", [NT * 128, D])
    ys = dout("ys", [NS, D])
    stp = dout("stp", [H, DK, DV])
    sts = dout("sts", [NS, H, DK, DV])
    swkp = dout("swkp", [128, 256])
    swvp = dout("swvp", [128, 256])
    swks = dout("swks", [NS, 128, 256])
    swvs = dout("swvs", [NS, 128, 256])
    mkp = dout("mkp", [2, 256, 256])
    mvp = dout("mvp", [2, 256, 256])

    xa = nc.dram_tensor("xa_scr", [NT1 * 128, D], F32).ap()
    xb = nc.dram_tensor("xb_scr", [NT1 * 128, D], F32).ap()
    ebd = nc.dram_tensor("ebd_scr", [12, 128, 384], F32)
    b_xa = [Buf() for _ in range(NT1)]
    b_xb = [Buf() for _ in range(NT1)]
    b_ebd = Buf()
    b_y = Buf()

    with ExitStack() as ges:
        S = Sched(nc, ges)

        def ck(n):
            if stop == -100 - n:
                S.flush()
                raise StopBuild(nc)

        def mk(es, name, shape, dt=F32):
            return TB(es.enter_context(nc.sbuf_tensor(name, list(shape), dt)), name)

        PS = [ges.enter_context(nc.psum_tensor("PS%d" % i, [128, 1024], F32)) for i in range(4)]
        bB = [Buf("bank%d" % i, excl=True) for i in range(8)]

        def bank(i):
            return PS[i // 2][:, (i % 2) * 512:(i % 2 + 1) * 512]

        def bankb(i):
            return bank(i).bitcast(BF16)

        ident_f = mk(ges, "ident_f", [128, 128])
        ident_b = mk(ges, "ident_b", [128, 128], BF16)
        cmask = mk(ges, "cmask", [128, 128])
        id16r = mk(ges, "id16r", [128, 16, 16])
        ones_row = mk(ges, "ones_row", [1, 128], BF16)
        ones96 = mk(ges, "ones96", [96, 128])
        flag = mk(ges, "flag", [128, 1])
        KmT = [mk(ges, "KmT%d" % l, [128, 2, 2, 256], BF16) for l in range(2)]
        Vm = [mk(ges, "Vm%d" % l, [128, 2, 4, 66], BF16) for l in range(2)]
        xs_sb = mk(ges, "xs_sb", [NS, D])
        st = {n: mk(ges, "st_" + n, [128, 4]) for n in ("ss", "ln", "rstd", "ssg", "lng", "rstdg", "rden", "el")}
        rden12 = mk(ges, "rden12", [128, 12])
        den12 = mk(ges, "den12", [128, 12])
        xt = [mk(ges, "xt%d" % i, [128, D]) for i in range(3)]
        hb = mk(ges, "hb", [128, D], BF16)
        hb2 = mk(ges, "hb2", [128, D], BF16)
        hT = mk(ges, "hT", [128, 8, 128], BF16)
        hT2 = mk(ges, "hT2", [128, 8, 128], BF16)
        mixcat = mk(ges, "mixcat", [128, D], BF16)
        mcT = mk(ges, "mcT", [128, 8, 128], BF16)
        tt = mk(ges, "tt", [128, D])
        pT_sb = mk(ges, "pT_sb", [128, 8, 128], BF16)
        qmT_sb = mk(ges, "qmT_sb", [128, 2, 128], BF16)
        z16 = mk(ges, "z16", [16, 512])

        xt_rr = [0]

        def next_xt():
            t = xt[xt_rr[0] % 3]
            xt_rr[0] += 1
            return t

        def load_gain(dst, src_row):
            S.dma("sp", dst[:], src_row.partition_broadcast(128), writes=[dst])

        def load_w(dst, src2d, kchunks, ncols, col0=0, dcol0=0):
            v = src2d.rearrange("(kc p) n -> p kc n", p=128)
            step = 2048
            for kc in range(kchunks):
                c = 0
                while c < ncols:
                    n = min(step, ncols - c)
                    S.dma("pool", dst[:, kc, dcol0 + c:dcol0 + c + n], v[:, kc, col0 + c:col0 + c + n], writes=[dst])
                    c += n

        def rms_stats(x_ap, P, x_reads, junk, n=D):
            ss, ln, rstd = st["ss"], st["ln"], st["rstd"]
            S.op("act", lambda h: h.activation(out=junk[0:P, 0:n], in_=x_ap, func=AF.Square, accum_out=ss[0:P, 0:1]),
                 reads=x_reads, writes=[junk, ss])
            S.op("act", lambda h: h.activation(out=ln[0:P, 0:1], in_=ss[0:P, 0:1], func=AF.Ln, scale=1.0 / n, bias=eps_t[0:P, 0:1]),
                 reads=[ss, eps_t], writes=[ln])
            S.op("act", lambda h: h.activation(out=rstd[0:P, 0:1], in_=ln[0:P, 0:1], func=AF.Exp, scale=-0.5),
                 reads=[ln], writes=[rstd])

        def norm_to(x_ap, P, x_reads, gain, dst):
            S.op("dve", lambda h: h.scalar_tensor_tensor(out=dst[0:P, :], in0=x_ap, scalar=st["rstd"][0:P, 0:1], in1=gain[0:P, :],
                                                         op0=ALU.mult, op1=ALU.mult),
                 reads=list(x_reads) + [st["rstd"], gain], writes=[dst])

        def transpose8(src, P, dstT, tb=0):
            def f(h):
                for kc in range(8):
                    ins = h.transpose(out=bankb(tb)[:, kc * 128:kc * 128 + P], in_=src[0:P, kc * 128:(kc + 1) * 128],
                                      identity=ident_b[0:P, 0:P])
                return ins
            S.op("pe", f, reads=[src, ident_b], writes=[bB[tb]])
            S.op("act", lambda h: h.copy(out=dstT[:, :, 0:P], in_=bankb(tb)[:, :].rearrange("p (a b) -> p a b", a=8)[:, :, 0:P]),
                 reads=[bB[tb]], writes=[dstT])

        def mm(out_ap, pairs, reads, wbank):
            def f(h):
                n = len(pairs)
                for i, (l, r) in enumerate(pairs):
                    ins = h.matmul(out_ap, lhsT=l, rhs=r, start=(i == 0), stop=(i == n - 1))
                return ins
            S.op("pe", f, reads=reads, writes=wbank)

        def post_norm_residual(mix_pair, P, gain, xtile_ap, xtile_tb):
            mix = PS[mix_pair][0:P, :]
            rms_stats(mix, P, [bB[2 * mix_pair], bB[2 * mix_pair + 1]], tt)
            S.op("dve", lambda h: h.scalar_tensor_tensor(out=tt[0:P, :], in0=mix, scalar=st["rstd"][0:P, 0:1], in1=gain[0:P, :],
                                                         op0=ALU.mult, op1=ALU.mult),
                 reads=[bB[2 * mix_pair], bB[2 * mix_pair + 1], st["rstd"], gain], writes=[tt])
            S.op("pool", lambda h: h.tensor_tensor(out=xtile_ap, in0=xtile_ap, in1=tt[0:P, :], op=ALU.add),
                 reads=[tt], writes=[xtile_tb])

        def out_proj(P, Wo, gain, xtile_ap, xtile_tb):
            transpose8(mixcat, P, mcT, tb=0)
            for half in range(2):
                mm(bank(6 + half)[0:P, :], [(mcT[:, kc, 0:P], Wo[:, kc, half * 512:(half + 1) * 512]) for kc in range(8)],
                   [mcT, Wo], [bB[6 + half]])
            post_norm_residual(3, P, gain, xtile_ap, xtile_tb)

        def mem_attend_prompt(l, qT_ap_fn, qreads, P):
            def f(h):
                for hh in range(4):
                    c, p = divmod(hh, 2)
                    for mc in range(2):
                        ins = h.matmul(bank(4 + hh // 2)[:, ((hh % 2) * 2 + mc) * 128:((hh % 2) * 2 + mc) * 128 + P],
                                       lhsT=KmT[l][:, p, c, mc * 128:(mc + 1) * 128], rhs=qT_ap_fn(c, p),
                                       start=True, stop=True)
                return ins
            S.op("pe", f, reads=[KmT[l]] + qreads, writes=[bB[4], bB[5]])
            ck(30)
            S.op("act", lambda h: h.activation(out=pT_sb[:, :, 0:P], in_=PS[2][:, :].rearrange("p (a b) -> p a b", a=8)[:, :, 0:P],
                                               func=AF.Exp, scale=0.125),
                 reads=[bB[4], bB[5]], writes=[pT_sb])

            def g(h):
                for hh in range(4):
                    for mc in range(2):
                        ins = h.matmul(bank(1)[0:P, hh * 65:(hh + 1) * 65], lhsT=pT_sb[:, hh * 2 + mc, 0:P], rhs=Vm[l][:, mc, hh, 0:65],
                                       start=(mc == 0), stop=(mc == 1))
                return ins
            ck(31)
            S.op("pe", g, reads=[pT_sb, Vm[l]], writes=[bB[1]])
            ck(32)
            finish_mem(P)

        def finish_mem(P):
            om = bank(1)[0:P, 0:260].rearrange("p (a b) -> p a b", a=4)
            S.op("dve", lambda h: h.reciprocal(out=st["rden"][0:P, 0:4], in_=om[:, :, 64]), reads=[bB[1]], writes=[st["rden"]])
            S.op("dve", lambda h: h.tensor_tensor(out=mixcat[0:P, 768:1024].rearrange("p (a b) -> p a b", a=4), in0=om[:, :, 0:64],
                                                  in1=bc(st["rden"][0:P, 0:4], 2, 64), op=ALU.mult),
                 reads=[bB[1], st["rden"]], writes=[mixcat])

        mem_sample_fn = []

        def zero_acc(regions, banks):
            def f(h):
                for r_ in regions:
                    n = r_.shape[-1]
                    ins = h.matmul(r_, lhsT=z16[0:16, 0:NS], rhs=z16[0:16, 0:n], start=True, stop=False)
                return ins
            S.op("pe", f, reads=[z16], writes=banks)

        def alloc_sample_attn(es, tag, onesel=None):
            Kc = [mk(es, tag + "Kc%d" % i, [128, 2, 256]) for i in range(2)]
            Vc = [mk(es, tag + "Vc%d" % i, [128, 2, 4, 65]) for i in range(2)]
            prod = mk(es, tag + "prod", [128, 768])
            sc8 = mk(es, tag + "sc8", [128, 12])
            pp8 = mk(es, tag + "pp8", [128, 12])
            Pm = mk(es, tag + "Pm", [128, 12, NS])
            if onesel is None:
                onesel = mk(es, tag + "onesel", [NS, NS, 128])
            S.op("pool", lambda h: h.memset(Vc[0][:], 1.0), writes=[Vc[0]])
            S.op("pool", lambda h: h.memset(Vc[1][:], 1.0), writes=[Vc[1]])
            S.op("dve", lambda h: h.tensor_copy(out=onesel[:], in_=bc(ident_f[0:NS, 0:NS], 2, 128)), reads=[ident_f], writes=[onesel])
            return Kc, Vc, prod, sc8, pp8, Pm, onesel

        with ExitStack() as es:
            eps_t = mk(ges, "eps_t", [128, 1])
            S.op("pool", lambda h: h.memset(eps_t[:], EPS), writes=[eps_t])
            S.op("pool", lambda h: h.memset(z16[:], 0.0), writes=[z16])
            S.dma("sp", ident_f[:], c_ident, writes=[ident_f])
            S.dma("sp", cmask[:], c_cmask, writes=[cmask])
            S.dma("sp", id16r[:].rearrange("p a b -> p (a b)"), c_id16r, writes=[id16r])
            S.dma("sp", flag[:], c_flag, writes=[flag])
            S.dma("sp", xs_sb[:], xs_in, writes=[xs_sb])
            S.op("dve", lambda h: h.tensor_copy(out=ident_b[:], in_=ident_f[:]), reads=[ident_f], writes=[ident_b])
            S.op("pool", lambda h: h.memset(ones_row[:], 1.0), writes=[ones_row])
            S.op("pool", lambda h: h.memset(ones96[:], 1.0), writes=[ones96])
            for l in range(2):
                S.op("pool", lambda h, l=l: h.memset(Vm[l][:], 1.0), writes=[Vm[l]])
                S.op("pool", lambda h, l=l: h.memset(KmT[l][:], 0.0), writes=[KmT[l]])
            gm = mk(es, "gm", [128, D])
            Wm = mk(es, "Wm", [128, 8, 512], BF16)
            hmT = mk(es, "hmT", [128, 8, 256], BF16)
            kvf = mk(es, "kvf", [128, 512])
            xm = [mk(es, "xm%d" % i, [128, D]) for i in range(2)]
            for mt in range(2):
                S.dma("sp", xm[mt][:], mem_in[mt * 128:(mt + 1) * 128, :], writes=[xm[mt]])
            if stop == -4:
                S.flush()
                return nc
            for l in range(2):
                load_gain(gm, g_mem[l, :])
                load_w(Wm, w_mem_kv[l], 8, 512)
                for mt in range(2):
                    rms_stats(xm[mt][:], 128, [xm[mt]], hb)
                    if stop == -3:
                        S.flush()
                        return nc
                    norm_to(xm[mt][:], 128, [xm[mt]], gm, hb)
                    if stop == -2:
                        S.flush()
                        return nc
                    transpose8(hb, 128, hT, tb=0)
                    if stop == -1:
                        S.flush()
                        return nc
                    S.op("dve", lambda h, mt=mt: h.tensor_copy(out=hmT[:, :, mt * 128:(mt + 1) * 128], in_=hT[:]),
                         reads=[hT], writes=[hmT])
                for c in range(2):
                    mm(bank(2)[:, 0:256], [(Wm[:, kc, c * 128:(c + 1) * 128], hmT[:, kc, :]) for kc in range(8)], [Wm, hmT], [bB[2]])
                    S.op("act", lambda h, c=c, l=l: h.copy(out=KmT[l][0:64, 0, c, :], in_=bank(2)[0:64, 0:256]), reads=[bB[2]], writes=[KmT[l]])
                    S.op("act", lambda h, c=c, l=l: h.copy(out=KmT[l][64:128, 1, c, :], in_=bank(2)[64:128, 0:256]), reads=[bB[2]], writes=[KmT[l]])
                if stop == -10:
                    S.flush()
                    return nc
                for mt in range(2):
                    mm(bank(3)[:, 0:512], [(hmT[:, kc, mt * 128:(mt + 1) * 128], Wm[:, kc, :]) for kc in range(8)], [Wm, hmT], [bB[3]])
                    S.op("dve", lambda h: h.tensor_copy(out=kvf[:], in_=bank(3)[:, 0:512]), reads=[bB[3]], writes=[kvf])
                    if stop == -11:
                        S.flush()
                        return nc
                    S.op("act", lambda h, l=l, mt=mt: h.copy(out=Vm[l][:, mt, :, 0:64],
                                                             in_=bank(3)[:, 256:512].rearrange("p (a b) -> p a b", a=4)),
                         reads=[bB[3]], writes=[Vm[l]])
                    if stop == -12:
                        S.flush()
                        return nc
                    S.dma("sp", mkp[l, mt * 128:(mt + 1) * 128, :], kvf[:, 0:256], reads=[kvf])
                    S.dma("sp", mvp[l, mt * 128:(mt + 1) * 128, :], kvf[:, 256:512], reads=[kvf])
            S.flush()
        if 0 <= stop <= 0:
            return nc

        with ExitStack() as es:
            Wa = mk(es, "Wa", [128, 8, 2576], BF16)
            Wo = mk(es, "Wo0", [128, 8, D], BF16)
            wgu = mk(es, "wgu", [16, 384], BF16)
            bg = mk(es, "bg", [1, 384], BF16)
            gpre = mk(es, "gpre0", [128, D])
            gpost = mk(es, "gpost0", [128, D])
            gn = mk(es, "gn", [128, DV])
            Sst = mk(es, "Sst", [96, 4, DV])
            S_bf = mk(es, "S_bf", [96, 4, DV], BF16)
            glr_sb = mk(es, "glr_sb", [16, 128], BF16)
            e1 = mk(es, "e1", [96, 4, 128])
            spl = mk(es, "spl", [96, 4, 128])
            cc = mk(es, "cc", [96, 4, 128])
            Einv = mk(es, "Einv", [96, 4, 128])
            Edec = mk(es, "Edec", [96, 4, 128])
            keT = mk(es, "keT", [96, 4, 128], BF16)
            qeT = mk(es, "qeT", [96, 4, 128], BF16)
            kdT = mk(es, "kdT", [96, 4, 128], BF16)
            kd_sb = mk(es, "kd_sb", [128, 384], BF16)
            v_sb = mk(es, "v_sb", [128, 768], BF16)
            attnT_sb = mk(es, "attnT_sb", [128, 4, 128], BF16)
            on = mk(es, "on", [128, 768])
            er = mk(es, "er", [128, 768])
            sg = mk(es, "sg", [128, 768])

            load_w(Wa, w_in_a, 8, 2576)
            load_w(Wo, w_out[0], 8, D)
            S.dma("pool", wgu[:], w_gate_up, writes=[wgu])
            S.dma("pool", bg[:], b_gate, writes=[bg])
            load_gain(gpre, g_mix_pre[0, :])
            load_gain(gpost, g_mix_post[0, :])
            S.dma("sp", gn[:], gla_norm[0, :].partition_broadcast(128), writes=[gn])
            S.op("pool", lambda h: h.memset(Sst[:], 0.0), writes=[Sst])
            S.op("pool", lambda h: h.memset(S_bf[:], 0.0), writes=[S_bf])

            elast = st["el"]

            def gate_path(P):
                mm(bank(1)[0:16, 0:P], [(Wa[:, kc, 2304:2320], hT[:, kc, 0:P]) for kc in range(8)], [Wa, hT], [bB[1]])
                S.op("dve", lambda h: h.tensor_copy(out=glr_sb[:, 0:P], in_=bank(1)[0:16, 0:P]), reads=[bB[1]], writes=[glr_sb])

            def zgate(P):
                def f(h):
                    for hh in range(4):
                        o = bank(1)[0:96, hh * 128:hh * 128 + P]
                        h.matmul(o, lhsT=wgu[0:16, 96 * hh:96 * hh + 96], rhs=glr_sb[0:16, 0:P], start=True, stop=False)
                        ins = h.matmul(o, lhsT=bg[0:1, 96 * hh:96 * hh + 96], rhs=ones_row[0:1, 0:P], start=False, stop=True)
                    return ins
                S.op("pe", f, reads=[wgu, bg, glr_sb, ones_row], writes=[bB[1]])
                zv = bank(1)[0:96, :].rearrange("p (a b) -> p a b", a=4)[:, :, 0:P]
                S.op("act", lambda h: h.activation(out=e1[:, :, 0:P], in_=zv, func=AF.Exp, scale=-1.0), reads=[bB[1]], writes=[e1])
                S.op("act", lambda h: h.activation(out=spl[:, :, 0:P], in_=e1[:, :, 0:P], func=AF.Ln, scale=1.0, bias=one_t[0:96, 0:1]),
                     reads=[e1, one_t], writes=[spl])

            one_t = mk(es, "one_t", [128, 1])
            S.op("pool", lambda h: h.memset(one_t[:], 1.0), writes=[one_t])

            def proj_fm(col0, bk, P):
                def f(h):
                    for hh in range(4):
                        for kc in range(8):
                            ins = h.matmul(bank(bk)[0:96, hh * 128:hh * 128 + P], lhsT=Wa[:, kc, col0 + 96 * hh:col0 + 96 * hh + 96],
                                           rhs=hT[:, kc, 0:P], start=(kc == 0), stop=(kc == 7))
                    return ins
                S.op("pe", f, reads=[Wa, hT], writes=[bB[bk]])

            def proj_tm(col0, pair, P):
                for half in range(2):
                    mm(bank(2 * pair + half)[0:P, 0:384],
                       [(hT[:, kc, 0:P], Wa[:, kc, col0 + 384 * half:col0 + 384 * (half + 1)]) for kc in range(8)],
                       [Wa, hT], [bB[2 * pair + half]])

            def pair768(pair, P):
                return PS[pair][0:P, :].rearrange("p (a b) -> p a b", a=2)[:, :, 0:384]

            def gla_norm_gate(P, o_pair, r_pair):
                ov = pair768(o_pair, P).rearrange("p a (i n) -> p a i n", i=2)
                for hh in range(4):
                    S.op("act", lambda h, hh=hh: h.activation(out=sg[0:P, 0:DV], in_=ov[:, hh // 2, hh % 2, :], func=AF.Square,
                                                              accum_out=st["ssg"][0:P, hh:hh + 1]),
                         reads=[bB[2 * o_pair], bB[2 * o_pair + 1]], writes=[sg, st["ssg"]])
                S.op("act", lambda h: h.activation(out=st["lng"][0:P, :], in_=st["ssg"][0:P, :], func=AF.Ln, scale=1.0 / DV,
                                                   bias=eps_t[0:P, 0:1]), reads=[st["ssg"], eps_t], writes=[st["lng"]])
                S.op("act", lambda h: h.activation(out=st["rstdg"][0:P, :], in_=st["lng"][0:P, :], func=AF.Exp, scale=-0.5),
                     reads=[st["lng"]], writes=[st["rstdg"]])
                for hh in range(4):
                    S.op("dve", lambda h, hh=hh: h.scalar_tensor_tensor(out=on[0:P, hh * DV:(hh + 1) * DV], in0=ov[:, hh // 2, hh % 2, :],
                                                                        scalar=st["rstdg"][0:P, hh:hh + 1], in1=gn[0:P, :],
                                                                        op0=ALU.mult, op1=ALU.mult),
                         reads=[bB[2 * o_pair], bB[2 * o_pair + 1], st["rstdg"], gn], writes=[on])
                rv = pair768(r_pair, P)
                er3 = er[0:P, :].rearrange("p (a b) -> p a b", a=2)
                S.op("act", lambda h: h.activation(out=er3, in_=rv, func=AF.Exp, scale=-1.0),
                     reads=[bB[2 * r_pair], bB[2 * r_pair + 1]], writes=[er])
                S.op("pool", lambda h: h.tensor_scalar(out=er[0:P, :], in0=er[0:P, :], scalar1=1.0, scalar2=None, op0=ALU.add),
                     reads=[], writes=[er])
                S.op("dve", lambda h: h.reciprocal(out=er[0:P, :], in_=er[0:P, :]), reads=[], writes=[er])
                S.op("dve", lambda h: h.tensor_tensor(out=sg[0:P, :].rearrange("p (a b) -> p a b", a=2), in0=rv, in1=er3, op=ALU.mult),
                     reads=[bB[2 * r_pair], bB[2 * r_pair + 1], er], writes=[sg])
                S.op("dve", lambda h: h.tensor_tensor(out=mixcat[0:P, 0:768], in0=on[0:P, :], in1=sg[0:P, :], op=ALU.mult),
                     reads=[on, sg], writes=[mixcat])

            def gla_tile(row0, full, out_slot):
                P = 128
                xtile = next_xt()
                S.dma("sp", xtile[:], xin[row0:row0 + 128, :], writes=[xtile])
                rms_stats(xtile[:], P, [xtile], hb)
                norm_to(xtile[:], P, [xtile], gpre, hb)
                transpose8(hb, P, hT, tb=0)
                gate_path(P)
                ck(1)
                proj_fm(384, 2, P)
                ck(2)
                if full:
                    proj_fm(0, 3, P)
                zgate(P)
                ck(3)
                for hh in range(4):
                    S.op("dve", lambda h, hh=hh: h.tensor_tensor_scan(out=cc[:, hh, :], data0=ones96[:, :], data1=spl[:, hh, :], initial=0.0,
                                                                      op0=ALU.mult, op1=ALU.add),
                         reads=[spl, ones96], writes=[cc])
                S.op("act", lambda h: h.activation(out=Einv[:], in_=cc[:], func=AF.Exp, scale=1.0 / 16), reads=[cc], writes=[Einv])
                if full:
                    S.op("act", lambda h: h.activation(out=Edec[:], in_=cc[:], func=AF.Exp, scale=-1.0 / 16), reads=[cc], writes=[Edec])
                S.op("act", lambda h: h.activation(out=elast[0:96, 0:4], in_=cc[:, :, 127], func=AF.Exp, scale=-1.0 / 16),
                     reads=[cc], writes=[elast])
                ck(4)
                kT = bank(2)[0:96, :].rearrange("p (a b) -> p a b", a=4)
                qT = bank(3)[0:96, :].rearrange("p (a b) -> p a b", a=4)
                S.op("dve", lambda h: h.tensor_tensor(out=keT[:], in0=kT, in1=Einv[:], op=ALU.mult), reads=[bB[2], Einv], writes=[keT])
                if full:
                    S.op("dve", lambda h: h.scalar_tensor_tensor(out=qeT[:], in0=qT, scalar=DK ** -0.5, in1=Edec[:], op0=ALU.mult, op1=ALU.mult),
                         reads=[bB[3], Edec], writes=[qeT])
                for hh in range(4):
                    S.op("dve", lambda h, hh=hh: h.scalar_tensor_tensor(out=kdT[:, hh, :], in0=kT[:, hh, :], scalar=elast[0:96, hh:hh + 1],
                                                                        in1=Einv[:, hh, :], op0=ALU.mult, op1=ALU.mult),
                         reads=[bB[2], elast, Einv], writes=[kdT])

                ck(5)

                def trk(h):
                    for hh in range(4):
                        ins = h.transpose(out=bankb(0)[:, hh * 96:(hh + 1) * 96], in_=kdT[:, hh, :], identity=ident_b[0:96, 0:96])
                    return ins
                S.op("pe", trk, reads=[kdT, ident_b], writes=[bB[0]])
                S.op("act", lambda h: h.copy(out=kd_sb[:], in_=bankb(0)[:, 0:384]), reads=[bB[0]], writes=[kd_sb])
                ck(6)
                proj_tm(768, 2, P)
                S.op("act", lambda h: h.copy(out=v_sb[:].rearrange("p (a b) -> p a b", a=2), in_=pair768(2, P)),
                     reads=[bB[4], bB[5]], writes=[v_sb])
                ck(7)
                if full:
                    def fa(h):
                        for hh in range(4):
                            ins = h.matmul(bank(1)[:, hh * 128:(hh + 1) * 128], lhsT=keT[:, hh, :], rhs=qeT[:, hh, :], start=True, stop=True)
                        return ins
                    S.op("pe", fa, reads=[keT, qeT], writes=[bB[1]])
                    S.op("dve", lambda h: h.tensor_tensor(out=attnT_sb[:], in0=bank(1)[:, :].rearrange("p (a b) -> p a b", a=4),
                                                          in1=bc(cmask[:], 1, 4), op=ALU.mult),
                         reads=[bB[1], cmask], writes=[attnT_sb])

                    def fo(h):
                        for hh in range(4):
                            o = bank(4 + hh // 2)[:, (hh % 2) * DV:(hh % 2 + 1) * DV]
                            h.matmul(o, lhsT=attnT_sb[:, hh, :], rhs=v_sb[:, hh * DV:(hh + 1) * DV], start=True, stop=False)
                            ins = h.matmul(o, lhsT=qeT[:, hh, :], rhs=S_bf[:, hh, :], start=False, stop=True)
                        return ins
                    S.op("pe", fo, reads=[attnT_sb, v_sb, qeT, S_bf], writes=[bB[4], bB[5]])

                def fs(h):
                    for hh in range(4):
                        ins = h.matmul(bank(2 + hh // 2)[0:96, (hh % 2) * DV:(hh % 2 + 1) * DV], lhsT=kd_sb[:, 96 * hh:96 * hh + 96],
                                       rhs=v_sb[:, hh * DV:(hh + 1) * DV], start=True, stop=True)
                    return ins
                S.op("pe", fs, reads=[kd_sb, v_sb], writes=[bB[2], bB[3]])
                for hh in range(4):
                    S.op("dve", lambda h, hh=hh: h.scalar_tensor_tensor(out=Sst[:, hh, :], in0=Sst[:, hh, :], scalar=elast[0:96, hh:hh + 1],
                                                                        in1=bank(2 + hh // 2)[0:96, (hh % 2) * DV:(hh % 2 + 1) * DV],
                                                                        op0=ALU.mult, op1=ALU.add),
                         reads=[elast, bB[2], bB[3]], writes=[Sst])
                S.op("pool", lambda h: h.tensor_copy(out=S_bf[:], in_=Sst[:]), reads=[Sst], writes=[S_bf])
                ck(8)
                if not full:
                    return
                proj_tm(1536, 3, P)
                for c in range(2):
                    mm(bank(1)[:, c * 128:c * 128 + P], [(Wa[:, kc, 2320 + 128 * c:2320 + 128 * (c + 1)], hT[:, kc, 0:P]) for kc in range(8)],
                       [Wa, hT], [bB[1]])
                S.op("act", lambda h: h.copy(out=qmT_sb[:, :, 0:P], in_=bank(1)[:, 0:256].rearrange("p (a b) -> p a b", a=2)[:, :, 0:P]),
                     reads=[bB[1]], writes=[qmT_sb])
                ck(9)
                gla_norm_gate(P, 2, 3)
                ck(10)
                mem_attend_prompt(0, lambda c, p: qmT_sb[:, c, 0:P], [qmT_sb], P)
                ck(11)
                out_proj(P, Wo, gpost, xtile[:], xtile)
                ck(12)
                S.dma("sp", xa[out_slot * 128:(out_slot + 1) * 128, :], xtile[:], reads=[xtile], writes=[b_xa[out_slot]])

            esp = ExitStack()
            PB = []
            for par in range(4):
                d = {}
                d["hb"] = mk(esp, "p_hb%d" % par, [128, D], BF16)
                d["hT"] = mk(esp, "p_hT%d" % par, [128, 8, 128], BF16)
                d["glr"] = mk(esp, "p_glr%d" % par, [16, 128], BF16)
                d["sp"] = mk(esp, "p_sp%d" % par, [96, 4, 128])
                d["cc"] = mk(esp, "p_cc%d" % par, [96, 4, 128])
                d["kdT"] = mk(esp, "p_kdT%d" % par, [96, 4, 128], BF16)
                d["kd"] = mk(esp, "p_kd%d" % par, [128, 384], BF16)
                d["v"] = mk(esp, "p_v%d" % par, [128, 768], BF16)
                d["ss"] = mk(esp, "p_ss%d" % par, [128, 1])
                d["ln"] = mk(esp, "p_ln%d" % par, [128, 1])
                d["rs"] = mk(esp, "p_rs%d" % par, [128, 1])
                d["el"] = mk(esp, "p_el%d" % par, [96, 4])
                PB.append(d)

            def prefix_tile(t, Q):
                def q(fn, *a, **k):
                    Q.append((fn, a, k))

                par = t % 4
                d = PB[par]
                bT, bZ, bK, bX = 2 * par, 2 * par, 2 * par + 1, 2 * par + 1
                hbP, hTP, glrP, spP, ccP, kdTP, kdP, vP = d["hb"], d["hT"], d["glr"], d["sp"], d["cc"], d["kdT"], d["kd"], d["v"]
                ssP, lnP, rsP, elP = d["ss"], d["ln"], d["rs"], d["el"]
                xtile = next_xt()
                q(S.dma, "sp", xtile[:], xin[t * 128:(t + 1) * 128, :], writes=[xtile])
                q(S.op, "act", lambda h: h.activation(out=hbP[:], in_=xtile[:], func=AF.Square, accum_out=ssP[:, 0:1]), reads=[xtile], writes=[hbP, ssP])
                q(S.op, "act", lambda h: h.activation(out=lnP[:], in_=ssP[:], func=AF.Ln, scale=1.0 / D, bias=eps_t[:, 0:1]), reads=[ssP, eps_t], writes=[lnP])
                q(S.op, "act", lambda h: h.activation(out=rsP[:], in_=lnP[:], func=AF.Exp, scale=-0.5), reads=[lnP], writes=[rsP])
                q(S.op, "dve", lambda h: h.scalar_tensor_tensor(out=hbP[:], in0=xtile[:], scalar=rsP[:, 0:1], in1=gpre[:], op0=ALU.mult, op1=ALU.mult),
                     reads=[xtile, rsP, gpre], writes=[hbP])

                def ftr(h):
                    for kc in range(8):
                        ins = h.transpose(out=bankb(bT)[:, kc * 128:(kc + 1) * 128], in_=hbP[:, kc * 128:(kc + 1) * 128], identity=ident_b[:])
                    return ins
                q(S.op, "pe", ftr, reads=[hbP, ident_b], writes=[bB[bT]])
                q(S.op, "act", lambda h: h.copy(out=hTP[:].rearrange("p a b -> p (a b)"), in_=bankb(bT)[:, :]), reads=[bB[bT]], writes=[hTP])
                q(mm, bank(bZ)[0:16, 0:128], [(Wa[:, kc, 2304:2320], hTP[:, kc, :]) for kc in range(8)], [Wa, hTP], [bB[bZ]])
                q(S.op, "dve", lambda h: h.tensor_copy(out=glrP[:], in_=bank(bZ)[0:16, 0:128]), reads=[bB[bZ]], writes=[glrP])

                def fk_(h):
                    for hh in range(4):
                        for kc in range(8):
                            ins = h.matmul(bank(bK)[0:96, hh * 128:(hh + 1) * 128], lhsT=Wa[:, kc, 384 + 96 * hh:384 + 96 * hh + 96],
                                           rhs=hTP[:, kc, :], start=(kc == 0), stop=(kc == 7))
                    return ins
                q(S.op, "pe", fk_, reads=[Wa, hTP], writes=[bB[bK]])

                def fz(h):
                    for hh in range(4):
                        o = bank(bZ)[0:96, hh * 128:(hh + 1) * 128]
                        h.matmul(o, lhsT=wgu[0:16, 96 * hh:96 * hh + 96], rhs=glrP[0:16, :], start=True, stop=False)
                        ins = h.matmul(o, lhsT=bg[0:1, 96 * hh:96 * hh + 96], rhs=ones_row[0:1, :], start=False, stop=True)
                    return ins
                q(S.op, "pe", fz, reads=[wgu, bg, glrP, ones_row], writes=[bB[bZ]])
                zv = bank(bZ)[0:96, :].rearrange("p (a b) -> p a b", a=4)
                q(S.op, "act", lambda h: h.activation(out=spP[:], in_=zv, func=AF.Exp, scale=-1.0), reads=[bB[bZ]], writes=[spP])
                q(S.op, "act", lambda h: h.activation(out=spP[:], in_=spP[:], func=AF.Ln, scale=1.0, bias=one_t[0:96, 0:1]), reads=[one_t], writes=[spP])
                for hh in range(4):
                    q(S.op, "dve", lambda h, hh=hh: h.tensor_tensor_scan(out=ccP[:, hh, :], data0=ones96[:, :], data1=spP[:, hh, :], initial=0.0,
                                                                      op0=ALU.mult, op1=ALU.add), reads=[spP, ones96], writes=[ccP])
                q(S.op, "act", lambda h: h.activation(out=elP[:], in_=ccP[:, :, 127], func=AF.Exp, scale=-1.0 / 16), reads=[ccP], writes=[elP])
                q(S.op, "act", lambda h: h.activation(out=ccP[:], in_=ccP[:], func=AF.Exp, scale=1.0 / 16), reads=[], writes=[ccP])
                kT = bank(bK)[0:96, :].rearrange("p (a b) -> p a b", a=4)
                for hh in range(4):
                    q(S.op, "dve", lambda h, hh=hh: h.scalar_tensor_tensor(out=kdTP[:, hh, :], in0=kT[:, hh, :], scalar=elP[:, hh:hh + 1], in1=ccP[:, hh, :],
                                                                        op0=ALU.mult, op1=ALU.mult), reads=[bB[bK], elP, ccP], writes=[kdTP])

                def ftk(h):
                    for hh in range(4):
                        ins = h.transpose(out=bankb(bT)[:, hh * 96:(hh + 1) * 96], in_=kdTP[:, hh, :], identity=ident_b[0:96, 0:96])
                    return ins
                q(S.op, "pe", ftk, reads=[kdTP, ident_b], writes=[bB[bT]])
                q(S.op, "act", lambda h: h.copy(out=kdP[:], in_=bankb(bT)[:, 0:384]), reads=[bB[bT]], writes=[kdP])
                for half in range(2):
                    q(mm, bank(bX)[:, 0:384], [(hTP[:, kc, :], Wa[:, kc, 768 + 384 * half:768 + 384 * (half + 1)]) for kc in range(8)], [Wa, hTP], [bB[bX]])
                    if half == 0:
                        q(S.op, "act", lambda h: h.copy(out=vP[:, 0:384], in_=bank(bX)[:, 0:384]), reads=[bB[bX]], writes=[vP])
                    else:
                        q(S.op, "dve", lambda h: h.tensor_copy(out=vP[:, 384:768], in_=bank(bX)[:, 0:384]), reads=[bB[bX]], writes=[vP])
                for pr_, bk_ in ((0, bX), (1, bK)):
                    def fs_(h, pr_=pr_, bk_=bk_):
                        for i_ in range(2):
                            hh = 2 * pr_ + i_
                            ins = h.matmul(bank(bk_)[0:96, i_ * DV:(i_ + 1) * DV], lhsT=kdP[:, 96 * hh:96 * hh + 96], rhs=vP[:, hh * DV:(hh + 1) * DV],
                                           start=True, stop=True)
                        return ins
                    q(S.op, "pe", fs_, reads=[kdP, vP], writes=[bB[bk_]])
                    for i_ in range(2):
                        hh = 2 * pr_ + i_
                        q(S.op, "dve", lambda h, hh=hh, i_=i_, bk_=bk_: h.scalar_tensor_tensor(out=Sst[:, hh, :], in0=Sst[:, hh, :], scalar=elP[:, hh:hh + 1],
                                                                                            in1=bank(bk_)[0:96, i_ * DV:(i_ + 1) * DV], op0=ALU.mult, op1=ALU.add),
                             reads=[elP, bB[bk_]], writes=[Sst])

            for t0 in range(0, NPRE, 4):
                lists = []
                for t in range(t0, t0 + 4):
                    if t < NPRE:
                        Q = []
                        prefix_tile(t, Q)
                        lists.append(Q)
                for i_ in range(max(len(L) for L in lists)):
                    for L in lists:
                        if i_ < len(L):
                            fn_, a_, k_ = L[i_]
                            fn_(*a_, **k_)
            S.op("pool", lambda h: h.tensor_copy(out=S_bf[:], in_=Sst[:]), reads=[Sst], writes=[S_bf])
            S.flush()
            ck(100)
            esp.close()
            for t in range(NT1):
                gla_tile((NPRE + t) * 128, True, t)
            S.dma("sp", stp.rearrange("h d v -> d h v"), Sst[:], reads=[Sst])

            P = NS
            kq_s = mk(es, "kq_s", [96, 4, NS])
            eg = mk(es, "eg", [96, 4, NS])
            ktok = mk(es, "ktok", [NS, 384])
            vtok = mk(es, "vtok", [NS, 768])
            kmask = mk(es, "kmask", [NS, NS, 384])
            qmask = mk(es, "qmask", [96, 4, NS, NS])
            Sin = [mk(es, "Sin%d" % i, [96, 4, DV]) for i in range(2)]
            Snew = [mk(es, "Snew%d" % i, [96, 4, DV]) for i in range(2)]
            qm_tok = mk(es, "qm_tok", [NS, 256])
            rms_stats(xs_sb[:], P, [xs_sb], hb)
            norm_to(xs_sb[:], P, [xs_sb], gpre, hb)
            transpose8(hb, P, hT, tb=0)
            gate_path(P)
            proj_fm(384, 2, P)
            proj_fm(0, 3, P)
            zgate(P)
            S.op("act", lambda h: h.activation(out=eg[:], in_=spl[:, :, 0:P], func=AF.Exp, scale=-1.0 / 16), reads=[spl], writes=[eg])
            qTs = bank(3)[0:96, :].rearrange("p (a b) -> p a b", a=4)[:, :, 0:P]
            S.op("dve", lambda h: h.tensor_scalar(out=kq_s[:], in0=qTs, scalar1=DK ** -0.5, scalar2=None, op0=ALU.mult),
                 reads=[bB[3]], writes=[kq_s])
            S.op("dve", lambda h: h.tensor_tensor(out=qmask[:], in0=bc(kq_s[:], 3, NS), in1=bc(id16r[0:96, :, :], 1, 4), op=ALU.mult),
                 reads=[kq_s, id16r], writes=[qmask])
            mm(bank(2)[0:P, 0:384], [(hT[:, kc, 0:P], Wa[:, kc, 384:768]) for kc in range(8)], [Wa, hT], [bB[2]])
            S.op("act", lambda h: h.copy(out=ktok[:], in_=bank(2)[0:P, 0:384]), reads=[bB[2]], writes=[ktok])
            S.op("dve", lambda h: h.tensor_tensor(out=kmask[:], in0=bc(ktok[:], 1, NS), in1=bc(ident_f[0:NS, 0:NS], 2, 384), op=ALU.mult),
                 reads=[ktok, ident_f], writes=[kmask])
            proj_tm(768, 2, P)
            S.op("act", lambda h: h.copy(out=vtok[:].rearrange("p (a b) -> p a b", a=2), in_=pair768(2, P)), reads=[bB[4], bB[5]], writes=[vtok])
            mm(bank(1)[0:P, 0:256], [(hT[:, kc, 0:P], Wa[:, kc, 2320:2576]) for kc in range(8)], [Wa, hT], [bB[1]])
            S.op("act", lambda h: h.copy(out=qm_tok[:], in_=bank(1)[0:P, 0:256]), reads=[bB[1]], writes=[qm_tok])
            proj_tm(1536, 3, P)
            ck(20)
            zero_acc([bank(4)[0:NS, 0:384], bank(5)[0:NS, 0:384]], [bB[4], bB[5]])
            for i in range(NS):
                si, sn = Sin[i % 2], Snew[i % 2]
                S.dma("sp", si[:], st_in[i].rearrange("h d v -> d h v"), writes=[si])

                def fk(h, i=i):
                    for hh in range(4):
                        ins = h.matmul(bank(2 + hh // 2)[0:96, (hh % 2) * DV:(hh % 2 + 1) * DV], lhsT=kmask[0:NS, i, 96 * hh:96 * hh + 96],
                                       rhs=vtok[0:NS, hh * DV:(hh + 1) * DV], start=True, stop=True)
                    return ins
                S.op("pe", fk, reads=[kmask, vtok], writes=[bB[2], bB[3]])
                for hh in range(4):
                    S.op("dve", lambda h, hh=hh, i=i, si=si, sn=sn: h.scalar_tensor_tensor(
                        out=sn[:, hh, :], in0=si[:, hh, :], scalar=eg[:, hh, i:i + 1],
                        in1=bank(2 + hh // 2)[0:96, (hh % 2) * DV:(hh % 2 + 1) * DV], op0=ALU.mult, op1=ALU.add),
                        reads=[si, eg, bB[2], bB[3]], writes=[sn])
                S.dma("sp", sts[i].rearrange("h d v -> d h v"), sn[:], reads=[sn])

                def fq(h, i=i, sn=sn):
                    for hh in range(4):
                        ins = h.matmul(bank(4 + hh // 2)[0:NS, (hh % 2) * DV:(hh % 2 + 1) * DV], lhsT=qmask[:, hh, i, :], rhs=sn[:, hh, :],
                                       start=False, stop=(i == NS - 1 and hh % 2 == 1))
                    return ins
                S.op("pe", fq, reads=[qmask, sn], writes=[bB[4], bB[5]])
            ck(21)
            gla_norm_gate(P, 2, 3)
            ck(22)
            sab = alloc_sample_attn(es, "p2", onesel=TB_sub(kmask, 128))

            def mem_sample(l, qtok, sab):
                Kc, Vc, prod, sc8, pp8, Pm, onesel = sab
                zero_acc([bank(1)[0:NS, 0:260]], [bB[1]])
                for i in range(NS):
                    kc_, vc_ = Kc[i % 2], Vc[i % 2]
                    S.dma("sp", kc_[:], mk_in[l, i].rearrange("(mc p) n -> p mc n", p=128), writes=[kc_])
                    for mc in range(2):
                        S.dma("sp", vc_[:, mc, :, 0:64], mv_in[l, i, mc * 128:(mc + 1) * 128, :].rearrange("p (a b) -> p a b", a=4), writes=[vc_])
                    mm(bank(0)[:, 0:256], [(onesel[0:NS, i, :], qtok[0:NS, :])], [onesel, qtok], [bB[0]])
                    S.op("dve", lambda h, kc_=kc_: h.tensor_tensor(out=prod[:, 0:512].rearrange("p (a b) -> p a b", a=2), in0=kc_[:],
                                                                   in1=bc(bank(0)[:, 0:256], 1, 2), op=ALU.mult),
                         reads=[kc_, bB[0]], writes=[prod])
                    S.op("dve", lambda h: h.tensor_reduce(out=sc8[:, 0:8], in_=prod[:, 0:512].rearrange("p (a b) -> p a b", b=64), axis=AX.X, op=ALU.add),
                         reads=[prod], writes=[sc8])
                    S.op("act", lambda h: h.activation(out=pp8[:, 0:8], in_=sc8[:, 0:8], func=AF.Exp, scale=0.125), reads=[sc8], writes=[pp8])
                    S.op("dve", lambda h, i=i: h.tensor_tensor(out=Pm[:, 0:8, :], in0=bc(pp8[:, 0:8], 2, NS), in1=bc(id16r[:, i, :], 1, 8), op=ALU.mult),
                         reads=[pp8, id16r], writes=[Pm])

                    def fm(h, i=i, vc_=vc_):
                        for hh in range(4):
                            for mc in range(2):
                                ins = h.matmul(bank(1)[0:NS, hh * 65:(hh + 1) * 65], lhsT=Pm[:, mc * 4 + hh, :], rhs=vc_[:, mc, hh, :],
                                               start=False, stop=(i == NS - 1 and mc == 1 and hh == 3))
                        return ins
                    S.op("pe", fm, reads=[Pm, vc_], writes=[bB[1]])
                finish_mem(NS)

            mem_sample(0, qm_tok, sab)
            ck(23)
            mem_sample_fn.append(mem_sample)
            out_proj(P, Wo, gpost, xs_sb[:], xs_sb)
            S.flush()
        if 0 <= stop <= 1:
            S.dma("sp", ys, xs_sb[:], reads=[xs_sb])
            S.flush()
            return nc

        def mlp_phase(l, tiles, es):
            Wu = mk(es, "Wu%d" % l, [128, 8, 4096], BF16)
            Wd = mk(es, "Wd%d" % l, [128, 32, D], BF16)
            gpre = mk(es, "gfpre%d" % l, [128, D])
            gpost = mk(es, "gfpost%d" % l, [128, D])
            hTg = mk(es, "hTg%d" % l, [128, 8, 256], BF16)
            actT = mk(es, "actT%d" % l, [128, 32, 256], BF16)
            load_w(Wu, w_up[l], 8, 4096)
            load_w(Wd, w_down[l], 32, D)
            load_gain(gpre, g_ffn_pre[l, :])
            load_gain(gpost, g_ffn_post[l, :])
            groups = []
            i = 0
            while i < len(tiles):
                if tiles[i][0] == 128 and i + 1 < len(tiles) and tiles[i + 1][0] == 128:
                    groups.append(tiles[i:i + 2])
                    i += 2
                else:
                    groups.append(tiles[i:i + 1])
                    i += 1
            xtl = {}

            def n_elem(g, only=None):
                for ti, (P, src, sbuf, dst, dbuf) in enumerate(groups[g]):
                    if only is not None and ti != only:
                        continue
                    if src is None:
                        xtile, xap = xs_sb, xs_sb[:]
                    else:
                        xtile = next_xt()
                        xap = xtile[0:P, :]
                        S.dma("sp", xap, src, reads=[sbuf], writes=[xtile])
                    xtl[(g, ti)] = (xtile, xap)

            def n_pe(g):
                off = 0
                for ti, (P, src, sbuf, dst, dbuf) in enumerate(groups[g]):
                    xtile, xap = xtl[(g, ti)]
                    rms_stats(xap, P, [xtile], hb)
                    norm_to(xap, P, [xtile], gpre, hb)
                    transpose8(hb, P, hT, tb=ti % 2)
                    S.op("act", lambda h, off=off, P=P: h.copy(out=hTg[:, :, off:off + P], in_=hT[:, :, 0:P]), reads=[hT], writes=[hTg])
                    off += P

            def up(g):
                N = sum(t[0] for t in groups[g])
                for ffc in range(32):
                    bk = (2, 3, 0, 1)[ffc % 4]
                    mm(bank(bk)[:, 0:N], [(Wu[:, kc, ffc * 128:(ffc + 1) * 128], hTg[:, kc, 0:N]) for kc in range(8)], [Wu, hTg], [bB[bk]])
                    S.op("act", lambda h, ffc=ffc, bk=bk, N=N: h.activation(out=actT[:, ffc, 0:N], in_=bank(bk)[:, 0:N], func=AF.Square),
                         reads=[bB[bk]], writes=[actT])
                    S.op("dve", lambda h, ffc=ffc, bk=bk, N=N: h.scalar_tensor_tensor(out=actT[:, ffc, 0:N], in0=bank(bk)[:, 0:N], scalar=0.0,
                                                                                      in1=actT[:, ffc, 0:N], op0=ALU.is_gt, op1=ALU.mult),
                         reads=[bB[bk]], writes=[actT])

            def down(g):
                off = 0
                for ti, (P, src, sbuf, dst, dbuf) in enumerate(groups[g]):
                    xtile, xap = xtl[(g, ti)]
                    pair = 2 + ti % 2
                    for half in range(2):
                        mm(bank(2 * pair + half)[0:P, :], [(actT[:, ffc, off:off + P], Wd[:, ffc, half * 512:(half + 1) * 512]) for ffc in range(32)],
                           [actT, Wd], [bB[2 * pair + half]])
                    post_norm_residual(pair, P, gpost, xap, xtile)
                    if dst is not None:
                        S.dma("sp", dst, xap, reads=[xtile], writes=[dbuf])
                    off += P

            n_elem(0)
            n_pe(0)
            for g in range(len(groups)):
                up(g)
                if g + 1 < len(groups):
                    n_elem(g + 1, only=0)
                down(g)
                if g + 1 < len(groups):
                    if len(groups[g + 1]) > 1:
                        n_elem(g + 1, only=1)
                    n_pe(g + 1)

        with ExitStack() as es:
            tiles = [(128, xa[t * 128:(t + 1) * 128, :], b_xa[t], xb[t * 128:(t + 1) * 128, :], b_xb[t]) for t in range(NT1)]
            tiles.append((NS, None, None, None, None))
            mlp_phase(0, tiles, es)
            S.flush()
        if 0 <= stop <= 2:
            S.dma("sp", ys, xs_sb[:], reads=[xs_sb])
            S.flush()
            return nc

        with ExitStack() as es:
            es2 = ExitStack()
            Wkv = mk(es, "Wkv", [128, 8, 512], BF16)
            Wb = mk(es, "Wb", [128, 8, 1024], BF16)
            Wo = mk(es, "Wo1", [128, 8, D], BF16)
            gkv = mk(es, "gkv", [128, D])
            gpre = mk(es, "gpre1", [128, D])
            gpost = mk(es, "gpost1", [128, D])
            EBs = mk(es, "EBs", [128, 12])
            eb0 = mk(es, "eb0", [NS, 12])
            esink = mk(es, "esink", [128, 12])
            rb_sb = mk(es, "rb_sb", [32, 12])
            rbh = mk(es, "rbh", [32, 12, 128])
            ohu = mk(es, "ohu", [32, 384])
            validu = mk(es, "validu", [128, 384])
            ebt = mk(es, "ebt", [128, 384])
            kvf = mk(es, "kvf1", [128, 512])
            KT_all = mk(es2, "KT_all", [128, 2, 2, NT1 * 128], BF16)
            V_all = mk(es2, "V_all", [128, NT1, 4, 66], BF16)
            QT_all = mk(es2, "QT_all", [128, 6, NT * 128], BF16)
            QmT_all = mk(es2, "QmT_all", [128, 2, NT * 128], BF16)
            EB = mk(es2, "EB", [128, 12, 2, 128])
            es_sb = mk(es2, "es_sb", [128, 2, 384])
            pTg = [mk(es2, "pTg%d" % g, [128, 2, 3, 128], BF16) for g in range(4)]

            load_w(Wkv, w_kv, 8, 512)
            wbv = w_in_b.rearrange("(kc p) n -> p kc n", p=128)
            for i in range(6):
                s_, j_ = divmod(i, 3)
                for half in range(2):
                    hd = 6 * s_ + 3 * half + j_
                    S.dma("pool", Wb[:, :, i * 128 + half * 64:i * 128 + half * 64 + 64], wbv[:, :, hd * 64:(hd + 1) * 64], writes=[Wb])
            S.dma("pool", Wb[:, :, 768:1024], wbv[:, :, 768:1024], writes=[Wb])
            load_w(Wo, w_out[1], 8, D)
            load_gain(gkv, g_kv[0, :])
            load_gain(gpre, g_mix_pre[1, :])
            load_gain(gpost, g_mix_post[1, :])
            S.op("pool", lambda h: h.memset(V_all[:], 1.0), writes=[V_all])
            S.op("pool", lambda h: h.memset(KT_all[:], 0.0), writes=[KT_all])
            S.dma("sp", rb_sb[:], rel_bias, writes=[rb_sb])
            S.dma("sp", ohu[:], c_ohu, writes=[ohu])
            S.dma("sp", validu[:], c_valid, writes=[validu])
            S.dma("sp", esink[:], sinks[0, :].partition_broadcast(128), writes=[esink])
            S.op("act", lambda h: h.activation(out=esink[:], in_=esink[:], func=AF.Exp), reads=[], writes=[esink])
            S.op("dve", lambda h: h.tensor_copy(out=rbh[:], in_=bc(rb_sb[:], 2, 128)), reads=[rb_sb], writes=[rbh])
            for hd in range(12):
                mm(bank(2)[:, 0:384], [(rbh[:, hd, :], ohu[:])], [rbh, ohu], [bB[2]])
                S.op("act", lambda h: h.activation(out=ebt[:], in_=bank(2)[:, 0:384], func=AF.Exp), reads=[bB[2]], writes=[ebt])
                S.op("dve", lambda h: h.tensor_tensor(out=ebt[:], in0=ebt[:], in1=validu[:], op=ALU.mult), reads=[validu], writes=[ebt])
                S.dma("sp", ebd.ap()[hd], ebt[:], reads=[ebt], writes=[b_ebd])
            for hd in range(12):
                for kb in range(2):
                    S.dma("sp", EB[:, hd, kb, :], bass.AP(ebd, hd * 128 * 384 + 255 - 128 * kb, [[383, 128], [1, 128]]), reads=[b_ebd], writes=[EB])
                S.dma("sp", EBs[:, hd:hd + 1], bass.AP(ebd, hd * 128 * 384 + 255, [[383, 128], [1, 1]]), reads=[b_ebd], writes=[EBs], allow_slow_non_contiguous=True)
                S.dma("sp", eb0[:, hd:hd + 1], bass.AP(ebd, hd * 128 * 384 + 127, [[384, NS], [1, 1]]), reads=[b_ebd], writes=[eb0], allow_slow_non_contiguous=True)
            def b1_tile(t):
                P = 128
                xtile = next_xt()
                S.dma("sp", xtile[:], xb[t * 128:(t + 1) * 128, :], reads=[b_xb[t]], writes=[xtile])
                rms_stats(xtile[:], P, [xtile], hb)
                norm_to(xtile[:], P, [xtile], gkv, hb)
                transpose8(hb, P, hT, tb=0)
                for s_ in range(2):
                    mm(bank(2)[:, s_ * 128:(s_ + 1) * 128], [(Wkv[:, kc, s_ * 128:(s_ + 1) * 128], hT[:, kc, :]) for kc in range(8)], [Wkv, hT], [bB[2]])
                S.op("act", lambda h: h.copy(out=KT_all[0:64, 0, :, t * 128:(t + 1) * 128], in_=bank(2)[0:64, 0:256].rearrange("p (a b) -> p a b", a=2)),
                     reads=[bB[2]], writes=[KT_all])
                S.op("act", lambda h: h.copy(out=KT_all[64:128, 1, :, t * 128:(t + 1) * 128], in_=bank(2)[64:128, 0:256].rearrange("p (a b) -> p a b", a=2)),
                     reads=[bB[2]], writes=[KT_all])
                mm(bank(3)[:, 0:512], [(hT[:, kc, :], Wkv[:, kc, :]) for kc in range(8)], [Wkv, hT], [bB[3]])
                S.op("dve", lambda h: h.tensor_copy(out=V_all[:, t, :, 0:64], in_=bank(3)[:, 256:512].rearrange("p (a b) -> p a b", a=4)),
                     reads=[bB[3]], writes=[V_all])
                if t == NT:
                    S.op("act", lambda h: h.copy(out=kvf[:], in_=bank(3)[:, 0:512]), reads=[bB[3]], writes=[kvf])
                    S.dma("sp", swkp, kvf[:, 0:256], reads=[kvf])
                    S.dma("sp", swvp, kvf[:, 256:512], reads=[kvf])
                if t == 0:
                    return
                norm_to(xtile[:], P, [xtile], gpre, hb2)
                transpose8(hb2, P, hT2, tb=1)
                for i in range(8):
                    bk = 4 + i % 4
                    mm(bank(bk)[:, 0:128], [(Wb[:, kc, i * 128:(i + 1) * 128], hT2[:, kc, :]) for kc in range(8)], [Wb, hT2], [bB[bk]])
                    dst = QT_all[:, i, (t - 1) * 128:t * 128] if i < 6 else QmT_all[:, i - 6, (t - 1) * 128:t * 128]
                    S.op("act" if i % 2 else "dve",
                         (lambda h, dst=dst, bk=bk: h.copy(out=dst, in_=bank(bk)[:, 0:128])) if i % 2 else
                         (lambda h, dst=dst, bk=bk: h.tensor_copy(out=dst, in_=bank(bk)[:, 0:128])),
                         reads=[bB[bk]], writes=[QT_all if i < 6 else QmT_all])

            for t in range(NT1):
                b1_tile(t)
                if t == 0:
                    S.op("dve", lambda h: h.tensor_scalar(out=V_all[:, 0, :, :], in0=V_all[:, 0, :, :], scalar1=flag[:, 0:1], scalar2=None, op0=ALU.mult),
                         reads=[flag], writes=[V_all])

            def swa_finish(P, extra=None):
                o3 = [bank(6 + i)[0:P, 0:390].rearrange("p (a b) -> p a b", a=6) for i in range(2)]
                for i in range(2):
                    S.op("dve", lambda h, i=i: h.tensor_tensor(out=den12[0:P, 6 * i:6 * i + 6], in0=o3[i][:, :, 64], in1=esink[0:P, 6 * i:6 * i + 6], op=ALU.add),
                         reads=[bB[6 + i], esink], writes=[den12])
                if extra is not None:
                    S.op("dve", lambda h: h.tensor_tensor(out=den12[0:P, :], in0=den12[0:P, :], in1=extra[0:P, :], op=ALU.add),
                         reads=[extra], writes=[den12])
                S.op("dve", lambda h: h.reciprocal(out=rden12[0:P, :], in_=den12[0:P, :]), reads=[den12], writes=[rden12])

            def swa_norm(P, srcs, src_reads):
                for i in range(2):
                    S.op("dve", lambda h, i=i: h.tensor_tensor(out=mixcat[0:P, 384 * i:384 * (i + 1)].rearrange("p (a b) -> p a b", a=6), in0=srcs[i],
                                                               in1=bc(rden12[0:P, 6 * i:6 * i + 6], 2, 64), op=ALU.mult),
                         reads=list(src_reads) + [rden12], writes=[mixcat])

            def b2(t):
                P = 128
                ebt_ = EB
                for g in range(4):
                    s_, p_ = divmod(g, 2)
                    pr = 1 + g % 2
                    def fsc(h, g=g, s_=s_, p_=p_, pr=pr):
                        for kb in range(2):
                            ins = h.matmul(bank(2 * pr + kb)[:, 0:384], lhsT=KT_all[:, p_, s_, (t + kb) * 128:(t + kb + 1) * 128],
                                           rhs=QT_all[:, 3 * s_:3 * s_ + 3, t * 128:(t + 1) * 128], start=True, stop=True)
                        return ins
                    S.op("pe", fsc, reads=[KT_all, QT_all], writes=[bB[2 * pr], bB[2 * pr + 1]])
                    S.op("act", lambda h, pr=pr: h.activation(out=es_sb[:], in_=pair768(pr, 128), func=AF.Exp, scale=0.125),
                         reads=[bB[2 * pr], bB[2 * pr + 1]], writes=[es_sb])
                    S.op("dve", lambda h, g=g, ebt_=ebt_: h.tensor_tensor(out=pTg[g][:], in0=es_sb[:].rearrange("p a (j q) -> p a j q", j=3),
                                                                          in1=ebt_[:, 3 * g:3 * g + 3, :, :].rearrange("p j a q -> p a j q"), op=ALU.mult),
                         reads=[es_sb, ebt_], writes=[pTg[g]])
                    def fo(h, g=g):
                        for j in range(3):
                            hd = 3 * g + j
                            for kb in range(2):
                                ins = h.matmul(bank(6 + hd // 6)[:, (hd % 6) * 65:(hd % 6 + 1) * 65], lhsT=pTg[g][:, kb, j, :],
                                               rhs=V_all[:, t + kb, g, 0:65], start=(kb == 0), stop=(kb == 1))
                        return ins
                    S.op("pe", fo, reads=[pTg[g], V_all], writes=[bB[6], bB[7]])
                swa_finish(P)
                o3 = [bank(6 + i)[0:P, 0:390].rearrange("p (a b) -> p a b", a=6)[:, :, 0:64] for i in range(2)]
                swa_norm(P, o3, [bB[6], bB[7]])
                mem_attend_prompt(1, lambda c, p: QmT_all[:, c, t * 128:(t + 1) * 128], [QmT_all], P)
                xtile = next_xt()
                S.dma("sp", xtile[:], xb[(t + 1) * 128:(t + 2) * 128, :], reads=[b_xb[t + 1]], writes=[xtile])
                out_proj(P, Wo, gpost, xtile[:], xtile)
                S.dma("sp", xa[t * 128:(t + 1) * 128, :], xtile[:], reads=[xtile], writes=[b_xa[t]])

            for t in range(NT):
                b2(t)

            S.flush()
            es2.close()
            P = NS
            sab = alloc_sample_attn(es, "p4")
            Kc, Vc, prod, sc8, pp8, Pm, onesel = sab
            knew = mk(es, "knew", [NS, 512])
            qtok = mk(es, "qtok", [NS, 1024])
            Ks = [mk(es, "Ks%d" % i, [128, 4, 64]) for i in range(2)]
            Vs = [mk(es, "Vs%d" % i, [128, 4, 65]) for i in range(2)]
            sn12 = mk(es, "sn12", [NS, 12])
            pn12 = mk(es, "pn12", [NS, 12])
            tmpv = mk(es, "tmpv", [NS, 768])
            osum = mk(es, "osum", [NS, 768])
            S.op("pool", lambda h: h.memset(Vs[0][:], 1.0), writes=[Vs[0]])
            S.op("pool", lambda h: h.memset(Vs[1][:], 1.0), writes=[Vs[1]])
            rms_stats(xs_sb[:], P, [xs_sb], hb)
            norm_to(xs_sb[:], P, [xs_sb], gkv, hb)
            transpose8(hb, P, hT, tb=0)
            mm(bank(3)[0:P, 0:512], [(hT[:, kc, 0:P], Wkv[:, kc, :]) for kc in range(8)], [Wkv, hT], [bB[3]])
            S.op("act", lambda h: h.copy(out=knew[:], in_=bank(3)[0:P, 0:512]), reads=[bB[3]], writes=[knew])
            norm_to(xs_sb[:], P, [xs_sb], gpre, hb2)
            transpose8(hb2, P, hT2, tb=1)
            for half in range(2):
                mm(bank(4 + half)[0:P, :], [(hT2[:, kc, 0:P], Wb[:, kc, half * 512:(half + 1) * 512]) for kc in range(8)], [Wb, hT2], [bB[4 + half]])
            S.op("act", lambda h: h.copy(out=qtok[:], in_=PS[2][0:P, :]), reads=[bB[4], bB[5]], writes=[qtok])
            S.dma("sp", swks[:, 0:127, :], swk_in[:, 1:128, :])
            S.dma("sp", swvs[:, 0:127, :], swv_in[:, 1:128, :])
            S.dma("sp", swks[:, 127, :], knew[:, 0:256], reads=[knew])
            S.dma("sp", swvs[:, 127, :], knew[:, 256:512], reads=[knew])

            def qview(ap2d, s_):
                return ap2d[:, s_ * 384:(s_ + 1) * 384].rearrange("p (j a d) -> p a j d", j=3, a=2)

            zero_acc([bank(6)[0:NS, 0:390], bank(7)[0:NS, 0:390]], [bB[6], bB[7]])
            for i in range(NS):
                ks_, vs_ = Ks[i % 2], Vs[i % 2]
                S.dma("sp", ks_[:], swk_in[i].rearrange("w (g d) -> w g d", g=4), writes=[ks_])
                S.dma("sp", vs_[:, :, 0:64], swv_in[i].rearrange("w (g d) -> w g d", g=4), writes=[vs_])
                for half in range(2):
                    mm(bank(2 + half)[:, 0:384], [(onesel[0:NS, i, :], qtok[0:NS, half * 384:(half + 1) * 384])], [onesel, qtok], [bB[2 + half]])
                for s_ in range(2):
                    S.op("dve", lambda h, s_=s_, ks_=ks_: h.tensor_tensor(
                        out=prod[:, 384 * s_:384 * (s_ + 1)].rearrange("p (a j d) -> p a j d", a=2, j=3),
                        in0=bc(ks_[:, 2 * s_:2 * s_ + 2, :], 2, 3), in1=qview(bank(2 + s_)[:, 0:384], 0), op=ALU.mult),
                        reads=[ks_, bB[2 + s_]], writes=[prod])
                S.op("dve", lambda h: h.tensor_reduce(out=sc8[:, :], in_=prod[:, :].rearrange("p (a b) -> p a b", b=64), axis=AX.X, op=ALU.add),
                     reads=[prod], writes=[sc8])
                S.op("act", lambda h: h.activation(out=pp8[:, :], in_=sc8[:, :], func=AF.Exp, scale=0.125), reads=[sc8], writes=[pp8])
                S.op("dve", lambda h: h.tensor_tensor(out=pp8[:, :], in0=pp8[:, :], in1=EBs[:, :], op=ALU.mult), reads=[EBs], writes=[pp8])
                S.op("dve", lambda h, i=i: h.tensor_tensor(out=Pm[:, :, :], in0=bc(pp8[:, :], 2, NS), in1=bc(id16r[:, i, :], 1, 12), op=ALU.mult),
                     reads=[pp8, id16r], writes=[Pm])

                def fo2(h, i=i, vs_=vs_):
                    for hd in range(12):
                        ins = h.matmul(bank(6 + hd // 6)[0:NS, (hd % 6) * 65:(hd % 6 + 1) * 65], lhsT=Pm[:, hd, :], rhs=vs_[:, hd // 3, :],
                                       start=False, stop=(i == NS - 1 and hd % 6 == 5))
                    return ins
                S.op("pe", fo2, reads=[Pm, vs_], writes=[bB[6], bB[7]])
            for s_ in range(2):
                S.op("dve", lambda h, s_=s_: h.tensor_tensor(
                    out=prod[0:NS, 384 * s_:384 * (s_ + 1)].rearrange("p (a j d) -> p a j d", a=2, j=3),
                    in0=bc(knew[:, 0:256].rearrange("p (g d) -> p g d", g=4)[:, 2 * s_:2 * s_ + 2, :], 2, 3), in1=qview(qtok[:, 0:768], s_), op=ALU.mult),
                    reads=[knew, qtok], writes=[prod])
            S.op("dve", lambda h: h.tensor_reduce(out=sn12[:], in_=prod[0:NS, :].rearrange("p (a b) -> p a b", b=64), axis=AX.X, op=ALU.add),
                 reads=[prod], writes=[sn12])
            S.op("act", lambda h: h.activation(out=pn12[:], in_=sn12[:], func=AF.Exp, scale=0.125), reads=[sn12], writes=[pn12])
            S.op("dve", lambda h: h.tensor_tensor(out=pn12[:], in0=pn12[:], in1=eb0[:], op=ALU.mult), reads=[eb0], writes=[pn12])
            S.op("dve", lambda h: h.tensor_tensor(out=tmpv[:].rearrange("p (g j d) -> p g j d", g=4, j=3),
                                                  in0=bc(knew[:, 256:512].rearrange("p (g d) -> p g d", g=4), 2, 3),
                                                  in1=bc(pn12[:].rearrange("p (g j) -> p g j", g=4), 3, 64), op=ALU.mult),
                 reads=[knew, pn12], writes=[tmpv])
            swa_finish(P, extra=pn12)
            for i in range(2):
                S.op("dve", lambda h, i=i: h.tensor_tensor(out=osum[:, 384 * i:384 * (i + 1)].rearrange("p (a b) -> p a b", a=6),
                                                           in0=bank(6 + i)[0:P, 0:390].rearrange("p (a b) -> p a b", a=6)[:, :, 0:64],
                                                           in1=tmpv[:, 384 * i:384 * (i + 1)].rearrange("p (a b) -> p a b", a=6), op=ALU.add),
                     reads=[bB[6 + i], tmpv], writes=[osum])
            swa_norm(P, [osum[:, 384 * i:384 * (i + 1)].rearrange("p (a b) -> p a b", a=6) for i in range(2)], [osum])
            mem_sample_fn[0](1, TB_view(qtok, 768), sab)
            out_proj(P, Wo, gpost, xs_sb[:], xs_sb)
            S.flush()
        if 0 <= stop <= 3:
            S.dma("sp", ys, xs_sb[:], reads=[xs_sb])
            S.flush()
            return nc

        with ExitStack() as es:
            tiles = [(128, xa[t * 128:(t + 1) * 128, :], b_xa[t], y[t * 128:(t + 1) * 128, :], b_y) for t in range(NT)]
            tiles.append((NS, None, None, None, None))
            mlp_phase(1, tiles, es)
            S.dma("sp", ys, xs_sb[:], reads=[xs_sb])
            S.flush()
    return nc


class TB_sub:
    def __init__(self, tb, n):
        self.tb = tb
        self.b = tb.b
        self.n = n

    def __getitem__(self, k):
        if k == slice(None, None, None):
            return self.tb.t[:, :, 0:self.n]
        a, i, c = k
        assert c == slice(None, None, None)
        return self.tb.t[a, i, 0:self.n]


class TB_view:
    def __init__(self, tb, c0):
        self.tb = tb
        self.b = tb.b
        self.c0 = c0

    def __getitem__(self, k):
        rows, cols = k
        assert cols == slice(None, None, None)
        return self.tb.t[rows, self.c0:self.c0 + 256]


def _consts():
    c = {}
    c["c_ident"] = np.eye(128, dtype=np.float32)
    s = np.arange(128)[:, None]
    t = np.arange(128)[None, :]
    c["c_cmask"] = (s <= t).astype(np.float32)
    u = np.arange(384)
    dist = u - 127
    valid = (dist >= 0) & (dist < 128)
    n = np.maximum(dist, 0)
    nf = np.maximum(n, 1).astype(np.float32)
    large = 16 + (np.log(nf / np.float32(16)) / np.float32(math.log(128 / 16)) * np.float32(16)).astype(np.int32)
    large = np.minimum(large, 31)
    bucket = np.where(n < 16, n, large)
    oh = np.zeros((32, 384), np.float32)
    oh[bucket, u] = 1.0
    oh[:, ~valid] = 0.0
    c["c_ohu"] = oh
    c["c_valid"] = np.repeat(valid.astype(np.float32)[None, :], 128, axis=0)
    c["c_id16r"] = np.repeat(np.eye(16, dtype=np.float32).reshape(1, 256), 128, axis=0)
    return c


_NC_CACHE = {}


def kernel(**inp):
    NT, NPRE, NS = 16, 47, 16
    f = lambda a: np.ascontiguousarray(np.asarray(a, dtype=np.float32))
    xp = f(inp["x_prompt"])
    B, L, _ = xp.shape
    NT = L // (4 * 128)
    NPRE = 3 * NT - 1
    key = (NT, NPRE, NS)
    if key not in _NC_CACHE:
        import os
        _NC_CACHE[key] = build(NT, NPRE, NS, stop=int(os.environ.get('KSTOP', '99')))
    nc = _NC_CACHE[key]
    consts = _consts()
    shared = {}
    for k in ("norm_mix_pre", "norm_mix_post", "norm_ffn_pre", "norm_ffn_post", "norm_mem", "w_mem_kv", "w_gate_up", "b_gate",
              "gla_norm", "sinks", "rel_bias", "w_out", "w_ffn_up", "w_ffn_down"):
        shared[k] = f(inp[k])
    shared["w_in_a"] = f(inp["w_in_a"])[0]
    shared["w_in_b"] = f(inp["w_in_b"])[0]
    shared["w_gate_up"] = f(inp["w_gate_up"])[0]
    shared["norm_kv"] = f(inp["norm_kv"]).reshape(1, D)
    shared["w_kv"] = f(inp["w_kv"])
    shared.update(consts)
    xs = f(inp["x_sample"]).reshape(-1, D)
    st = f(inp["state_gla"])[0]
    swk = f(inp["cache_swa_k"]).reshape(-1, 128, 256)
    swv = f(inp["cache_swa_v"]).reshape(-1, 128, 256)
    mkc = f(inp["cache_mem_k"]).reshape(2, -1, 256, 256)
    mvc = f(inp["cache_mem_v"]).reshape(2, -1, 256, 256)
    mem = f(inp["mem_prompt"])
    pad = np.zeros(((NPRE + 1) * 128, D), np.float32)
    in_maps = []
    for c in range(NCORES):
        b, j = divmod(c, 4)
        xpad = np.concatenate([pad, xp[b]], axis=0)
        m = dict(shared)
        m["xin"] = np.ascontiguousarray(xpad[j * NT * 128:(j * NT + NPRE + 1 + NT) * 128])
        sl = slice(c * NS, (c + 1) * NS)
        m["xs_in"] = xs[sl]
        m["st_in"] = st[sl]
        m["swk_in"] = swk[sl]
        m["swv_in"] = swv[sl]
        m["mk_in"] = np.ascontiguousarray(mkc[:, sl])
        m["mv_in"] = np.ascontiguousarray(mvc[:, sl])
        m["mem_in"] = mem[b]
        m["c_flag"] = np.full((128, 1), 1.0 if j > 0 else 0.0, np.float32)
        in_maps.append(m)
    res = run_bass_kernel_spmd(nc, in_maps, core_ids=list(range(NCORES)))
    R = res.results
    y_prompt = np.stack([np.concatenate([R[4 * b + j]["y"] for j in range(4)], axis=0) for b in range(B)])
    y_sample = np.concatenate([R[c]["ys"] for c in range(NCORES)], axis=0).reshape(-1, 1, D)
    stp = np.stack([R[4 * b + 3]["stp"] for b in range(B)])[None]
    sts = np.concatenate([R[c]["sts"] for c in range(NCORES)], axis=0)[None]
    swkp = np.stack([R[4 * b + 3]["swkp"] for b in range(B)]).reshape(B, 128, 4, 64)
    swvp = np.stack([R[4 * b + 3]["swvp"] for b in range(B)]).reshape(B, 128, 4, 64)
    swks = np.concatenate([R[c]["swks"] for c in range(NCORES)], axis=0).reshape(-1, 128, 4, 64)
    swvs = np.concatenate([R[c]["swvs"] for c in range(NCORES)], axis=0).reshape(-1, 128, 4, 64)
    mkp = np.stack([R[4 * b]["mkp"] for b in range(B)], axis=1).reshape(2, B, 256, 4, 64)
    mvp = np.stack([R[4 * b]["mvp"] for b in range(B)], axis=1).reshape(2, B, 256, 4, 64)
    return (y_prompt, y_sample, stp, sts, swkp, swvp, swks, swvs, mkp, mvp)
```

```python
import math
import numpy as np
from contextlib import ExitStack
import concourse.bass as bass
import concourse.mybir as mybir
from concourse.bass_utils import run_bass_kernel_spmd

F32 = mybir.dt.float32
BF16 = mybir.dt.bfloat16
ALU = mybir.AluOpType
AF = mybir.ActivationFunctionType
AX = mybir.AxisListType

D = 1024
DK, DV, H = 96, 192, 4
EPS = 1e-6
NCORES = 8


class Buf:
    __slots__ = ("name", "w", "r", "excl")

    def __init__(self, name="", excl=False):
        self.name = name
        self.w = None
        self.r = {}
        self.excl = excl


class StopBuild(Exception):
    pass


class TB:
    def __init__(self, t, name=""):
        self.t = t
        self.b = Buf(name)

    def __getitem__(self, k):
        return self.t[k]


class Sched:
    ENGS = ("pe", "act", "dve", "pool", "sp")
    BLK = {"pe": "tensor", "act": "scalar", "dve": "vector", "pool": "gpsimd", "sp": "sync"}

    def __init__(self, nc, es, n_dma=(28, 14, 8)):
        self.nc = nc
        self.sem = {e: es.enter_context(nc.semaphore("s_" + e)) for e in self.ENGS}
        self.cnt = {e: 0 for e in self.ENGS}
        self.known = {e: {e2: 0 for e2 in self.ENGS} for e in self.ENGS}
        self.dq = {}
        k = 0
        for q, n in zip(("sp", "pool", "act"), n_dma):
            self.dq[q] = list(range(k, k + n))
            k += n
        self.dsem = [es.enter_context(nc.semaphore("d%d" % i)) for i in range(k)]
        self.dcnt = [0] * k
        self.drr = {q: 0 for q in self.dq}
        self.kdma = {e: [0] * k for e in self.ENGS}
        self.prog = {e: [] for e in self.ENGS}

    @staticmethod
    def _b(x):
        return getattr(x, "b", x)

    def _deps(self, reads, writes):
        deps = []
        for b in reads:
            b = self._b(b)
            if b.w is not None:
                deps.append(b.w)
        for b in writes:
            b = self._b(b)
            if b.w is not None:
                deps.append(b.w)
            deps.extend(b.r.values())
        return deps

    def _resolve(self, eng, deps):
        need = {}
        for ev in deps:
            if ev[0] == "c":
                _, e2, k = ev
                if e2 == eng and eng == "pe":
                    continue
                if self.known[eng][e2] >= k:
                    continue
                need[("c", e2)] = max(need.get(("c", e2), 0), k)
            else:
                _, idx, v = ev
                if self.kdma[eng][idx] >= v:
                    continue
                need[("d", idx)] = max(need.get(("d", idx), 0), v)
        waits = []
        for (kind, key), v in need.items():
            if kind == "c":
                self.known[eng][key] = v
                waits.append((self.sem[key], v))
            else:
                self.kdma[eng][key] = v
                waits.append((self.dsem[key], v))
        return waits

    def _mark(self, ev, key, reads, writes):
        for b in reads:
            self._b(b).r[key] = ev
        for b in writes:
            b = self._b(b)
            b.w = ev
            b.r = {}

    def op(self, eng, fn, reads=(), writes=()):
        ex = [b for b in reads if self._b(b).excl]
        if ex:
            reads = [b for b in reads if not self._b(b).excl]
            writes = list(writes) + ex
        waits = self._resolve(eng, self._deps(reads, writes))
        self.cnt[eng] += 1
        ev = ("c", eng, self.cnt[eng])
        self.prog[eng].append((waits, fn, None))
        self._mark(ev, eng, reads, writes)
        return ev

    def dma(self, q, out, in_, reads=(), writes=(), **kw):
        waits = self._resolve(q, self._deps(reads, writes))
        pool = self.dq[q]
        idx = pool[self.drr[q] % len(pool)]
        self.drr[q] += 1
        v0 = 16 * self.dcnt[idx]
        if v0 > 0 and self.kdma[q][idx] < v0:
            self.kdma[q][idx] = v0
            waits.append((self.dsem[idx], v0))
        self.dcnt[idx] += 1
        ev = ("d", idx, 16 * self.dcnt[idx])
        self.prog[q].append((waits, lambda h: h.dma_start(out=out, in_=in_, **kw), idx))
        self._mark(ev, ("d", idx), reads, writes)
        return ev

    def flush(self):
        nc = self.nc
        tails = {}
        for q, pool in self.dq.items():
            tl = []
            for idx in pool:
                v = 16 * self.dcnt[idx]
                if v > 0 and self.kdma[q][idx] < v:
                    tl.append((self.dsem[idx], v))
            tails[q] = tl
        with nc.Block() as block:
            for e in self.ENGS:
                items = self.prog[e]
                tl = tails.get(e, [])
                if not items and not tl:
                    continue

                def body(h, items=items, tl=tl, e=e):
                    for waits, fn, didx in items:
                        for s, v in waits:
                            h.wait_ge(s, v)
                        ins = fn(h)
                        if didx is None:
                            ins.then_inc(self.sem[e], 1)
                        else:
                            ins.then_inc(self.dsem[didx], 16)
                    for s, v in tl:
                        h.wait_ge(s, v)

                getattr(block, self.BLK[e])(body)
        self.prog = {e: [] for e in self.ENGS}
        for e in self.ENGS:
            for e2 in self.ENGS:
                self.known[e][e2] = self.cnt[e2]
            for i in range(len(self.dcnt)):
                self.kdma[e][i] = 16 * self.dcnt[i]


def bc(ap, dim, n):
    u = ap.unsqueeze(dim)
    shp = list(u.shape)
    shp[dim] = n
    return u.broadcast_to(shp)


def qcol(h):
    s, r = divmod(h, 6)
    half, j = divmod(r, 3)
    return (3 * s + j) * 128 + half * 64


def build(NT=16, NPRE=47, NS=16, stop=99):
    try:
        return _build(NT, NPRE, NS, stop)
    except StopBuild as e:
        return e.args[0]


def _build(NT=16, NPRE=47, NS=16, stop=99):
    nc = bass.Bass("TRN2", target_bir_lowering=False)
    NTI = NPRE + 1 + NT
    NT1 = NT + 1

    def din(name, shape):
        return nc.dram_tensor(name, list(shape), F32, kind="ExternalInput").ap()

    def dout(name, shape):
        return nc.dram_tensor(name, list(shape), F32, kind="ExternalOutput").ap()

    xin = din("xin", [NTI * 128, D])
    xs_in = din("xs_in", [NS, D])
    st_in = din("st_in", [NS, H, DK, DV])
    swk_in = din("swk_in", [NS, 128, 256])
    swv_in = din("swv_in", [NS, 128, 256])
    mk_in = din("mk_in", [2, NS, 256, 256])
    mv_in = din("mv_in", [2, NS, 256, 256])
    mem_in = din("mem_in", [256, D])
    g_mix_pre = din("norm_mix_pre", [2, D])
    g_mix_post = din("norm_mix_post", [2, D])
    g_ffn_pre = din("norm_ffn_pre", [2, D])
    g_ffn_post = din("norm_ffn_post", [2, D])
    g_mem = din("norm_mem", [2, D])
    w_mem_kv = din("w_mem_kv", [2, D, 512])
    w_in_a = din("w_in_a", [D, 2576])
    w_gate_up = din("w_gate_up", [16, 384])
    b_gate = din("b_gate", [1, 384])
    gla_norm = din("gla_norm", [1, DV])
    w_in_b = din("w_in_b", [D, 1024])
    sinks = din("sinks", [1, 12])
    g_kv = din("norm_kv", [1, D])
    w_kv = din("w_kv", [D, 512])
    rel_bias = din("rel_bias", [32, 12])
    w_out = din("w_out", [2, D, D])
    w_up = din("w_ffn_up", [2, D, 4096])
    w_down = din("w_ffn_down", [2, 4096, D])
    c_ident = din("c_ident", [128, 128])
    c_cmask = din("c_cmask", [128, 128])
    c_ohu = din("c_ohu", [32, 384])
    c_valid = din("c_valid", [128, 384])
    c_id16r = din("c_id16r", [128, 256])
    c_flag = din("c_flag", [128, 1])

    y = dout("y", [NT * 128, D])
    ys = dout("ys", [NS, D])
    stp = dout("stp", [H, DK, DV])
    sts = dout("sts", [NS, H, DK, DV])
    swkp = dout("swkp", [128, 256])
    swvp = dout("swvp", [128, 256])
    swks = dout("swks", [NS, 128, 256])
    swvs = dout("swvs", [NS, 128, 256])
    mkp = dout("mkp", [2, 256, 256])
    mvp = dout("mvp", [2, 256, 256])

    xa = nc.dram_tensor("xa_scr", [NT1 * 128, D], F32).ap()
    xb = nc.dram_tensor("xb_scr", [NT1 * 128, D], F32).ap()
    ebd = nc.dram_tensor("ebd_scr", [12, 128, 384], F32)
    b_xa = [Buf() for _ in range(NT1)]
    b_xb = [Buf() for _ in range(NT1)]
    b_ebd = Buf()
    b_y = Buf()

    with ExitStack() as ges:
        S = Sched(nc, ges)

        def ck(n):
            if stop == -100 - n:
                S.flush()
                raise StopBuild(nc)

        def mk(es, name, shape, dt=F32):
            return TB(es.enter_context(nc.sbuf_tensor(name, list(shape), dt)), name)

        PS = [ges.enter_context(nc.psum_tensor("PS%d" % i, [128, 1024], F32)) for i in range(4)]
        bB = [Buf("bank%d" % i, excl=True) for i in range(8)]

        def bank(i):
            return PS[i // 2][:, (i % 2) * 512:(i % 2 + 1) * 512]

        def bankb(i):
            return bank(i).bitcast(BF16)

        ident_f = mk(ges, "ident_f", [128, 128])
        ident_b = mk(ges, "ident_b", [128, 128], BF16)
        cmask = mk(ges, "cmask", [128, 128])
        id16r = mk(ges, "id16r", [128, 16, 16])
        ones_row = mk(ges, "ones_row", [1, 128], BF16)
        ones96 = mk(ges, "ones96", [96, 128])
        flag = mk(ges, "flag", [128, 1])
        KmT = [mk(ges, "KmT%d" % l, [128, 2, 2, 256], BF16) for l in range(2)]
        Vm = [mk(ges, "Vm%d" % l, [128, 2, 4, 66], BF16) for l in range(2)]
        xs_sb = mk(ges, "xs_sb", [NS, D])
        st = {n: mk(ges, "st_" + n, [128, 4]) for n in ("ss", "ln", "rstd", "ssg", "lng", "rstdg", "rden", "el")}
        rden12 = mk(ges, "rden12", [128, 12])
        den12 = mk(ges, "den12", [128, 12])
        xt = [mk(ges, "xt%d" % i, [128, D]) for i in range(3)]
        hb = mk(ges, "hb", [128, D], BF16)
        hb2 = mk(ges, "hb2", [128, D], BF16)
        hT = mk(ges, "hT", [128, 8, 128], BF16)
        hT2 = mk(ges, "hT2", [128, 8, 128], BF16)
        mixcat = mk(ges, "mixcat", [128, D], BF16)
        mcT = mk(ges, "mcT", [128, 8, 128], BF16)
        tt = mk(ges, "tt", [128, D])
        pT_sb = mk(ges, "pT_sb", [128, 8, 128], BF16)
        qmT_sb = mk(ges, "qmT_sb", [128, 2, 128], BF16)
        z16 = mk(ges, "z16", [16, 512])

        xt_rr = [0]

        def next_xt():
            t = xt[xt_rr[0] % 3]
            xt_rr[0] += 1
            return t

        def load_gain(dst, src_row):
            S.dma("sp", dst[:], src_row.partition_broadcast(128), writes=[dst])

        def load_w(dst, src2d, kchunks, ncols, col0=0, dcol0=0):
            v = src2d.rearrange("(kc p) n -> p kc n", p=128)
            step = 2048
            for kc in range(kchunks):
                c = 0
                while c < ncols:
                    n = min(step, ncols - c)
                    S.dma("pool", dst[:, kc, dcol0 + c:dcol0 + c + n], v[:, kc, col0 + c:col0 + c + n], writes=[dst])
                    c += n

        def rms_stats(x_ap, P, x_reads, junk, n=D):
            ss, ln, rstd = st["ss"], st["ln"], st["rstd"]
            S.op("act", lambda h: h.activation(out=junk[0:P, 0:n], in_=x_ap, func=AF.Square, accum_out=ss[0:P, 0:1]),
                 reads=x_reads, writes=[junk, ss])
            S.op("act", lambda h: h.activation(out=ln[0:P, 0:1], in_=ss[0:P, 0:1], func=AF.Ln, scale=1.0 / n, bias=eps_t[0:P, 0:1]),
                 reads=[ss, eps_t], writes=[ln])
            S.op("act", lambda h: h.activation(out=rstd[0:P, 0:1], in_=ln[0:P, 0:1], func=AF.Exp, scale=-0.5),
                 reads=[ln], writes=[rstd])

        def norm_to(x_ap, P, x_reads, gain, dst):
            S.op("dve", lambda h: h.scalar_tensor_tensor(out=dst[0:P, :], in0=x_ap, scalar=st["rstd"][0:P, 0:1], in1=gain[0:P, :],
                                                         op0=ALU.mult, op1=ALU.mult),
                 reads=list(x_reads) + [st["rstd"], gain], writes=[dst])

        def transpose8(src, P, dstT, tb=0):
            def f(h):
                for kc in range(8):
                    ins = h.transpose(out=bankb(tb)[:, kc * 128:kc * 128 + P], in_=src[0:P, kc * 128:(kc + 1) * 128],
                                      identity=ident_b[0:P, 0:P])
                return ins
            S.op("pe", f, reads=[src, ident_b], writes=[bB[tb]])
            S.op("act", lambda h: h.copy(out=dstT[:, :, 0:P], in_=bankb(tb)[:, :].rearrange("p (a b) -> p a b", a=8)[:, :, 0:P]),
                 reads=[bB[tb]], writes=[dstT])

        def mm(out_ap, pairs, reads, wbank):
            def f(h):
                n = len(pairs)
                for i, (l, r) in enumerate(pairs):
                    ins = h.matmul(out_ap, lhsT=l, rhs=r, start=(i == 0), stop=(i == n - 1))
                return ins
            S.op("pe", f, reads=reads, writes=wbank)

        def post_norm_residual(mix_pair, P, gain, xtile_ap, xtile_tb):
            mix = PS[mix_pair][0:P, :]
            rms_stats(mix, P, [bB[2 * mix_pair], bB[2 * mix_pair + 1]], tt)
            S.op("dve", lambda h: h.scalar_tensor_tensor(out=tt[0:P, :], in0=mix, scalar=st["rstd"][0:P, 0:1], in1=gain[0:P, :],
                                                         op0=ALU.mult, op1=ALU.mult),
                 reads=[bB[2 * mix_pair], bB[2 * mix_pair + 1], st["rstd"], gain], writes=[tt])
            S.op("pool", lambda h: h.tensor_tensor(out=xtile_ap, in0=xtile_ap, in1=tt[0:P, :], op=ALU.add),
                 reads=[tt], writes=[xtile_tb])

        def out_proj(P, Wo, gain, xtile_ap, xtile_tb):
            transpose8(mixcat, P, mcT, tb=0)
            for half in range(2):
                mm(bank(6 + half)[0:P, :], [(mcT[:, kc, 0:P], Wo[:, kc, half * 512:(half + 1) * 512]) for kc in range(8)],
                   [mcT, Wo], [bB[6 + half]])
            post_norm_residual(3, P, gain, xtile_ap, xtile_tb)

        def mem_attend_prompt(l, qT_ap_fn, qreads, P):
            def f(h):
                for hh in range(4):
                    c, p = divmod(hh, 2)
                    for mc in range(2):
                        ins = h.matmul(bank(4 + hh // 2)[:, ((hh % 2) * 2 + mc) * 128:((hh % 2) * 2 + mc) * 128 + P],
                                       lhsT=KmT[l][:, p, c, mc * 128:(mc + 1) * 128], rhs=qT_ap_fn(c, p),
                                       start=True, stop=True)
                return ins
            S.op("pe", f, reads=[KmT[l]] + qreads, writes=[bB[4], bB[5]])
            ck(30)
            S.op("act", lambda h: h.activation(out=pT_sb[:, :, 0:P], in_=PS[2][:, :].rearrange("p (a b) -> p a b", a=8)[:, :, 0:P],
                                               func=AF.Exp, scale=0.125),
                 reads=[bB[4], bB[5]], writes=[pT_sb])

            def g(h):
                for hh in range(4):
                    for mc in range(2):
                        ins = h.matmul(bank(1)[0:P, hh * 65:(hh + 1) * 65], lhsT=pT_sb[:, hh * 2 + mc, 0:P], rhs=Vm[l][:, mc, hh, 0:65],
                                       start=(mc == 0), stop=(mc == 1))
                return ins
            ck(31)
            S.op("pe", g, reads=[pT_sb, Vm[l]], writes=[bB[1]])
            ck(32)
            finish_mem(P)

        def finish_mem(P):
            om = bank(1)[0:P, 0:260].rearrange("p (a b) -> p a b", a=4)
            S.op("dve", lambda h: h.reciprocal(out=st["rden"][0:P, 0:4], in_=om[:, :, 64]), reads=[bB[1]], writes=[st["rden"]])
            S.op("dve", lambda h: h.tensor_tensor(out=mixcat[0:P, 768:1024].rearrange("p (a b) -> p a b", a=4), in0=om[:, :, 0:64],
                                                  in1=bc(st["rden"][0:P, 0:4], 2, 64), op=ALU.mult),
                 reads=[bB[1], st["rden"]], writes=[mixcat])

        mem_sample_fn = []

        def zero_acc(regions, banks):
            def f(h):
                for r_ in regions:
                    n = r_.shape[-1]
                    ins = h.matmul(r_, lhsT=z16[0:16, 0:NS], rhs=z16[0:16, 0:n], start=True, stop=False)
                return ins
            S.op("pe", f, reads=[z16], writes=banks)

        def alloc_sample_attn(es, tag, onesel=None):
            Kc = [mk(es, tag + "Kc%d" % i, [128, 2, 256]) for i in range(2)]
            Vc = [mk(es, tag + "Vc%d" % i, [128, 2, 4, 65]) for i in range(2)]
            prod = mk(es, tag + "prod", [128, 768])
            sc8 = mk(es, tag + "sc8", [128, 12])
            pp8 = mk(es, tag + "pp8", [128, 12])
            Pm = mk(es, tag + "Pm", [128, 12, NS])
            if onesel is None:
                onesel = mk(es, tag + "onesel", [NS, NS, 128])
            S.op("pool", lambda h: h.memset(Vc[0][:], 1.0), writes=[Vc[0]])
            S.op("pool", lambda h: h.memset(Vc[1][:], 1.0), writes=[Vc[1]])
            S.op("dve", lambda h: h.tensor_copy(out=onesel[:], in_=bc(ident_f[0:NS, 0:NS], 2, 128)), reads=[ident_f], writes=[onesel])
            return Kc, Vc, prod, sc8, pp8, Pm, onesel

        with ExitStack() as es:
            eps_t = mk(ges, "eps_t", [128, 1])
            S.op("pool", lambda h: h.memset(eps_t[:], EPS), writes=[eps_t])
            S.op("pool", lambda h: h.memset(z16[:], 0.0), writes=[z16])
            S.dma("sp", ident_f[:], c_ident, writes=[ident_f])
            S.dma("sp", cmask[:], c_cmask, writes=[cmask])
            S.dma("sp", id16r[:].rearrange("p a b -> p (a b)"), c_id16r, writes=[id16r])
            S.dma("sp", flag[:], c_flag, writes=[flag])
            S.dma("sp", xs_sb[:], xs_in, writes=[xs_sb])
            S.op("dve", lambda h: h.tensor_copy(out=ident_b[:], in_=ident_f[:]), reads=[ident_f], writes=[ident_b])
            S.op("pool", lambda h: h.memset(ones_row[:], 1.0), writes=[ones_row])
            S.op("pool", lambda h: h.memset(ones96[:], 1.0), writes=[ones96])
            for l in range(2):
                S.op("pool", lambda h, l=l: h.memset(Vm[l][:], 1.0), writes=[Vm[l]])
                S.op("pool", lambda h, l=l: h.memset(KmT[l][:], 0.0), writes=[KmT[l]])
            gm = mk(es, "gm", [128, D])
            Wm = mk(es, "Wm", [128, 8, 512], BF16)
            hmT = mk(es, "hmT", [128, 8, 256], BF16)
            kvf = mk(es, "kvf", [128, 512])
            xm = [mk(es, "xm%d" % i, [128, D]) for i in range(2)]
            for mt in range(2):
                S.dma("sp", xm[mt][:], mem_in[mt * 128:(mt + 1) * 128, :], writes=[xm[mt]])
            if stop == -4:
                S.flush()
                return nc
            for l in range(2):
                load_gain(gm, g_mem[l, :])
                load_w(Wm, w_mem_kv[l], 8, 512)
                for mt in range(2):
                    rms_stats(xm[mt][:], 128, [xm[mt]], hb)
                    if stop == -3:
                        S.flush()
                        return nc
                    norm_to(xm[mt][:], 128, [xm[mt]], gm, hb)
                    if stop == -2:
                        S.flush()
                        return nc
                    transpose8(hb, 128, hT, tb=0)
                    if stop == -1:
                        S.flush()
                        return nc
                    S.op("dve", lambda h, mt=mt: h.tensor_copy(out=hmT[:, :, mt * 128:(mt + 1) * 128], in_=hT[:]),
                         reads=[hT], writes=[hmT])
                for c in range(2):
                    mm(bank(2)[:, 0:256], [(Wm[:, kc, c * 128:(c + 1) * 128], hmT[:, kc, :]) for kc in range(8)], [Wm, hmT], [bB[2]])
                    S.op("act", lambda h, c=c, l=l: h.copy(out=KmT[l][0:64, 0, c, :], in_=bank(2)[0:64, 0:256]), reads=[bB[2]], writes=[KmT[l]])
                    S.op("act", lambda h, c=c, l=l: h.copy(out=KmT[l][64:128, 1, c, :], in_=bank(2)[64:128, 0:256]), reads=[bB[2]], writes=[KmT[l]])
                if stop == -10:
                    S.flush()
                    return nc
                for mt in range(2):
                    mm(bank(3)[:, 0:512], [(hmT[:, kc, mt * 128:(mt + 1) * 128], Wm[:, kc, :]) for kc in range(8)], [Wm, hmT], [bB[3]])
                    S.op("dve", lambda h: h.tensor_copy(out=kvf[:], in_=bank(3)[:, 0:512]), reads=[bB[3]], writes=[kvf])
                    if stop == -11:
                        S.flush()
                        return nc
                    S.op("act", lambda h, l=l, mt=mt: h.copy(out=Vm[l][:, mt, :, 0:64],
                                                             in_=bank(3)[:, 256:512].rearrange("p (a b) -> p a b", a=4)),
                         reads=[bB[3]], writes=[Vm[l]])
                    if stop == -12:
                        S.flush()
                        return nc
                    S.dma("sp", mkp[l, mt * 128:(mt + 1) * 128, :], kvf[:, 0:256], reads=[kvf])
                    S.dma("sp", mvp[l, mt * 128:(mt + 1) * 128, :], kvf[:, 256:512], reads=[kvf])
            S.flush()
        if 0 <= stop <= 0:
            return nc

        with ExitStack() as es:
            Wa = mk(es, "Wa", [128, 8, 2576], BF16)
            Wo = mk(es, "Wo0", [128, 8, D], BF16)
            wgu = mk(es, "wgu", [16, 384], BF16)
            bg = mk(es, "bg", [1, 384], BF16)
            gpre = mk(es, "gpre0", [128, D])
            gpost = mk(es, "gpost0", [128, D])
            gn = mk(es, "gn", [128, DV])
            Sst = mk(es, "Sst", [96, 4, DV])
            S_bf = mk(es, "S_bf", [96, 4, DV], BF16)
            glr_sb = mk(es, "glr_sb", [16, 128], BF16)
            e1 = mk(es, "e1", [96, 4, 128])
            spl = mk(es, "spl", [96, 4, 128])
            cc = mk(es, "cc", [96, 4, 128])
            Einv = mk(es, "Einv", [96, 4, 128])
            Edec = mk(es, "Edec", [96, 4, 128])
            keT = mk(es, "keT", [96, 4, 128], BF16)
            qeT = mk(es, "qeT", [96, 4, 128], BF16)
            kdT = mk(es, "kdT", [96, 4, 128], BF16)
            kd_sb = mk(es, "kd_sb", [128, 384], BF16)
            v_sb = mk(es, "v_sb", [128, 768], BF16)
            attnT_sb = mk(es, "attnT_sb", [128, 4, 128], BF16)
            on = mk(es, "on", [128, 768])
            er = mk(es, "er", [128, 768])
            sg = mk(es, "sg", [128, 768])

            load_w(Wa, w_in_a, 8, 2576)
            load_w(Wo, w_out[0], 8, D)
            S.dma("pool", wgu[:], w_gate_up, writes=[wgu])
            S.dma("pool", bg[:], b_gate, writes=[bg])
            load_gain(gpre, g_mix_pre[0, :])
            load_gain(gpost, g_mix_post[0, :])
            S.dma("sp", gn[:], gla_norm[0, :].partition_broadcast(128), writes=[gn])
            S.op("pool", lambda h: h.memset(Sst[:], 0.0), writes=[Sst])
            S.op("pool", lambda h: h.memset(S_bf[:], 0.0), writes=[S_bf])

            elast = st["el"]

            def gate_path(P):
                mm(bank(1)[0:16, 0:P], [(Wa[:, kc, 2304:2320], hT[:, kc, 0:P]) for kc in range(8)], [Wa, hT], [bB[1]])
                S.op("dve", lambda h: h.tensor_copy(out=glr_sb[:, 0:P], in_=bank(1)[0:16, 0:P]), reads=[bB[1]], writes=[glr_sb])

            def zgate(P):
                def f(h):
                    for hh in range(4):
                        o = bank(1)[0:96, hh * 128:hh * 128 + P]
                        h.matmul(o, lhsT=wgu[0:16, 96 * hh:96 * hh + 96], rhs=glr_sb[0:16, 0:P], start=True, stop=False)
                        ins = h.matmul(o, lhsT=bg[0:1, 96 * hh:96 * hh + 96], rhs=ones_row[0:1, 0:P], start=False, stop=True)
                    return ins
                S.op("pe", f, reads=[wgu, bg, glr_sb, ones_row], writes=[bB[1]])
                zv = bank(1)[0:96, :].rearrange("p (a b) -> p a b", a=4)[:, :, 0:P]
                S.op("act", lambda h: h.activation(out=e1[:, :, 0:P], in_=zv, func=AF.Exp, scale=-1.0), reads=[bB[1]], writes=[e1])
                S.op("act", lambda h: h.activation(out=spl[:, :, 0:P], in_=e1[:, :, 0:P], func=AF.Ln, scale=1.0, bias=one_t[0:96, 0:1]),
                     reads=[e1, one_t], writes=[spl])

            one_t = mk(es, "one_t", [128, 1])
            S.op("pool", lambda h: h.memset(one_t[:], 1.0), writes=[one_t])

            def proj_fm(col0, bk, P):
                def f(h):
                    for hh in range(4):
                        for kc in range(8):
                            ins = h.matmul(bank(bk)[0:96, hh * 128:hh * 128 + P], lhsT=Wa[:, kc, col0 + 96 * hh:col0 + 96 * hh + 96],
                                           rhs=hT[:, kc, 0:P], start=(kc == 0), stop=(kc == 7))
                    return ins
                S.op("pe", f, reads=[Wa, hT], writes=[bB[bk]])

            def proj_tm(col0, pair, P):
                for half in range(2):
                    mm(bank(2 * pair + half)[0:P, 0:384],
                       [(hT[:, kc, 0:P], Wa[:, kc, col0 + 384 * half:col0 + 384 * (half + 1)]) for kc in range(8)],
                       [Wa, hT], [bB[2 * pair + half]])

            def pair768(pair, P):
                return PS[pair][0:P, :].rearrange("p (a b) -> p a b", a=2)[:, :, 0:384]

            def gla_norm_gate(P, o_pair, r_pair):
                ov = pair768(o_pair, P).rearrange("p a (i n) -> p a i n", i=2)
                for hh in range(4):
                    S.op("act", lambda h, hh=hh: h.activation(out=sg[0:P, 0:DV], in_=ov[:, hh // 2, hh % 2, :], func=AF.Square,
                                                              accum_out=st["ssg"][0:P, hh:hh + 1]),
                         reads=[bB[2 * o_pair], bB[2 * o_pair + 1]], writes=[sg, st["ssg"]])
                S.op("act", lambda h: h.activation(out=st["lng"][0:P, :], in_=st["ssg"][0:P, :], func=AF.Ln, scale=1.0 / DV,
                                                   bias=eps_t[0:P, 0:1]), reads=[st["ssg"], eps_t], writes=[st["lng"]])
                S.op("act", lambda h: h.activation(out=st["rstdg"][0:P, :], in_=st["lng"][0:P, :], func=AF.Exp, scale=-0.5),
                     reads=[st["lng"]], writes=[st["rstdg"]])
                for hh in range(4):
                    S.op("dve", lambda h, hh=hh: h.scalar_tensor_tensor(out=on[0:P, hh * DV:(hh + 1) * DV], in0=ov[:, hh // 2, hh % 2, :],
                                                                        scalar=st["rstdg"][0:P, hh:hh + 1], in1=gn[0:P, :],
                                                                        op0=ALU.mult, op1=ALU.mult),
                         reads=[bB[2 * o_pair], bB[2 * o_pair + 1], st["rstdg"], gn], writes=[on])
                rv = pair768(r_pair, P)
                er3 = er[0:P, :].rearrange("p (a b) -> p a b", a=2)
                S.op("act", lambda h: h.activation(out=er3, in_=rv, func=AF.Exp, scale=-1.0),
                     reads=[bB[2 * r_pair], bB[2 * r_pair + 1]], writes=[er])
                S.op("pool", lambda h: h.tensor_scalar(out=er[0:P, :], in0=er[0:P, :], scalar1=1.0, scalar2=None, op0=ALU.add),
                     reads=[], writes=[er])
                S.op("dve", lambda h: h.reciprocal(out=er[0:P, :], in_=er[0:P, :]), reads=[], writes=[er])
                S.op("dve", lambda h: h.tensor_tensor(out=sg[0:P, :].rearrange("p (a b) -> p a b", a=2), in0=rv, in1=er3, op=ALU.mult),
                     reads=[bB[2 * r_pair], bB[2 * r_pair + 1], er], writes=[sg])
                S.op("dve", lambda h: h.tensor_tensor(out=mixcat[0:P, 0:768], in0=on[0:P, :], in1=sg[0:P, :], op=ALU.mult),
                     reads=[on, sg], writes=[mixcat])

            def gla_tile(row0, full, out_slot):
                P = 128
                xtile = next_xt()
                S.dma("sp", xtile[:], xin[row0:row0 + 128, :], writes=[xtile])
                rms_stats(xtile[:], P, [xtile], hb)
                norm_to(xtile[:], P, [xtile], gpre, hb)
                transpose8(hb, P, hT, tb=0)
                gate_path(P)
                ck(1)
                proj_fm(384, 2, P)
                ck(2)
                if full:
                    proj_fm(0, 3, P)
                zgate(P)
                ck(3)
                for hh in range(4):
                    S.op("dve", lambda h, hh=hh: h.tensor_tensor_scan(out=cc[:, hh, :], data0=ones96[:, :], data1=spl[:, hh, :], initial=0.0,
                                                                      op0=ALU.mult, op1=ALU.add),
                         reads=[spl, ones96], writes=[cc])
                S.op("act", lambda h: h.activation(out=Einv[:], in_=cc[:], func=AF.Exp, scale=1.0 / 16), reads=[cc], writes=[Einv])
                if full:
                    S.op("act", lambda h: h.activation(out=Edec[:], in_=cc[:], func=AF.Exp, scale=-1.0 / 16), reads=[cc], writes=[Edec])
                S.op("act", lambda h: h.activation(out=elast[0:96, 0:4], in_=cc[:, :, 127], func=AF.Exp, scale=-1.0 / 16),
                     reads=[cc], writes=[elast])
                ck(4)
                kT = bank(2)[0:96, :].rearrange("p (a b) -> p a b", a=4)
                qT = bank(3)[0:96, :].rearrange("p (a b) -> p a b", a=4)
                S.op("dve", lambda h: h.tensor_tensor(out=keT[:], in0=kT, in1=Einv[:], op=ALU.mult), reads=[bB[2], Einv], writes=[keT])
                if full:
                    S.op("dve", lambda h: h.scalar_tensor_tensor(out=qeT[:], in0=qT, scalar=DK ** -0.5, in1=Edec[:], op0=ALU.mult, op1=ALU.mult),
                         reads=[bB[3], Edec], writes=[qeT])
                for hh in range(4):
                    S.op("dve", lambda h, hh=hh: h.scalar_tensor_tensor(out=kdT[:, hh, :], in0=kT[:, hh, :], scalar=elast[0:96, hh:hh + 1],
                                                                        in1=Einv[:, hh, :], op0=ALU.mult, op1=ALU.mult),
                         reads=[bB[2], elast, Einv], writes=[kdT])

                ck(5)

                def trk(h):
                    for hh in range(4):
                        ins = h.transpose(out=bankb(0)[:, hh * 96:(hh + 1) * 96], in_=kdT[:, hh, :], identity=ident_b[0:96, 0:96])
                    return ins
                S.op("pe", trk, reads=[kdT, ident_b], writes=[bB[0]])
                S.op("act", lambda h: h.copy(out=kd_sb[:], in_=bankb(0)[:, 0:384]), reads=[bB[0]], writes=[kd_sb])
                ck(6)
                proj_tm(768, 2, P)
                S.op("act", lambda h: h.copy(out=v_sb[:].rearrange("p (a b) -> p a b", a=2), in_=pair768(2, P)),
                     reads=[bB[4], bB[5]], writes=[v_sb])
                ck(7)
                if full:
                    def fa(h):
                        for hh in range(4):
                            ins = h.matmul(bank(1)[:, hh * 128:(hh + 1) * 128], lhsT=keT[:, hh, :], rhs=qeT[:, hh, :], start=True, stop=True)
                        return ins
                    S.op("pe", fa, reads=[keT, qeT], writes=[bB[1]])
                    S.op("dve", lambda h: h.tensor_tensor(out=attnT_sb[:], in0=bank(1)[:, :].rearrange("p (a b) -> p a b", a=4),
                                                          in1=bc(cmask[:], 1, 4), op=ALU.mult),
                         reads=[bB[1], cmask], writes=[attnT_sb])

                    def fo(h):
                        for hh in range(4):
                            o = bank(4 + hh // 2)[:, (hh % 2) * DV:(hh % 2 + 1) * DV]
                            h.matmul(o, lhsT=attnT_sb[:, hh, :], rhs=v_sb[:, hh * DV:(hh + 1) * DV], start=True, stop=False)
                            ins = h.matmul(o, lhsT=qeT[:, hh, :], rhs=S_bf[:, hh, :], start=False, stop=True)
                        return ins
                    S.op("pe", fo, reads=[attnT_sb, v_sb, qeT, S_bf], writes=[bB[4], bB[5]])

                def fs(h):
                    for hh in range(4):
                        ins = h.matmul(bank(2 + hh // 2)[0:96, (hh % 2) * DV:(hh % 2 + 1) * DV], lhsT=kd_sb[:, 96 * hh:96 * hh + 96],
                                       rhs=v_sb[:, hh * DV:(hh + 1) * DV], start=True, stop=True)
                    return ins
                S.op("pe", fs, reads=[kd_sb, v_sb], writes=[bB[2], bB[3]])
                for hh in range(4):
                    S.op("dve", lambda h, hh=hh: h.scalar_tensor_tensor(out=Sst[:, hh, :], in0=Sst[:, hh, :], scalar=elast[0:96, hh:hh + 1],
                                                                        in1=bank(2 + hh // 2)[0:96, (hh % 2) * DV:(hh % 2 + 1) * DV],
                                                                        op0=ALU.mult, op1=ALU.add),
                         reads=[elast, bB[2], bB[3]], writes=[Sst])
                S.op("pool", lambda h: h.tensor_copy(out=S_bf[:], in_=Sst[:]), reads=[Sst], writes=[S_bf])
                ck(8)
                if not full:
                    return
                proj_tm(1536, 3, P)
                for c in range(2):
                    mm(bank(1)[:, c * 128:c * 128 + P], [(Wa[:, kc, 2320 + 128 * c:2320 + 128 * (c + 1)], hT[:, kc, 0:P]) for kc in range(8)],
                       [Wa, hT], [bB[1]])
                S.op("act", lambda h: h.copy(out=qmT_sb[:, :, 0:P], in_=bank(1)[:, 0:256].rearrange("p (a b) -> p a b", a=2)[:, :, 0:P]),
                     reads=[bB[1]], writes=[qmT_sb])
                ck(9)
                gla_norm_gate(P, 2, 3)
                ck(10)
                mem_attend_prompt(0, lambda c, p: qmT_sb[:, c, 0:P], [qmT_sb], P)
                ck(11)
                out_proj(P, Wo, gpost, xtile[:], xtile)
                ck(12)
                S.dma("sp", xa[out_slot * 128:(out_slot + 1) * 128, :], xtile[:], reads=[xtile], writes=[b_xa[out_slot]])

            esp = ExitStack()
            PB = []
            for par in range(4):
                d = {}
                d["hb"] = mk(esp, "p_hb%d" % par, [128, D], BF16)
                d["hT"] = mk(esp, "p_hT%d" % par, [128, 8, 128], BF16)
                d["glr"] = mk(esp, "p_glr%d" % par, [16, 128], BF16)
                d["sp"] = mk(esp, "p_sp%d" % par, [96, 4, 128])
                d["cc"] = mk(esp, "p_cc%d" % par, [96, 4, 128])
                d["kdT"] = mk(esp, "p_kdT%d" % par, [96, 4, 128], BF16)
                d["kd"] = mk(esp, "p_kd%d" % par, [128, 384], BF16)
                d["v"] = mk(esp, "p_v%d" % par, [128, 768], BF16)
                d["ss"] = mk(esp, "p_ss%d" % par, [128, 1])
                d["ln"] = mk(esp, "p_ln%d" % par, [128, 1])
                d["rs"] = mk(esp, "p_rs%d" % par, [128, 1])
                d["el"] = mk(esp, "p_el%d" % par, [96, 4])
                PB.append(d)

            def prefix_tile(t, Q):
                def q(fn, *a, **k):
                    Q.append((fn, a, k))

                par = t % 4
                d = PB[par]
                bT, bZ, bK, bX = 2 * par, 2 * par, 2 * par + 1, 2 * par + 1
                hbP, hTP, glrP, spP, ccP, kdTP, kdP, vP = d["hb"], d["hT"], d["glr"], d["sp"], d["cc"], d["kdT"], d["kd"], d["v"]
                ssP, lnP, rsP, elP = d["ss"], d["ln"], d["rs"], d["el"]
                xtile = next_xt()
                q(S.dma, "sp", xtile[:], xin[t * 128:(t + 1) * 128, :], writes=[xtile])
                q(S.op, "act", lambda h: h.activation(out=hbP[:], in_=xtile[:], func=AF.Square, accum_out=ssP[:, 0:1]), reads=[xtile], writes=[hbP, ssP])
                q(S.op, "act", lambda h: h.activation(out=lnP[:], in_=ssP[:], func=AF.Ln, scale=1.0 / D, bias=eps_t[:, 0:1]), reads=[ssP, eps_t], writes=[lnP])
                q(S.op, "act", lambda h: h.activation(out=rsP[:], in_=lnP[:], func=AF.Exp, scale=-0.5), reads=[lnP], writes=[rsP])
                q(S.op, "dve", lambda h: h.scalar_tensor_tensor(out=hbP[:], in0=xtile[:], scalar=rsP[:, 0:1], in1=gpre[:], op0=ALU.mult, op1=ALU.mult),
                     reads=[xtile, rsP, gpre], writes=[hbP])

                def ftr(h):
                    for kc in range(8):
                        ins = h.transpose(out=bankb(bT)[:, kc * 128:(kc + 1) * 128], in_=hbP[:, kc * 128:(kc + 1) * 128], identity=ident_b[:])
                    return ins
                q(S.op, "pe", ftr, reads=[hbP, ident_b], writes=[bB[bT]])
                q(S.op, "act", lambda h: h.copy(out=hTP[:].rearrange("p a b -> p (a b)"), in_=bankb(bT)[:, :]), reads=[bB[bT]], writes=[hTP])
                q(mm, bank(bZ)[0:16, 0:128], [(Wa[:, kc, 2304:2320], hTP[:, kc, :]) for kc in range(8)], [Wa, hTP], [bB[bZ]])
                q(S.op, "dve", lambda h: h.tensor_copy(out=glrP[:], in_=bank(bZ)[0:16, 0:128]), reads=[bB[bZ]], writes=[glrP])

                def fk_(h):
                    for hh in range(4):
                        for kc in range(8):
                            ins = h.matmul(bank(bK)[0:96, hh * 128:(hh + 1) * 128], lhsT=Wa[:, kc, 384 + 96 * hh:384 + 96 * hh + 96],
                                           rhs=hTP[:, kc, :], start=(kc == 0), stop=(kc == 7))
                    return ins
                q(S.op, "pe", fk_, reads=[Wa, hTP], writes=[bB[bK]])

                def fz(h):
                    for hh in range(4):
                        o = bank(bZ)[0:96, hh * 128:(hh + 1) * 128]
                        h.matmul(o, lhsT=wgu[0:16, 96 * hh:96 * hh + 96], rhs=glrP[0:16, :], start=True, stop=False)
                        ins = h.matmul(o, lhsT=bg[0:1, 96 * hh:96 * hh + 96], rhs=ones_row[0:1, :], start=False, stop=True)
                    return ins
                q(S.op, "pe", fz, reads=[wgu, bg, glrP, ones_row], writes=[bB[bZ]])
                zv = bank(bZ)[0:96, :].rearrange("p (a b) -> p a b", a=4)
                q(S.op, "act", lambda h: h.activation(out=spP[:], in_=zv, func=AF.Exp, scale=-1.0), reads=[bB[bZ]], writes=[spP])
                q(S.op, "act", lambda h: h.activation(out=spP[:], in_=spP[:], func=AF.Ln, scale=1.0, bias=one_t[0:96, 0:1]), reads=[one_t], writes=[spP])
                for hh in range(4):
                    q(S.op, "dve", lambda h, hh=hh: h.tensor_tensor_scan(out=ccP[:, hh, :], data0=ones96[:, :], data1=spP[:, hh, :], initial=0.0,
                                                                      op0=ALU.mult, op1=ALU.add), reads=[spP, ones96], writes=[ccP])
                q(S.op, "act", lambda h: h.activation(out=elP[:], in_=ccP[:, :, 127], func=AF.Exp, scale=-1.0 / 16), reads=[ccP], writes=[elP])
                q(S.op, "act", lambda h: h.activation(out=ccP[:], in_=ccP[:], func=AF.Exp, scale=1.0 / 16), reads=[], writes=[ccP])
                kT = bank(bK)[0:96, :].rearrange("p (a b) -> p a b", a=4)
                for hh in range(4):
                    q(S.op, "dve", lambda h, hh=hh: h.scalar_tensor_tensor(out=kdTP[:, hh, :], in0=kT[:, hh, :], scalar=elP[:, hh:hh + 1], in1=ccP[:, hh, :],
                                                                        op0=ALU.mult, op1=ALU.mult), reads=[bB[bK], elP, ccP], writes=[kdTP])

                def ftk(h):
                    for hh in range(4):
                        ins = h.transpose(out=bankb(bT)[:, hh * 96:(hh + 1) * 96], in_=kdTP[:, hh, :], identity=ident_b[0:96, 0:96])
                    return ins
                q(S.op, "pe", ftk, reads=[kdTP, ident_b], writes=[bB[bT]])
                q(S.op, "act", lambda h: h.copy(out=kdP[:], in_=bankb(bT)[:, 0:384]), reads=[bB[bT]], writes=[kdP])
                for half in range(2):
                    q(mm, bank(bX)[:, 0:384], [(hTP[:, kc, :], Wa[:, kc, 768 + 384 * half:768 + 384 * (half + 1)]) for kc in range(8)], [Wa, hTP], [bB[bX]])
                    if half == 0:
                        q(S.op, "act", lambda h: h.copy(out=vP[:, 0:384], in_=bank(bX)[:, 0:384]), reads=[bB[bX]], writes=[vP])
                    else:
                        q(S.op, "dve", lambda h: h.tensor_copy(out=vP[:, 384:768], in_=bank(bX)[:, 0:384]), reads=[bB[bX]], writes=[vP])
                for pr_, bk_ in ((0, bX), (1, bK)):
                    def fs_(h, pr_=pr_, bk_=bk_):
                        for i_ in range(2):
                            hh = 2 * pr_ + i_
                            ins = h.matmul(bank(bk_)[0:96, i_ * DV:(i_ + 1) * DV], lhsT=kdP[:, 96 * hh:96 * hh + 96], rhs=vP[:, hh * DV:(hh + 1) * DV],
                                           start=True, stop=True)
                        return ins
                    q(S.op, "pe", fs_, reads=[kdP, vP], writes=[bB[bk_]])
                    for i_ in range(2):
                        hh = 2 * pr_ + i_
                        q(S.op, "dve", lambda h, hh=hh, i_=i_, bk_=bk_: h.scalar_tensor_tensor(out=Sst[:, hh, :], in0=Sst[:, hh, :], scalar=elP[:, hh:hh + 1],
                                                                                            in1=bank(bk_)[0:96, i_ * DV:(i_ + 1) * DV], op0=ALU.mult, op1=ALU.add),
                             reads=[elP, bB[bk_]], writes=[Sst])

            for t0 in range(0, NPRE, 4):
                lists = []
                for t in range(t0, t0 + 4):
                    if t < NPRE:
                        Q = []
                        prefix_tile(t, Q)
                        lists.append(Q)
                for i_ in range(max(len(L) for L in lists)):
                    for L in lists:
                        if i_ < len(L):
                            fn_, a_, k_ = L[i_]
                            fn_(*a_, **k_)
            S.op("pool", lambda h: h.tensor_copy(out=S_bf[:], in_=Sst[:]), reads=[Sst], writes=[S_bf])
            S.flush()
            ck(100)
            esp.close()
            for t in range(NT1):
                gla_tile((NPRE + t) * 128, True, t)
            S.dma("sp", stp.rearrange("h d v -> d h v"), Sst[:], reads=[Sst])

            P = NS
            kq_s = mk(es, "kq_s", [96, 4, NS])
            eg = mk(es, "eg", [96, 4, NS])
            ktok = mk(es, "ktok", [NS, 384])
            vtok = mk(es, "vtok", [NS, 768])
            kmask = mk(es, "kmask", [NS, NS, 384])
            qmask = mk(es, "qmask", [96, 4, NS, NS])
            Sin = [mk(es, "Sin%d" % i, [96, 4, DV]) for i in range(2)]
            Snew = [mk(es, "Snew%d" % i, [96, 4, DV]) for i in range(2)]
            qm_tok = mk(es, "qm_tok", [NS, 256])
            rms_stats(xs_sb[:], P, [xs_sb], hb)
            norm_to(xs_sb[:], P, [xs_sb], gpre, hb)
            transpose8(hb, P, hT, tb=0)
            gate_path(P)
            proj_fm(384, 2, P)
            proj_fm(0, 3, P)
            zgate(P)
            S.op("act", lambda h: h.activation(out=eg[:], in_=spl[:, :, 0:P], func=AF.Exp, scale=-1.0 / 16), reads=[spl], writes=[eg])
            qTs = bank(3)[0:96, :].rearrange("p (a b) -> p a b", a=4)[:, :, 0:P]
            S.op("dve", lambda h: h.tensor_scalar(out=kq_s[:], in0=qTs, scalar1=DK ** -0.5, scalar2=None, op0=ALU.mult),
                 reads=[bB[3]], writes=[kq_s])
            S.op("dve", lambda h: h.tensor_tensor(out=qmask[:], in0=bc(kq_s[:], 3, NS), in1=bc(id16r[0:96, :, :], 1, 4), op=ALU.mult),
                 reads=[kq_s, id16r], writes=[qmask])
            mm(bank(2)[0:P, 0:384], [(hT[:, kc, 0:P], Wa[:, kc, 384:768]) for kc in range(8)], [Wa, hT], [bB[2]])
            S.op("act", lambda h: h.copy(out=ktok[:], in_=bank(2)[0:P, 0:384]), reads=[bB[2]], writes=[ktok])
            S.op("dve", lambda h: h.tensor_tensor(out=kmask[:], in0=bc(ktok[:], 1, NS), in1=bc(ident_f[0:NS, 0:NS], 2, 384), op=ALU.mult),
                 reads=[ktok, ident_f], writes=[kmask])
            proj_tm(768, 2, P)
            S.op("act", lambda h: h.copy(out=vtok[:].rearrange("p (a b) -> p a b", a=2), in_=pair768(2, P)), reads=[bB[4], bB[5]], writes=[vtok])
            mm(bank(1)[0:P, 0:256], [(hT[:, kc, 0:P], Wa[:, kc, 2320:2576]) for kc in range(8)], [Wa, hT], [bB[1]])
            S.op("act", lambda h: h.copy(out=qm_tok[:], in_=bank(1)[0:P, 0:256]), reads=[bB[1]], writes=[qm_tok])
            proj_tm(1536, 3, P)
            ck(20)
            zero_acc([bank(4)[0:NS, 0:384], bank(5)[0:NS, 0:384]], [bB[4], bB[5]])
            for i in range(NS):
                si, sn = Sin[i % 2], Snew[i % 2]
                S.dma("sp", si[:], st_in[i].rearrange("h d v -> d h v"), writes=[si])

                def fk(h, i=i):
                    for hh in range(4):
                        ins = h.matmul(bank(2 + hh // 2)[0:96, (hh % 2) * DV:(hh % 2 + 1) * DV], lhsT=kmask[0:NS, i, 96 * hh:96 * hh + 96],
                                       rhs=vtok[0:NS, hh * DV:(hh + 1) * DV], start=True, stop=True)
                    return ins
                S.op("pe", fk, reads=[kmask, vtok], writes=[bB[2], bB[3]])
                for hh in range(4):
                    S.op("dve", lambda h, hh=hh, i=i, si=si, sn=sn: h.scalar_tensor_tensor(
                        out=sn[:, hh, :], in0=si[:, hh, :], scalar=eg[:, hh, i:i + 1],
                        in1=bank(2 + hh // 2)[0:96, (hh % 2) * DV:(hh % 2 + 1) * DV], op0=ALU.mult, op1=ALU.add),
                        reads=[si, eg, bB[2], bB[3]], writes=[sn])
                S.dma("sp", sts[i].rearrange("h d v -> d h v"), sn[:], reads=[sn])

                def fq(h, i=i, sn=sn):
                    for hh in range(4):
                        ins = h.matmul(bank(4 + hh // 2)[0:NS, (hh % 2) * DV:(hh % 2 + 1) * DV], lhsT=qmask[:, hh, i, :], rhs=sn[:, hh, :],
                                       start=False, stop=(i == NS - 1 and hh % 2 == 1))
                    return ins
                S.op("pe", fq, reads=[qmask, sn], writes=[bB[4], bB[5]])
            ck(21)
            gla_norm_gate(P, 2, 3)
            ck(22)
            sab = alloc_sample_attn(es, "p2", onesel=TB_sub(kmask, 128))

            def mem_sample(l, qtok, sab):
                Kc, Vc, prod, sc8, pp8, Pm, onesel = sab
                zero_acc([bank(1)[0:NS, 0:260]], [bB[1]])
                for i in range(NS):
                    kc_, vc_ = Kc[i % 2], Vc[i % 2]
                    S.dma("sp", kc_[:], mk_in[l, i].rearrange("(mc p) n -> p mc n", p=128), writes=[kc_])
                    for mc in range(2):
                        S.dma("sp", vc_[:, mc, :, 0:64], mv_in[l, i, mc * 128:(mc + 1) * 128, :].rearrange("p (a b) -> p a b", a=4), writes=[vc_])
                    mm(bank(0)[:, 0:256], [(onesel[0:NS, i, :], qtok[0:NS, :])], [onesel, qtok], [bB[0]])
                    S.op("dve", lambda h, kc_=kc_: h.tensor_tensor(out=prod[:, 0:512].rearrange("p (a b) -> p a b", a=2), in0=kc_[:],
                                                                   in1=bc(bank(0)[:, 0:256], 1, 2), op=ALU.mult),
                         reads=[kc_, bB[0]], writes=[prod])
                    S.op("dve", lambda h: h.tensor_reduce(out=sc8[:, 0:8], in_=prod[:, 0:512].rearrange("p (a b) -> p a b", b=64), axis=AX.X, op=ALU.add),
                         reads=[prod], writes=[sc8])
                    S.op("act", lambda h: h.activation(out=pp8[:, 0:8], in_=sc8[:, 0:8], func=AF.Exp, scale=0.125), reads=[sc8], writes=[pp8])
                    S.op("dve", lambda h, i=i: h.tensor_tensor(out=Pm[:, 0:8, :], in0=bc(pp8[:, 0:8], 2, NS), in1=bc(id16r[:, i, :], 1, 8), op=ALU.mult),
                         reads=[pp8, id16r], writes=[Pm])

                    def fm(h, i=i, vc_=vc_):
                        for hh in range(4):
                            for mc in range(2):
                                ins = h.matmul(bank(1)[0:NS, hh * 65:(hh + 1) * 65], lhsT=Pm[:, mc * 4 + hh, :], rhs=vc_[:, mc, hh, :],
                                               start=False, stop=(i == NS - 1 and mc == 1 and hh == 3))
                        return ins
                    S.op("pe", fm, reads=[Pm, vc_], writes=[bB[1]])
                finish_mem(NS)

            mem_sample(0, qm_tok, sab)
            ck(23)
            mem_sample_fn.append(mem_sample)
            out_proj(P, Wo, gpost, xs_sb[:], xs_sb)
            S.flush()
        if 0 <= stop <= 1:
            S.dma("sp", ys, xs_sb[:], reads=[xs_sb])
            S.flush()
            return nc

        def mlp_phase(l, tiles, es):
            Wu = mk(es, "Wu%d" % l, [128, 8, 4096], BF16)
            Wd = mk(es, "Wd%d" % l, [128, 32, D], BF16)
            gpre = mk(es, "gfpre%d" % l, [128, D])
            gpost = mk(es, "gfpost%d" % l, [128, D])
            hTg = mk(es, "hTg%d" % l, [128, 8, 256], BF16)
            actT = mk(es, "actT%d" % l, [128, 32, 256], BF16)
            load_w(Wu, w_up[l], 8, 4096)
            load_w(Wd, w_down[l], 32, D)
            load_gain(gpre, g_ffn_pre[l, :])
            load_gain(gpost, g_ffn_post[l, :])
            groups = []
            i = 0
            while i < len(tiles):
                if tiles[i][0] == 128 and i + 1 < len(tiles) and tiles[i + 1][0] == 128:
                    groups.append(tiles[i:i + 2])
                    i += 2
                else:
                    groups.append(tiles[i:i + 1])
                    i += 1
            xtl = {}

            def n_elem(g, only=None):
                for ti, (P, src, sbuf, dst, dbuf) in enumerate(groups[g]):
                    if only is not None and ti != only:
                        continue
                    if src is None:
                        xtile, xap = xs_sb, xs_sb[:]
                    else:
                        xtile = next_xt()
                        xap = xtile[0:P, :]
                        S.dma("sp", xap, src, reads=[sbuf], writes=[xtile])
                    xtl[(g, ti)] = (xtile, xap)

            def n_pe(g):
                off = 0
                for ti, (P, src, sbuf, dst, dbuf) in enumerate(groups[g]):
                    xtile, xap = xtl[(g, ti)]
                    rms_stats(xap, P, [xtile], hb)
                    norm_to(xap, P, [xtile], gpre, hb)
                    transpose8(hb, P, hT, tb=ti % 2)
                    S.op("act", lambda h, off=off, P=P: h.copy(out=hTg[:, :, off:off + P], in_=hT[:, :, 0:P]), reads=[hT], writes=[hTg])
                    off += P

            def up(g):
                N = sum(t[0] for t in groups[g])
                for ffc in range(32):
                    bk = (2, 3, 0, 1)[ffc % 4]
                    mm(bank(bk)[:, 0:N], [(Wu[:, kc, ffc * 128:(ffc + 1) * 128], hTg[:, kc, 0:N]) for kc in range(8)], [Wu, hTg], [bB[bk]])
                    S.op("act", lambda h, ffc=ffc, bk=bk, N=N: h.activation(out=actT[:, ffc, 0:N], in_=bank(bk)[:, 0:N], func=AF.Square),
                         reads=[bB[bk]], writes=[actT])
                    S.op("dve", lambda h, ffc=ffc, bk=bk, N=N: h.scalar_tensor_tensor(out=actT[:, ffc, 0:N], in0=bank(bk)[:, 0:N], scalar=0.0,
                                                                                      in1=actT[:, ffc, 0:N], op0=ALU.is_gt, op1=ALU.mult),
                         reads=[bB[bk]], writes=[actT])

            def down(g):
                off = 0
                for ti, (P, src, sbuf, dst, dbuf) in enumerate(groups[g]):
                    xtile, xap = xtl[(g, ti)]
                    pair = 2 + ti % 2
                    for half in range(2):
                        mm(bank(2 * pair + half)[0:P, :], [(actT[:, ffc, off:off + P], Wd[:, ffc, half * 512:(half + 1) * 512]) for ffc in range(32)],
                           [actT, Wd], [bB[2 * pair + half]])
                    post_norm_residual(pair, P, gpost, xap, xtile)
                    if dst is not None:
                        S.dma("sp", dst, xap, reads=[xtile], writes=[dbuf])
                    off += P

            n_elem(0)
            n_pe(0)
            for g in range(len(groups)):
                up(g)
                if g + 1 < len(groups):
                    n_elem(g + 1, only=0)
                down(g)
                if g + 1 < len(groups):
                    if len(groups[g + 1]) > 1:
                        n_elem(g + 1, only=1)
                    n_pe(g + 1)

        with ExitStack() as es:
            tiles = [(128, xa[t * 128:(t + 1) * 128, :], b_xa[t], xb[t * 128:(t + 1) * 128, :], b_xb[t]) for t in range(NT1)]
            tiles.append((NS, None, None, None, None))
            mlp_phase(0, tiles, es)
            S.flush()
        if 0 <= stop <= 2:
            S.dma("sp", ys, xs_sb[:], reads=[xs_sb])
            S.flush()
            return nc

        with ExitStack() as es:
            es2 = ExitStack()
            Wkv = mk(es, "Wkv", [128, 8, 512], BF16)
            Wb = mk(es, "Wb", [128, 8, 1024], BF16)
            Wo = mk(es, "Wo1", [128, 8, D], BF16)
            gkv = mk(es, "gkv", [128, D])
            gpre = mk(es, "gpre1", [128, D])
            gpost = mk(es, "gpost1", [128, D])
            EBs = mk(es, "EBs", [128, 12])
            eb0 = mk(es, "eb0", [NS, 12])
            esink = mk(es, "esink", [128, 12])
            rb_sb = mk(es, "rb_sb", [32, 12])
            rbh = mk(es, "rbh", [32, 12, 128])
            ohu = mk(es, "ohu", [32, 384])
            validu = mk(es, "validu", [128, 384])
            ebt = mk(es, "ebt", [128, 384])
            kvf = mk(es, "kvf1", [128, 512])
            KT_all = mk(es2, "KT_all", [128, 2, 2, NT1 * 128], BF16)
            V_all = mk(es2, "V_all", [128, NT1, 4, 66], BF16)
            QT_all = mk(es2, "QT_all", [128, 6, NT * 128], BF16)
            QmT_all = mk(es2, "QmT_all", [128, 2, NT * 128], BF16)
            EB = mk(es2, "EB", [128, 12, 2, 128])
            es_sb = mk(es2, "es_sb", [128, 2, 384])
            pTg = [mk(es2, "pTg%d" % g, [128, 2, 3, 128], BF16) for g in range(4)]

            load_w(Wkv, w_kv, 8, 512)
            wbv = w_in_b.rearrange("(kc p) n -> p kc n", p=128)
            for i in range(6):
                s_, j_ = divmod(i, 3)
                for half in range(2):
                    hd = 6 * s_ + 3 * half + j_
                    S.dma("pool", Wb[:, :, i * 128 + half * 64:i * 128 + half * 64 + 64], wbv[:, :, hd * 64:(hd + 1) * 64], writes=[Wb])
            S.dma("pool", Wb[:, :, 768:1024], wbv[:, :, 768:1024], writes=[Wb])
            load_w(Wo, w_out[1], 8, D)
            load_gain(gkv, g_kv[0, :])
            load_gain(gpre, g_mix_pre[1, :])
            load_gain(gpost, g_mix_post[1, :])
            S.dma("sp", rb_sb[:], rel_bias, writes=[rb_sb])
            S.dma("sp", ohu[:], c_ohu, writes=[ohu])
            S.dma("sp", validu[:], c_valid, writes=[validu])
            S.dma("sp", esink[:], sinks[0, :].partition_broadcast(128), writes=[esink])
            S.op("act", lambda h: h.activation(out=esink[:], in_=esink[:], func=AF.Exp), reads=[], writes=[esink])
            S.op("dve", lambda h: h.tensor_copy(out=rbh[:], in_=bc(rb_sb[:], 2, 128)), reads=[rb_sb], writes=[rbh])
            for hd in range(12):
                mm(bank(2)[:, 0:384], [(rbh[:, hd, :], ohu[:])], [rbh, ohu], [bB[2]])
                S.op("act", lambda h: h.activation(out=ebt[:], in_=bank(2)[:, 0:384], func=AF.Exp), reads=[bB[2]], writes=[ebt])
                S.op("dve", lambda h: h.tensor_tensor(out=ebt[:], in0=ebt[:], in1=validu[:], op=ALU.mult), reads=[validu], writes=[ebt])
                S.dma("sp", ebd.ap()[hd], ebt[:], reads=[ebt], writes=[b_ebd])
            for hd in range(12):
                for kb in range(2):
                    S.dma("sp", EB[:, hd, kb, :], bass.AP(ebd, hd * 128 * 384 + 255 - 128 * kb, [[383, 128], [1, 128]]), reads=[b_ebd], writes=[EB])
                S.dma("sp", EBs[:, hd:hd + 1], bass.AP(ebd, hd * 128 * 384 + 255, [[383, 128], [1, 1]]), reads=[b_ebd], writes=[EBs], allow_slow_non_contiguous=True)
                S.dma("sp", eb0[:, hd:hd + 1], bass.AP(ebd, hd * 128 * 384 + 127, [[384, NS], [1, 1]]), reads=[b_ebd], writes=[eb0], allow_slow_non_contiguous=True)
            bKT = [Buf() for _ in range(NT1)]
            bV = [Buf() for _ in range(NT1)]
            bQ = [[Buf() for _ in range(6)] for _ in range(NT)]
            bQm = [[Buf() for _ in range(2)] for _ in range(NT)]
            B1S = [dict(hb=hb, hb2=hb2, hT=hT, hT2=hT2, ss=st["ss"], ln=st["ln"], rs=st["rstd"])]
            B1S.append(dict(hb=mk(es2, "b1_hb", [128, D], BF16), hb2=mk(es2, "b1_hb2", [128, D], BF16),
                            hT=mk(es2, "b1_hT", [128, 8, 128], BF16), hT2=mk(es2, "b1_hT2", [128, 8, 128], BF16),
                            ss=mk(es2, "b1_ss", [128, 1]), ln=mk(es2, "b1_ln", [128, 1]), rs=mk(es2, "b1_rs", [128, 1])))

            S.op("pool", lambda h: h.memset(V_all[:], 1.0), writes=bV)
            S.op("pool", lambda h: h.memset(KT_all[:], 0.0), writes=bKT)

            def b1_tile(t, Q, par):
                def q(fn, *a, **k):
                    Q.append((fn, a, k))
                W = B1S[par]
                hbP, hb2P, hTP, hT2P, ssP, lnP, rsP = W["hb"], W["hb2"], W["hT"], W["hT2"], W["ss"], W["ln"], W["rs"]
                bT, bA, bBk, bC = 4 * par, 4 * par + 1, 4 * par + 2, 4 * par + 3
                xtile = next_xt()
                q(S.dma, "sp", xtile[:], xb[t * 128:(t + 1) * 128, :], reads=[b_xb[t]], writes=[xtile])
                q(S.op, "act", lambda h: h.activation(out=hbP[:], in_=xtile[:], func=AF.Square, accum_out=ssP[:, 0:1]), reads=[xtile], writes=[hbP, ssP])
                q(S.op, "act", lambda h: h.activation(out=lnP[:, 0:1], in_=ssP[:, 0:1], func=AF.Ln, scale=1.0 / D, bias=eps_t[:, 0:1]), reads=[ssP, eps_t], writes=[lnP])
                q(S.op, "act", lambda h: h.activation(out=rsP[:, 0:1], in_=lnP[:, 0:1], func=AF.Exp, scale=-0.5), reads=[lnP], writes=[rsP])
                q(S.op, "dve", lambda h: h.scalar_tensor_tensor(out=hbP[:], in0=xtile[:], scalar=rsP[:, 0:1], in1=gkv[:], op0=ALU.mult, op1=ALU.mult),
                  reads=[xtile, rsP, gkv], writes=[hbP])

                def ftr(h, src=hbP):
                    for kc in range(8):
                        ins = h.transpose(out=bankb(bT)[:, kc * 128:(kc + 1) * 128], in_=src[:, kc * 128:(kc + 1) * 128], identity=ident_b[:])
                    return ins
                q(S.op, "pe", ftr, reads=[hbP, ident_b], writes=[bB[bT]])
                q(S.op, "act", lambda h: h.copy(out=hTP[:].rearrange("p a b -> p (a b)"), in_=bankb(bT)[:, :]), reads=[bB[bT]], writes=[hTP])
                for s_ in range(2):
                    q(mm, bank(bA)[:, s_ * 128:(s_ + 1) * 128], [(Wkv[:, kc, s_ * 128:(s_ + 1) * 128], hTP[:, kc, :]) for kc in range(8)], [Wkv, hTP], [bB[bA]])
                q(S.op, "act", lambda h: h.copy(out=KT_all[0:64, 0, :, t * 128:(t + 1) * 128], in_=bank(bA)[0:64, 0:256].rearrange("p (a b) -> p a b", a=2)),
                  reads=[bB[bA]], writes=[bKT[t]])
                q(S.op, "act", lambda h: h.copy(out=KT_all[64:128, 1, :, t * 128:(t + 1) * 128], in_=bank(bA)[64:128, 0:256].rearrange("p (a b) -> p a b", a=2)),
                  reads=[bB[bA]], writes=[bKT[t]])
                q(mm, bank(bBk)[:, 0:512], [(hTP[:, kc, :], Wkv[:, kc, :]) for kc in range(8)], [Wkv, hTP], [bB[bBk]])
                q(S.op, "dve", lambda h: h.tensor_copy(out=V_all[:, t, :, 0:64], in_=bank(bBk)[:, 256:512].rearrange("p (a b) -> p a b", a=4)),
                  reads=[bB[bBk]], writes=[bV[t]])
                if t == NT:
                    q(S.op, "act", lambda h: h.copy(out=kvf[:], in_=bank(bBk)[:, 0:512]), reads=[bB[bBk]], writes=[kvf])
                    q(S.dma, "sp", swkp, kvf[:, 0:256], reads=[kvf])
                    q(S.dma, "sp", swvp, kvf[:, 256:512], reads=[kvf])
                if t == 0:
                    q(S.op, "dve", lambda h: h.tensor_scalar(out=V_all[:, 0, :, :], in0=V_all[:, 0, :, :], scalar1=flag[:, 0:1], scalar2=None, op0=ALU.mult),
                      reads=[flag], writes=[bV[0]])
                    return
                q(S.op, "dve", lambda h: h.scalar_tensor_tensor(out=hb2P[:], in0=xtile[:], scalar=rsP[:, 0:1], in1=gpre[:], op0=ALU.mult, op1=ALU.mult),
                  reads=[xtile, rsP, gpre], writes=[hb2P])
                q(S.op, "pe", lambda h: ftr(h, src=hb2P), reads=[hb2P, ident_b], writes=[bB[bT]])
                q(S.op, "act", lambda h: h.copy(out=hT2P[:].rearrange("p a b -> p (a b)"), in_=bankb(bT)[:, :]), reads=[bB[bT]], writes=[hT2P])
                for i in range(8):
                    bk = (bA, bBk, bC)[i % 3]
                    q(mm, bank(bk)[:, 0:128], [(Wb[:, kc, i * 128:(i + 1) * 128], hT2P[:, kc, :]) for kc in range(8)], [Wb, hT2P], [bB[bk]])
                    dst = QT_all[:, i, (t - 1) * 128:t * 128] if i < 6 else QmT_all[:, i - 6, (t - 1) * 128:t * 128]
                    tok = bQ[t - 1][i] if i < 6 else bQm[t - 1][i - 6]
                    q(S.op, "act" if i % 2 else "dve",
                      (lambda h, dst=dst, bk=bk: h.copy(out=dst, in_=bank(bk)[:, 0:128])) if i % 2 else
                      (lambda h, dst=dst, bk=bk: h.tensor_copy(out=dst, in_=bank(bk)[:, 0:128])),
                      reads=[bB[bk]], writes=[tok])

            for t0 in range(0, NT1, 2):
                lists = []
                for par, t in enumerate(range(t0, min(t0 + 2, NT1))):
                    Q = []
                    b1_tile(t, Q, par)
                    lists.append(Q)
                for i_ in range(max(len(L) for L in lists)):
                    for L in lists:
                        if i_ < len(L):
                            fn_, a_, k_ = L[i_]
                            fn_(*a_, **k_)

            def swa_finish(P, extra=None):
                o3 = [bank(6 + i)[0:P, 0:390].rearrange("p (a b) -> p a b", a=6) for i in range(2)]
                for i in range(2):
                    S.op("dve", lambda h, i=i: h.tensor_tensor(out=den12[0:P, 6 * i:6 * i + 6], in0=o3[i][:, :, 64], in1=esink[0:P, 6 * i:6 * i + 6], op=ALU.add),
                         reads=[bB[6 + i], esink], writes=[den12])
                if extra is not None:
                    S.op("dve", lambda h: h.tensor_tensor(out=den12[0:P, :], in0=den12[0:P, :], in1=extra[0:P, :], op=ALU.add),
                         reads=[extra], writes=[den12])
                S.op("dve", lambda h: h.reciprocal(out=rden12[0:P, :], in_=den12[0:P, :]), reads=[den12], writes=[rden12])

            def swa_norm(P, srcs, src_reads):
                for i in range(2):
                    S.op("dve", lambda h, i=i: h.tensor_tensor(out=mixcat[0:P, 384 * i:384 * (i + 1)].rearrange("p (a b) -> p a b", a=6), in0=srcs[i],
                                                               in1=bc(rden12[0:P, 6 * i:6 * i + 6], 2, 64), op=ALU.mult),
                         reads=list(src_reads) + [rden12], writes=[mixcat])

            def b2(t):
                P = 128
                ebt_ = EB
                for g in range(4):
                    s_, p_ = divmod(g, 2)
                    pr = 1 + g % 2
                    def fsc(h, g=g, s_=s_, p_=p_, pr=pr):
                        for kb in range(2):
                            ins = h.matmul(bank(2 * pr + kb)[:, 0:384], lhsT=KT_all[:, p_, s_, (t + kb) * 128:(t + kb + 1) * 128],
                                           rhs=QT_all[:, 3 * s_:3 * s_ + 3, t * 128:(t + 1) * 128], start=True, stop=True)
                        return ins
                    S.op("pe", fsc, reads=[bKT[t], bKT[t + 1]] + bQ[t], writes=[bB[2 * pr], bB[2 * pr + 1]])
                    S.op("act", lambda h, pr=pr: h.activation(out=es_sb[:], in_=pair768(pr, 128), func=AF.Exp, scale=0.125),
                         reads=[bB[2 * pr], bB[2 * pr + 1]], writes=[es_sb])
                    S.op("dve", lambda h, g=g, ebt_=ebt_: h.tensor_tensor(out=pTg[g][:], in0=es_sb[:].rearrange("p a (j q) -> p a j q", j=3),
                                                                          in1=ebt_[:, 3 * g:3 * g + 3, :, :].rearrange("p j a q -> p a j q"), op=ALU.mult),
                         reads=[es_sb, ebt_], writes=[pTg[g]])
                    def fo(h, g=g):
                        for j in range(3):
                            hd = 3 * g + j
                            for kb in range(2):
                                ins = h.matmul(bank(6 + hd // 6)[:, (hd % 6) * 65:(hd % 6 + 1) * 65], lhsT=pTg[g][:, kb, j, :],
                                               rhs=V_all[:, t + kb, g, 0:65], start=(kb == 0), stop=(kb == 1))
                        return ins
                    S.op("pe", fo, reads=[pTg[g], bV[t], bV[t + 1]], writes=[bB[6], bB[7]])
                swa_finish(P)
                o3 = [bank(6 + i)[0:P, 0:390].rearrange("p (a b) -> p a b", a=6)[:, :, 0:64] for i in range(2)]
                swa_norm(P, o3, [bB[6], bB[7]])
                mem_attend_prompt(1, lambda c, p: QmT_all[:, c, t * 128:(t + 1) * 128], bQm[t], P)
                xtile = next_xt()
                S.dma("sp", xtile[:], xb[(t + 1) * 128:(t + 2) * 128, :], reads=[b_xb[t + 1]], writes=[xtile])
                out_proj(P, Wo, gpost, xtile[:], xtile)
                S.dma("sp", xa[t * 128:(t + 1) * 128, :], xtile[:], reads=[xtile], writes=[b_xa[t]])

            for t in range(NT):
                b2(t)

            S.flush()
            es2.close()
            P = NS
            sab = alloc_sample_attn(es, "p4")
            Kc, Vc, prod, sc8, pp8, Pm, onesel = sab
            knew = mk(es, "knew", [NS, 512])
            qtok = mk(es, "qtok", [NS, 1024])
            Ks = [mk(es, "Ks%d" % i, [128, 4, 64]) for i in range(2)]
            Vs = [mk(es, "Vs%d" % i, [128, 4, 65]) for i in range(2)]
            sn12 = mk(es, "sn12", [NS, 12])
            pn12 = mk(es, "pn12", [NS, 12])
            tmpv = mk(es, "tmpv", [NS, 768])
            osum = mk(es, "osum", [NS, 768])
            S.op("pool", lambda h: h.memset(Vs[0][:], 1.0), writes=[Vs[0]])
            S.op("pool", lambda h: h.memset(Vs[1][:], 1.0), writes=[Vs[1]])
            rms_stats(xs_sb[:], P, [xs_sb], hb)
            norm_to(xs_sb[:], P, [xs_sb], gkv, hb)
            transpose8(hb, P, hT, tb=0)
            mm(bank(3)[0:P, 0:512], [(hT[:, kc, 0:P], Wkv[:, kc, :]) for kc in range(8)], [Wkv, hT], [bB[3]])
            S.op("act", lambda h: h.copy(out=knew[:], in_=bank(3)[0:P, 0:512]), reads=[bB[3]], writes=[knew])
            norm_to(xs_sb[:], P, [xs_sb], gpre, hb2)
            transpose8(hb2, P, hT2, tb=1)
            for half in range(2):
                mm(bank(4 + half)[0:P, :], [(hT2[:, kc, 0:P], Wb[:, kc, half * 512:(half + 1) * 512]) for kc in range(8)], [Wb, hT2], [bB[4 + half]])
            S.op("act", lambda h: h.copy(out=qtok[:], in_=PS[2][0:P, :]), reads=[bB[4], bB[5]], writes=[qtok])
            S.dma("sp", swks[:, 0:127, :], swk_in[:, 1:128, :])
            S.dma("sp", swvs[:, 0:127, :], swv_in[:, 1:128, :])
            S.dma("sp", swks[:, 127, :], knew[:, 0:256], reads=[knew])
            S.dma("sp", swvs[:, 127, :], knew[:, 256:512], reads=[knew])

            def qview(ap2d, s_):
                return ap2d[:, s_ * 384:(s_ + 1) * 384].rearrange("p (j a d) -> p a j d", j=3, a=2)

            zero_acc([bank(6)[0:NS, 0:390], bank(7)[0:NS, 0:390]], [bB[6], bB[7]])
            for i in range(NS):
                ks_, vs_ = Ks[i % 2], Vs[i % 2]
                S.dma("sp", ks_[:], swk_in[i].rearrange("w (g d) -> w g d", g=4), writes=[ks_])
                S.dma("sp", vs_[:, :, 0:64], swv_in[i].rearrange("w (g d) -> w g d", g=4), writes=[vs_])
                for half in range(2):
                    mm(bank(2 + half)[:, 0:384], [(onesel[0:NS, i, :], qtok[0:NS, half * 384:(half + 1) * 384])], [onesel, qtok], [bB[2 + half]])
                for s_ in range(2):
                    S.op("dve", lambda h, s_=s_, ks_=ks_: h.tensor_tensor(
                        out=prod[:, 384 * s_:384 * (s_ + 1)].rearrange("p (a j d) -> p a j d", a=2, j=3),
                        in0=bc(ks_[:, 2 * s_:2 * s_ + 2, :], 2, 3), in1=qview(bank(2 + s_)[:, 0:384], 0), op=ALU.mult),
                        reads=[ks_, bB[2 + s_]], writes=[prod])
                S.op("dve", lambda h: h.tensor_reduce(out=sc8[:, :], in_=prod[:, :].rearrange("p (a b) -> p a b", b=64), axis=AX.X, op=ALU.add),
                     reads=[prod], writes=[sc8])
                S.op("act", lambda h: h.activation(out=pp8[:, :], in_=sc8[:, :], func=AF.Exp, scale=0.125), reads=[sc8], writes=[pp8])
                S.op("dve", lambda h: h.tensor_tensor(out=pp8[:, :], in0=pp8[:, :], in1=EBs[:, :], op=ALU.mult), reads=[EBs], writes=[pp8])
                S.op("dve", lambda h, i=i: h.tensor_tensor(out=Pm[:, :, :], in0=bc(pp8[:, :], 2, NS), in1=bc(id16r[:, i, :], 1, 12), op=ALU.mult),
                     reads=[pp8, id16r], writes=[Pm])

                def fo2(h, i=i, vs_=vs_):
                    for hd in range(12):
                        ins = h.matmul(bank(6 + hd // 6)[0:NS, (hd % 6) * 65:(hd % 6 + 1) * 65], lhsT=Pm[:, hd, :], rhs=vs_[:, hd // 3, :],
                                       start=False, stop=(i == NS - 1 and hd % 6 == 5))
                    return ins
                S.op("pe", fo2, reads=[Pm, vs_], writes=[bB[6], bB[7]])
            for s_ in range(2):
                S.op("dve", lambda h, s_=s_: h.tensor_tensor(
                    out=prod[0:NS, 384 * s_:384 * (s_ + 1)].rearrange("p (a j d) -> p a j d", a=2, j=3),
                    in0=bc(knew[:, 0:256].rearrange("p (g d) -> p g d", g=4)[:, 2 * s_:2 * s_ + 2, :], 2, 3), in1=qview(qtok[:, 0:768], s_), op=ALU.mult),
                    reads=[knew, qtok], writes=[prod])
            S.op("dve", lambda h: h.tensor_reduce(out=sn12[:], in_=prod[0:NS, :].rearrange("p (a b) -> p a b", b=64), axis=AX.X, op=ALU.add),
                 reads=[prod], writes=[sn12])
            S.op("act", lambda h: h.activation(out=pn12[:], in_=sn12[:], func=AF.Exp, scale=0.125), reads=[sn12], writes=[pn12])
            S.op("dve", lambda h: h.tensor_tensor(out=pn12[:], in0=pn12[:], in1=eb0[:], op=ALU.mult), reads=[eb0], writes=[pn12])
            S.op("dve", lambda h: h.tensor_tensor(out=tmpv[:].rearrange("p (g j d) -> p g j d", g=4, j=3),
                                                  in0=bc(knew[:, 256:512].rearrange("p (g d) -> p g d", g=4), 2, 3),
                                                  in1=bc(pn12[:].rearrange("p (g j) -> p g j", g=4), 3, 64), op=ALU.mult),
                 reads=[knew, pn12], writes=[tmpv])
            swa_finish(P, extra=pn12)
            for i in range(2):
                S.op("dve", lambda h, i=i: h.tensor_tensor(out=osum[:, 384 * i:384 * (i + 1)].rearrange("p (a b) -> p a b", a=6),
                                                           in0=bank(6 + i)[0:P, 0:390].rearrange("p (a b) -> p a b", a=6)[:, :, 0:64],
                                                           in1=tmpv[:, 384 * i:384 * (i + 1)].rearrange("p (a b) -> p a b", a=6), op=ALU.add),
                     reads=[bB[6 + i], tmpv], writes=[osum])
            swa_norm(P, [osum[:, 384 * i:384 * (i + 1)].rearrange("p (a b) -> p a b", a=6) for i in range(2)], [osum])
            mem_sample_fn[0](1, TB_view(qtok, 768), sab)
            out_proj(P, Wo, gpost, xs_sb[:], xs_sb)
            S.flush()
        if 0 <= stop <= 3:
            S.dma("sp", ys, xs_sb[:], reads=[xs_sb])
            S.flush()
            return nc

        with ExitStack() as es:
            tiles = [(128, xa[t * 128:(t + 1) * 128, :], b_xa[t], y[t * 128:(t + 1) * 128, :], b_y) for t in range(NT)]
            tiles.append((NS, None, None, None, None))
            mlp_phase(1, tiles, es)
            S.dma("sp", ys, xs_sb[:], reads=[xs_sb])
            S.flush()
    return nc


class TB_sub:
    def __init__(self, tb, n):
        self.tb = tb
        self.b = tb.b
        self.n = n

    def __getitem__(self, k):
        if k == slice(None, None, None):
            return self.tb.t[:, :, 0:self.n]
        a, i, c = k
        assert c == slice(None, None, None)
        return self.tb.t[a, i, 0:self.n]


class TB_view:
    def __init__(self, tb, c0):
        self.tb = tb
        self.b = tb.b
        self.c0 = c0

    def __getitem__(self, k):
        rows, cols = k
        assert cols == slice(None, None, None)
        return self.tb.t[rows, self.c0:self.c0 + 256]


def _consts():
    c = {}
    c["c_ident"] = np.eye(128, dtype=np.float32)
    s = np.arange(128)[:, None]
    t = np.arange(128)[None, :]
    c["c_cmask"] = (s <= t).astype(np.float32)
    u = np.arange(384)
    dist = u - 127
    valid = (dist >= 0) & (dist < 128)
    n = np.maximum(dist, 0)
    nf = np.maximum(n, 1).astype(np.float32)
    large = 16 + (np.log(nf / np.float32(16)) / np.float32(math.log(128 / 16)) * np.float32(16)).astype(np.int32)
    large = np.minimum(large, 31)
    bucket = np.where(n < 16, n, large)
    oh = np.zeros((32, 384), np.float32)
    oh[bucket, u] = 1.0
    oh[:, ~valid] = 0.0
    c["c_ohu"] = oh
    c["c_valid"] = np.repeat(valid.astype(np.float32)[None, :], 128, axis=0)
    c["c_id16r"] = np.repeat(np.eye(16, dtype=np.float32).reshape(1, 256), 128, axis=0)
    return c


_NC_CACHE = {}


def kernel(**inp):
    NT, NPRE, NS = 16, 47, 16
    f = lambda a: np.ascontiguousarray(np.asarray(a, dtype=np.float32))
    xp = f(inp["x_prompt"])
    B, L, _ = xp.shape
    NT = L // (4 * 128)
    NPRE = 3 * NT - 1
    key = (NT, NPRE, NS)
    if key not in _NC_CACHE:
        import os
        _NC_CACHE[key] = build(NT, NPRE, NS, stop=int(os.environ.get('KSTOP', '99')))
    nc = _NC_CACHE[key]
    consts = _consts()
    shared = {}
    for k in ("norm_mix_pre", "norm_mix_post", "norm_ffn_pre", "norm_ffn_post", "norm_mem", "w_mem_kv", "w_gate_up", "b_gate",
              "gla_norm", "sinks", "rel_bias", "w_out", "w_ffn_up", "w_ffn_down"):
        shared[k] = f(inp[k])
    shared["w_in_a"] = f(inp["w_in_a"])[0]
    shared["w_in_b"] = f(inp["w_in_b"])[0]
    shared["w_gate_up"] = f(inp["w_gate_up"])[0]
    shared["norm_kv"] = f(inp["norm_kv"]).reshape(1, D)
    shared["w_kv"] = f(inp["w_kv"])
    shared.update(consts)
    xs = f(inp["x_sample"]).reshape(-1, D)
    st = f(inp["state_gla"])[0]
    swk = f(inp["cache_swa_k"]).reshape(-1, 128, 256)
    swv = f(inp["cache_swa_v"]).reshape(-1, 128, 256)
    mkc = f(inp["cache_mem_k"]).reshape(2, -1, 256, 256)
    mvc = f(inp["cache_mem_v"]).reshape(2, -1, 256, 256)
    mem = f(inp["mem_prompt"])
    pad = np.zeros(((NPRE + 1) * 128, D), np.float32)
    in_maps = []
    for c in range(NCORES):
        b, j = divmod(c, 4)
        xpad = np.concatenate([pad, xp[b]], axis=0)
        m = dict(shared)
        m["xin"] = np.ascontiguousarray(xpad[j * NT * 128:(j * NT + NPRE + 1 + NT) * 128])
        sl = slice(c * NS, (c + 1) * NS)
        m["xs_in"] = xs[sl]
        m["st_in"] = st[sl]
        m["swk_in"] = swk[sl]
        m["swv_in"] = swv[sl]
        m["mk_in"] = np.ascontiguousarray(mkc[:, sl])
        m["mv_in"] = np.ascontiguousarray(mvc[:, sl])
        m["mem_in"] = mem[b]
        m["c_flag"] = np.full((128, 1), 1.0 if j > 0 else 0.0, np.float32)
        in_maps.append(m)
    res = run_bass_kernel_spmd(nc, in_maps, core_ids=list(range(NCORES)))
    R = res.results
    y_prompt = np.stack([np.concatenate([R[4 * b + j]["y"] for j in range(4)], axis=0) for b in range(B)])
    y_sample = np.concatenate([R[c]["ys"] for c in range(NCORES)], axis=0).reshape(-1, 1, D)
    stp = np.stack([R[4 * b + 3]["stp"] for b in range(B)])[None]
    sts = np.concatenate([R[c]["sts"] for c in range(NCORES)], axis=0)[None]
    swkp = np.stack([R[4 * b + 3]["swkp"] for b in range(B)]).reshape(B, 128, 4, 64)
    swvp = np.stack([R[4 * b + 3]["swvp"] for b in range(B)]).reshape(B, 128, 4, 64)
    swks = np.concatenate([R[c]["swks"] for c in range(NCORES)], axis=0).reshape(-1, 128, 4, 64)
    swvs = np.concatenate([R[c]["swvs"] for c in range(NCORES)], axis=0).reshape(-1, 128, 4, 64)
    mkp = np.stack([R[4 * b]["mkp"] for b in range(B)], axis=1).reshape(2, B, 256, 4, 64)
    mvp = np.stack([R[4 * b]["mvp"] for b in range(B)], axis=1).reshape(2, B, 256, 4, 64)
    return (y_prompt, y_sample, stp, sts, swkp, swvp, swks, swvs, mkp, mvp)
```

```python
import math
import numpy as np
from contextlib import ExitStack
import concourse.bass as bass
import concourse.mybir as mybir
from concourse.bass_utils import run_bass_kernel_spmd

F32 = mybir.dt.float32
BF16 = mybir.dt.bfloat16
ALU = mybir.AluOpType
AF = mybir.ActivationFunctionType
AX = mybir.AxisListType

D = 1024
DK, DV, H = 96, 192, 4
EPS = 1e-6
NCORES = 8


class Buf:
    __slots__ = ("name", "w", "r", "excl")

    def __init__(self, name="", excl=False):
        self.name = name
        self.w = None
        self.r = {}
        self.excl = excl


class StopBuild(Exception):
    pass


class TB:
    def __init__(self, t, name=""):
        self.t = t
        self.b = Buf(name)

    def __getitem__(self, k):
        return self.t[k]


class Sched:
    ENGS = ("pe", "act", "dve", "pool", "sp")
    BLK = {"pe": "tensor", "act": "scalar", "dve": "vector", "pool": "gpsimd", "sp": "sync"}

    def __init__(self, nc, es, n_dma=(28, 14, 8)):
        self.nc = nc
        self.sem = {e: es.enter_context(nc.semaphore("s_" + e)) for e in self.ENGS}
        self.cnt = {e: 0 for e in self.ENGS}
        self.known = {e: {e2: 0 for e2 in self.ENGS} for e in self.ENGS}
        self.dq = {}
        k = 0
        for q, n in zip(("sp", "pool", "act"), n_dma):
            self.dq[q] = list(range(k, k + n))
            k += n
        self.dsem = [es.enter_context(nc.semaphore("d%d" % i)) for i in range(k)]
        self.dcnt = [0] * k
        self.drr = {q: 0 for q in self.dq}
        self.kdma = {e: [0] * k for e in self.ENGS}
        self.prog = {e: [] for e in self.ENGS}

    @staticmethod
    def _b(x):
        return getattr(x, "b", x)

    def _deps(self, reads, writes):
        deps = []
        for b in reads:
            b = self._b(b)
            if b.w is not None:
                deps.append(b.w)
        for b in writes:
            b = self._b(b)
            if b.w is not None:
                deps.append(b.w)
            deps.extend(b.r.values())
        return deps

    def _resolve(self, eng, deps):
        need = {}
        for ev in deps:
            if ev[0] == "c":
                _, e2, k = ev
                if e2 == eng and eng == "pe":
                    continue
                if self.known[eng][e2] >= k:
                    continue
                need[("c", e2)] = max(need.get(("c", e2), 0), k)
            else:
                _, idx, v = ev
                if self.kdma[eng][idx] >= v:
                    continue
                need[("d", idx)] = max(need.get(("d", idx), 0), v)
        waits = []
        for (kind, key), v in need.items():
            if kind == "c":
                self.known[eng][key] = v
                waits.append((self.sem[key], v))
            else:
                self.kdma[eng][key] = v
                waits.append((self.dsem[key], v))
        return waits

    def _mark(self, ev, key, reads, writes):
        for b in reads:
            self._b(b).r[key] = ev
        for b in writes:
            b = self._b(b)
            b.w = ev
            b.r = {}

    def op(self, eng, fn, reads=(), writes=()):
        ex = [b for b in reads if self._b(b).excl]
        if ex:
            reads = [b for b in reads if not self._b(b).excl]
            writes = list(writes) + ex
        waits = self._resolve(eng, self._deps(reads, writes))
        self.cnt[eng] += 1
        ev = ("c", eng, self.cnt[eng])
        self.prog[eng].append((waits, fn, None))
        self._mark(ev, eng, reads, writes)
        return ev

    def dma(self, q, out, in_, reads=(), writes=(), **kw):
        waits = self._resolve(q, self._deps(reads, writes))
        pool = self.dq[q]
        idx = pool[self.drr[q] % len(pool)]
        self.drr[q] += 1
        v0 = 16 * self.dcnt[idx]
        if v0 > 0 and self.kdma[q][idx] < v0:
            self.kdma[q][idx] = v0
            waits.append((self.dsem[idx], v0))
        self.dcnt[idx] += 1
        ev = ("d", idx, 16 * self.dcnt[idx])
        self.prog[q].append((waits, lambda h: h.dma_start(out=out, in_=in_, **kw), idx))
        self._mark(ev, ("d", idx), reads, writes)
        return ev

    def flush(self):
        nc = self.nc
        tails = {}
        for q, pool in self.dq.items():
            tl = []
            for idx in pool:
                v = 16 * self.dcnt[idx]
                if v > 0 and self.kdma[q][idx] < v:
                    tl.append((self.dsem[idx], v))
            tails[q] = tl
        with nc.Block() as block:
            for e in self.ENGS:
                items = self.prog[e]
                tl = tails.get(e, [])
                if not items and not tl:
                    continue

                def body(h, items=items, tl=tl, e=e):
                    for waits, fn, didx in items:
                        for s, v in waits:
                            h.wait_ge(s, v)
                        ins = fn(h)
                        if didx is None:
                            ins.then_inc(self.sem[e], 1)
                        else:
                            ins.then_inc(self.dsem[didx], 16)
                    for s, v in tl:
                        h.wait_ge(s, v)

                getattr(block, self.BLK[e])(body)
        self.prog = {e: [] for e in self.ENGS}
        for e in self.ENGS:
            for e2 in self.ENGS:
                self.known[e][e2] = self.cnt[e2]
            for i in range(len(self.dcnt)):
                self.kdma[e][i] = 16 * self.dcnt[i]


def bc(ap, dim, n):
    u = ap.unsqueeze(dim)
    shp = list(u.shape)
    shp[dim] = n
    return u.broadcast_to(shp)


def qcol(h):
    s, r = divmod(h, 6)
    half, j = divmod(r, 3)
    return (3 * s + j) * 128 + half * 64


def build(NT=16, NPRE=47, NS=16, stop=99):
    try:
        return _build(NT, NPRE, NS, stop)
    except StopBuild as e:
        return e.args[0]


def _build(NT=16, NPRE=47, NS=16, stop=99):
    nc = bass.Bass("TRN2", target_bir_lowering=False)
    NTI = NPRE + 1 + NT
    NT1 = NT + 1

    def din(name, shape):
        return nc.dram_tensor(name, list(shape), F32, kind="ExternalInput").ap()

    def dout(name, shape):
        return nc.dram_tensor(name, list(shape), F32, kind="ExternalOutput").ap()

    xin = din("xin", [NTI * 128, D])
    xs_in = din("xs_in", [NS, D])
    st_in = din("st_in", [NS, H, DK, DV])
    swk_in = din("swk_in", [NS, 128, 256])
    swv_in = din("swv_in", [NS, 128, 256])
    mk_in = din("mk_in", [2, NS, 256, 256])
    mv_in = din("mv_in", [2, NS, 256, 256])
    mem_in = din("mem_in", [256, D])
    g_mix_pre = din("norm_mix_pre", [2, D])
    g_mix_post = din("norm_mix_post", [2, D])
    g_ffn_pre = din("norm_ffn_pre", [2, D])
    g_ffn_post = din("norm_ffn_post", [2, D])
    g_mem = din("norm_mem", [2, D])
    w_mem_kv = din("w_mem_kv", [2, D, 512])
    w_in_a = din("w_in_a", [D, 2576])
    w_gate_up = din("w_gate_up", [16, 384])
    b_gate = din("b_gate", [1, 384])
    gla_norm = din("gla_norm", [1, DV])
    w_in_b = din("w_in_b", [D, 1024])
    sinks = din("sinks", [1, 12])
    g_kv = din("norm_kv", [1, D])
    w_kv = din("w_kv", [D, 512])
    rel_bias = din("rel_bias", [32, 12])
    w_out = din("w_out", [2, D, D])
    w_up = din("w_ffn_up", [2, D, 4096])
    w_down = din("w_ffn_down", [2, 4096, D])
    c_ident = din("c_ident", [128, 128])
    c_cmask = din("c_cmask", [128, 128])
    c_ohu = din("c_ohu", [32, 384])
    c_valid = din("c_valid", [128, 384])
    c_id16r = din("c_id16r", [128, 256])
    c_flag = din("c_flag", [128, 1])

    y = dout("y", [NT * 128, D])
    ys = dout("ys", [NS, D])
    stp = dout("stp", [H, DK, DV])
    sts = dout("sts", [NS, H, DK, DV])
    swkp = dout("swkp", [128, 256])
    swvp = dout("swvp", [128, 256])
    swks = dout("swks", [NS, 128, 256])
    swvs = dout("swvs", [NS, 128, 256])
    mkp = dout("mkp", [2, 256, 256])
    mvp = dout("mvp", [2, 256, 256])

    xa = nc.dram_tensor("xa_scr", [NT1 * 128, D], F32).ap()
    xb = nc.dram_tensor("xb_scr", [NT1 * 128, D], F32).ap()
    ebd = nc.dram_tensor("ebd_scr", [12, 128, 384], F32)
    b_xa = [Buf() for _ in range(NT1)]
    b_xb = [Buf() for _ in range(NT1)]
    b_ebd = Buf()
    b_y = Buf()

    with ExitStack() as ges:
        S = Sched(nc, ges)

        def ck(n):
            if stop == -100 - n:
                S.flush()
                raise StopBuild(nc)

        def mk(es, name, shape, dt=F32):
            return TB(es.enter_context(nc.sbuf_tensor(name, list(shape), dt)), name)

        PS = [ges.enter_context(nc.psum_tensor("PS%d" % i, [128, 1024], F32)) for i in range(4)]
        bB = [Buf("bank%d" % i, excl=True) for i in range(8)]

        def bank(i):
            return PS[i // 2][:, (i % 2) * 512:(i % 2 + 1) * 512]

        def bankb(i):
            return bank(i).bitcast(BF16)

        ident_f = mk(ges, "ident_f", [128, 128])
        ident_b = mk(ges, "ident_b", [128, 128], BF16)
        cmask = mk(ges, "cmask", [128, 128])
        id16r = mk(ges, "id16r", [128, 16, 16])
        ones_row = mk(ges, "ones_row", [1, 128], BF16)
        ones96 = mk(ges, "ones96", [96, 128])
        flag = mk(ges, "flag", [128, 1])
        KmT = [mk(ges, "KmT%d" % l, [128, 2, 2, 256], BF16) for l in range(2)]
        Vm = [mk(ges, "Vm%d" % l, [128, 2, 4, 66], BF16) for l in range(2)]
        xs_sb = mk(ges, "xs_sb", [NS, D])
        st = {n: mk(ges, "st_" + n, [128, 4]) for n in ("ss", "ln", "rstd", "ssg", "lng", "rstdg", "rden", "el")}
        rden12 = mk(ges, "rden12", [128, 12])
        den12 = mk(ges, "den12", [128, 12])
        xt = [mk(ges, "xt%d" % i, [128, D]) for i in range(3)]
        hb = mk(ges, "hb", [128, D], BF16)
        hb2 = mk(ges, "hb2", [128, D], BF16)
        hT = mk(ges, "hT", [128, 8, 128], BF16)
        hT2 = mk(ges, "hT2", [128, 8, 128], BF16)
        mixcat = mk(ges, "mixcat", [128, D], BF16)
        mcT = mk(ges, "mcT", [128, 8, 128], BF16)
        tt = mk(ges, "tt", [128, D])
        pT_sb = mk(ges, "pT_sb", [128, 8, 128], BF16)
        qmT_sb = mk(ges, "qmT_sb", [128, 2, 128], BF16)
        z16 = mk(ges, "z16", [16, 512])

        xt_rr = [0]

        def next_xt():
            t = xt[xt_rr[0] % 3]
            xt_rr[0] += 1
            return t

        def load_gain(dst, src_row):
            S.dma("sp", dst[:], src_row.partition_broadcast(128), writes=[dst])

        def load_w(dst, src2d, kchunks, ncols, col0=0, dcol0=0):
            v = src2d.rearrange("(kc p) n -> p kc n", p=128)
            step = 2048
            for kc in range(kchunks):
                c = 0
                while c < ncols:
                    n = min(step, ncols - c)
                    S.dma("pool", dst[:, kc, dcol0 + c:dcol0 + c + n], v[:, kc, col0 + c:col0 + c + n], writes=[dst])
                    c += n

        def rms_stats(x_ap, P, x_reads, junk, n=D):
            ss, ln, rstd = st["ss"], st["ln"], st["rstd"]
            S.op("act", lambda h: h.activation(out=junk[0:P, 0:n], in_=x_ap, func=AF.Square, accum_out=ss[0:P, 0:1]),
                 reads=x_reads, writes=[junk, ss])
            S.op("act", lambda h: h.activation(out=ln[0:P, 0:1], in_=ss[0:P, 0:1], func=AF.Ln, scale=1.0 / n, bias=eps_t[0:P, 0:1]),
                 reads=[ss, eps_t], writes=[ln])
            S.op("act", lambda h: h.activation(out=rstd[0:P, 0:1], in_=ln[0:P, 0:1], func=AF.Exp, scale=-0.5),
                 reads=[ln], writes=[rstd])

        def norm_to(x_ap, P, x_reads, gain, dst):
            S.op("dve", lambda h: h.scalar_tensor_tensor(out=dst[0:P, :], in0=x_ap, scalar=st["rstd"][0:P, 0:1], in1=gain[0:P, :],
                                                         op0=ALU.mult, op1=ALU.mult),
                 reads=list(x_reads) + [st["rstd"], gain], writes=[dst])

        def transpose8(src, P, dstT, tb=0):
            def f(h):
                for kc in range(8):
                    ins = h.transpose(out=bankb(tb)[:, kc * 128:kc * 128 + P], in_=src[0:P, kc * 128:(kc + 1) * 128],
                                      identity=ident_b[0:P, 0:P])
                return ins
            S.op("pe", f, reads=[src, ident_b], writes=[bB[tb]])
            S.op("act", lambda h: h.copy(out=dstT[:, :, 0:P], in_=bankb(tb)[:, :].rearrange("p (a b) -> p a b", a=8)[:, :, 0:P]),
                 reads=[bB[tb]], writes=[dstT])

        def mm(out_ap, pairs, reads, wbank):
            def f(h):
                n = len(pairs)
                for i, (l, r) in enumerate(pairs):
                    ins = h.matmul(out_ap, lhsT=l, rhs=r, start=(i == 0), stop=(i == n - 1))
                return ins
            S.op("pe", f, reads=reads, writes=wbank)

        def post_norm_residual(mix_pair, P, gain, xtile_ap, xtile_tb):
            mix = PS[mix_pair][0:P, :]
            rms_stats(mix, P, [bB[2 * mix_pair], bB[2 * mix_pair + 1]], tt)
            S.op("dve", lambda h: h.scalar_tensor_tensor(out=tt[0:P, :], in0=mix, scalar=st["rstd"][0:P, 0:1], in1=gain[0:P, :],
                                                         op0=ALU.mult, op1=ALU.mult),
                 reads=[bB[2 * mix_pair], bB[2 * mix_pair + 1], st["rstd"], gain], writes=[tt])
            S.op("pool", lambda h: h.tensor_tensor(out=xtile_ap, in0=xtile_ap, in1=tt[0:P, :], op=ALU.add),
                 reads=[tt], writes=[xtile_tb])

        def out_proj(P, Wo, gain, xtile_ap, xtile_tb):
            transpose8(mixcat, P, mcT, tb=0)
            for half in range(2):
                mm(bank(6 + half)[0:P, :], [(mcT[:, kc, 0:P], Wo[:, kc, half * 512:(half + 1) * 512]) for kc in range(8)],
                   [mcT, Wo], [bB[6 + half]])
            post_norm_residual(3, P, gain, xtile_ap, xtile_tb)

        def mem_attend_prompt(l, qT_ap_fn, qreads, P):
            def f(h):
                for hh in range(4):
                    c, p = divmod(hh, 2)
                    for mc in range(2):
                        ins = h.matmul(bank(4 + hh // 2)[:, ((hh % 2) * 2 + mc) * 128:((hh % 2) * 2 + mc) * 128 + P],
                                       lhsT=KmT[l][:, p, c, mc * 128:(mc + 1) * 128], rhs=qT_ap_fn(c, p),
                                       start=True, stop=True)
                return ins
            S.op("pe", f, reads=[KmT[l]] + qreads, writes=[bB[4], bB[5]])
            ck(30)
            S.op("act", lambda h: h.activation(out=pT_sb[:, :, 0:P], in_=PS[2][:, :].rearrange("p (a b) -> p a b", a=8)[:, :, 0:P],
                                               func=AF.Exp, scale=0.125),
                 reads=[bB[4], bB[5]], writes=[pT_sb])

            def g(h):
                for hh in range(4):
                    for mc in range(2):
                        ins = h.matmul(bank(1)[0:P, hh * 65:(hh + 1) * 65], lhsT=pT_sb[:, hh * 2 + mc, 0:P], rhs=Vm[l][:, mc, hh, 0:65],
                                       start=(mc == 0), stop=(mc == 1))
                return ins
            ck(31)
            S.op("pe", g, reads=[pT_sb, Vm[l]], writes=[bB[1]])
            ck(32)
            finish_mem(P)

        def finish_mem(P):
            om = bank(1)[0:P, 0:260].rearrange("p (a b) -> p a b", a=4)
            S.op("dve", lambda h: h.reciprocal(out=st["rden"][0:P, 0:4], in_=om[:, :, 64]), reads=[bB[1]], writes=[st["rden"]])
            S.op("dve", lambda h: h.tensor_tensor(out=mixcat[0:P, 768:1024].rearrange("p (a b) -> p a b", a=4), in0=om[:, :, 0:64],
                                                  in1=bc(st["rden"][0:P, 0:4], 2, 64), op=ALU.mult),
                 reads=[bB[1], st["rden"]], writes=[mixcat])

        mem_sample_fn = []

        def zero_acc(regions, banks):
            def f(h):
                for r_ in regions:
                    n = r_.shape[-1]
                    ins = h.matmul(r_, lhsT=z16[0:16, 0:NS], rhs=z16[0:16, 0:n], start=True, stop=False)
                return ins
            S.op("pe", f, reads=[z16], writes=banks)

        def alloc_sample_attn(es, tag, onesel=None):
            Kc = [mk(es, tag + "Kc%d" % i, [128, 2, 256]) for i in range(2)]
            Vc = [mk(es, tag + "Vc%d" % i, [128, 2, 4, 65]) for i in range(2)]
            prod = mk(es, tag + "prod", [128, 768])
            sc8 = mk(es, tag + "sc8", [128, 12])
            pp8 = mk(es, tag + "pp8", [128, 12])
            Pm = mk(es, tag + "Pm", [128, 12, NS])
            if onesel is None:
                onesel = mk(es, tag + "onesel", [NS, NS, 128])
            S.op("pool", lambda h: h.memset(Vc[0][:], 1.0), writes=[Vc[0]])
            S.op("pool", lambda h: h.memset(Vc[1][:], 1.0), writes=[Vc[1]])
            S.op("dve", lambda h: h.tensor_copy(out=onesel[:], in_=bc(ident_f[0:NS, 0:NS], 2, 128)), reads=[ident_f], writes=[onesel])
            return Kc, Vc, prod, sc8, pp8, Pm, onesel

        with ExitStack() as es:
            eps_t = mk(ges, "eps_t", [128, 1])
            S.op("pool", lambda h: h.memset(eps_t[:], EPS), writes=[eps_t])
            S.op("pool", lambda h: h.memset(z16[:], 0.0), writes=[z16])
            S.dma("sp", ident_f[:], c_ident, writes=[ident_f])
            S.dma("sp", cmask[:], c_cmask, writes=[cmask])
            S.dma("sp", id16r[:].rearrange("p a b -> p (a b)"), c_id16r, writes=[id16r])
            S.dma("sp", flag[:], c_flag, writes=[flag])
            S.dma("sp", xs_sb[:], xs_in, writes=[xs_sb])
            S.op("dve", lambda h: h.tensor_copy(out=ident_b[:], in_=ident_f[:]), reads=[ident_f], writes=[ident_b])
            S.op("pool", lambda h: h.memset(ones_row[:], 1.0), writes=[ones_row])
            S.op("pool", lambda h: h.memset(ones96[:], 1.0), writes=[ones96])
            for l in range(2):
                S.op("pool", lambda h, l=l: h.memset(Vm[l][:], 1.0), writes=[Vm[l]])
                S.op("pool", lambda h, l=l: h.memset(KmT[l][:], 0.0), writes=[KmT[l]])
            gm = mk(es, "gm", [128, D])
            Wm = mk(es, "Wm", [128, 8, 512], BF16)
            hmT = mk(es, "hmT", [128, 8, 256], BF16)
            kvf = mk(es, "kvf", [128, 512])
            xm = [mk(es, "xm%d" % i, [128, D]) for i in range(2)]
            for mt in range(2):
                S.dma("sp", xm[mt][:], mem_in[mt * 128:(mt + 1) * 128, :], writes=[xm[mt]])
            if stop == -4:
                S.flush()
                return nc
            for l in range(2):
                load_gain(gm, g_mem[l, :])
                load_w(Wm, w_mem_kv[l], 8, 512)
                for mt in range(2):
                    rms_stats(xm[mt][:], 128, [xm[mt]], hb)
                    if stop == -3:
                        S.flush()
                        return nc
                    norm_to(xm[mt][:], 128, [xm[mt]], gm, hb)
                    if stop == -2:
                        S.flush()
                        return nc
                    transpose8(hb, 128, hT, tb=0)
                    if stop == -1:
                        S.flush()
                        return nc
                    S.op("dve", lambda h, mt=mt: h.tensor_copy(out=hmT[:, :, mt * 128:(mt + 1) * 128], in_=hT[:]),
                         reads=[hT], writes=[hmT])
                for c in range(2):
                    mm(bank(2)[:, 0:256], [(Wm[:, kc, c * 128:(c + 1) * 128], hmT[:, kc, :]) for kc in range(8)], [Wm, hmT], [bB[2]])
                    S.op("act", lambda h, c=c, l=l: h.copy(out=KmT[l][0:64, 0, c, :], in_=bank(2)[0:64, 0:256]), reads=[bB[2]], writes=[KmT[l]])
                    S.op("act", lambda h, c=c, l=l: h.copy(out=KmT[l][64:128, 1, c, :], in_=bank(2)[64:128, 0:256]), reads=[bB[2]], writes=[KmT[l]])
                if stop == -10:
                    S.flush()
                    return nc
                for mt in range(2):
                    mm(bank(3)[:, 0:512], [(hmT[:, kc, mt * 128:(mt + 1) * 128], Wm[:, kc, :]) for kc in range(8)], [Wm, hmT], [bB[3]])
                    S.op("dve", lambda h: h.tensor_copy(out=kvf[:], in_=bank(3)[:, 0:512]), reads=[bB[3]], writes=[kvf])
                    if stop == -11:
                        S.flush()
                        return nc
                    S.op("act", lambda h, l=l, mt=mt: h.copy(out=Vm[l][:, mt, :, 0:64],
                                                             in_=bank(3)[:, 256:512].rearrange("p (a b) -> p a b", a=4)),
                         reads=[bB[3]], writes=[Vm[l]])
                    if stop == -12:
                        S.flush()
                        return nc
                    S.dma("sp", mkp[l, mt * 128:(mt + 1) * 128, :], kvf[:, 0:256], reads=[kvf])
                    S.dma("sp", mvp[l, mt * 128:(mt + 1) * 128, :], kvf[:, 256:512], reads=[kvf])
            S.flush()
        if 0 <= stop <= 0:
            return nc

        with ExitStack() as es:
            Wa = mk(es, "Wa", [128, 8, 2576], BF16)
            Wo = mk(es, "Wo0", [128, 8, D], BF16)
            wgu = mk(es, "wgu", [16, 384], BF16)
            bg = mk(es, "bg", [1, 384], BF16)
            gpre = mk(es, "gpre0", [128, D])
            gpost = mk(es, "gpost0", [128, D])
            gn = mk(es, "gn", [128, DV])
            Sst = mk(es, "Sst", [96, 4, DV])
            S_bf = mk(es, "S_bf", [96, 4, DV], BF16)
            glr_sb = mk(es, "glr_sb", [16, 128], BF16)
            e1 = mk(es, "e1", [96, 4, 128])
            spl = mk(es, "spl", [96, 4, 128])
            cc = mk(es, "cc", [96, 4, 128])
            Einv = mk(es, "Einv", [96, 4, 128])
            Edec = mk(es, "Edec", [96, 4, 128])
            keT = mk(es, "keT", [96, 4, 128], BF16)
            qeT = mk(es, "qeT", [96, 4, 128], BF16)
            kdT = mk(es, "kdT", [96, 4, 128], BF16)
            kd_sb = mk(es, "kd_sb", [128, 384], BF16)
            v_sb = mk(es, "v_sb", [128, 768], BF16)
            attnT_sb = mk(es, "attnT_sb", [128, 4, 128], BF16)
            on = mk(es, "on", [128, 768])
            er = mk(es, "er", [128, 768])
            sg = mk(es, "sg", [128, 768])

            load_w(Wa, w_in_a, 8, 2576)
            load_w(Wo, w_out[0], 8, D)
            S.dma("pool", wgu[:], w_gate_up, writes=[wgu])
            S.dma("pool", bg[:], b_gate, writes=[bg])
            load_gain(gpre, g_mix_pre[0, :])
            load_gain(gpost, g_mix_post[0, :])
            S.dma("sp", gn[:], gla_norm[0, :].partition_broadcast(128), writes=[gn])
            S.op("pool", lambda h: h.memset(Sst[:], 0.0), writes=[Sst])
            S.op("pool", lambda h: h.memset(S_bf[:], 0.0), writes=[S_bf])

            elast = st["el"]

            def gate_path(P):
                mm(bank(1)[0:16, 0:P], [(Wa[:, kc, 2304:2320], hT[:, kc, 0:P]) for kc in range(8)], [Wa, hT], [bB[1]])
                S.op("dve", lambda h: h.tensor_copy(out=glr_sb[:, 0:P], in_=bank(1)[0:16, 0:P]), reads=[bB[1]], writes=[glr_sb])

            def zgate(P):
                def f(h):
                    for hh in range(4):
                        o = bank(1)[0:96, hh * 128:hh * 128 + P]
                        h.matmul(o, lhsT=wgu[0:16, 96 * hh:96 * hh + 96], rhs=glr_sb[0:16, 0:P], start=True, stop=False)
                        ins = h.matmul(o, lhsT=bg[0:1, 96 * hh:96 * hh + 96], rhs=ones_row[0:1, 0:P], start=False, stop=True)
                    return ins
                S.op("pe", f, reads=[wgu, bg, glr_sb, ones_row], writes=[bB[1]])
                zv = bank(1)[0:96, :].rearrange("p (a b) -> p a b", a=4)[:, :, 0:P]
                S.op("act", lambda h: h.activation(out=e1[:, :, 0:P], in_=zv, func=AF.Exp, scale=-1.0), reads=[bB[1]], writes=[e1])
                S.op("act", lambda h: h.activation(out=spl[:, :, 0:P], in_=e1[:, :, 0:P], func=AF.Ln, scale=1.0, bias=one_t[0:96, 0:1]),
                     reads=[e1, one_t], writes=[spl])

            one_t = mk(es, "one_t", [128, 1])
            S.op("pool", lambda h: h.memset(one_t[:], 1.0), writes=[one_t])

            def proj_fm(col0, bk, P):
                def f(h):
                    for hh in range(4):
                        for kc in range(8):
                            ins = h.matmul(bank(bk)[0:96, hh * 128:hh * 128 + P], lhsT=Wa[:, kc, col0 + 96 * hh:col0 + 96 * hh + 96],
                                           rhs=hT[:, kc, 0:P], start=(kc == 0), stop=(kc == 7))
                    return ins
                S.op("pe", f, reads=[Wa, hT], writes=[bB[bk]])

            def proj_tm(col0, pair, P):
                for half in range(2):
                    mm(bank(2 * pair + half)[0:P, 0:384],
                       [(hT[:, kc, 0:P], Wa[:, kc, col0 + 384 * half:col0 + 384 * (half + 1)]) for kc in range(8)],
                       [Wa, hT], [bB[2 * pair + half]])

            def pair768(pair, P):
                return PS[pair][0:P, :].rearrange("p (a b) -> p a b", a=2)[:, :, 0:384]

            def gla_norm_gate(P, o_pair, r_pair):
                ov = pair768(o_pair, P).rearrange("p a (i n) -> p a i n", i=2)
                for hh in range(4):
                    S.op("act", lambda h, hh=hh: h.activation(out=sg[0:P, 0:DV], in_=ov[:, hh // 2, hh % 2, :], func=AF.Square,
                                                              accum_out=st["ssg"][0:P, hh:hh + 1]),
                         reads=[bB[2 * o_pair], bB[2 * o_pair + 1]], writes=[sg, st["ssg"]])
                S.op("act", lambda h: h.activation(out=st["lng"][0:P, :], in_=st["ssg"][0:P, :], func=AF.Ln, scale=1.0 / DV,
                                                   bias=eps_t[0:P, 0:1]), reads=[st["ssg"], eps_t], writes=[st["lng"]])
                S.op("act", lambda h: h.activation(out=st["rstdg"][0:P, :], in_=st["lng"][0:P, :], func=AF.Exp, scale=-0.5),
                     reads=[st["lng"]], writes=[st["rstdg"]])
                for hh in range(4):
                    S.op("dve", lambda h, hh=hh: h.scalar_tensor_tensor(out=on[0:P, hh * DV:(hh + 1) * DV], in0=ov[:, hh // 2, hh % 2, :],
                                                                        scalar=st["rstdg"][0:P, hh:hh + 1], in1=gn[0:P, :],
                                                                        op0=ALU.mult, op1=ALU.mult),
                         reads=[bB[2 * o_pair], bB[2 * o_pair + 1], st["rstdg"], gn], writes=[on])
                rv = pair768(r_pair, P)
                er3 = er[0:P, :].rearrange("p (a b) -> p a b", a=2)
                S.op("act", lambda h: h.activation(out=er3, in_=rv, func=AF.Exp, scale=-1.0),
                     reads=[bB[2 * r_pair], bB[2 * r_pair + 1]], writes=[er])
                S.op("pool", lambda h: h.tensor_scalar(out=er[0:P, :], in0=er[0:P, :], scalar1=1.0, scalar2=None, op0=ALU.add),
                     reads=[], writes=[er])
                S.op("dve", lambda h: h.reciprocal(out=er[0:P, :], in_=er[0:P, :]), reads=[], writes=[er])
                S.op("dve", lambda h: h.tensor_tensor(out=sg[0:P, :].rearrange("p (a b) -> p a b", a=2), in0=rv, in1=er3, op=ALU.mult),
                     reads=[bB[2 * r_pair], bB[2 * r_pair + 1], er], writes=[sg])
                S.op("dve", lambda h: h.tensor_tensor(out=mixcat[0:P, 0:768], in0=on[0:P, :], in1=sg[0:P, :], op=ALU.mult),
                     reads=[on, sg], writes=[mixcat])

            def gla_tile(row0, full, out_slot):
                P = 128
                xtile = next_xt()
                S.dma("sp", xtile[:], xin[row0:row0 + 128, :], writes=[xtile])
                rms_stats(xtile[:], P, [xtile], hb)
                norm_to(xtile[:], P, [xtile], gpre, hb)
                transpose8(hb, P, hT, tb=0)
                gate_path(P)
                ck(1)
                proj_fm(384, 2, P)
                ck(2)
                if full:
                    proj_fm(0, 3, P)
                zgate(P)
                ck(3)
                proj_tm(768, 2, P)
                if full:
                    proj_tm(1536, 3, P)
                S.op("act", lambda h: h.copy(out=v_sb[:].rearrange("p (a b) -> p a b", a=2), in_=pair768(2, P)),
                     reads=[bB[4], bB[5]], writes=[v_sb])
                for hh in range(4):
                    S.op("dve", lambda h, hh=hh: h.tensor_tensor_scan(out=cc[:, hh, :], data0=ones96[:, :], data1=spl[:, hh, :], initial=0.0,
                                                                      op0=ALU.mult, op1=ALU.add),
                         reads=[spl, ones96], writes=[cc])
                S.op("act", lambda h: h.activation(out=Einv[:], in_=cc[:], func=AF.Exp, scale=1.0 / 16), reads=[cc], writes=[Einv])
                if full:
                    S.op("act", lambda h: h.activation(out=Edec[:], in_=cc[:], func=AF.Exp, scale=-1.0 / 16), reads=[cc], writes=[Edec])
                S.op("act", lambda h: h.activation(out=elast[0:96, 0:4], in_=cc[:, :, 127], func=AF.Exp, scale=-1.0 / 16),
                     reads=[cc], writes=[elast])
                ck(4)
                kT = bank(2)[0:96, :].rearrange("p (a b) -> p a b", a=4)
                qT = bank(3)[0:96, :].rearrange("p (a b) -> p a b", a=4)
                S.op("dve", lambda h: h.tensor_tensor(out=keT[:], in0=kT, in1=Einv[:], op=ALU.mult), reads=[bB[2], Einv], writes=[keT])
                if full:
                    S.op("dve", lambda h: h.scalar_tensor_tensor(out=qeT[:], in0=qT, scalar=DK ** -0.5, in1=Edec[:], op0=ALU.mult, op1=ALU.mult),
                         reads=[bB[3], Edec], writes=[qeT])
                for hh in range(4):
                    S.op("dve", lambda h, hh=hh: h.scalar_tensor_tensor(out=kdT[:, hh, :], in0=kT[:, hh, :], scalar=elast[0:96, hh:hh + 1],
                                                                        in1=Einv[:, hh, :], op0=ALU.mult, op1=ALU.mult),
                         reads=[bB[2], elast, Einv], writes=[kdT])

                ck(5)

                def trk(h):
                    for hh in range(4):
                        ins = h.transpose(out=bankb(0)[:, hh * 96:(hh + 1) * 96], in_=kdT[:, hh, :], identity=ident_b[0:96, 0:96])
                    return ins
                S.op("pe", trk, reads=[kdT, ident_b], writes=[bB[0]])
                S.op("act", lambda h: h.copy(out=kd_sb[:], in_=bankb(0)[:, 0:384]), reads=[bB[0]], writes=[kd_sb])
                ck(6)
                ck(7)
                if full:
                    def fa(h):
                        for hh in range(4):
                            ins = h.matmul(bank(1)[:, hh * 128:(hh + 1) * 128], lhsT=keT[:, hh, :], rhs=qeT[:, hh, :], start=True, stop=True)
                        return ins
                    S.op("pe", fa, reads=[keT, qeT], writes=[bB[1]])
                    S.op("dve", lambda h: h.tensor_tensor(out=attnT_sb[:], in0=bank(1)[:, :].rearrange("p (a b) -> p a b", a=4),
                                                          in1=bc(cmask[:], 1, 4), op=ALU.mult),
                         reads=[bB[1], cmask], writes=[attnT_sb])

                    def fo(h):
                        for hh in range(4):
                            o = bank(4 + hh // 2)[:, (hh % 2) * DV:(hh % 2 + 1) * DV]
                            h.matmul(o, lhsT=attnT_sb[:, hh, :], rhs=v_sb[:, hh * DV:(hh + 1) * DV], start=True, stop=False)
                            ins = h.matmul(o, lhsT=qeT[:, hh, :], rhs=S_bf[:, hh, :], start=False, stop=True)
                        return ins
                    S.op("pe", fo, reads=[attnT_sb, v_sb, qeT, S_bf], writes=[bB[4], bB[5]])

                def fs(h):
                    for hh in range(4):
                        ins = h.matmul(bank(2 + hh // 2)[0:96, (hh % 2) * DV:(hh % 2 + 1) * DV], lhsT=kd_sb[:, 96 * hh:96 * hh + 96],
                                       rhs=v_sb[:, hh * DV:(hh + 1) * DV], start=True, stop=True)
                    return ins
                S.op("pe", fs, reads=[kd_sb, v_sb], writes=[bB[2], bB[3]])
                for hh in range(4):
                    S.op("dve", lambda h, hh=hh: h.scalar_tensor_tensor(out=Sst[:, hh, :], in0=Sst[:, hh, :], scalar=elast[0:96, hh:hh + 1],
                                                                        in1=bank(2 + hh // 2)[0:96, (hh % 2) * DV:(hh % 2 + 1) * DV],
                                                                        op0=ALU.mult, op1=ALU.add),
                         reads=[elast, bB[2], bB[3]], writes=[Sst])
                S.op("pool", lambda h: h.tensor_copy(out=S_bf[:], in_=Sst[:]), reads=[Sst], writes=[S_bf])
                ck(8)
                if not full:
                    return
                for c in range(2):
                    mm(bank(1)[:, c * 128:c * 128 + P], [(Wa[:, kc, 2320 + 128 * c:2320 + 128 * (c + 1)], hT[:, kc, 0:P]) for kc in range(8)],
                       [Wa, hT], [bB[1]])
                S.op("act", lambda h: h.copy(out=qmT_sb[:, :, 0:P], in_=bank(1)[:, 0:256].rearrange("p (a b) -> p a b", a=2)[:, :, 0:P]),
                     reads=[bB[1]], writes=[qmT_sb])
                ck(9)
                gla_norm_gate(P, 2, 3)
                ck(10)
                mem_attend_prompt(0, lambda c, p: qmT_sb[:, c, 0:P], [qmT_sb], P)
                ck(11)
                out_proj(P, Wo, gpost, xtile[:], xtile)
                ck(12)
                S.dma("sp", xa[out_slot * 128:(out_slot + 1) * 128, :], xtile[:], reads=[xtile], writes=[b_xa[out_slot]])

            esp = ExitStack()
            PB = []
            for par in range(4):
                d = {}
                d["hb"] = mk(esp, "p_hb%d" % par, [128, D], BF16)
                d["hT"] = mk(esp, "p_hT%d" % par, [128, 8, 128], BF16)
                d["glr"] = mk(esp, "p_glr%d" % par, [16, 128], BF16)
                d["sp"] = mk(esp, "p_sp%d" % par, [96, 4, 128])
                d["cc"] = mk(esp, "p_cc%d" % par, [96, 4, 128])
                d["kdT"] = mk(esp, "p_kdT%d" % par, [96, 4, 128], BF16)
                d["kd"] = mk(esp, "p_kd%d" % par, [128, 384], BF16)
                d["v"] = mk(esp, "p_v%d" % par, [128, 768], BF16)
                d["ss"] = mk(esp, "p_ss%d" % par, [128, 1])
                d["ln"] = mk(esp, "p_ln%d" % par, [128, 1])
                d["rs"] = mk(esp, "p_rs%d" % par, [128, 1])
                d["el"] = mk(esp, "p_el%d" % par, [96, 4])
                PB.append(d)

            def prefix_tile(t, Q):
                def q(fn, *a, **k):
                    Q.append((fn, a, k))

                par = t % 4
                d = PB[par]
                bT, bZ, bK, bX = 2 * par, 2 * par, 2 * par + 1, 2 * par + 1
                hbP, hTP, glrP, spP, ccP, kdTP, kdP, vP = d["hb"], d["hT"], d["glr"], d["sp"], d["cc"], d["kdT"], d["kd"], d["v"]
                ssP, lnP, rsP, elP = d["ss"], d["ln"], d["rs"], d["el"]
                xtile = next_xt()
                q(S.dma, "sp", xtile[:], xin[t * 128:(t + 1) * 128, :], writes=[xtile])
                q(S.op, "act", lambda h: h.activation(out=hbP[:], in_=xtile[:], func=AF.Square, accum_out=ssP[:, 0:1]), reads=[xtile], writes=[hbP, ssP])
                q(S.op, "act", lambda h: h.activation(out=lnP[:], in_=ssP[:], func=AF.Ln, scale=1.0 / D, bias=eps_t[:, 0:1]), reads=[ssP, eps_t], writes=[lnP])
                q(S.op, "act", lambda h: h.activation(out=rsP[:], in_=lnP[:], func=AF.Exp, scale=-0.5), reads=[lnP], writes=[rsP])
                q(S.op, "dve", lambda h: h.scalar_tensor_tensor(out=hbP[:], in0=xtile[:], scalar=rsP[:, 0:1], in1=gpre[:], op0=ALU.mult, op1=ALU.mult),
                     reads=[xtile, rsP, gpre], writes=[hbP])

                def ftr(h):
                    for kc in range(8):
                        ins = h.transpose(out=bankb(bT)[:, kc * 128:(kc + 1) * 128], in_=hbP[:, kc * 128:(kc + 1) * 128], identity=ident_b[:])
                    return ins
                q(S.op, "pe", ftr, reads=[hbP, ident_b], writes=[bB[bT]])
                q(S.op, "act", lambda h: h.copy(out=hTP[:].rearrange("p a b -> p (a b)"), in_=bankb(bT)[:, :]), reads=[bB[bT]], writes=[hTP])
                q(mm, bank(bZ)[0:16, 0:128], [(Wa[:, kc, 2304:2320], hTP[:, kc, :]) for kc in range(8)], [Wa, hTP], [bB[bZ]])
                q(S.op, "dve", lambda h: h.tensor_copy(out=glrP[:], in_=bank(bZ)[0:16, 0:128]), reads=[bB[bZ]], writes=[glrP])

                def fk_(h):
                    for hh in range(4):
                        for kc in range(8):
                            ins = h.matmul(bank(bK)[0:96, hh * 128:(hh + 1) * 128], lhsT=Wa[:, kc, 384 + 96 * hh:384 + 96 * hh + 96],
                                           rhs=hTP[:, kc, :], start=(kc == 0), stop=(kc == 7))
                    return ins
                q(S.op, "pe", fk_, reads=[Wa, hTP], writes=[bB[bK]])

                def fz(h):
                    for hh in range(4):
                        o = bank(bZ)[0:96, hh * 128:(hh + 1) * 128]
                        h.matmul(o, lhsT=wgu[0:16, 96 * hh:96 * hh + 96], rhs=glrP[0:16, :], start=True, stop=False)
                        ins = h.matmul(o, lhsT=bg[0:1, 96 * hh:96 * hh + 96], rhs=ones_row[0:1, :], start=False, stop=True)
                    return ins
                q(S.op, "pe", fz, reads=[wgu, bg, glrP, ones_row], writes=[bB[bZ]])
                zv = bank(bZ)[0:96, :].rearrange("p (a b) -> p a b", a=4)
                q(S.op, "act", lambda h: h.activation(out=spP[:], in_=zv, func=AF.Exp, scale=-1.0), reads=[bB[bZ]], writes=[spP])
                q(S.op, "act", lambda h: h.activation(out=spP[:], in_=spP[:], func=AF.Ln, scale=1.0, bias=one_t[0:96, 0:1]), reads=[one_t], writes=[spP])
                for hh in range(4):
                    q(S.op, "dve", lambda h, hh=hh: h.tensor_tensor_scan(out=ccP[:, hh, :], data0=ones96[:, :], data1=spP[:, hh, :], initial=0.0,
                                                                      op0=ALU.mult, op1=ALU.add), reads=[spP, ones96], writes=[ccP])
                q(S.op, "act", lambda h: h.activation(out=elP[:], in_=ccP[:, :, 127], func=AF.Exp, scale=-1.0 / 16), reads=[ccP], writes=[elP])
                q(S.op, "act", lambda h: h.activation(out=ccP[:], in_=ccP[:], func=AF.Exp, scale=1.0 / 16), reads=[], writes=[ccP])
                kT = bank(bK)[0:96, :].rearrange("p (a b) -> p a b", a=4)
                for hh in range(4):
                    q(S.op, "dve", lambda h, hh=hh: h.scalar_tensor_tensor(out=kdTP[:, hh, :], in0=kT[:, hh, :], scalar=elP[:, hh:hh + 1], in1=ccP[:, hh, :],
                                                                        op0=ALU.mult, op1=ALU.mult), reads=[bB[bK], elP, ccP], writes=[kdTP])

                def ftk(h):
                    for hh in range(4):
                        ins = h.transpose(out=bankb(bT)[:, hh * 96:(hh + 1) * 96], in_=kdTP[:, hh, :], identity=ident_b[0:96, 0:96])
                    return ins
                q(S.op, "pe", ftk, reads=[kdTP, ident_b], writes=[bB[bT]])
                q(S.op, "act", lambda h: h.copy(out=kdP[:], in_=bankb(bT)[:, 0:384]), reads=[bB[bT]], writes=[kdP])
                for half in range(2):
                    q(mm, bank(bX)[:, 0:384], [(hTP[:, kc, :], Wa[:, kc, 768 + 384 * half:768 + 384 * (half + 1)]) for kc in range(8)], [Wa, hTP], [bB[bX]])
                    if half == 0:
                        q(S.op, "act", lambda h: h.copy(out=vP[:, 0:384], in_=bank(bX)[:, 0:384]), reads=[bB[bX]], writes=[vP])
                    else:
                        q(S.op, "dve", lambda h: h.tensor_copy(out=vP[:, 384:768], in_=bank(bX)[:, 0:384]), reads=[bB[bX]], writes=[vP])
                for pr_, bk_ in ((0, bX), (1, bK)):
                    def fs_(h, pr_=pr_, bk_=bk_):
                        for i_ in range(2):
                            hh = 2 * pr_ + i_
                            ins = h.matmul(bank(bk_)[0:96, i_ * DV:(i_ + 1) * DV], lhsT=kdP[:, 96 * hh:96 * hh + 96], rhs=vP[:, hh * DV:(hh + 1) * DV],
                                           start=True, stop=True)
                        return ins
                    q(S.op, "pe", fs_, reads=[kdP, vP], writes=[bB[bk_]])
                    for i_ in range(2):
                        hh = 2 * pr_ + i_
                        q(S.op, "dve", lambda h, hh=hh, i_=i_, bk_=bk_: h.scalar_tensor_tensor(out=Sst[:, hh, :], in0=Sst[:, hh, :], scalar=elP[:, hh:hh + 1],
                                                                                            in1=bank(bk_)[0:96, i_ * DV:(i_ + 1) * DV], op0=ALU.mult, op1=ALU.add),
                             reads=[elP, bB[bk_]], writes=[Sst])

            for t0 in range(0, NPRE, 4):
                lists = []
                for t in range(t0, t0 + 4):
                    if t < NPRE:
                        Q = []
                        prefix_tile(t, Q)
                        lists.append(Q)
                for i_ in range(max(len(L) for L in lists)):
                    for L in lists:
                        if i_ < len(L):
                            fn_, a_, k_ = L[i_]
                            fn_(*a_, **k_)
            S.op("pool", lambda h: h.tensor_copy(out=S_bf[:], in_=Sst[:]), reads=[Sst], writes=[S_bf])
            S.flush()
            ck(100)
            esp.close()
            for t in range(NT1):
                gla_tile((NPRE + t) * 128, True, t)
            S.dma("sp", stp.rearrange("h d v -> d h v"), Sst[:], reads=[Sst])

            P = NS
            kq_s = mk(es, "kq_s", [96, 4, NS])
            eg = mk(es, "eg", [96, 4, NS])
            ktok = mk(es, "ktok", [NS, 384])
            vtok = mk(es, "vtok", [NS, 768])
            kmask = mk(es, "kmask", [NS, NS, 384])
            qmask = mk(es, "qmask", [96, 4, NS, NS])
            Sin = [mk(es, "Sin%d" % i, [96, 4, DV]) for i in range(2)]
            Snew = [mk(es, "Snew%d" % i, [96, 4, DV]) for i in range(2)]
            qm_tok = mk(es, "qm_tok", [NS, 256])
            rms_stats(xs_sb[:], P, [xs_sb], hb)
            norm_to(xs_sb[:], P, [xs_sb], gpre, hb)
            transpose8(hb, P, hT, tb=0)
            gate_path(P)
            proj_fm(384, 2, P)
            proj_fm(0, 3, P)
            zgate(P)
            S.op("act", lambda h: h.activation(out=eg[:], in_=spl[:, :, 0:P], func=AF.Exp, scale=-1.0 / 16), reads=[spl], writes=[eg])
            qTs = bank(3)[0:96, :].rearrange("p (a b) -> p a b", a=4)[:, :, 0:P]
            S.op("dve", lambda h: h.tensor_scalar(out=kq_s[:], in0=qTs, scalar1=DK ** -0.5, scalar2=None, op0=ALU.mult),
                 reads=[bB[3]], writes=[kq_s])
            S.op("dve", lambda h: h.tensor_tensor(out=qmask[:], in0=bc(kq_s[:], 3, NS), in1=bc(id16r[0:96, :, :], 1, 4), op=ALU.mult),
                 reads=[kq_s, id16r], writes=[qmask])
            mm(bank(2)[0:P, 0:384], [(hT[:, kc, 0:P], Wa[:, kc, 384:768]) for kc in range(8)], [Wa, hT], [bB[2]])
            S.op("act", lambda h: h.copy(out=ktok[:], in_=bank(2)[0:P, 0:384]), reads=[bB[2]], writes=[ktok])
            S.op("dve", lambda h: h.tensor_tensor(out=kmask[:], in0=bc(ktok[:], 1, NS), in1=bc(ident_f[0:NS, 0:NS], 2, 384), op=ALU.mult),
                 reads=[ktok, ident_f], writes=[kmask])
            proj_tm(768, 2, P)
            S.op("act", lambda h: h.copy(out=vtok[:].rearrange("p (a b) -> p a b", a=2), in_=pair768(2, P)), reads=[bB[4], bB[5]], writes=[vtok])
            mm(bank(1)[0:P, 0:256], [(hT[:, kc, 0:P], Wa[:, kc, 2320:2576]) for kc in range(8)], [Wa, hT], [bB[1]])
            S.op("act", lambda h: h.copy(out=qm_tok[:], in_=bank(1)[0:P, 0:256]), reads=[bB[1]], writes=[qm_tok])
            proj_tm(1536, 3, P)
            ck(20)
            zero_acc([bank(4)[0:NS, 0:384], bank(5)[0:NS, 0:384]], [bB[4], bB[5]])
            for i in range(NS):
                si, sn = Sin[i % 2], Snew[i % 2]
                S.dma("sp", si[:], st_in[i].rearrange("h d v -> d h v"), writes=[si])

                def fk(h, i=i):
                    for hh in range(4):
                        ins = h.matmul(bank(2 + hh // 2)[0:96, (hh % 2) * DV:(hh % 2 + 1) * DV], lhsT=kmask[0:NS, i, 96 * hh:96 * hh + 96],
                                       rhs=vtok[0:NS, hh * DV:(hh + 1) * DV], start=True, stop=True)
                    return ins
                S.op("pe", fk, reads=[kmask, vtok], writes=[bB[2], bB[3]])
                for hh in range(4):
                    S.op("dve", lambda h, hh=hh, i=i, si=si, sn=sn: h.scalar_tensor_tensor(
                        out=sn[:, hh, :], in0=si[:, hh, :], scalar=eg[:, hh, i:i + 1],
                        in1=bank(2 + hh // 2)[0:96, (hh % 2) * DV:(hh % 2 + 1) * DV], op0=ALU.mult, op1=ALU.add),
                        reads=[si, eg, bB[2], bB[3]], writes=[sn])
                S.dma("sp", sts[i].rearrange("h d v -> d h v"), sn[:], reads=[sn])

                def fq(h, i=i, sn=sn):
                    for hh in range(4):
                        ins = h.matmul(bank(4 + hh // 2)[0:NS, (hh % 2) * DV:(hh % 2 + 1) * DV], lhsT=qmask[:, hh, i, :], rhs=sn[:, hh, :],
                                       start=False, stop=(i == NS - 1 and hh % 2 == 1))
                    return ins
                S.op("pe", fq, reads=[qmask, sn], writes=[bB[4], bB[5]])
            ck(21)
            gla_norm_gate(P, 2, 3)
            ck(22)
            sab = alloc_sample_attn(es, "p2", onesel=TB_sub(kmask, 128))

            def mem_sample(l, qtok, sab):
                Kc, Vc, prod, sc8, pp8, Pm, onesel = sab
                zero_acc([bank(1)[0:NS, 0:260]], [bB[1]])
                for i in range(NS):
                    kc_, vc_ = Kc[i % 2], Vc[i % 2]
                    S.dma("sp", kc_[:], mk_in[l, i].rearrange("(mc p) n -> p mc n", p=128), writes=[kc_])
                    for mc in range(2):
                        S.dma("sp", vc_[:, mc, :, 0:64], mv_in[l, i, mc * 128:(mc + 1) * 128, :].rearrange("p (a b) -> p a b", a=4), writes=[vc_])
                    mm(bank(0)[:, 0:256], [(onesel[0:NS, i, :], qtok[0:NS, :])], [onesel, qtok], [bB[0]])
                    S.op("dve", lambda h, kc_=kc_: h.tensor_tensor(out=prod[:, 0:512].rearrange("p (a b) -> p a b", a=2), in0=kc_[:],
                                                                   in1=bc(bank(0)[:, 0:256], 1, 2), op=ALU.mult),
                         reads=[kc_, bB[0]], writes=[prod])
                    S.op("dve", lambda h: h.tensor_reduce(out=sc8[:, 0:8], in_=prod[:, 0:512].rearrange("p (a b) -> p a b", b=64), axis=AX.X, op=ALU.add),
                         reads=[prod], writes=[sc8])
                    S.op("act", lambda h: h.activation(out=pp8[:, 0:8], in_=sc8[:, 0:8], func=AF.Exp, scale=0.125), reads=[sc8], writes=[pp8])
                    S.op("dve", lambda h, i=i: h.tensor_tensor(out=Pm[:, 0:8, :], in0=bc(pp8[:, 0:8], 2, NS), in1=bc(id16r[:, i, :], 1, 8), op=ALU.mult),
                         reads=[pp8, id16r], writes=[Pm])

                    def fm(h, i=i, vc_=vc_):
                        for hh in range(4):
                            for mc in range(2):
                                ins = h.matmul(bank(1)[0:NS, hh * 65:(hh + 1) * 65], lhsT=Pm[:, mc * 4 + hh, :], rhs=vc_[:, mc, hh, :],
                                               start=False, stop=(i == NS - 1 and mc == 1 and hh == 3))
                        return ins
                    S.op("pe", fm, reads=[Pm, vc_], writes=[bB[1]])
                finish_mem(NS)

            mem_sample(0, qm_tok, sab)
            ck(23)
            mem_sample_fn.append(mem_sample)
            out_proj(P, Wo, gpost, xs_sb[:], xs_sb)
            S.flush()
        if 0 <= stop <= 1:
            S.dma("sp", ys, xs_sb[:], reads=[xs_sb])
            S.flush()
            return nc

        def mlp_phase(l, tiles, es):
            Wu = mk(es, "Wu%d" % l, [128, 8, 4096], BF16)
            Wd = mk(es, "Wd%d" % l, [128, 32, D], BF16)
            gpre = mk(es, "gfpre%d" % l, [128, D])
            gpost = mk(es, "gfpost%d" % l, [128, D])
            hTg = mk(es, "hTg%d" % l, [128, 8, 256], BF16)
            actT = mk(es, "actT%d" % l, [128, 32, 256], BF16)
            load_w(Wu, w_up[l], 8, 4096)
            load_w(Wd, w_down[l], 32, D)
            load_gain(gpre, g_ffn_pre[l, :])
            load_gain(gpost, g_ffn_post[l, :])
            groups = []
            i = 0
            while i < len(tiles):
                if tiles[i][0] == 128 and i + 1 < len(tiles) and tiles[i + 1][0] == 128:
                    groups.append(tiles[i:i + 2])
                    i += 2
                else:
                    groups.append(tiles[i:i + 1])
                    i += 1
            xtl = {}

            def n_elem(g, only=None):
                for ti, (P, src, sbuf, dst, dbuf) in enumerate(groups[g]):
                    if only is not None and ti != only:
                        continue
                    if src is None:
                        xtile, xap = xs_sb, xs_sb[:]
                    else:
                        xtile = next_xt()
                        xap = xtile[0:P, :]
                        S.dma("sp", xap, src, reads=[sbuf], writes=[xtile])
                    xtl[(g, ti)] = (xtile, xap)

            def n_pe(g):
                off = 0
                for ti, (P, src, sbuf, dst, dbuf) in enumerate(groups[g]):
                    xtile, xap = xtl[(g, ti)]
                    rms_stats(xap, P, [xtile], hb)
                    norm_to(xap, P, [xtile], gpre, hb)
                    transpose8(hb, P, hT, tb=ti % 2)
                    S.op("act", lambda h, off=off, P=P: h.copy(out=hTg[:, :, off:off + P], in_=hT[:, :, 0:P]), reads=[hT], writes=[hTg])
                    off += P

            def up(g):
                N = sum(t[0] for t in groups[g])
                for ffc in range(32):
                    bk = (2, 3, 0, 1)[ffc % 4]
                    mm(bank(bk)[:, 0:N], [(Wu[:, kc, ffc * 128:(ffc + 1) * 128], hTg[:, kc, 0:N]) for kc in range(8)], [Wu, hTg], [bB[bk]])
                    S.op("act", lambda h, ffc=ffc, bk=bk, N=N: h.activation(out=actT[:, ffc, 0:N], in_=bank(bk)[:, 0:N], func=AF.Square),
                         reads=[bB[bk]], writes=[actT])
                    S.op("dve", lambda h, ffc=ffc, bk=bk, N=N: h.scalar_tensor_tensor(out=actT[:, ffc, 0:N], in0=bank(bk)[:, 0:N], scalar=0.0,
                                                                                      in1=actT[:, ffc, 0:N], op0=ALU.is_gt, op1=ALU.mult),
                         reads=[bB[bk]], writes=[actT])

            def down(g):
                off = 0
                for ti, (P, src, sbuf, dst, dbuf) in enumerate(groups[g]):
                    xtile, xap = xtl[(g, ti)]
                    pair = 2 + ti % 2
                    for half in range(2):
                        mm(bank(2 * pair + half)[0:P, :], [(actT[:, ffc, off:off + P], Wd[:, ffc, half * 512:(half + 1) * 512]) for ffc in range(32)],
                           [actT, Wd], [bB[2 * pair + half]])
                    post_norm_residual(pair, P, gpost, xap, xtile)
                    if dst is not None:
                        S.dma("sp", dst, xap, reads=[xtile], writes=[dbuf])
                    off += P

            n_elem(0)
            n_pe(0)
            for g in range(len(groups)):
                up(g)
                if g + 1 < len(groups):
                    n_elem(g + 1, only=0)
                down(g)
                if g + 1 < len(groups):
                    if len(groups[g + 1]) > 1:
                        n_elem(g + 1, only=1)
                    n_pe(g + 1)

        with ExitStack() as es:
            tiles = [(128, xa[t * 128:(t + 1) * 128, :], b_xa[t], xb[t * 128:(t + 1) * 128, :], b_xb[t]) for t in range(NT1)]
            tiles.append((NS, None, None, None, None))
            mlp_phase(0, tiles, es)
            S.flush()
        if 0 <= stop <= 2:
            S.dma("sp", ys, xs_sb[:], reads=[xs_sb])
            S.flush()
            return nc

        with ExitStack() as es:
            es2 = ExitStack()
            Wkv = mk(es, "Wkv", [128, 8, 512], BF16)
            Wb = mk(es, "Wb", [128, 8, 1024], BF16)
            Wo = mk(es, "Wo1", [128, 8, D], BF16)
            gkv = mk(es, "gkv", [128, D])
            gpre = mk(es, "gpre1", [128, D])
            gpost = mk(es, "gpost1", [128, D])
            EBs = mk(es, "EBs", [128, 12])
            eb0 = mk(es, "eb0", [NS, 12])
            esink = mk(es, "esink", [128, 12])
            rb_sb = mk(es, "rb_sb", [32, 12])
            rbh = mk(es, "rbh", [32, 12, 128])
            ohu = mk(es, "ohu", [32, 384])
            validu = mk(es, "validu", [128, 384])
            ebt = mk(es, "ebt", [128, 384])
            kvf = mk(es, "kvf1", [128, 512])
            KT_all = mk(es2, "KT_all", [128, 2, 2, NT1 * 128], BF16)
            V_all = mk(es2, "V_all", [128, NT1, 4, 66], BF16)
            QT_all = mk(es2, "QT_all", [128, 6, NT * 128], BF16)
            QmT_all = mk(es2, "QmT_all", [128, 2, NT * 128], BF16)
            EB = mk(es2, "EB", [128, 12, 2, 128])
            es_sb = mk(es2, "es_sb", [128, 2, 384])
            pTg = [mk(es2, "pTg%d" % g, [128, 2, 3, 128], BF16) for g in range(4)]

            load_w(Wkv, w_kv, 8, 512)
            wbv = w_in_b.rearrange("(kc p) n -> p kc n", p=128)
            for i in range(6):
                s_, j_ = divmod(i, 3)
                for half in range(2):
                    hd = 6 * s_ + 3 * half + j_
                    S.dma("pool", Wb[:, :, i * 128 + half * 64:i * 128 + half * 64 + 64], wbv[:, :, hd * 64:(hd + 1) * 64], writes=[Wb])
            S.dma("pool", Wb[:, :, 768:1024], wbv[:, :, 768:1024], writes=[Wb])
            load_w(Wo, w_out[1], 8, D)
            load_gain(gkv, g_kv[0, :])
            load_gain(gpre, g_mix_pre[1, :])
            load_gain(gpost, g_mix_post[1, :])
            S.dma("sp", rb_sb[:], rel_bias, writes=[rb_sb])
            S.dma("sp", ohu[:], c_ohu, writes=[ohu])
            S.dma("sp", validu[:], c_valid, writes=[validu])
            S.dma("sp", esink[:], sinks[0, :].partition_broadcast(128), writes=[esink])
            S.op("act", lambda h: h.activation(out=esink[:], in_=esink[:], func=AF.Exp), reads=[], writes=[esink])
            S.op("dve", lambda h: h.tensor_copy(out=rbh[:], in_=bc(rb_sb[:], 2, 128)), reads=[rb_sb], writes=[rbh])
            for hd in range(12):
                mm(bank(2)[:, 0:384], [(rbh[:, hd, :], ohu[:])], [rbh, ohu], [bB[2]])
                S.op("act", lambda h: h.activation(out=ebt[:], in_=bank(2)[:, 0:384], func=AF.Exp), reads=[bB[2]], writes=[ebt])
                S.op("dve", lambda h: h.tensor_tensor(out=ebt[:], in0=ebt[:], in1=validu[:], op=ALU.mult), reads=[validu], writes=[ebt])
                S.dma("sp", ebd.ap()[hd], ebt[:], reads=[ebt], writes=[b_ebd])
            for hd in range(12):
                for kb in range(2):
                    S.dma("sp", EB[:, hd, kb, :], bass.AP(ebd, hd * 128 * 384 + 255 - 128 * kb, [[383, 128], [1, 128]]), reads=[b_ebd], writes=[EB])
                S.dma("sp", EBs[:, hd:hd + 1], bass.AP(ebd, hd * 128 * 384 + 255, [[383, 128], [1, 1]]), reads=[b_ebd], writes=[EBs], allow_slow_non_contiguous=True)
                S.dma("sp", eb0[:, hd:hd + 1], bass.AP(ebd, hd * 128 * 384 + 127, [[384, NS], [1, 1]]), reads=[b_ebd], writes=[eb0], allow_slow_non_contiguous=True)
            bKT = [Buf() for _ in range(NT1)]
            bV = [Buf() for _ in range(NT1)]
            bQ = [[Buf() for _ in range(6)] for _ in range(NT)]
            bQm = [[Buf() for _ in range(2)] for _ in range(NT)]
            B1S = [dict(hb=hb, hb2=hb2, hT=hT, hT2=hT2, ss=st["ss"], ln=st["ln"], rs=st["rstd"])]
            B1S.append(dict(hb=mk(es2, "b1_hb", [128, D], BF16), hb2=mk(es2, "b1_hb2", [128, D], BF16),
                            hT=mk(es2, "b1_hT", [128, 8, 128], BF16), hT2=mk(es2, "b1_hT2", [128, 8, 128], BF16),
                            ss=mk(es2, "b1_ss", [128, 1]), ln=mk(es2, "b1_ln", [128, 1]), rs=mk(es2, "b1_rs", [128, 1])))

            S.op("pool", lambda h: h.memset(V_all[:], 1.0), writes=bV)
            S.op("pool", lambda h: h.memset(KT_all[:], 0.0), writes=bKT)

            def b1_tile(t, Q, par):
                def q(fn, *a, **k):
                    Q.append((fn, a, k))
                W = B1S[par]
                hbP, hb2P, hTP, hT2P, ssP, lnP, rsP = W["hb"], W["hb2"], W["hT"], W["hT2"], W["ss"], W["ln"], W["rs"]
                bT, bA, bBk, bC = 4 * par, 4 * par + 1, 4 * par + 2, 4 * par + 3
                xtile = next_xt()
                q(S.dma, "sp", xtile[:], xb[t * 128:(t + 1) * 128, :], reads=[b_xb[t]], writes=[xtile])
                q(S.op, "act", lambda h: h.activation(out=hbP[:], in_=xtile[:], func=AF.Square, accum_out=ssP[:, 0:1]), reads=[xtile], writes=[hbP, ssP])
                q(S.op, "act", lambda h: h.activation(out=lnP[:, 0:1], in_=ssP[:, 0:1], func=AF.Ln, scale=1.0 / D, bias=eps_t[:, 0:1]), reads=[ssP, eps_t], writes=[lnP])
                q(S.op, "act", lambda h: h.activation(out=rsP[:, 0:1], in_=lnP[:, 0:1], func=AF.Exp, scale=-0.5), reads=[lnP], writes=[rsP])
                q(S.op, "dve", lambda h: h.scalar_tensor_tensor(out=hbP[:], in0=xtile[:], scalar=rsP[:, 0:1], in1=gkv[:], op0=ALU.mult, op1=ALU.mult),
                  reads=[xtile, rsP, gkv], writes=[hbP])

                def ftr(h, src=hbP):
                    for kc in range(8):
                        ins = h.transpose(out=bankb(bT)[:, kc * 128:(kc + 1) * 128], in_=src[:, kc * 128:(kc + 1) * 128], identity=ident_b[:])
                    return ins
                q(S.op, "pe", ftr, reads=[hbP, ident_b], writes=[bB[bT]])
                q(S.op, "act", lambda h: h.copy(out=hTP[:].rearrange("p a b -> p (a b)"), in_=bankb(bT)[:, :]), reads=[bB[bT]], writes=[hTP])
                for s_ in range(2):
                    q(mm, bank(bA)[:, s_ * 128:(s_ + 1) * 128], [(Wkv[:, kc, s_ * 128:(s_ + 1) * 128], hTP[:, kc, :]) for kc in range(8)], [Wkv, hTP], [bB[bA]])
                q(S.op, "act", lambda h: h.copy(out=KT_all[0:64, 0, :, t * 128:(t + 1) * 128], in_=bank(bA)[0:64, 0:256].rearrange("p (a b) -> p a b", a=2)),
                  reads=[bB[bA]], writes=[bKT[t]])
                q(S.op, "act", lambda h: h.copy(out=KT_all[64:128, 1, :, t * 128:(t + 1) * 128], in_=bank(bA)[64:128, 0:256].rearrange("p (a b) -> p a b", a=2)),
                  reads=[bB[bA]], writes=[bKT[t]])
                q(mm, bank(bBk)[:, 0:512], [(hTP[:, kc, :], Wkv[:, kc, :]) for kc in range(8)], [Wkv, hTP], [bB[bBk]])
                q(S.op, "dve", lambda h: h.tensor_copy(out=V_all[:, t, :, 0:64], in_=bank(bBk)[:, 256:512].rearrange("p (a b) -> p a b", a=4)),
                  reads=[bB[bBk]], writes=[bV[t]])
                if t == NT:
                    q(S.op, "act", lambda h: h.copy(out=kvf[:], in_=bank(bBk)[:, 0:512]), reads=[bB[bBk]], writes=[kvf])
                    q(S.dma, "sp", swkp, kvf[:, 0:256], reads=[kvf])
                    q(S.dma, "sp", swvp, kvf[:, 256:512], reads=[kvf])
                if t == 0:
                    q(S.op, "dve", lambda h: h.tensor_scalar(out=V_all[:, 0, :, :], in0=V_all[:, 0, :, :], scalar1=flag[:, 0:1], scalar2=None, op0=ALU.mult),
                      reads=[flag], writes=[bV[0]])
                    return
                q(S.op, "dve", lambda h: h.scalar_tensor_tensor(out=hb2P[:], in0=xtile[:], scalar=rsP[:, 0:1], in1=gpre[:], op0=ALU.mult, op1=ALU.mult),
                  reads=[xtile, rsP, gpre], writes=[hb2P])
                q(S.op, "pe", lambda h: ftr(h, src=hb2P), reads=[hb2P, ident_b], writes=[bB[bT]])
                q(S.op, "act", lambda h: h.copy(out=hT2P[:].rearrange("p a b -> p (a b)"), in_=bankb(bT)[:, :]), reads=[bB[bT]], writes=[hT2P])
                for i in range(8):
                    bk = (bA, bBk, bC)[i % 3]
                    q(mm, bank(bk)[:, 0:128], [(Wb[:, kc, i * 128:(i + 1) * 128], hT2P[:, kc, :]) for kc in range(8)], [Wb, hT2P], [bB[bk]])
                    dst = QT_all[:, i, (t - 1) * 128:t * 128] if i < 6 else QmT_all[:, i - 6, (t - 1) * 128:t * 128]
                    tok = bQ[t - 1][i] if i < 6 else bQm[t - 1][i - 6]
                    q(S.op, "act" if i % 2 else "dve",
                      (lambda h, dst=dst, bk=bk: h.copy(out=dst, in_=bank(bk)[:, 0:128])) if i % 2 else
                      (lambda h, dst=dst, bk=bk: h.tensor_copy(out=dst, in_=bank(bk)[:, 0:128])),
                      reads=[bB[bk]], writes=[tok])

            for t0 in range(0, NT1, 2):
                lists = []
                for par, t in enumerate(range(t0, min(t0 + 2, NT1))):
                    Q = []
                    b1_tile(t, Q, par)
                    lists.append(Q)
                for i_ in range(max(len(L) for L in lists)):
                    for L in lists:
                        if i_ < len(L):
                            fn_, a_, k_ = L[i_]
                            fn_(*a_, **k_)

            def swa_finish(P, extra=None):
                o3 = [bank(6 + i)[0:P, 0:390].rearrange("p (a b) -> p a b", a=6) for i in range(2)]
                for i in range(2):
                    S.op("dve", lambda h, i=i: h.tensor_tensor(out=den12[0:P, 6 * i:6 * i + 6], in0=o3[i][:, :, 64], in1=esink[0:P, 6 * i:6 * i + 6], op=ALU.add),
                         reads=[bB[6 + i], esink], writes=[den12])
                if extra is not None:
                    S.op("dve", lambda h: h.tensor_tensor(out=den12[0:P, :], in0=den12[0:P, :], in1=extra[0:P, :], op=ALU.add),
                         reads=[extra], writes=[den12])
                S.op("dve", lambda h: h.reciprocal(out=rden12[0:P, :], in_=den12[0:P, :]), reads=[den12], writes=[rden12])

            def swa_norm(P, srcs, src_reads):
                for i in range(2):
                    S.op("dve", lambda h, i=i: h.tensor_tensor(out=mixcat[0:P, 384 * i:384 * (i + 1)].rearrange("p (a b) -> p a b", a=6), in0=srcs[i],
                                                               in1=bc(rden12[0:P, 6 * i:6 * i + 6], 2, 64), op=ALU.mult),
                         reads=list(src_reads) + [rden12], writes=[mixcat])

            def b2(t):
                P = 128
                ebt_ = EB
                for g in range(4):
                    s_, p_ = divmod(g, 2)
                    pr = 1 + g % 2
                    def fsc(h, g=g, s_=s_, p_=p_, pr=pr):
                        for kb in range(2):
                            ins = h.matmul(bank(2 * pr + kb)[:, 0:384], lhsT=KT_all[:, p_, s_, (t + kb) * 128:(t + kb + 1) * 128],
                                           rhs=QT_all[:, 3 * s_:3 * s_ + 3, t * 128:(t + 1) * 128], start=True, stop=True)
                        return ins
                    S.op("pe", fsc, reads=[bKT[t], bKT[t + 1]] + bQ[t], writes=[bB[2 * pr], bB[2 * pr + 1]])
                    S.op("act", lambda h, pr=pr: h.activation(out=es_sb[:], in_=pair768(pr, 128), func=AF.Exp, scale=0.125),
                         reads=[bB[2 * pr], bB[2 * pr + 1]], writes=[es_sb])
                    S.op("dve", lambda h, g=g, ebt_=ebt_: h.tensor_tensor(out=pTg[g][:], in0=es_sb[:].rearrange("p a (j q) -> p a j q", j=3),
                                                                          in1=ebt_[:, 3 * g:3 * g + 3, :, :].rearrange("p j a q -> p a j q"), op=ALU.mult),
                         reads=[es_sb, ebt_], writes=[pTg[g]])
                    def fo(h, g=g):
                        for j in range(3):
                            hd = 3 * g + j
                            for kb in range(2):
                                ins = h.matmul(bank(6 + hd // 6)[:, (hd % 6) * 65:(hd % 6 + 1) * 65], lhsT=pTg[g][:, kb, j, :],
                                               rhs=V_all[:, t + kb, g, 0:65], start=(kb == 0), stop=(kb == 1))
                        return ins
                    S.op("pe", fo, reads=[pTg[g], bV[t], bV[t + 1]], writes=[bB[6], bB[7]])
                swa_finish(P)
                o3 = [bank(6 + i)[0:P, 0:390].rearrange("p (a b) -> p a b", a=6)[:, :, 0:64] for i in range(2)]
                swa_norm(P, o3, [bB[6], bB[7]])
                mem_attend_prompt(1, lambda c, p: QmT_all[:, c, t * 128:(t + 1) * 128], bQm[t], P)
                xtile = next_xt()
                S.dma("sp", xtile[:], xb[(t + 1) * 128:(t + 2) * 128, :], reads=[b_xb[t + 1]], writes=[xtile])
                out_proj(P, Wo, gpost, xtile[:], xtile)
                S.dma("sp", xa[t * 128:(t + 1) * 128, :], xtile[:], reads=[xtile], writes=[b_xa[t]])

            for t in range(NT):
                b2(t)

            S.flush()
            es2.close()
            P = NS
            sab = alloc_sample_attn(es, "p4")
            Kc, Vc, prod, sc8, pp8, Pm, onesel = sab
            knew = mk(es, "knew", [NS, 512])
            qtok = mk(es, "qtok", [NS, 1024])
            Ks = [mk(es, "Ks%d" % i, [128, 4, 64]) for i in range(2)]
            Vs = [mk(es, "Vs%d" % i, [128, 4, 65]) for i in range(2)]
            sn12 = mk(es, "sn12", [NS, 12])
            pn12 = mk(es, "pn12", [NS, 12])
            tmpv = mk(es, "tmpv", [NS, 768])
            osum = mk(es, "osum", [NS, 768])
            S.op("pool", lambda h: h.memset(Vs[0][:], 1.0), writes=[Vs[0]])
            S.op("pool", lambda h: h.memset(Vs[1][:], 1.0), writes=[Vs[1]])
            rms_stats(xs_sb[:], P, [xs_sb], hb)
            norm_to(xs_sb[:], P, [xs_sb], gkv, hb)
            transpose8(hb, P, hT, tb=0)
            mm(bank(3)[0:P, 0:512], [(hT[:, kc, 0:P], Wkv[:, kc, :]) for kc in range(8)], [Wkv, hT], [bB[3]])
            S.op("act", lambda h: h.copy(out=knew[:], in_=bank(3)[0:P, 0:512]), reads=[bB[3]], writes=[knew])
            norm_to(xs_sb[:], P, [xs_sb], gpre, hb2)
            transpose8(hb2, P, hT2, tb=1)
            for half in range(2):
                mm(bank(4 + half)[0:P, :], [(hT2[:, kc, 0:P], Wb[:, kc, half * 512:(half + 1) * 512]) for kc in range(8)], [Wb, hT2], [bB[4 + half]])
            S.op("act", lambda h: h.copy(out=qtok[:], in_=PS[2][0:P, :]), reads=[bB[4], bB[5]], writes=[qtok])
            S.dma("sp", swks[:, 0:127, :], swk_in[:, 1:128, :])
            S.dma("sp", swvs[:, 0:127, :], swv_in[:, 1:128, :])
            S.dma("sp", swks[:, 127, :], knew[:, 0:256], reads=[knew])
            S.dma("sp", swvs[:, 127, :], knew[:, 256:512], reads=[knew])

            def qview(ap2d, s_):
                return ap2d[:, s_ * 384:(s_ + 1) * 384].rearrange("p (j a d) -> p a j d", j=3, a=2)

            zero_acc([bank(6)[0:NS, 0:390], bank(7)[0:NS, 0:390]], [bB[6], bB[7]])
            for i in range(NS):
                ks_, vs_ = Ks[i % 2], Vs[i % 2]
                S.dma("sp", ks_[:], swk_in[i].rearrange("w (g d) -> w g d", g=4), writes=[ks_])
                S.dma("sp", vs_[:, :, 0:64], swv_in[i].rearrange("w (g d) -> w g d", g=4), writes=[vs_])
                for half in range(2):
                    mm(bank(2 + half)[:, 0:384], [(onesel[0:NS, i, :], qtok[0:NS, half * 384:(half + 1) * 384])], [onesel, qtok], [bB[2 + half]])
                for s_ in range(2):
                    S.op("dve", lambda h, s_=s_, ks_=ks_: h.tensor_tensor(
                        out=prod[:, 384 * s_:384 * (s_ + 1)].rearrange("p (a j d) -> p a j d", a=2, j=3),
                        in0=bc(ks_[:, 2 * s_:2 * s_ + 2, :], 2, 3), in1=qview(bank(2 + s_)[:, 0:384], 0), op=ALU.mult),
                        reads=[ks_, bB[2 + s_]], writes=[prod])
                S.op("dve", lambda h: h.tensor_reduce(out=sc8[:, :], in_=prod[:, :].rearrange("p (a b) -> p a b", b=64), axis=AX.X, op=ALU.add),
                     reads=[prod], writes=[sc8])
                S.op("act", lambda h: h.activation(out=pp8[:, :], in_=sc8[:, :], func=AF.Exp, scale=0.125), reads=[sc8], writes=[pp8])
                S.op("dve", lambda h: h.tensor_tensor(out=pp8[:, :], in0=pp8[:, :], in1=EBs[:, :], op=ALU.mult), reads=[EBs], writes=[pp8])
                S.op("dve", lambda h, i=i: h.tensor_tensor(out=Pm[:, :, :], in0=bc(pp8[:, :], 2, NS), in1=bc(id16r[:, i, :], 1, 12), op=ALU.mult),
                     reads=[pp8, id16r], writes=[Pm])

                def fo2(h, i=i, vs_=vs_):
                    for hd in range(12):
                        ins = h.matmul(bank(6 + hd // 6)[0:NS, (hd % 6) * 65:(hd % 6 + 1) * 65], lhsT=Pm[:, hd, :], rhs=vs_[:, hd // 3, :],
                                       start=False, stop=(i == NS - 1 and hd % 6 == 5))
                    return ins
                S.op("pe", fo2, reads=[Pm, vs_], writes=[bB[6], bB[7]])
            for s_ in range(2):
                S.op("dve", lambda h, s_=s_: h.tensor_tensor(
                    out=prod[0:NS, 384 * s_:384 * (s_ + 1)].rearrange("p (a j d) -> p a j d", a=2, j=3),
                    in0=bc(knew[:, 0:256].rearrange("p (g d) -> p g d", g=4)[:, 2 * s_:2 * s_ + 2, :], 2, 3), in1=qview(qtok[:, 0:768], s_), op=ALU.mult),
                    reads=[knew, qtok], writes=[prod])
            S.op("dve", lambda h: h.tensor_reduce(out=sn12[:], in_=prod[0:NS, :].rearrange("p (a b) -> p a b", b=64), axis=AX.X, op=ALU.add),
                 reads=[prod], writes=[sn12])
            S.op("act", lambda h: h.activation(out=pn12[:], in_=sn12[:], func=AF.Exp, scale=0.125), reads=[sn12], writes=[pn12])
            S.op("dve", lambda h: h.tensor_tensor(out=pn12[:], in0=pn12[:], in1=eb0[:], op=ALU.mult), reads=[eb0], writes=[pn12])
            S.op("dve", lambda h: h.tensor_tensor(out=tmpv[:].rearrange("p (g j d) -> p g j d", g=4, j=3),
                                                  in0=bc(knew[:, 256:512].rearrange("p (g d) -> p g d", g=4), 2, 3),
                                                  in1=bc(pn12[:].rearrange("p (g j) -> p g j", g=4), 3, 64), op=ALU.mult),
                 reads=[knew, pn12], writes=[tmpv])
            swa_finish(P, extra=pn12)
            for i in range(2):
                S.op("dve", lambda h, i=i: h.tensor_tensor(out=osum[:, 384 * i:384 * (i + 1)].rearrange("p (a b) -> p a b", a=6),
                                                           in0=bank(6 + i)[0:P, 0:390].rearrange("p (a b) -> p a b", a=6)[:, :, 0:64],
                                                           in1=tmpv[:, 384 * i:384 * (i + 1)].rearrange("p (a b) -> p a b", a=6), op=ALU.add),
                     reads=[bB[6 + i], tmpv], writes=[osum])
            swa_norm(P, [osum[:, 384 * i:384 * (i + 1)].rearrange("p (a b) -> p a b", a=6) for i in range(2)], [osum])
            mem_sample_fn[0](1, TB_view(qtok, 768), sab)
            out_proj(P, Wo, gpost, xs_sb[:], xs_sb)
            S.flush()
        if 0 <= stop <= 3:
            S.dma("sp", ys, xs_sb[:], reads=[xs_sb])
            S.flush()
            return nc

        with ExitStack() as es:
            tiles = [(128, xa[t * 128:(t + 1) * 128, :], b_xa[t], y[t * 128:(t + 1) * 128, :], b_y) for t in range(NT)]
            tiles.append((NS, None, None, None, None))
            mlp_phase(1, tiles, es)
            S.dma("sp", ys, xs_sb[:], reads=[xs_sb])
            S.flush()
    return nc


class TB_sub:
    def __init__(self, tb, n):
        self.tb = tb
        self.b = tb.b
        self.n = n

    def __getitem__(self, k):
        if k == slice(None, None, None):
            return self.tb.t[:, :, 0:self.n]
        a, i, c = k
        assert c == slice(None, None, None)
        return self.tb.t[a, i, 0:self.n]


class TB_view:
    def __init__(self, tb, c0):
        self.tb = tb
        self.b = tb.b
        self.c0 = c0

    def __getitem__(self, k):
        rows, cols = k
        assert cols == slice(None, None, None)
        return self.tb.t[rows, self.c0:self.c0 + 256]


def _consts():
    c = {}
    c["c_ident"] = np.eye(128, dtype=np.float32)
    s = np.arange(128)[:, None]
    t = np.arange(128)[None, :]
    c["c_cmask"] = (s <= t).astype(np.float32)
    u = np.arange(384)
    dist = u - 127
    valid = (dist >= 0) & (dist < 128)
    n = np.maximum(dist, 0)
    nf = np.maximum(n, 1).astype(np.float32)
    large = 16 + (np.log(nf / np.float32(16)) / np.float32(math.log(128 / 16)) * np.float32(16)).astype(np.int32)
    large = np.minimum(large, 31)
    bucket = np.where(n < 16, n, large)
    oh = np.zeros((32, 384), np.float32)
    oh[bucket, u] = 1.0
    oh[:, ~valid] = 0.0
    c["c_ohu"] = oh
    c["c_valid"] = np.repeat(valid.astype(np.float32)[None, :], 128, axis=0)
    c["c_id16r"] = np.repeat(np.eye(16, dtype=np.float32).reshape(1, 256), 128, axis=0)
    return c


_NC_CACHE = {}


def kernel(**inp):
    NT, NPRE, NS = 16, 47, 16
    f = lambda a: np.ascontiguousarray(np.asarray(a, dtype=np.float32))
    xp = f(inp["x_prompt"])
    B, L, _ = xp.shape
    NT = L // (4 * 128)
    NPRE = 3 * NT - 1
    key = (NT, NPRE, NS)
    if key not in _NC_CACHE:
        import os
        _NC_CACHE[key] = build(NT, NPRE, NS, stop=int(os.environ.get('KSTOP', '99')))
    nc = _NC_CACHE[key]
    consts = _consts()
    shared = {}
    for k in ("norm_mix_pre", "norm_mix_post", "norm_ffn_pre", "norm_ffn_post", "norm_mem", "w_mem_kv", "w_gate_up", "b_gate",
              "gla_norm", "sinks", "rel_bias", "w_out", "w_ffn_up", "w_ffn_down"):
        shared[k] = f(inp[k])
    shared["w_in_a"] = f(inp["w_in_a"])[0]
    shared["w_in_b"] = f(inp["w_in_b"])[0]
    shared["w_gate_up"] = f(inp["w_gate_up"])[0]
    shared["norm_kv"] = f(inp["norm_kv"]).reshape(1, D)
    shared["w_kv"] = f(inp["w_kv"])
    shared.update(consts)
    xs = f(inp["x_sample"]).reshape(-1, D)
    st = f(inp["state_gla"])[0]
    swk = f(inp["cache_swa_k"]).reshape(-1, 128, 256)
    swv = f(inp["cache_swa_v"]).reshape(-1, 128, 256)
    mkc = f(inp["cache_mem_k"]).reshape(2, -1, 256, 256)
    mvc = f(inp["cache_mem_v"]).reshape(2, -1, 256, 256)
    mem = f(inp["mem_prompt"])
    pad = np.zeros(((NPRE + 1) * 128, D), np.float32)
    in_maps = []
    for c in range(NCORES):
        b, j = divmod(c, 4)
        xpad = np.concatenate([pad, xp[b]], axis=0)
        m = dict(shared)
        m["xin"] = np.ascontiguousarray(xpad[j * NT * 128:(j * NT + NPRE + 1 + NT) * 128])
        sl = slice(c * NS, (c + 1) * NS)
        m["xs_in"] = xs[sl]
        m["st_in"] = st[sl]
        m["swk_in"] = swk[sl]
        m["swv_in"] = swv[sl]
        m["mk_in"] = np.ascontiguousarray(mkc[:, sl])
        m["mv_in"] = np.ascontiguousarray(mvc[:, sl])
        m["mem_in"] = mem[b]
        m["c_flag"] = np.full((128, 1), 1.0 if j > 0 else 0.0, np.float32)
        in_maps.append(m)
    res = run_bass_kernel_spmd(nc, in_maps, core_ids=list(range(NCORES)))
    R = res.results
    y_prompt = np.stack([np.concatenate([R[4 * b + j]["y"] for j in range(4)], axis=0) for b in range(B)])
    y_sample = np.concatenate([R[c]["ys"] for c in range(NCORES)], axis=0).reshape(-1, 1, D)
    stp = np.stack([R[4 * b + 3]["stp"] for b in range(B)])[None]
    sts = np.concatenate([R[c]["sts"] for c in range(NCORES)], axis=0)[None]
    swkp = np.stack([R[4 * b + 3]["swkp"] for b in range(B)]).reshape(B, 128, 4, 64)
    swvp = np.stack([R[4 * b + 3]["swvp"] for b in range(B)]).reshape(B, 128, 4, 64)
    swks = np.concatenate([R[c]["swks"] for c in range(NCORES)], axis=0).reshape(-1, 128, 4, 64)
    swvs = np.concatenate([R[c]["swvs"] for c in range(NCORES)], axis=0).reshape(-1, 128, 4, 64)
    mkp = np.stack([R[4 * b]["mkp"] for b in range(B)], axis=1).reshape(2, B, 256, 4, 64)
    mvp = np.stack([R[4 * b]["mvp"] for b in range(B)], axis=1).reshape(2, B, 256, 4, 64)
    return (y_prompt, y_sample, stp, sts, swkp, swvp, swks, swvs, mkp, mvp)
```

```python
import math
import numpy as np
from contextlib import ExitStack
import concourse.bass as bass
import concourse.mybir as mybir
from concourse.bass_utils import run_bass_kernel_spmd

F32 = mybir.dt.float32
BF16 = mybir.dt.bfloat16
ALU = mybir.AluOpType
AF = mybir.ActivationFunctionType
AX = mybir.AxisListType

D = 1024
DK, DV, H = 96, 192, 4
EPS = 1e-6
NCORES = 8


class Buf:
    __slots__ = ("name", "w", "r", "excl")

    def __init__(self, name="", excl=False):
        self.name = name
        self.w = None
        self.r = {}
        self.excl = excl


class StopBuild(Exception):
    pass


class TB:
    def __init__(self, t, name=""):
        self.t = t
        self.b = Buf(name)

    def __getitem__(self, k):
        return self.t[k]


class Sched:
    ENGS = ("pe", "act", "dve", "pool", "sp")
    BLK = {"pe": "tensor", "act": "scalar", "dve": "vector", "pool": "gpsimd", "sp": "sync"}

    def __init__(self, nc, es, n_dma=(28, 14, 8)):
        self.nc = nc
        self.sem = {e: es.enter_context(nc.semaphore("s_" + e)) for e in self.ENGS}
        self.cnt = {e: 0 for e in self.ENGS}
        self.known = {e: {e2: 0 for e2 in self.ENGS} for e in self.ENGS}
        self.dq = {}
        k = 0
        for q, n in zip(("sp", "pool", "act"), n_dma):
            self.dq[q] = list(range(k, k + n))
            k += n
        self.dsem = [es.enter_context(nc.semaphore("d%d" % i)) for i in range(k)]
        self.dcnt = [0] * k
        self.drr = {q: 0 for q in self.dq}
        self.kdma = {e: [0] * k for e in self.ENGS}
        self.prog = {e: [] for e in self.ENGS}

    @staticmethod
    def _b(x):
        return getattr(x, "b", x)

    def _deps(self, reads, writes):
        deps = []
        for b in reads:
            b = self._b(b)
            if b.w is not None:
                deps.append(b.w)
        for b in writes:
            b = self._b(b)
            if b.w is not None:
                deps.append(b.w)
            deps.extend(b.r.values())
        return deps

    def _resolve(self, eng, deps):
        need = {}
        for ev in deps:
            if ev[0] == "c":
                _, e2, k = ev
                if e2 == eng and eng == "pe":
                    continue
                if self.known[eng][e2] >= k:
                    continue
                need[("c", e2)] = max(need.get(("c", e2), 0), k)
            else:
                _, idx, v = ev
                if self.kdma[eng][idx] >= v:
                    continue
                need[("d", idx)] = max(need.get(("d", idx), 0), v)
        waits = []
        for (kind, key), v in need.items():
            if kind == "c":
                self.known[eng][key] = v
                waits.append((self.sem[key], v))
            else:
                self.kdma[eng][key] = v
                waits.append((self.dsem[key], v))
        return waits

    def _mark(self, ev, key, reads, writes):
        for b in reads:
            self._b(b).r[key] = ev
        for b in writes:
            b = self._b(b)
            b.w = ev
            b.r = {}

    def op(self, eng, fn, reads=(), writes=()):
        ex = [b for b in reads if self._b(b).excl]
        if ex:
            reads = [b for b in reads if not self._b(b).excl]
            writes = list(writes) + ex
        waits = self._resolve(eng, self._deps(reads, writes))
        self.cnt[eng] += 1
        ev = ("c", eng, self.cnt[eng])
        self.prog[eng].append((waits, fn, None))
        self._mark(ev, eng, reads, writes)
        return ev

    def dma(self, q, out, in_, reads=(), writes=(), **kw):
        waits = self._resolve(q, self._deps(reads, writes))
        pool = self.dq[q]
        idx = pool[self.drr[q] % len(pool)]
        self.drr[q] += 1
        v0 = 16 * self.dcnt[idx]
        if v0 > 0 and self.kdma[q][idx] < v0:
            self.kdma[q][idx] = v0
            waits.append((self.dsem[idx], v0))
        self.dcnt[idx] += 1
        ev = ("d", idx, 16 * self.dcnt[idx])
        self.prog[q].append((waits, lambda h: h.dma_start(out=out, in_=in_, **kw), idx))
        self._mark(ev, ("d", idx), reads, writes)
        return ev

    def flush(self):
        nc = self.nc
        tails = {}
        for q, pool in self.dq.items():
            tl = []
            for idx in pool:
                v = 16 * self.dcnt[idx]
                if v > 0 and self.kdma[q][idx] < v:
                    tl.append((self.dsem[idx], v))
            tails[q] = tl
        with nc.Block() as block:
            for e in self.ENGS:
                items = self.prog[e]
                tl = tails.get(e, [])
                if not items and not tl:
                    continue

                def body(h, items=items, tl=tl, e=e):
                    for waits, fn, didx in items:
                        for s, v in waits:
                            h.wait_ge(s, v)
                        ins = fn(h)
                        if didx is None:
                            ins.then_inc(self.sem[e], 1)
                        else:
                            ins.then_inc(self.dsem[didx], 16)
                    for s, v in tl:
                        h.wait_ge(s, v)

                getattr(block, self.BLK[e])(body)
        self.prog = {e: [] for e in self.ENGS}
        for e in self.ENGS:
            for e2 in self.ENGS:
                self.known[e][e2] = self.cnt[e2]
            for i in range(len(self.dcnt)):
                self.kdma[e][i] = 16 * self.dcnt[i]


def bc(ap, dim, n):
    u = ap.unsqueeze(dim)
    shp = list(u.shape)
    shp[dim] = n
    return u.broadcast_to(shp)


def qcol(h):
    s, r = divmod(h, 6)
    half, j = divmod(r, 3)
    return (3 * s + j) * 128 + half * 64


def build(NT=16, NPRE=47, NS=16, stop=99):
    try:
        return _build(NT, NPRE, NS, stop)
    except StopBuild as e:
        return e.args[0]


def _build(NT=16, NPRE=47, NS=16, stop=99):
    nc = bass.Bass("TRN2", target_bir_lowering=False)
    NTI = NPRE + 1 + NT
    NT1 = NT + 1

    def din(name, shape):
        return nc.dram_tensor(name, list(shape), F32, kind="ExternalInput").ap()

    def dout(name, shape):
        return nc.dram_tensor(name, list(shape), F32, kind="ExternalOutput").ap()

    xin = din("xin", [NTI * 128, D])
    xs_in = din("xs_in", [NS, D])
    st_in = din("st_in", [NS, H, DK, DV])
    swk_in = din("swk_in", [NS, 128, 256])
    swv_in = din("swv_in", [NS, 128, 256])
    mk_in = din("mk_in", [2, NS, 256, 256])
    mv_in = din("mv_in", [2, NS, 256, 256])
    mem_in = din("mem_in", [256, D])
    g_mix_pre = din("norm_mix_pre", [2, D])
    g_mix_post = din("norm_mix_post", [2, D])
    g_ffn_pre = din("norm_ffn_pre", [2, D])
    g_ffn_post = din("norm_ffn_post", [2, D])
    g_mem = din("norm_mem", [2, D])
    w_mem_kv = din("w_mem_kv", [2, D, 512])
    w_in_a = din("w_in_a", [D, 2576])
    w_gate_up = din("w_gate_up", [16, 384])
    b_gate = din("b_gate", [1, 384])
    gla_norm = din("gla_norm", [1, DV])
    w_in_b = din("w_in_b", [D, 1024])
    sinks = din("sinks", [1, 12])
    g_kv = din("norm_kv", [1, D])
    w_kv = din("w_kv", [D, 512])
    rel_bias = din("rel_bias", [32, 12])
    w_out = din("w_out", [2, D, D])
    w_up = din("w_ffn_up", [2, D, 4096])
    w_down = din("w_ffn_down", [2, 4096, D])
    c_ident = din("c_ident", [128, 128])
    c_cmask = din("c_cmask", [128, 128])
    c_ohu = din("c_ohu", [32, 384])
    c_valid = din("c_valid", [128, 384])
    c_id16r = din("c_id16r", [128, 256])
    c_flag = din("c_flag", [128, 1])

    y = dout("y", [NT * 128, D])
    ys = dout("ys", [NS, D])
    stp = dout("stp", [H, DK, DV])
    sts = dout("sts", [NS, H, DK, DV])
    swkp = dout("swkp", [128, 256])
    swvp = dout("swvp", [128, 256])
    swks = dout("swks", [NS, 128, 256])
    swvs = dout("swvs", [NS, 128, 256])
    mkp = dout("mkp", [2, 256, 256])
    mvp = dout("mvp", [2, 256, 256])

    xa = nc.dram_tensor("xa_scr", [NT1 * 128, D], F32).ap()
    xb = nc.dram_tensor("xb_scr", [NT1 * 128, D], F32).ap()
    ebd = nc.dram_tensor("ebd_scr", [12, 128, 384], F32)
    b_xa = [Buf() for _ in range(NT1)]
    b_xb = [Buf() for _ in range(NT1)]
    b_ebd = Buf()
    b_y = Buf()

    with ExitStack() as ges:
        S = Sched(nc, ges)

        def ck(n):
            if stop == -100 - n:
                S.flush()
                raise StopBuild(nc)

        def mk(es, name, shape, dt=F32):
            return TB(es.enter_context(nc.sbuf_tensor(name, list(shape), dt)), name)

        PS = [ges.enter_context(nc.psum_tensor("PS%d" % i, [128, 1024], F32)) for i in range(4)]
        bB = [Buf("bank%d" % i, excl=True) for i in range(8)]

        def bank(i):
            return PS[i // 2][:, (i % 2) * 512:(i % 2 + 1) * 512]

        def bankb(i):
            return bank(i).bitcast(BF16)

        ident_f = mk(ges, "ident_f", [128, 128])
        ident_b = mk(ges, "ident_b", [128, 128], BF16)
        cmask = mk(ges, "cmask", [128, 128])
        id16r = mk(ges, "id16r", [128, 16, 16])
        ones_row = mk(ges, "ones_row", [1, 128], BF16)
        ones96 = mk(ges, "ones96", [96, 128])
        flag = mk(ges, "flag", [128, 1])
        KmT = [mk(ges, "KmT%d" % l, [128, 2, 2, 256], BF16) for l in range(2)]
        Vm = [mk(ges, "Vm%d" % l, [128, 2, 4, 66], BF16) for l in range(2)]
        xs_sb = mk(ges, "xs_sb", [NS, D])
        st = {n: mk(ges, "st_" + n, [128, 4]) for n in ("ss", "ln", "rstd", "ssg", "lng", "rstdg", "rden", "el")}
        rden12 = mk(ges, "rden12", [128, 12])
        den12 = mk(ges, "den12", [128, 12])
        xt = [mk(ges, "xt%d" % i, [128, D]) for i in range(3)]
        hb = mk(ges, "hb", [128, D], BF16)
        hb2 = mk(ges, "hb2", [128, D], BF16)
        hT = mk(ges, "hT", [128, 8, 128], BF16)
        hT2 = mk(ges, "hT2", [128, 8, 128], BF16)
        mixcat = mk(ges, "mixcat", [128, D], BF16)
        mcT = mk(ges, "mcT", [128, 8, 128], BF16)
        tt = mk(ges, "tt", [128, D])
        pT_sb = mk(ges, "pT_sb", [128, 8, 128], BF16)
        qmT_sb = mk(ges, "qmT_sb", [128, 2, 128], BF16)
        z16 = mk(ges, "z16", [16, 512])

        xt_rr = [0]

        def next_xt():
            t = xt[xt_rr[0] % 3]
            xt_rr[0] += 1
            return t

        def load_gain(dst, src_row):
            S.dma("sp", dst[:], src_row.partition_broadcast(128), writes=[dst])

        def load_w(dst, src2d, kchunks, ncols, col0=0, dcol0=0):
            v = src2d.rearrange("(kc p) n -> p kc n", p=128)
            step = 2048
            for kc in range(kchunks):
                c = 0
                while c < ncols:
                    n = min(step, ncols - c)
                    S.dma("pool", dst[:, kc, dcol0 + c:dcol0 + c + n], v[:, kc, col0 + c:col0 + c + n], writes=[dst])
                    c += n

        def rms_stats(x_ap, P, x_reads, junk, n=D):
            ss, ln, rstd = st["ss"], st["ln"], st["rstd"]
            S.op("act", lambda h: h.activation(out=junk[0:P, 0:n], in_=x_ap, func=AF.Square, accum_out=ss[0:P, 0:1]),
                 reads=x_reads, writes=[junk, ss])
            S.op("act", lambda h: h.activation(out=ln[0:P, 0:1], in_=ss[0:P, 0:1], func=AF.Ln, scale=1.0 / n, bias=eps_t[0:P, 0:1]),
                 reads=[ss, eps_t], writes=[ln])
            S.op("act", lambda h: h.activation(out=rstd[0:P, 0:1], in_=ln[0:P, 0:1], func=AF.Exp, scale=-0.5),
                 reads=[ln], writes=[rstd])

        def norm_to(x_ap, P, x_reads, gain, dst):
            S.op("dve", lambda h: h.scalar_tensor_tensor(out=dst[0:P, :], in0=x_ap, scalar=st["rstd"][0:P, 0:1], in1=gain[0:P, :],
                                                         op0=ALU.mult, op1=ALU.mult),
                 reads=list(x_reads) + [st["rstd"], gain], writes=[dst])

        def transpose8(src, P, dstT, tb=0):
            def f(h):
                for kc in range(8):
                    ins = h.transpose(out=bankb(tb)[:, kc * 128:kc * 128 + P], in_=src[0:P, kc * 128:(kc + 1) * 128],
                                      identity=ident_b[0:P, 0:P])
                return ins
            S.op("pe", f, reads=[src, ident_b], writes=[bB[tb]])
            S.op("act", lambda h: h.copy(out=dstT[:, :, 0:P], in_=bankb(tb)[:, :].rearrange("p (a b) -> p a b", a=8)[:, :, 0:P]),
                 reads=[bB[tb]], writes=[dstT])

        def mm(out_ap, pairs, reads, wbank):
            def f(h):
                n = len(pairs)
                for i, (l, r) in enumerate(pairs):
                    ins = h.matmul(out_ap, lhsT=l, rhs=r, start=(i == 0), stop=(i == n - 1))
                return ins
            S.op("pe", f, reads=reads, writes=wbank)

        def post_norm_residual(mix_pair, P, gain, xtile_ap, xtile_tb):
            mix = PS[mix_pair][0:P, :]
            rms_stats(mix, P, [bB[2 * mix_pair], bB[2 * mix_pair + 1]], tt)
            S.op("dve", lambda h: h.scalar_tensor_tensor(out=tt[0:P, :], in0=mix, scalar=st["rstd"][0:P, 0:1], in1=gain[0:P, :],
                                                         op0=ALU.mult, op1=ALU.mult),
                 reads=[bB[2 * mix_pair], bB[2 * mix_pair + 1], st["rstd"], gain], writes=[tt])
            S.op("pool", lambda h: h.tensor_tensor(out=xtile_ap, in0=xtile_ap, in1=tt[0:P, :], op=ALU.add),
                 reads=[tt], writes=[xtile_tb])

        def out_proj(P, Wo, gain, xtile_ap, xtile_tb):
            transpose8(mixcat, P, mcT, tb=0)
            for half in range(2):
                mm(bank(6 + half)[0:P, :], [(mcT[:, kc, 0:P], Wo[:, kc, half * 512:(half + 1) * 512]) for kc in range(8)],
                   [mcT, Wo], [bB[6 + half]])
            post_norm_residual(3, P, gain, xtile_ap, xtile_tb)

        def mem_attend_prompt(l, qT_ap_fn, qreads, P):
            def f(h):
                for hh in range(4):
                    c, p = divmod(hh, 2)
                    for mc in range(2):
                        ins = h.matmul(bank(4 + hh // 2)[:, ((hh % 2) * 2 + mc) * 128:((hh % 2) * 2 + mc) * 128 + P],
                                       lhsT=KmT[l][:, p, c, mc * 128:(mc + 1) * 128], rhs=qT_ap_fn(c, p),
                                       start=True, stop=True)
                return ins
            S.op("pe", f, reads=[KmT[l]] + qreads, writes=[bB[4], bB[5]])
            ck(30)
            S.op("act", lambda h: h.activation(out=pT_sb[:, :, 0:P], in_=PS[2][:, :].rearrange("p (a b) -> p a b", a=8)[:, :, 0:P],
                                               func=AF.Exp, scale=0.125),
                 reads=[bB[4], bB[5]], writes=[pT_sb])

            def g(h):
                for hh in range(4):
                    for mc in range(2):
                        ins = h.matmul(bank(1)[0:P, hh * 65:(hh + 1) * 65], lhsT=pT_sb[:, hh * 2 + mc, 0:P], rhs=Vm[l][:, mc, hh, 0:65],
                                       start=(mc == 0), stop=(mc == 1))
                return ins
            ck(31)
            S.op("pe", g, reads=[pT_sb, Vm[l]], writes=[bB[1]])
            ck(32)
            finish_mem(P)

        def finish_mem(P):
            om = bank(1)[0:P, 0:260].rearrange("p (a b) -> p a b", a=4)
            S.op("dve", lambda h: h.reciprocal(out=st["rden"][0:P, 0:4], in_=om[:, :, 64]), reads=[bB[1]], writes=[st["rden"]])
            S.op("dve", lambda h: h.tensor_tensor(out=mixcat[0:P, 768:1024].rearrange("p (a b) -> p a b", a=4), in0=om[:, :, 0:64],
                                                  in1=bc(st["rden"][0:P, 0:4], 2, 64), op=ALU.mult),
                 reads=[bB[1], st["rden"]], writes=[mixcat])

        mem_sample_fn = []

        def zero_acc(regions, banks):
            def f(h):
                for r_ in regions:
                    n = r_.shape[-1]
                    ins = h.matmul(r_, lhsT=z16[0:16, 0:NS], rhs=z16[0:16, 0:n], start=True, stop=False)
                return ins
            S.op("pe", f, reads=[z16], writes=banks)

        def alloc_sample_attn(es, tag, onesel=None):
            Kc = [mk(es, tag + "Kc%d" % i, [128, 2, 256]) for i in range(2)]
            Vc = [mk(es, tag + "Vc%d" % i, [128, 2, 4, 65]) for i in range(2)]
            prod = mk(es, tag + "prod", [128, 768])
            sc8 = mk(es, tag + "sc8", [128, 12])
            pp8 = mk(es, tag + "pp8", [128, 12])
            Pm = mk(es, tag + "Pm", [128, 12, NS])
            if onesel is None:
                onesel = mk(es, tag + "onesel", [NS, NS, 128])
            S.op("pool", lambda h: h.memset(Vc[0][:], 1.0), writes=[Vc[0]])
            S.op("pool", lambda h: h.memset(Vc[1][:], 1.0), writes=[Vc[1]])
            S.op("dve", lambda h: h.tensor_copy(out=onesel[:], in_=bc(ident_f[0:NS, 0:NS], 2, 128)), reads=[ident_f], writes=[onesel])
            return Kc, Vc, prod, sc8, pp8, Pm, onesel

        with ExitStack() as es:
            eps_t = mk(ges, "eps_t", [128, 1])
            S.op("pool", lambda h: h.memset(eps_t[:], EPS), writes=[eps_t])
            S.op("pool", lambda h: h.memset(z16[:], 0.0), writes=[z16])
            S.dma("sp", ident_f[:], c_ident, writes=[ident_f])
            S.dma("sp", cmask[:], c_cmask, writes=[cmask])
            S.dma("sp", id16r[:].rearrange("p a b -> p (a b)"), c_id16r, writes=[id16r])
            S.dma("sp", flag[:], c_flag, writes=[flag])
            S.dma("sp", xs_sb[:], xs_in, writes=[xs_sb])
            S.op("dve", lambda h: h.tensor_copy(out=ident_b[:], in_=ident_f[:]), reads=[ident_f], writes=[ident_b])
            S.op("pool", lambda h: h.memset(ones_row[:], 1.0), writes=[ones_row])
            S.op("pool", lambda h: h.memset(ones96[:], 1.0), writes=[ones96])
            for l in range(2):
                S.op("pool", lambda h, l=l: h.memset(Vm[l][:], 1.0), writes=[Vm[l]])
                S.op("pool", lambda h, l=l: h.memset(KmT[l][:], 0.0), writes=[KmT[l]])
            gm = mk(es, "gm", [128, D])
            Wm = mk(es, "Wm", [128, 8, 512], BF16)
            hmT = mk(es, "hmT", [128, 8, 256], BF16)
            kvf = mk(es, "kvf", [128, 512])
            xm = [mk(es, "xm%d" % i, [128, D]) for i in range(2)]
            for mt in range(2):
                S.dma("sp", xm[mt][:], mem_in[mt * 128:(mt + 1) * 128, :], writes=[xm[mt]])
            if stop == -4:
                S.flush()
                return nc
            for l in range(2):
                load_gain(gm, g_mem[l, :])
                load_w(Wm, w_mem_kv[l], 8, 512)
                for mt in range(2):
                    rms_stats(xm[mt][:], 128, [xm[mt]], hb)
                    if stop == -3:
                        S.flush()
                        return nc
                    norm_to(xm[mt][:], 128, [xm[mt]], gm, hb)
                    if stop == -2:
                        S.flush()
                        return nc
                    transpose8(hb, 128, hT, tb=0)
                    if stop == -1:
                        S.flush()
                        return nc
                    S.op("dve", lambda h, mt=mt: h.tensor_copy(out=hmT[:, :, mt * 128:(mt + 1) * 128], in_=hT[:]),
                         reads=[hT], writes=[hmT])
                for c in range(2):
                    mm(bank(2)[:, 0:256], [(Wm[:, kc, c * 128:(c + 1) * 128], hmT[:, kc, :]) for kc in range(8)], [Wm, hmT], [bB[2]])
                    S.op("act", lambda h, c=c, l=l: h.copy(out=KmT[l][0:64, 0, c, :], in_=bank(2)[0:64, 0:256]), reads=[bB[2]], writes=[KmT[l]])
                    S.op("act", lambda h, c=c, l=l: h.copy(out=KmT[l][64:128, 1, c, :], in_=bank(2)[64:128, 0:256]), reads=[bB[2]], writes=[KmT[l]])
                if stop == -10:
                    S.flush()
                    return nc
                for mt in range(2):
                    mm(bank(3)[:, 0:512], [(hmT[:, kc, mt * 128:(mt + 1) * 128], Wm[:, kc, :]) for kc in range(8)], [Wm, hmT], [bB[3]])
                    S.op("dve", lambda h: h.tensor_copy(out=kvf[:], in_=bank(3)[:, 0:512]), reads=[bB[3]], writes=[kvf])
                    if stop == -11:
                        S.flush()
                        return nc
                    S.op("act", lambda h, l=l, mt=mt: h.copy(out=Vm[l][:, mt, :, 0:64],
                                                             in_=bank(3)[:, 256:512].rearrange("p (a b) -> p a b", a=4)),
                         reads=[bB[3]], writes=[Vm[l]])
                    if stop == -12:
                        S.flush()
                        return nc
                    S.dma("sp", mkp[l, mt * 128:(mt + 1) * 128, :], kvf[:, 0:256], reads=[kvf])
                    S.dma("sp", mvp[l, mt * 128:(mt + 1) * 128, :], kvf[:, 256:512], reads=[kvf])
            S.flush()
        if 0 <= stop <= 0:
            return nc

        with ExitStack() as es:
            Wa = mk(es, "Wa", [128, 8, 2576], BF16)
            Wo = mk(es, "Wo0", [128, 8, D], BF16)
            wgu = mk(es, "wgu", [16, 384], BF16)
            bg = mk(es, "bg", [1, 384], BF16)
            gpre = mk(es, "gpre0", [128, D])
            gpost = mk(es, "gpost0", [128, D])
            gn = mk(es, "gn", [128, DV])
            Sst = mk(es, "Sst", [96, 4, DV])
            S_bf = mk(es, "S_bf", [96, 4, DV], BF16)
            glr_sb = mk(es, "glr_sb", [16, 128], BF16)
            e1 = mk(es, "e1", [96, 4, 128])
            spl = mk(es, "spl", [96, 4, 128])
            cc = mk(es, "cc", [96, 4, 128])
            Einv = mk(es, "Einv", [96, 4, 128])
            Edec = mk(es, "Edec", [96, 4, 128])
            keT = mk(es, "keT", [96, 4, 128], BF16)
            qeT = mk(es, "qeT", [96, 4, 128], BF16)
            kdT = mk(es, "kdT", [96, 4, 128], BF16)
            kd_sb = mk(es, "kd_sb", [128, 384], BF16)
            v_sb = mk(es, "v_sb", [128, 768], BF16)
            attnT_sb = mk(es, "attnT_sb", [128, 4, 128], BF16)
            on = mk(es, "on", [128, 768])
            er = mk(es, "er", [128, 768])
            sg = mk(es, "sg", [128, 768])

            load_w(Wa, w_in_a, 8, 2576)
            load_w(Wo, w_out[0], 8, D)
            S.dma("pool", wgu[:], w_gate_up, writes=[wgu])
            S.dma("pool", bg[:], b_gate, writes=[bg])
            load_gain(gpre, g_mix_pre[0, :])
            load_gain(gpost, g_mix_post[0, :])
            S.dma("sp", gn[:], gla_norm[0, :].partition_broadcast(128), writes=[gn])
            S.op("pool", lambda h: h.memset(Sst[:], 0.0), writes=[Sst])
            S.op("pool", lambda h: h.memset(S_bf[:], 0.0), writes=[S_bf])

            elast = st["el"]

            def gate_path(P):
                mm(bank(1)[0:16, 0:P], [(Wa[:, kc, 2304:2320], hT[:, kc, 0:P]) for kc in range(8)], [Wa, hT], [bB[1]])
                S.op("dve", lambda h: h.tensor_copy(out=glr_sb[:, 0:P], in_=bank(1)[0:16, 0:P]), reads=[bB[1]], writes=[glr_sb])

            def zgate(P):
                def f(h):
                    for hh in range(4):
                        o = bank(1)[0:96, hh * 128:hh * 128 + P]
                        h.matmul(o, lhsT=wgu[0:16, 96 * hh:96 * hh + 96], rhs=glr_sb[0:16, 0:P], start=True, stop=False)
                        ins = h.matmul(o, lhsT=bg[0:1, 96 * hh:96 * hh + 96], rhs=ones_row[0:1, 0:P], start=False, stop=True)
                    return ins
                S.op("pe", f, reads=[wgu, bg, glr_sb, ones_row], writes=[bB[1]])
                zv = bank(1)[0:96, :].rearrange("p (a b) -> p a b", a=4)[:, :, 0:P]
                S.op("act", lambda h: h.activation(out=e1[:, :, 0:P], in_=zv, func=AF.Exp, scale=-1.0), reads=[bB[1]], writes=[e1])
                S.op("act", lambda h: h.activation(out=spl[:, :, 0:P], in_=e1[:, :, 0:P], func=AF.Ln, scale=1.0, bias=one_t[0:96, 0:1]),
                     reads=[e1, one_t], writes=[spl])

            one_t = mk(es, "one_t", [128, 1])
            S.op("pool", lambda h: h.memset(one_t[:], 1.0), writes=[one_t])

            def proj_fm(col0, bk, P):
                def f(h):
                    for hh in range(4):
                        for kc in range(8):
                            ins = h.matmul(bank(bk)[0:96, hh * 128:hh * 128 + P], lhsT=Wa[:, kc, col0 + 96 * hh:col0 + 96 * hh + 96],
                                           rhs=hT[:, kc, 0:P], start=(kc == 0), stop=(kc == 7))
                    return ins
                S.op("pe", f, reads=[Wa, hT], writes=[bB[bk]])

            def proj_tm(col0, pair, P):
                for half in range(2):
                    mm(bank(2 * pair + half)[0:P, 0:384],
                       [(hT[:, kc, 0:P], Wa[:, kc, col0 + 384 * half:col0 + 384 * (half + 1)]) for kc in range(8)],
                       [Wa, hT], [bB[2 * pair + half]])

            def pair768(pair, P):
                return PS[pair][0:P, :].rearrange("p (a b) -> p a b", a=2)[:, :, 0:384]

            def gla_norm_gate(P, o_pair, r_pair):
                ov = pair768(o_pair, P).rearrange("p a (i n) -> p a i n", i=2)
                for hh in range(4):
                    S.op("act", lambda h, hh=hh: h.activation(out=sg[0:P, 0:DV], in_=ov[:, hh // 2, hh % 2, :], func=AF.Square,
                                                              accum_out=st["ssg"][0:P, hh:hh + 1]),
                         reads=[bB[2 * o_pair], bB[2 * o_pair + 1]], writes=[sg, st["ssg"]])
                S.op("act", lambda h: h.activation(out=st["lng"][0:P, :], in_=st["ssg"][0:P, :], func=AF.Ln, scale=1.0 / DV,
                                                   bias=eps_t[0:P, 0:1]), reads=[st["ssg"], eps_t], writes=[st["lng"]])
                S.op("act", lambda h: h.activation(out=st["rstdg"][0:P, :], in_=st["lng"][0:P, :], func=AF.Exp, scale=-0.5),
                     reads=[st["lng"]], writes=[st["rstdg"]])
                for hh in range(4):
                    S.op("dve", lambda h, hh=hh: h.scalar_tensor_tensor(out=on[0:P, hh * DV:(hh + 1) * DV], in0=ov[:, hh // 2, hh % 2, :],
                                                                        scalar=st["rstdg"][0:P, hh:hh + 1], in1=gn[0:P, :],
                                                                        op0=ALU.mult, op1=ALU.mult),
                         reads=[bB[2 * o_pair], bB[2 * o_pair + 1], st["rstdg"], gn], writes=[on])
                rv = pair768(r_pair, P)
                er3 = er[0:P, :].rearrange("p (a b) -> p a b", a=2)
                S.op("act", lambda h: h.activation(out=er3, in_=rv, func=AF.Exp, scale=-1.0),
                     reads=[bB[2 * r_pair], bB[2 * r_pair + 1]], writes=[er])
                S.op("pool", lambda h: h.tensor_scalar(out=er[0:P, :], in0=er[0:P, :], scalar1=1.0, scalar2=None, op0=ALU.add),
                     reads=[], writes=[er])
                S.op("dve", lambda h: h.reciprocal(out=er[0:P, :], in_=er[0:P, :]), reads=[], writes=[er])
                S.op("dve", lambda h: h.tensor_tensor(out=sg[0:P, :].rearrange("p (a b) -> p a b", a=2), in0=rv, in1=er3, op=ALU.mult),
                     reads=[bB[2 * r_pair], bB[2 * r_pair + 1], er], writes=[sg])
                S.op("dve", lambda h: h.tensor_tensor(out=mixcat[0:P, 0:768], in0=on[0:P, :], in1=sg[0:P, :], op=ALU.mult),
                     reads=[on, sg], writes=[mixcat])

            def load_x0(row0):
                xt_ = next_xt()
                S.dma("sp", xt_[:], xin[row0:row0 + 128, :], writes=[xt_])
                return xt_

            def gla_tile(row0, full, out_slot, xtile=None):
                P = 128
                if xtile is None:
                    xtile = load_x0(row0)
                rms_stats(xtile[:], P, [xtile], hb)
                norm_to(xtile[:], P, [xtile], gpre, hb)
                transpose8(hb, P, hT, tb=0)
                gate_path(P)
                ck(1)
                proj_fm(384, 2, P)
                ck(2)
                if full:
                    proj_fm(0, 3, P)
                zgate(P)
                ck(3)
                proj_tm(768, 2, P)
                if full:
                    proj_tm(1536, 3, P)
                S.op("act", lambda h: h.copy(out=v_sb[:].rearrange("p (a b) -> p a b", a=2), in_=pair768(2, P)),
                     reads=[bB[4], bB[5]], writes=[v_sb])
                for hh in range(4):
                    S.op("dve", lambda h, hh=hh: h.tensor_tensor_scan(out=cc[:, hh, :], data0=ones96[:, :], data1=spl[:, hh, :], initial=0.0,
                                                                      op0=ALU.mult, op1=ALU.add),
                         reads=[spl, ones96], writes=[cc])
                S.op("act", lambda h: h.activation(out=Einv[:], in_=cc[:], func=AF.Exp, scale=1.0 / 16), reads=[cc], writes=[Einv])
                if full:
                    S.op("act", lambda h: h.activation(out=Edec[:], in_=cc[:], func=AF.Exp, scale=-1.0 / 16), reads=[cc], writes=[Edec])
                S.op("act", lambda h: h.activation(out=elast[0:96, 0:4], in_=cc[:, :, 127], func=AF.Exp, scale=-1.0 / 16),
                     reads=[cc], writes=[elast])
                ck(4)
                kT = bank(2)[0:96, :].rearrange("p (a b) -> p a b", a=4)
                qT = bank(3)[0:96, :].rearrange("p (a b) -> p a b", a=4)
                S.op("dve", lambda h: h.tensor_tensor(out=keT[:], in0=kT, in1=Einv[:], op=ALU.mult), reads=[bB[2], Einv], writes=[keT])
                if full:
                    S.op("dve", lambda h: h.scalar_tensor_tensor(out=qeT[:], in0=qT, scalar=DK ** -0.5, in1=Edec[:], op0=ALU.mult, op1=ALU.mult),
                         reads=[bB[3], Edec], writes=[qeT])
                for hh in range(4):
                    S.op("dve", lambda h, hh=hh: h.scalar_tensor_tensor(out=kdT[:, hh, :], in0=kT[:, hh, :], scalar=elast[0:96, hh:hh + 1],
                                                                        in1=Einv[:, hh, :], op0=ALU.mult, op1=ALU.mult),
                         reads=[bB[2], elast, Einv], writes=[kdT])

                ck(5)

                def trk(h):
                    for hh in range(4):
                        ins = h.transpose(out=bankb(0)[:, hh * 96:(hh + 1) * 96], in_=kdT[:, hh, :], identity=ident_b[0:96, 0:96])
                    return ins
                S.op("pe", trk, reads=[kdT, ident_b], writes=[bB[0]])
                S.op("act", lambda h: h.copy(out=kd_sb[:], in_=bankb(0)[:, 0:384]), reads=[bB[0]], writes=[kd_sb])
                ck(6)
                ck(7)
                if full:
                    def fa(h):
                        for hh in range(4):
                            ins = h.matmul(bank(1)[:, hh * 128:(hh + 1) * 128], lhsT=keT[:, hh, :], rhs=qeT[:, hh, :], start=True, stop=True)
                        return ins
                    S.op("pe", fa, reads=[keT, qeT], writes=[bB[1]])
                    S.op("dve", lambda h: h.tensor_tensor(out=attnT_sb[:], in0=bank(1)[:, :].rearrange("p (a b) -> p a b", a=4),
                                                          in1=bc(cmask[:], 1, 4), op=ALU.mult),
                         reads=[bB[1], cmask], writes=[attnT_sb])

                    def fo(h):
                        for hh in range(4):
                            o = bank(4 + hh // 2)[:, (hh % 2) * DV:(hh % 2 + 1) * DV]
                            h.matmul(o, lhsT=attnT_sb[:, hh, :], rhs=v_sb[:, hh * DV:(hh + 1) * DV], start=True, stop=False)
                            ins = h.matmul(o, lhsT=qeT[:, hh, :], rhs=S_bf[:, hh, :], start=False, stop=True)
                        return ins
                    S.op("pe", fo, reads=[attnT_sb, v_sb, qeT, S_bf], writes=[bB[4], bB[5]])

                def fs(h):
                    for hh in range(4):
                        ins = h.matmul(bank(2 + hh // 2)[0:96, (hh % 2) * DV:(hh % 2 + 1) * DV], lhsT=kd_sb[:, 96 * hh:96 * hh + 96],
                                       rhs=v_sb[:, hh * DV:(hh + 1) * DV], start=True, stop=True)
                    return ins
                S.op("pe", fs, reads=[kd_sb, v_sb], writes=[bB[2], bB[3]])
                for hh in range(4):
                    S.op("dve", lambda h, hh=hh: h.scalar_tensor_tensor(out=Sst[:, hh, :], in0=Sst[:, hh, :], scalar=elast[0:96, hh:hh + 1],
                                                                        in1=bank(2 + hh // 2)[0:96, (hh % 2) * DV:(hh % 2 + 1) * DV],
                                                                        op0=ALU.mult, op1=ALU.add),
                         reads=[elast, bB[2], bB[3]], writes=[Sst])
                S.op("pool", lambda h: h.tensor_copy(out=S_bf[:], in_=Sst[:]), reads=[Sst], writes=[S_bf])
                ck(8)
                if not full:
                    return
                for c in range(2):
                    mm(bank(1)[:, c * 128:c * 128 + P], [(Wa[:, kc, 2320 + 128 * c:2320 + 128 * (c + 1)], hT[:, kc, 0:P]) for kc in range(8)],
                       [Wa, hT], [bB[1]])
                S.op("act", lambda h: h.copy(out=qmT_sb[:, :, 0:P], in_=bank(1)[:, 0:256].rearrange("p (a b) -> p a b", a=2)[:, :, 0:P]),
                     reads=[bB[1]], writes=[qmT_sb])
                ck(9)
                gla_norm_gate(P, 2, 3)
                ck(10)
                mem_attend_prompt(0, lambda c, p: qmT_sb[:, c, 0:P], [qmT_sb], P)
                ck(11)
                out_proj(P, Wo, gpost, xtile[:], xtile)
                ck(12)
                S.dma("sp", xa[out_slot * 128:(out_slot + 1) * 128, :], xtile[:], reads=[xtile], writes=[b_xa[out_slot]])

            esp = ExitStack()
            PB = []
            for par in range(4):
                d = {}
                d["hb"] = mk(esp, "p_hb%d" % par, [128, D], BF16)
                d["hT"] = mk(esp, "p_hT%d" % par, [128, 8, 128], BF16)
                d["glr"] = mk(esp, "p_glr%d" % par, [16, 128], BF16)
                d["sp"] = mk(esp, "p_sp%d" % par, [96, 4, 128])
                d["cc"] = mk(esp, "p_cc%d" % par, [96, 4, 128])
                d["kdT"] = mk(esp, "p_kdT%d" % par, [96, 4, 128], BF16)
                d["kd"] = mk(esp, "p_kd%d" % par, [128, 384], BF16)
                d["v"] = mk(esp, "p_v%d" % par, [128, 768], BF16)
                d["ss"] = mk(esp, "p_ss%d" % par, [128, 1])
                d["ln"] = mk(esp, "p_ln%d" % par, [128, 1])
                d["rs"] = mk(esp, "p_rs%d" % par, [128, 1])
                d["el"] = mk(esp, "p_el%d" % par, [96, 4])
                PB.append(d)

            def prefix_tile(t, Q):
                def q(fn, *a, **k):
                    Q.append((fn, a, k))

                par = t % 4
                d = PB[par]
                bT, bZ, bK, bX = 2 * par, 2 * par, 2 * par + 1, 2 * par + 1
                hbP, hTP, glrP, spP, ccP, kdTP, kdP, vP = d["hb"], d["hT"], d["glr"], d["sp"], d["cc"], d["kdT"], d["kd"], d["v"]
                ssP, lnP, rsP, elP = d["ss"], d["ln"], d["rs"], d["el"]
                xtile = next_xt()
                q(S.dma, "sp", xtile[:], xin[t * 128:(t + 1) * 128, :], writes=[xtile])
                q(S.op, "act", lambda h: h.activation(out=hbP[:], in_=xtile[:], func=AF.Square, accum_out=ssP[:, 0:1]), reads=[xtile], writes=[hbP, ssP])
                q(S.op, "act", lambda h: h.activation(out=lnP[:], in_=ssP[:], func=AF.Ln, scale=1.0 / D, bias=eps_t[:, 0:1]), reads=[ssP, eps_t], writes=[lnP])
                q(S.op, "act", lambda h: h.activation(out=rsP[:], in_=lnP[:], func=AF.Exp, scale=-0.5), reads=[lnP], writes=[rsP])
                q(S.op, "dve", lambda h: h.scalar_tensor_tensor(out=hbP[:], in0=xtile[:], scalar=rsP[:, 0:1], in1=gpre[:], op0=ALU.mult, op1=ALU.mult),
                     reads=[xtile, rsP, gpre], writes=[hbP])

                def ftr(h):
                    for kc in range(8):
                        ins = h.transpose(out=bankb(bT)[:, kc * 128:(kc + 1) * 128], in_=hbP[:, kc * 128:(kc + 1) * 128], identity=ident_b[:])
                    return ins
                q(S.op, "pe", ftr, reads=[hbP, ident_b], writes=[bB[bT]])
                q(S.op, "act", lambda h: h.copy(out=hTP[:].rearrange("p a b -> p (a b)"), in_=bankb(bT)[:, :]), reads=[bB[bT]], writes=[hTP])
                q(mm, bank(bZ)[0:16, 0:128], [(Wa[:, kc, 2304:2320], hTP[:, kc, :]) for kc in range(8)], [Wa, hTP], [bB[bZ]])
                q(S.op, "dve", lambda h: h.tensor_copy(out=glrP[:], in_=bank(bZ)[0:16, 0:128]), reads=[bB[bZ]], writes=[glrP])

                def fk_(h):
                    for hh in range(4):
                        for kc in range(8):
                            ins = h.matmul(bank(bK)[0:96, hh * 128:(hh + 1) * 128], lhsT=Wa[:, kc, 384 + 96 * hh:384 + 96 * hh + 96],
                                           rhs=hTP[:, kc, :], start=(kc == 0), stop=(kc == 7))
                    return ins
                q(S.op, "pe", fk_, reads=[Wa, hTP], writes=[bB[bK]])

                def fz(h):
                    for hh in range(4):
                        o = bank(bZ)[0:96, hh * 128:(hh + 1) * 128]
                        h.matmul(o, lhsT=wgu[0:16, 96 * hh:96 * hh + 96], rhs=glrP[0:16, :], start=True, stop=False)
                        ins = h.matmul(o, lhsT=bg[0:1, 96 * hh:96 * hh + 96], rhs=ones_row[0:1, :], start=False, stop=True)
                    return ins
                q(S.op, "pe", fz, reads=[wgu, bg, glrP, ones_row], writes=[bB[bZ]])
                zv = bank(bZ)[0:96, :].rearrange("p (a b) -> p a b", a=4)
                q(S.op, "act", lambda h: h.activation(out=spP[:], in_=zv, func=AF.Exp, scale=-1.0), reads=[bB[bZ]], writes=[spP])
                q(S.op, "act", lambda h: h.activation(out=spP[:], in_=spP[:], func=AF.Ln, scale=1.0, bias=one_t[0:96, 0:1]), reads=[one_t], writes=[spP])
                for hh in range(4):
                    q(S.op, "dve", lambda h, hh=hh: h.tensor_tensor_scan(out=ccP[:, hh, :], data0=ones96[:, :], data1=spP[:, hh, :], initial=0.0,
                                                                      op0=ALU.mult, op1=ALU.add), reads=[spP, ones96], writes=[ccP])
                q(S.op, "act", lambda h: h.activation(out=elP[:], in_=ccP[:, :, 127], func=AF.Exp, scale=-1.0 / 16), reads=[ccP], writes=[elP])
                q(S.op, "act", lambda h: h.activation(out=ccP[:], in_=ccP[:], func=AF.Exp, scale=1.0 / 16), reads=[], writes=[ccP])
                kT = bank(bK)[0:96, :].rearrange("p (a b) -> p a b", a=4)
                for hh in range(4):
                    q(S.op, "dve", lambda h, hh=hh: h.scalar_tensor_tensor(out=kdTP[:, hh, :], in0=kT[:, hh, :], scalar=elP[:, hh:hh + 1], in1=ccP[:, hh, :],
                                                                        op0=ALU.mult, op1=ALU.mult), reads=[bB[bK], elP, ccP], writes=[kdTP])

                def ftk(h):
                    for hh in range(4):
                        ins = h.transpose(out=bankb(bT)[:, hh * 96:(hh + 1) * 96], in_=kdTP[:, hh, :], identity=ident_b[0:96, 0:96])
                    return ins
                q(S.op, "pe", ftk, reads=[kdTP, ident_b], writes=[bB[bT]])
                q(S.op, "act", lambda h: h.copy(out=kdP[:], in_=bankb(bT)[:, 0:384]), reads=[bB[bT]], writes=[kdP])
                for half in range(2):
                    q(mm, bank(bX)[:, 0:384], [(hTP[:, kc, :], Wa[:, kc, 768 + 384 * half:768 + 384 * (half + 1)]) for kc in range(8)], [Wa, hTP], [bB[bX]])
                    if half == 0:
                        q(S.op, "act", lambda h: h.copy(out=vP[:, 0:384], in_=bank(bX)[:, 0:384]), reads=[bB[bX]], writes=[vP])
                    else:
                        q(S.op, "dve", lambda h: h.tensor_copy(out=vP[:, 384:768], in_=bank(bX)[:, 0:384]), reads=[bB[bX]], writes=[vP])
                for pr_, bk_ in ((0, bX), (1, bK)):
                    def fs_(h, pr_=pr_, bk_=bk_):
                        for i_ in range(2):
                            hh = 2 * pr_ + i_
                            ins = h.matmul(bank(bk_)[0:96, i_ * DV:(i_ + 1) * DV], lhsT=kdP[:, 96 * hh:96 * hh + 96], rhs=vP[:, hh * DV:(hh + 1) * DV],
                                           start=True, stop=True)
                        return ins
                    q(S.op, "pe", fs_, reads=[kdP, vP], writes=[bB[bk_]])
                    for i_ in range(2):
                        hh = 2 * pr_ + i_
                        q(S.op, "dve", lambda h, hh=hh, i_=i_, bk_=bk_: h.scalar_tensor_tensor(out=Sst[:, hh, :], in0=Sst[:, hh, :], scalar=elP[:, hh:hh + 1],
                                                                                            in1=bank(bk_)[0:96, i_ * DV:(i_ + 1) * DV], op0=ALU.mult, op1=ALU.add),
                             reads=[elP, bB[bk_]], writes=[Sst])

            for t0 in range(0, NPRE, 4):
                lists = []
                for t in range(t0, t0 + 4):
                    if t < NPRE:
                        Q = []
                        prefix_tile(t, Q)
                        lists.append(Q)
                for i_ in range(max(len(L) for L in lists)):
                    for L in lists:
                        if i_ < len(L):
                            fn_, a_, k_ = L[i_]
                            fn_(*a_, **k_)
            S.op("pool", lambda h: h.tensor_copy(out=S_bf[:], in_=Sst[:]), reads=[Sst], writes=[S_bf])
            S.flush()
            ck(100)
            esp.close()
            pre_x = load_x0(NPRE * 128)
            for t in range(NT1):
                nxt_x = load_x0((NPRE + t + 1) * 128) if t + 1 < NT1 else None
                gla_tile((NPRE + t) * 128, True, t, xtile=pre_x)
                pre_x = nxt_x
            S.dma("sp", stp.rearrange("h d v -> d h v"), Sst[:], reads=[Sst])

            P = NS
            kq_s = mk(es, "kq_s", [96, 4, NS])
            eg = mk(es, "eg", [96, 4, NS])
            ktok = mk(es, "ktok", [NS, 384])
            vtok = mk(es, "vtok", [NS, 768])
            kmask = mk(es, "kmask", [NS, NS, 384])
            qmask = mk(es, "qmask", [96, 4, NS, NS])
            Sin = [mk(es, "Sin%d" % i, [96, 4, DV]) for i in range(2)]
            Snew = [mk(es, "Snew%d" % i, [96, 4, DV]) for i in range(2)]
            qm_tok = mk(es, "qm_tok", [NS, 256])
            rms_stats(xs_sb[:], P, [xs_sb], hb)
            norm_to(xs_sb[:], P, [xs_sb], gpre, hb)
            transpose8(hb, P, hT, tb=0)
            gate_path(P)
            proj_fm(384, 2, P)
            proj_fm(0, 3, P)
            zgate(P)
            S.op("act", lambda h: h.activation(out=eg[:], in_=spl[:, :, 0:P], func=AF.Exp, scale=-1.0 / 16), reads=[spl], writes=[eg])
            qTs = bank(3)[0:96, :].rearrange("p (a b) -> p a b", a=4)[:, :, 0:P]
            S.op("dve", lambda h: h.tensor_scalar(out=kq_s[:], in0=qTs, scalar1=DK ** -0.5, scalar2=None, op0=ALU.mult),
                 reads=[bB[3]], writes=[kq_s])
            S.op("dve", lambda h: h.tensor_tensor(out=qmask[:], in0=bc(kq_s[:], 3, NS), in1=bc(id16r[0:96, :, :], 1, 4), op=ALU.mult),
                 reads=[kq_s, id16r], writes=[qmask])
            mm(bank(2)[0:P, 0:384], [(hT[:, kc, 0:P], Wa[:, kc, 384:768]) for kc in range(8)], [Wa, hT], [bB[2]])
            S.op("act", lambda h: h.copy(out=ktok[:], in_=bank(2)[0:P, 0:384]), reads=[bB[2]], writes=[ktok])
            S.op("dve", lambda h: h.tensor_tensor(out=kmask[:], in0=bc(ktok[:], 1, NS), in1=bc(ident_f[0:NS, 0:NS], 2, 384), op=ALU.mult),
                 reads=[ktok, ident_f], writes=[kmask])
            proj_tm(768, 2, P)
            S.op("act", lambda h: h.copy(out=vtok[:].rearrange("p (a b) -> p a b", a=2), in_=pair768(2, P)), reads=[bB[4], bB[5]], writes=[vtok])
            mm(bank(1)[0:P, 0:256], [(hT[:, kc, 0:P], Wa[:, kc, 2320:2576]) for kc in range(8)], [Wa, hT], [bB[1]])
            S.op("act", lambda h: h.copy(out=qm_tok[:], in_=bank(1)[0:P, 0:256]), reads=[bB[1]], writes=[qm_tok])
            proj_tm(1536, 3, P)
            ck(20)
            zero_acc([bank(4)[0:NS, 0:384], bank(5)[0:NS, 0:384]], [bB[4], bB[5]])
            for i in range(NS):
                si, sn = Sin[i % 2], Snew[i % 2]
                S.dma("sp", si[:], st_in[i].rearrange("h d v -> d h v"), writes=[si])

                def fk(h, i=i):
                    for hh in range(4):
                        ins = h.matmul(bank(2 + hh // 2)[0:96, (hh % 2) * DV:(hh % 2 + 1) * DV], lhsT=kmask[0:NS, i, 96 * hh:96 * hh + 96],
                                       rhs=vtok[0:NS, hh * DV:(hh + 1) * DV], start=True, stop=True)
                    return ins
                S.op("pe", fk, reads=[kmask, vtok], writes=[bB[2], bB[3]])
                for hh in range(4):
                    S.op("dve", lambda h, hh=hh, i=i, si=si, sn=sn: h.scalar_tensor_tensor(
                        out=sn[:, hh, :], in0=si[:, hh, :], scalar=eg[:, hh, i:i + 1],
                        in1=bank(2 + hh // 2)[0:96, (hh % 2) * DV:(hh % 2 + 1) * DV], op0=ALU.mult, op1=ALU.add),
                        reads=[si, eg, bB[2], bB[3]], writes=[sn])
                S.dma("sp", sts[i].rearrange("h d v -> d h v"), sn[:], reads=[sn])

                def fq(h, i=i, sn=sn):
                    for hh in range(4):
                        ins = h.matmul(bank(4 + hh // 2)[0:NS, (hh % 2) * DV:(hh % 2 + 1) * DV], lhsT=qmask[:, hh, i, :], rhs=sn[:, hh, :],
                                       start=False, stop=(i == NS - 1 and hh % 2 == 1))
                    return ins
                S.op("pe", fq, reads=[qmask, sn], writes=[bB[4], bB[5]])
            ck(21)
            gla_norm_gate(P, 2, 3)
            ck(22)
            sab = alloc_sample_attn(es, "p2", onesel=TB_sub(kmask, 128))

            def mem_sample(l, qtok, sab):
                Kc, Vc, prod, sc8, pp8, Pm, onesel = sab
                zero_acc([bank(1)[0:NS, 0:260]], [bB[1]])
                for i in range(NS):
                    kc_, vc_ = Kc[i % 2], Vc[i % 2]
                    S.dma("sp", kc_[:], mk_in[l, i].rearrange("(mc p) n -> p mc n", p=128), writes=[kc_])
                    for mc in range(2):
                        S.dma("sp", vc_[:, mc, :, 0:64], mv_in[l, i, mc * 128:(mc + 1) * 128, :].rearrange("p (a b) -> p a b", a=4), writes=[vc_])
                    mm(bank(0)[:, 0:256], [(onesel[0:NS, i, :], qtok[0:NS, :])], [onesel, qtok], [bB[0]])
                    S.op("dve", lambda h, kc_=kc_: h.tensor_tensor(out=prod[:, 0:512].rearrange("p (a b) -> p a b", a=2), in0=kc_[:],
                                                                   in1=bc(bank(0)[:, 0:256], 1, 2), op=ALU.mult),
                         reads=[kc_, bB[0]], writes=[prod])
                    S.op("dve", lambda h: h.tensor_reduce(out=sc8[:, 0:8], in_=prod[:, 0:512].rearrange("p (a b) -> p a b", b=64), axis=AX.X, op=ALU.add),
                         reads=[prod], writes=[sc8])
                    S.op("act", lambda h: h.activation(out=pp8[:, 0:8], in_=sc8[:, 0:8], func=AF.Exp, scale=0.125), reads=[sc8], writes=[pp8])
                    S.op("dve", lambda h, i=i: h.tensor_tensor(out=Pm[:, 0:8, :], in0=bc(pp8[:, 0:8], 2, NS), in1=bc(id16r[:, i, :], 1, 8), op=ALU.mult),
                         reads=[pp8, id16r], writes=[Pm])

                    def fm(h, i=i, vc_=vc_):
                        for hh in range(4):
                            for mc in range(2):
                                ins = h.matmul(bank(1)[0:NS, hh * 65:(hh + 1) * 65], lhsT=Pm[:, mc * 4 + hh, :], rhs=vc_[:, mc, hh, :],
                                               start=False, stop=(i == NS - 1 and mc == 1 and hh == 3))
                        return ins
                    S.op("pe", fm, reads=[Pm, vc_], writes=[bB[1]])
                finish_mem(NS)

            mem_sample(0, qm_tok, sab)
            ck(23)
            mem_sample_fn.append(mem_sample)
            out_proj(P, Wo, gpost, xs_sb[:], xs_sb)
            S.flush()
        if 0 <= stop <= 1:
            S.dma("sp", ys, xs_sb[:], reads=[xs_sb])
            S.flush()
            return nc

        def mlp_phase(l, tiles, es):
            Wu = mk(es, "Wu%d" % l, [128, 8, 4096], BF16)
            Wd = mk(es, "Wd%d" % l, [128, 32, D], BF16)
            gpre = mk(es, "gfpre%d" % l, [128, D])
            gpost = mk(es, "gfpost%d" % l, [128, D])
            hTg = mk(es, "hTg%d" % l, [128, 8, 256], BF16)
            actT = mk(es, "actT%d" % l, [128, 32, 256], BF16)
            load_w(Wu, w_up[l], 8, 4096)
            load_w(Wd, w_down[l], 32, D)
            load_gain(gpre, g_ffn_pre[l, :])
            load_gain(gpost, g_ffn_post[l, :])
            groups = []
            i = 0
            while i < len(tiles):
                if tiles[i][0] == 128 and i + 1 < len(tiles) and tiles[i + 1][0] == 128:
                    groups.append(tiles[i:i + 2])
                    i += 2
                else:
                    groups.append(tiles[i:i + 1])
                    i += 1
            xtl = {}

            def n_elem(g, only=None):
                for ti, (P, src, sbuf, dst, dbuf) in enumerate(groups[g]):
                    if only is not None and ti != only:
                        continue
                    if src is None:
                        xtile, xap = xs_sb, xs_sb[:]
                    else:
                        xtile = next_xt()
                        xap = xtile[0:P, :]
                        S.dma("sp", xap, src, reads=[sbuf], writes=[xtile])
                    xtl[(g, ti)] = (xtile, xap)

            def n_pe(g):
                off = 0
                for ti, (P, src, sbuf, dst, dbuf) in enumerate(groups[g]):
                    xtile, xap = xtl[(g, ti)]
                    rms_stats(xap, P, [xtile], hb)
                    norm_to(xap, P, [xtile], gpre, hb)
                    transpose8(hb, P, hT, tb=ti % 2)
                    S.op("act", lambda h, off=off, P=P: h.copy(out=hTg[:, :, off:off + P], in_=hT[:, :, 0:P]), reads=[hT], writes=[hTg])
                    off += P

            def up(g):
                N = sum(t[0] for t in groups[g])
                for ffc in range(32):
                    bk = (2, 3, 0, 1)[ffc % 4]
                    mm(bank(bk)[:, 0:N], [(Wu[:, kc, ffc * 128:(ffc + 1) * 128], hTg[:, kc, 0:N]) for kc in range(8)], [Wu, hTg], [bB[bk]])
                    S.op("act", lambda h, ffc=ffc, bk=bk, N=N: h.activation(out=actT[:, ffc, 0:N], in_=bank(bk)[:, 0:N], func=AF.Square),
                         reads=[bB[bk]], writes=[actT])
                    S.op("dve", lambda h, ffc=ffc, bk=bk, N=N: h.scalar_tensor_tensor(out=actT[:, ffc, 0:N], in0=bank(bk)[:, 0:N], scalar=0.0,
                                                                                      in1=actT[:, ffc, 0:N], op0=ALU.is_gt, op1=ALU.mult),
                         reads=[bB[bk]], writes=[actT])

            def down(g):
                off = 0
                for ti, (P, src, sbuf, dst, dbuf) in enumerate(groups[g]):
                    xtile, xap = xtl[(g, ti)]
                    pair = 2 + ti % 2
                    for half in range(2):
                        mm(bank(2 * pair + half)[0:P, :], [(actT[:, ffc, off:off + P], Wd[:, ffc, half * 512:(half + 1) * 512]) for ffc in range(32)],
                           [actT, Wd], [bB[2 * pair + half]])
                    post_norm_residual(pair, P, gpost, xap, xtile)
                    if dst is not None:
                        S.dma("sp", dst, xap, reads=[xtile], writes=[dbuf])
                    off += P

            n_elem(0)
            n_pe(0)
            for g in range(len(groups)):
                up(g)
                if g + 1 < len(groups):
                    n_elem(g + 1, only=0)
                down(g)
                if g + 1 < len(groups):
                    if len(groups[g + 1]) > 1:
                        n_elem(g + 1, only=1)
                    n_pe(g + 1)

        with ExitStack() as es:
            tiles = [(128, xa[t * 128:(t + 1) * 128, :], b_xa[t], xb[t * 128:(t + 1) * 128, :], b_xb[t]) for t in range(NT1)]
            tiles.append((NS, None, None, None, None))
            mlp_phase(0, tiles, es)
            S.flush()
        if 0 <= stop <= 2:
            S.dma("sp", ys, xs_sb[:], reads=[xs_sb])
            S.flush()
            return nc

        with ExitStack() as es:
            es2 = ExitStack()
            Wkv = mk(es, "Wkv", [128, 8, 512], BF16)
            Wb = mk(es, "Wb", [128, 8, 1024], BF16)
            Wo = mk(es, "Wo1", [128, 8, D], BF16)
            gkv = mk(es, "gkv", [128, D])
            gpre = mk(es, "gpre1", [128, D])
            gpost = mk(es, "gpost1", [128, D])
            EBs = mk(es, "EBs", [128, 12])
            eb0 = mk(es, "eb0", [NS, 12])
            esink = mk(es, "esink", [128, 12])
            rb_sb = mk(es, "rb_sb", [32, 12])
            rbh = mk(es, "rbh", [32, 12, 128])
            ohu = mk(es, "ohu", [32, 384])
            validu = mk(es, "validu", [128, 384])
            ebt = mk(es, "ebt", [128, 384])
            kvf = mk(es, "kvf1", [128, 512])
            KT_all = mk(es2, "KT_all", [128, 2, 2, NT1 * 128], BF16)
            V_all = mk(es2, "V_all", [128, NT1, 4, 66], BF16)
            QT_all = mk(es2, "QT_all", [128, 6, NT * 128], BF16)
            QmT_all = mk(es2, "QmT_all", [128, 2, NT * 128], BF16)
            EB = mk(es2, "EB", [128, 12, 2, 128])
            es_sb = mk(es2, "es_sb", [128, 2, 384])
            pTg = [mk(es2, "pTg%d" % g, [128, 2, 3, 128], BF16) for g in range(4)]

            load_w(Wkv, w_kv, 8, 512)
            wbv = w_in_b.rearrange("(kc p) n -> p kc n", p=128)
            for i in range(6):
                s_, j_ = divmod(i, 3)
                for half in range(2):
                    hd = 6 * s_ + 3 * half + j_
                    S.dma("pool", Wb[:, :, i * 128 + half * 64:i * 128 + half * 64 + 64], wbv[:, :, hd * 64:(hd + 1) * 64], writes=[Wb])
            S.dma("pool", Wb[:, :, 768:1024], wbv[:, :, 768:1024], writes=[Wb])
            load_w(Wo, w_out[1], 8, D)
            load_gain(gkv, g_kv[0, :])
            load_gain(gpre, g_mix_pre[1, :])
            load_gain(gpost, g_mix_post[1, :])
            S.dma("sp", rb_sb[:], rel_bias, writes=[rb_sb])
            S.dma("sp", ohu[:], c_ohu, writes=[ohu])
            S.dma("sp", validu[:], c_valid, writes=[validu])
            S.dma("sp", esink[:], sinks[0, :].partition_broadcast(128), writes=[esink])
            S.op("act", lambda h: h.activation(out=esink[:], in_=esink[:], func=AF.Exp), reads=[], writes=[esink])
            S.op("dve", lambda h: h.tensor_copy(out=rbh[:], in_=bc(rb_sb[:], 2, 128)), reads=[rb_sb], writes=[rbh])
            for hd in range(12):
                mm(bank(2)[:, 0:384], [(rbh[:, hd, :], ohu[:])], [rbh, ohu], [bB[2]])
                S.op("act", lambda h: h.activation(out=ebt[:], in_=bank(2)[:, 0:384], func=AF.Exp), reads=[bB[2]], writes=[ebt])
                S.op("dve", lambda h: h.tensor_tensor(out=ebt[:], in0=ebt[:], in1=validu[:], op=ALU.mult), reads=[validu], writes=[ebt])
                S.dma("sp", ebd.ap()[hd], ebt[:], reads=[ebt], writes=[b_ebd])
            for hd in range(12):
                for kb in range(2):
                    S.dma("sp", EB[:, hd, kb, :], bass.AP(ebd, hd * 128 * 384 + 255 - 128 * kb, [[383, 128], [1, 128]]), reads=[b_ebd], writes=[EB])
                S.dma("sp", EBs[:, hd:hd + 1], bass.AP(ebd, hd * 128 * 384 + 255, [[383, 128], [1, 1]]), reads=[b_ebd], writes=[EBs], allow_slow_non_contiguous=True)
                S.dma("sp", eb0[:, hd:hd + 1], bass.AP(ebd, hd * 128 * 384 + 127, [[384, NS], [1, 1]]), reads=[b_ebd], writes=[eb0], allow_slow_non_contiguous=True)
            bKT = [Buf() for _ in range(NT1)]
            bV = [Buf() for _ in range(NT1)]
            bQ = [[Buf() for _ in range(6)] for _ in range(NT)]
            bQm = [[Buf() for _ in range(2)] for _ in range(NT)]
            B1S = [dict(hb=hb, hb2=hb2, hT=hT, hT2=hT2, ss=st["ss"], ln=st["ln"], rs=st["rstd"])]
            B1S.append(dict(hb=mk(es2, "b1_hb", [128, D], BF16), hb2=mk(es2, "b1_hb2", [128, D], BF16),
                            hT=mk(es2, "b1_hT", [128, 8, 128], BF16), hT2=mk(es2, "b1_hT2", [128, 8, 128], BF16),
                            ss=mk(es2, "b1_ss", [128, 1]), ln=mk(es2, "b1_ln", [128, 1]), rs=mk(es2, "b1_rs", [128, 1])))

            S.op("pool", lambda h: h.memset(V_all[:], 1.0), writes=bV)
            S.op("pool", lambda h: h.memset(KT_all[:], 0.0), writes=bKT)

            def b1_tile(t, Q, par):
                def q(fn, *a, **k):
                    Q.append((fn, a, k))
                W = B1S[par]
                hbP, hb2P, hTP, hT2P, ssP, lnP, rsP = W["hb"], W["hb2"], W["hT"], W["hT2"], W["ss"], W["ln"], W["rs"]
                bT, bA, bBk, bC = 4 * par, 4 * par + 1, 4 * par + 2, 4 * par + 3
                xtile = next_xt()
                q(S.dma, "sp", xtile[:], xb[t * 128:(t + 1) * 128, :], reads=[b_xb[t]], writes=[xtile])
                q(S.op, "act", lambda h: h.activation(out=hbP[:], in_=xtile[:], func=AF.Square, accum_out=ssP[:, 0:1]), reads=[xtile], writes=[hbP, ssP])
                q(S.op, "act", lambda h: h.activation(out=lnP[:, 0:1], in_=ssP[:, 0:1], func=AF.Ln, scale=1.0 / D, bias=eps_t[:, 0:1]), reads=[ssP, eps_t], writes=[lnP])
                q(S.op, "act", lambda h: h.activation(out=rsP[:, 0:1], in_=lnP[:, 0:1], func=AF.Exp, scale=-0.5), reads=[lnP], writes=[rsP])
                q(S.op, "dve", lambda h: h.scalar_tensor_tensor(out=hbP[:], in0=xtile[:], scalar=rsP[:, 0:1], in1=gkv[:], op0=ALU.mult, op1=ALU.mult),
                  reads=[xtile, rsP, gkv], writes=[hbP])

                def ftr(h, src=hbP):
                    for kc in range(8):
                        ins = h.transpose(out=bankb(bT)[:, kc * 128:(kc + 1) * 128], in_=src[:, kc * 128:(kc + 1) * 128], identity=ident_b[:])
                    return ins
                q(S.op, "pe", ftr, reads=[hbP, ident_b], writes=[bB[bT]])
                q(S.op, "act", lambda h: h.copy(out=hTP[:].rearrange("p a b -> p (a b)"), in_=bankb(bT)[:, :]), reads=[bB[bT]], writes=[hTP])
                for s_ in range(2):
                    q(mm, bank(bA)[:, s_ * 128:(s_ + 1) * 128], [(Wkv[:, kc, s_ * 128:(s_ + 1) * 128], hTP[:, kc, :]) for kc in range(8)], [Wkv, hTP], [bB[bA]])
                q(S.op, "act", lambda h: h.copy(out=KT_all[0:64, 0, :, t * 128:(t + 1) * 128], in_=bank(bA)[0:64, 0:256].rearrange("p (a b) -> p a b", a=2)),
                  reads=[bB[bA]], writes=[bKT[t]])
                q(S.op, "act", lambda h: h.copy(out=KT_all[64:128, 1, :, t * 128:(t + 1) * 128], in_=bank(bA)[64:128, 0:256].rearrange("p (a b) -> p a b", a=2)),
                  reads=[bB[bA]], writes=[bKT[t]])
                q(mm, bank(bBk)[:, 0:512], [(hTP[:, kc, :], Wkv[:, kc, :]) for kc in range(8)], [Wkv, hTP], [bB[bBk]])
                q(S.op, "dve", lambda h: h.tensor_copy(out=V_all[:, t, :, 0:64], in_=bank(bBk)[:, 256:512].rearrange("p (a b) -> p a b", a=4)),
                  reads=[bB[bBk]], writes=[bV[t]])
                if t == NT:
                    q(S.op, "act", lambda h: h.copy(out=kvf[:], in_=bank(bBk)[:, 0:512]), reads=[bB[bBk]], writes=[kvf])
                    q(S.dma, "sp", swkp, kvf[:, 0:256], reads=[kvf])
                    q(S.dma, "sp", swvp, kvf[:, 256:512], reads=[kvf])
                if t == 0:
                    q(S.op, "dve", lambda h: h.tensor_scalar(out=V_all[:, 0, :, :], in0=V_all[:, 0, :, :], scalar1=flag[:, 0:1], scalar2=None, op0=ALU.mult),
                      reads=[flag], writes=[bV[0]])
                    return
                q(S.op, "dve", lambda h: h.scalar_tensor_tensor(out=hb2P[:], in0=xtile[:], scalar=rsP[:, 0:1], in1=gpre[:], op0=ALU.mult, op1=ALU.mult),
                  reads=[xtile, rsP, gpre], writes=[hb2P])
                q(S.op, "pe", lambda h: ftr(h, src=hb2P), reads=[hb2P, ident_b], writes=[bB[bT]])
                q(S.op, "act", lambda h: h.copy(out=hT2P[:].rearrange("p a b -> p (a b)"), in_=bankb(bT)[:, :]), reads=[bB[bT]], writes=[hT2P])
                for i in range(8):
                    bk = (bA, bBk, bC)[i % 3]
                    q(mm, bank(bk)[:, 0:128], [(Wb[:, kc, i * 128:(i + 1) * 128], hT2P[:, kc, :]) for kc in range(8)], [Wb, hT2P], [bB[bk]])
                    dst = QT_all[:, i, (t - 1) * 128:t * 128] if i < 6 else QmT_all[:, i - 6, (t - 1) * 128:t * 128]
                    tok = bQ[t - 1][i] if i < 6 else bQm[t - 1][i - 6]
                    q(S.op, "act" if i % 2 else "dve",
                      (lambda h, dst=dst, bk=bk: h.copy(out=dst, in_=bank(bk)[:, 0:128])) if i % 2 else
                      (lambda h, dst=dst, bk=bk: h.tensor_copy(out=dst, in_=bank(bk)[:, 0:128])),
                      reads=[bB[bk]], writes=[tok])

            for t0 in range(0, NT1, 2):
                lists = []
                for par, t in enumerate(range(t0, min(t0 + 2, NT1))):
                    Q = []
                    b1_tile(t, Q, par)
                    lists.append(Q)
                for i_ in range(max(len(L) for L in lists)):
                    for L in lists:
                        if i_ < len(L):
                            fn_, a_, k_ = L[i_]
                            fn_(*a_, **k_)

            def swa_finish(P, extra=None):
                o3 = [bank(6 + i)[0:P, 0:390].rearrange("p (a b) -> p a b", a=6) for i in range(2)]
                for i in range(2):
                    S.op("dve", lambda h, i=i: h.tensor_tensor(out=den12[0:P, 6 * i:6 * i + 6], in0=o3[i][:, :, 64], in1=esink[0:P, 6 * i:6 * i + 6], op=ALU.add),
                         reads=[bB[6 + i], esink], writes=[den12])
                if extra is not None:
                    S.op("dve", lambda h: h.tensor_tensor(out=den12[0:P, :], in0=den12[0:P, :], in1=extra[0:P, :], op=ALU.add),
                         reads=[extra], writes=[den12])
                S.op("dve", lambda h: h.reciprocal(out=rden12[0:P, :], in_=den12[0:P, :]), reads=[den12], writes=[rden12])

            def swa_norm(P, srcs, src_reads):
                for i in range(2):
                    S.op("dve", lambda h, i=i: h.tensor_tensor(out=mixcat[0:P, 384 * i:384 * (i + 1)].rearrange("p (a b) -> p a b", a=6), in0=srcs[i],
                                                               in1=bc(rden12[0:P, 6 * i:6 * i + 6], 2, 64), op=ALU.mult),
                         reads=list(src_reads) + [rden12], writes=[mixcat])

            def b2(t):
                P = 128
                xtile = next_xt()
                S.dma("sp", xtile[:], xb[(t + 1) * 128:(t + 2) * 128, :], reads=[b_xb[t + 1]], writes=[xtile])
                ebt_ = EB
                for g in range(4):
                    s_, p_ = divmod(g, 2)
                    pr = 1 + g % 2
                    def fsc(h, g=g, s_=s_, p_=p_, pr=pr):
                        for kb in range(2):
                            ins = h.matmul(bank(2 * pr + kb)[:, 0:384], lhsT=KT_all[:, p_, s_, (t + kb) * 128:(t + kb + 1) * 128],
                                           rhs=QT_all[:, 3 * s_:3 * s_ + 3, t * 128:(t + 1) * 128], start=True, stop=True)
                        return ins
                    S.op("pe", fsc, reads=[bKT[t], bKT[t + 1]] + bQ[t], writes=[bB[2 * pr], bB[2 * pr + 1]])
                    S.op("act", lambda h, pr=pr: h.activation(out=es_sb[:], in_=pair768(pr, 128), func=AF.Exp, scale=0.125),
                         reads=[bB[2 * pr], bB[2 * pr + 1]], writes=[es_sb])
                    S.op("dve", lambda h, g=g, ebt_=ebt_: h.tensor_tensor(out=pTg[g][:], in0=es_sb[:].rearrange("p a (j q) -> p a j q", j=3),
                                                                          in1=ebt_[:, 3 * g:3 * g + 3, :, :].rearrange("p j a q -> p a j q"), op=ALU.mult),
                         reads=[es_sb, ebt_], writes=[pTg[g]])
                    def fo(h, g=g):
                        for j in range(3):
                            hd = 3 * g + j
                            for kb in range(2):
                                ins = h.matmul(bank(6 + hd // 6)[:, (hd % 6) * 65:(hd % 6 + 1) * 65], lhsT=pTg[g][:, kb, j, :],
                                               rhs=V_all[:, t + kb, g, 0:65], start=(kb == 0), stop=(kb == 1))
                        return ins
                    S.op("pe", fo, reads=[pTg[g], bV[t], bV[t + 1]], writes=[bB[6], bB[7]])
                swa_finish(P)
                o3 = [bank(6 + i)[0:P, 0:390].rearrange("p (a b) -> p a b", a=6)[:, :, 0:64] for i in range(2)]
                swa_norm(P, o3, [bB[6], bB[7]])
                mem_attend_prompt(1, lambda c, p: QmT_all[:, c, t * 128:(t + 1) * 128], bQm[t], P)
                out_proj(P, Wo, gpost, xtile[:], xtile)
                S.dma("sp", xa[t * 128:(t + 1) * 128, :], xtile[:], reads=[xtile], writes=[b_xa[t]])

            for t in range(NT):
                b2(t)

            S.flush()
            es2.close()
            P = NS
            sab = alloc_sample_attn(es, "p4")
            Kc, Vc, prod, sc8, pp8, Pm, onesel = sab
            knew = mk(es, "knew", [NS, 512])
            qtok = mk(es, "qtok", [NS, 1024])
            Ks = [mk(es, "Ks%d" % i, [128, 4, 64]) for i in range(2)]
            Vs = [mk(es, "Vs%d" % i, [128, 4, 65]) for i in range(2)]
            sn12 = mk(es, "sn12", [NS, 12])
            pn12 = mk(es, "pn12", [NS, 12])
            tmpv = mk(es, "tmpv", [NS, 768])
            osum = mk(es, "osum", [NS, 768])
            S.op("pool", lambda h: h.memset(Vs[0][:], 1.0), writes=[Vs[0]])
            S.op("pool", lambda h: h.memset(Vs[1][:], 1.0), writes=[Vs[1]])
            rms_stats(xs_sb[:], P, [xs_sb], hb)
            norm_to(xs_sb[:], P, [xs_sb], gkv, hb)
            transpose8(hb, P, hT, tb=0)
            mm(bank(3)[0:P, 0:512], [(hT[:, kc, 0:P], Wkv[:, kc, :]) for kc in range(8)], [Wkv, hT], [bB[3]])
            S.op("act", lambda h: h.copy(out=knew[:], in_=bank(3)[0:P, 0:512]), reads=[bB[3]], writes=[knew])
            norm_to(xs_sb[:], P, [xs_sb], gpre, hb2)
            transpose8(hb2, P, hT2, tb=1)
            for half in range(2):
                mm(bank(4 + half)[0:P, :], [(hT2[:, kc, 0:P], Wb[:, kc, half * 512:(half + 1) * 512]) for kc in range(8)], [Wb, hT2], [bB[4 + half]])
            S.op("act", lambda h: h.copy(out=qtok[:], in_=PS[2][0:P, :]), reads=[bB[4], bB[5]], writes=[qtok])
            S.dma("sp", swks[:, 0:127, :], swk_in[:, 1:128, :])
            S.dma("sp", swvs[:, 0:127, :], swv_in[:, 1:128, :])
            S.dma("sp", swks[:, 127, :], knew[:, 0:256], reads=[knew])
            S.dma("sp", swvs[:, 127, :], knew[:, 256:512], reads=[knew])

            def qview(ap2d, s_):
                return ap2d[:, s_ * 384:(s_ + 1) * 384].rearrange("p (j a d) -> p a j d", j=3, a=2)

            zero_acc([bank(6)[0:NS, 0:390], bank(7)[0:NS, 0:390]], [bB[6], bB[7]])
            for i in range(NS):
                ks_, vs_ = Ks[i % 2], Vs[i % 2]
                S.dma("sp", ks_[:], swk_in[i].rearrange("w (g d) -> w g d", g=4), writes=[ks_])
                S.dma("sp", vs_[:, :, 0:64], swv_in[i].rearrange("w (g d) -> w g d", g=4), writes=[vs_])
                for half in range(2):
                    mm(bank(2 + half)[:, 0:384], [(onesel[0:NS, i, :], qtok[0:NS, half * 384:(half + 1) * 384])], [onesel, qtok], [bB[2 + half]])
                for s_ in range(2):
                    S.op("dve", lambda h, s_=s_, ks_=ks_: h.tensor_tensor(
                        out=prod[:, 384 * s_:384 * (s_ + 1)].rearrange("p (a j d) -> p a j d", a=2, j=3),
                        in0=bc(ks_[:, 2 * s_:2 * s_ + 2, :], 2, 3), in1=qview(bank(2 + s_)[:, 0:384], 0), op=ALU.mult),
                        reads=[ks_, bB[2 + s_]], writes=[prod])
                S.op("dve", lambda h: h.tensor_reduce(out=sc8[:, :], in_=prod[:, :].rearrange("p (a b) -> p a b", b=64), axis=AX.X, op=ALU.add),
                     reads=[prod], writes=[sc8])
                S.op("act", lambda h: h.activation(out=pp8[:, :], in_=sc8[:, :], func=AF.Exp, scale=0.125), reads=[sc8], writes=[pp8])
                S.op("dve", lambda h: h.tensor_tensor(out=pp8[:, :], in0=pp8[:, :], in1=EBs[:, :], op=ALU.mult), reads=[EBs], writes=[pp8])
                S.op("dve", lambda h, i=i: h.tensor_tensor(out=Pm[:, :, :], in0=bc(pp8[:, :], 2, NS), in1=bc(id16r[:, i, :], 1, 12), op=ALU.mult),
                     reads=[pp8, id16r], writes=[Pm])

                def fo2(h, i=i, vs_=vs_):
                    for hd in range(12):
                        ins = h.matmul(bank(6 + hd // 6)[0:NS, (hd % 6) * 65:(hd % 6 + 1) * 65], lhsT=Pm[:, hd, :], rhs=vs_[:, hd // 3, :],
                                       start=False, stop=(i == NS - 1 and hd % 6 == 5))
                    return ins
                S.op("pe", fo2, reads=[Pm, vs_], writes=[bB[6], bB[7]])
            for s_ in range(2):
                S.op("dve", lambda h, s_=s_: h.tensor_tensor(
                    out=prod[0:NS, 384 * s_:384 * (s_ + 1)].rearrange("p (a j d) -> p a j d", a=2, j=3),
                    in0=bc(knew[:, 0:256].rearrange("p (g d) -> p g d", g=4)[:, 2 * s_:2 * s_ + 2, :], 2, 3), in1=qview(qtok[:, 0:768], s_), op=ALU.mult),
                    reads=[knew, qtok], writes=[prod])
            S.op("dve", lambda h: h.tensor_reduce(out=sn12[:], in_=prod[0:NS, :].rearrange("p (a b) -> p a b", b=64), axis=AX.X, op=ALU.add),
                 reads=[prod], writes=[sn12])
            S.op("act", lambda h: h.activation(out=pn12[:], in_=sn12[:], func=AF.Exp, scale=0.125), reads=[sn12], writes=[pn12])
            S.op("dve", lambda h: h.tensor_tensor(out=pn12[:], in0=pn12[:], in1=eb0[:], op=ALU.mult), reads=[eb0], writes=[pn12])
            S.op("dve", lambda h: h.tensor_tensor(out=tmpv[:].rearrange("p (g j d) -> p g j d", g=4, j=3),
                                                  in0=bc(knew[:, 256:512].rearrange("p (g d) -> p g d", g=4), 2, 3),
                                                  in1=bc(pn12[:].rearrange("p (g j) -> p g j", g=4), 3, 64), op=ALU.mult),
                 reads=[knew, pn12], writes=[tmpv])
            swa_finish(P, extra=pn12)
            for i in range(2):
                S.op("dve", lambda h, i=i: h.tensor_tensor(out=osum[:, 384 * i:384 * (i + 1)].rearrange("p (a b) -> p a b", a=6),
                                                           in0=bank(6 + i)[0:P, 0:390].rearrange("p (a b) -> p a b", a=6)[:, :, 0:64],
                                                           in1=tmpv[:, 384 * i:384 * (i + 1)].rearrange("p (a b) -> p a b", a=6), op=ALU.add),
                     reads=[bB[6 + i], tmpv], writes=[osum])
            swa_norm(P, [osum[:, 384 * i:384 * (i + 1)].rearrange("p (a b) -> p a b", a=6) for i in range(2)], [osum])
            mem_sample_fn[0](1, TB_view(qtok, 768), sab)
            out_proj(P, Wo, gpost, xs_sb[:], xs_sb)
            S.flush()
        if 0 <= stop <= 3:
            S.dma("sp", ys, xs_sb[:], reads=[xs_sb])
            S.flush()
            return nc

        with ExitStack() as es:
            tiles = [(128, xa[t * 128:(t + 1) * 128, :], b_xa[t], y[t * 128:(t + 1) * 128, :], b_y) for t in range(NT)]
            tiles.append((NS, None, None, None, None))
            mlp_phase(1, tiles, es)
            S.dma("sp", ys, xs_sb[:], reads=[xs_sb])
            S.flush()
    return nc


class TB_sub:
    def __init__(self, tb, n):
        self.tb = tb
        self.b = tb.b
        self.n = n

    def __getitem__(self, k):
        if k == slice(None, None, None):
            return self.tb.t[:, :, 0:self.n]
        a, i, c = k
        assert c == slice(None, None, None)
        return self.tb.t[a, i, 0:self.n]


class TB_view:
    def __init__(self, tb, c0):
        self.tb = tb
        self.b = tb.b
        self.c0 = c0

    def __getitem__(self, k):
        rows, cols = k
        assert cols == slice(None, None, None)
        return self.tb.t[rows, self.c0:self.c0 + 256]


def _consts():
    c = {}
    c["c_ident"] = np.eye(128, dtype=np.float32)
    s = np.arange(128)[:, None]
    t = np.arange(128)[None, :]
    c["c_cmask"] = (s <= t).astype(np.float32)
    u = np.arange(384)
    dist = u - 127
    valid = (dist >= 0) & (dist < 128)
    n = np.maximum(dist, 0)
    nf = np.maximum(n, 1).astype(np.float32)
    large = 16 + (np.log(nf / np.float32(16)) / np.float32(math.log(128 / 16)) * np.float32(16)).astype(np.int32)
    large = np.minimum(large, 31)
    bucket = np.where(n < 16, n, large)
    oh = np.zeros((32, 384), np.float32)
    oh[bucket, u] = 1.0
    oh[:, ~valid] = 0.0
    c["c_ohu"] = oh
    c["c_valid"] = np.repeat(valid.astype(np.float32)[None, :], 128, axis=0)
    c["c_id16r"] = np.repeat(np.eye(16, dtype=np.float32).reshape(1, 256), 128, axis=0)
    return c


_NC_CACHE = {}


def kernel(**inp):
    NT, NPRE, NS = 16, 47, 16
    f = lambda a: np.ascontiguousarray(np.asarray(a, dtype=np.float32))
    xp = f(inp["x_prompt"])
    B, L, _ = xp.shape
    NT = L // (4 * 128)
    NPRE = 3 * NT - 1
    key = (NT, NPRE, NS)
    if key not in _NC_CACHE:
        import os
        _NC_CACHE[key] = build(NT, NPRE, NS, stop=int(os.environ.get('KSTOP', '99')))
    nc = _NC_CACHE[key]
    consts = _consts()
    shared = {}
    for k in ("norm_mix_pre", "norm_mix_post", "norm_ffn_pre", "norm_ffn_post", "norm_mem", "w_mem_kv", "w_gate_up", "b_gate",
              "gla_norm", "sinks", "rel_bias", "w_out", "w_ffn_up", "w_ffn_down"):
        shared[k] = f(inp[k])
    shared["w_in_a"] = f(inp["w_in_a"])[0]
    shared["w_in_b"] = f(inp["w_in_b"])[0]
    shared["w_gate_up"] = f(inp["w_gate_up"])[0]
    shared["norm_kv"] = f(inp["norm_kv"]).reshape(1, D)
    shared["w_kv"] = f(inp["w_kv"])
    shared.update(consts)
    xs = f(inp["x_sample"]).reshape(-1, D)
    st = f(inp["state_gla"])[0]
    swk = f(inp["cache_swa_k"]).reshape(-1, 128, 256)
    swv = f(inp["cache_swa_v"]).reshape(-1, 128, 256)
    mkc = f(inp["cache_mem_k"]).reshape(2, -1, 256, 256)
    mvc = f(inp["cache_mem_v"]).reshape(2, -1, 256, 256)
    mem = f(inp["mem_prompt"])
    pad = np.zeros(((NPRE + 1) * 128, D), np.float32)
    in_maps = []
    for c in range(NCORES):
        b, j = divmod(c, 4)
        xpad = np.concatenate([pad, xp[b]], axis=0)
        m = dict(shared)
        m["xin"] = np.ascontiguousarray(xpad[j * NT * 128:(j * NT + NPRE + 1 + NT) * 128])
        sl = slice(c * NS, (c + 1) * NS)
        m["xs_in"] = xs[sl]
        m["st_in"] = st[sl]
        m["swk_in"] = swk[sl]
        m["swv_in"] = swv[sl]
        m["mk_in"] = np.ascontiguousarray(mkc[:, sl])
        m["mv_in"] = np.ascontiguousarray(mvc[:, sl])
        m["mem_in"] = mem[b]
        m["c_flag"] = np.full((128, 1), 1.0 if j > 0 else 0.0, np.float32)
        in_maps.append(m)
    res = run_bass_kernel_spmd(nc, in_maps, core_ids=list(range(NCORES)))
    R = res.results
    y_prompt = np.stack([np.concatenate([R[4 * b + j]["y"] for j in range(4)], axis=0) for b in range(B)])
    y_sample = np.concatenate([R[c]["ys"] for c in range(NCORES)], axis=0).reshape(-1, 1, D)
    stp = np.stack([R[4 * b + 3]["stp"] for b in range(B)])[None]
    sts = np.concatenate([R[c]["sts"] for c in range(NCORES)], axis=0)[None]
    swkp = np.stack([R[4 * b + 3]["swkp"] for b in range(B)]).reshape(B, 128, 4, 64)
    swvp = np.stack([R[4 * b + 3]["swvp"] for b in range(B)]).reshape(B, 128, 4, 64)
    swks = np.concatenate([R[c]["swks"] for c in range(NCORES)], axis=0).reshape(-1, 128, 4, 64)
    swvs = np.concatenate([R[c]["swvs"] for c in range(NCORES)], axis=0).reshape(-1, 128, 4, 64)
    mkp = np.stack([R[4 * b]["mkp"] for b in range(B)], axis=1).reshape(2, B, 256, 4, 64)
    mvp = np.stack([R[4 * b]["mvp"] for b in range(B)], axis=1).reshape(2, B, 256, 4, 64)
    return (y_prompt, y_sample, stp, sts, swkp, swvp, swks, swvs, mkp, mvp)
```

```python
import math
import numpy as np
from contextlib import ExitStack
import concourse.bass as bass
import concourse.mybir as mybir
from concourse.bass_utils import run_bass_kernel_spmd

F32 = mybir.dt.float32
BF16 = mybir.dt.bfloat16
ALU = mybir.AluOpType
AF = mybir.ActivationFunctionType
AX = mybir.AxisListType

D = 1024
DK, DV, H = 96, 192, 4
EPS = 1e-6
NCORES = 8


class Buf:
    __slots__ = ("name", "w", "r", "excl")

    def __init__(self, name="", excl=False):
        self.name = name
        self.w = None
        self.r = {}
        self.excl = excl


class StopBuild(Exception):
    pass


class TB:
    def __init__(self, t, name=""):
        self.t = t
        self.b = Buf(name)

    def __getitem__(self, k):
        return self.t[k]


class Sched:
    ENGS = ("pe", "act", "dve", "pool", "sp")
    BLK = {"pe": "tensor", "act": "scalar", "dve": "vector", "pool": "gpsimd", "sp": "sync"}

    def __init__(self, nc, es, n_dma=(28, 14, 8)):
        self.nc = nc
        self.sem = {e: es.enter_context(nc.semaphore("s_" + e)) for e in self.ENGS}
        self.cnt = {e: 0 for e in self.ENGS}
        self.known = {e: {e2: 0 for e2 in self.ENGS} for e in self.ENGS}
        self.dq = {}
        k = 0
        for q, n in zip(("sp", "pool", "act"), n_dma):
            self.dq[q] = list(range(k, k + n))
            k += n
        self.dsem = [es.enter_context(nc.semaphore("d%d" % i)) for i in range(k)]
        self.dcnt = [0] * k
        self.drr = {q: 0 for q in self.dq}
        self.kdma = {e: [0] * k for e in self.ENGS}
        self.prog = {e: [] for e in self.ENGS}

    @staticmethod
    def _b(x):
        return getattr(x, "b", x)

    def _deps(self, reads, writes):
        deps = []
        for b in reads:
            b = self._b(b)
            if b.w is not None:
                deps.append(b.w)
        for b in writes:
            b = self._b(b)
            if b.w is not None:
                deps.append(b.w)
            deps.extend(b.r.values())
        return deps

    def _resolve(self, eng, deps):
        need = {}
        for ev in deps:
            if ev[0] == "c":
                _, e2, k = ev
                if e2 == eng and eng == "pe":
                    continue
                if self.known[eng][e2] >= k:
                    continue
                need[("c", e2)] = max(need.get(("c", e2), 0), k)
            else:
                _, idx, v = ev
                if self.kdma[eng][idx] >= v:
                    continue
                need[("d", idx)] = max(need.get(("d", idx), 0), v)
        waits = []
        for (kind, key), v in need.items():
            if kind == "c":
                self.known[eng][key] = v
                waits.append((self.sem[key], v))
            else:
                self.kdma[eng][key] = v
                waits.append((self.dsem[key], v))
        return waits

    def _mark(self, ev, key, reads, writes):
        for b in reads:
            self._b(b).r[key] = ev
        for b in writes:
            b = self._b(b)
            b.w = ev
            b.r = {}

    def op(self, eng, fn, reads=(), writes=()):
        ex = [b for b in reads if self._b(b).excl]
        if ex:
            reads = [b for b in reads if not self._b(b).excl]
            writes = list(writes) + ex
        waits = self._resolve(eng, self._deps(reads, writes))
        self.cnt[eng] += 1
        ev = ("c", eng, self.cnt[eng])
        self.prog[eng].append((waits, fn, None))
        self._mark(ev, eng, reads, writes)
        return ev

    def dma(self, q, out, in_, reads=(), writes=(), **kw):
        waits = self._resolve(q, self._deps(reads, writes))
        pool = self.dq[q]
        idx = pool[self.drr[q] % len(pool)]
        self.drr[q] += 1
        v0 = 16 * self.dcnt[idx]
        if v0 > 0 and self.kdma[q][idx] < v0:
            self.kdma[q][idx] = v0
            waits.append((self.dsem[idx], v0))
        self.dcnt[idx] += 1
        ev = ("d", idx, 16 * self.dcnt[idx])
        self.prog[q].append((waits, lambda h: h.dma_start(out=out, in_=in_, **kw), idx))
        self._mark(ev, ("d", idx), reads, writes)
        return ev

    def flush(self):
        nc = self.nc
        tails = {}
        for q, pool in self.dq.items():
            tl = []
            for idx in pool:
                v = 16 * self.dcnt[idx]
                if v > 0 and self.kdma[q][idx] < v:
                    tl.append((self.dsem[idx], v))
            tails[q] = tl
        with nc.Block() as block:
            for e in self.ENGS:
                items = self.prog[e]
                tl = tails.get(e, [])
                if not items and not tl:
                    continue

                def body(h, items=items, tl=tl, e=e):
                    for waits, fn, didx in items:
                        for s, v in waits:
                            h.wait_ge(s, v)
                        ins = fn(h)
                        if didx is None:
                            ins.then_inc(self.sem[e], 1)
                        else:
                            ins.then_inc(self.dsem[didx], 16)
                    for s, v in tl:
                        h.wait_ge(s, v)

                getattr(block, self.BLK[e])(body)
        self.prog = {e: [] for e in self.ENGS}
        for e in self.ENGS:
            for e2 in self.ENGS:
                self.known[e][e2] = self.cnt[e2]
            for i in range(len(self.dcnt)):
                self.kdma[e][i] = 16 * self.dcnt[i]


def bc(ap, dim, n):
    u = ap.unsqueeze(dim)
    shp = list(u.shape)
    shp[dim] = n
    return u.broadcast_to(shp)


def qcol(h):
    s, r = divmod(h, 6)
    half, j = divmod(r, 3)
    return (3 * s + j) * 128 + half * 64


def build(NT=16, NPRE=47, NS=16, stop=99):
    try:
        return _build(NT, NPRE, NS, stop)
    except StopBuild as e:
        return e.args[0]


def _build(NT=16, NPRE=47, NS=16, stop=99):
    nc = bass.Bass("TRN2", target_bir_lowering=False)
    NTI = NPRE + 1 + NT
    NT1 = NT + 1

    def din(name, shape):
        return nc.dram_tensor(name, list(shape), F32, kind="ExternalInput").ap()

    def dout(name, shape):
        return nc.dram_tensor(name, list(shape), F32, kind="ExternalOutput").ap()

    xin = din("xin", [NTI * 128, D])
    xs_in = din("xs_in", [NS, D])
    st_in = din("st_in", [NS, H, DK, DV])
    swk_in = din("swk_in", [NS, 128, 256])
    swv_in = din("swv_in", [NS, 128, 256])
    mk_in = din("mk_in", [2, NS, 256, 256])
    mv_in = din("mv_in", [2, NS, 256, 256])
    mem_in = din("mem_in", [256, D])
    g_mix_pre = din("norm_mix_pre", [2, D])
    g_mix_post = din("norm_mix_post", [2, D])
    g_ffn_pre = din("norm_ffn_pre", [2, D])
    g_ffn_post = din("norm_ffn_post", [2, D])
    g_mem = din("norm_mem", [2, D])
    w_mem_kv = din("w_mem_kv", [2, D, 512])
    w_in_a = din("w_in_a", [D, 2576])
    w_gate_up = din("w_gate_up", [16, 384])
    b_gate = din("b_gate", [1, 384])
    gla_norm = din("gla_norm", [1, DV])
    w_in_b = din("w_in_b", [D, 1024])
    sinks = din("sinks", [1, 12])
    g_kv = din("norm_kv", [1, D])
    w_kv = din("w_kv", [D, 512])
    rel_bias = din("rel_bias", [32, 12])
    w_out = din("w_out", [2, D, D])
    w_up = din("w_ffn_up", [2, D, 4096])
    w_down = din("w_ffn_down", [2, 4096, D])
    c_ident = din("c_ident", [128, 128])
    c_cmask = din("c_cmask", [128, 128])
    c_ohu = din("c_ohu", [32, 384])
    c_valid = din("c_valid", [128, 384])
    c_id16r = din("c_id16r", [128, 256])
    c_flag = din("c_flag", [128, 1])

    y = dout("y", [NT * 128, D])
    ys = dout("ys", [NS, D])
    stp = dout("stp", [H, DK, DV])
    sts = dout("sts", [NS, H, DK, DV])
    swkp = dout("swkp", [128, 256])
    swvp = dout("swvp", [128, 256])
    swks = dout("swks", [NS, 128, 256])
    swvs = dout("swvs", [NS, 128, 256])
    mkp = dout("mkp", [2, 256, 256])
    mvp = dout("mvp", [2, 256, 256])

    xa = nc.dram_tensor("xa_scr", [NT1 * 128, D], F32).ap()
    xb = nc.dram_tensor("xb_scr", [NT1 * 128, D], F32).ap()
    ebd = nc.dram_tensor("ebd_scr", [12, 128, 384], F32)
    b_xa = [Buf() for _ in range(NT1)]
    b_xb = [Buf() for _ in range(NT1)]
    b_ebd = Buf()
    b_y = Buf()

    with ExitStack() as ges:
        S = Sched(nc, ges)

        def ck(n):
            if stop == -100 - n:
                S.flush()
                raise StopBuild(nc)

        def mk(es, name, shape, dt=F32):
            return TB(es.enter_context(nc.sbuf_tensor(name, list(shape), dt)), name)

        PS = [ges.enter_context(nc.psum_tensor("PS%d" % i, [128, 1024], F32)) for i in range(4)]
        bB = [Buf("bank%d" % i, excl=True) for i in range(8)]

        def bank(i):
            return PS[i // 2][:, (i % 2) * 512:(i % 2 + 1) * 512]

        def bankb(i):
            return bank(i).bitcast(BF16)

        ident_f = mk(ges, "ident_f", [128, 128])
        ident_b = mk(ges, "ident_b", [128, 128], BF16)
        cmask = mk(ges, "cmask", [128, 128])
        id16r = mk(ges, "id16r", [128, 16, 16])
        ones_row = mk(ges, "ones_row", [1, 128], BF16)
        ones96 = mk(ges, "ones96", [96, 128])
        flag = mk(ges, "flag", [128, 1])
        KmT = [mk(ges, "KmT%d" % l, [128, 2, 2, 256], BF16) for l in range(2)]
        Vm = [mk(ges, "Vm%d" % l, [128, 2, 4, 66], BF16) for l in range(2)]
        xs_sb = mk(ges, "xs_sb", [NS, D])
        st = {n: mk(ges, "st_" + n, [128, 4]) for n in ("ss", "ln", "rstd", "ssg", "lng", "rstdg", "rden", "el")}
        rden12 = mk(ges, "rden12", [128, 12])
        den12 = mk(ges, "den12", [128, 12])
        xt = [mk(ges, "xt%d" % i, [128, D]) for i in range(3)]
        hb = mk(ges, "hb", [128, D], BF16)
        hb2 = mk(ges, "hb2", [128, D], BF16)
        hT = mk(ges, "hT", [128, 8, 128], BF16)
        hT2 = mk(ges, "hT2", [128, 8, 128], BF16)
        mixcat = mk(ges, "mixcat", [128, D], BF16)
        mcT = mk(ges, "mcT", [128, 8, 128], BF16)
        tt = mk(ges, "tt", [128, D])
        pT_sb = mk(ges, "pT_sb", [128, 8, 128], BF16)
        qmT_sb = mk(ges, "qmT_sb", [128, 2, 128], BF16)
        z16 = mk(ges, "z16", [16, 512])

        xt_rr = [0]

        def next_xt():
            t = xt[xt_rr[0] % 3]
            xt_rr[0] += 1
            return t

        def load_gain(dst, src_row):
            S.dma("sp", dst[:], src_row.partition_broadcast(128), writes=[dst])

        def load_w(dst, src2d, kchunks, ncols, col0=0, dcol0=0):
            v = src2d.rearrange("(kc p) n -> p kc n", p=128)
            step = 2048
            for kc in range(kchunks):
                c = 0
                while c < ncols:
                    n = min(step, ncols - c)
                    S.dma("pool", dst[:, kc, dcol0 + c:dcol0 + c + n], v[:, kc, col0 + c:col0 + c + n], writes=[dst])
                    c += n

        def rms_stats(x_ap, P, x_reads, junk, n=D):
            ss, ln, rstd = st["ss"], st["ln"], st["rstd"]
            S.op("act", lambda h: h.activation(out=junk[0:P, 0:n], in_=x_ap, func=AF.Square, accum_out=ss[0:P, 0:1]),
                 reads=x_reads, writes=[junk, ss])
            S.op("act", lambda h: h.activation(out=ln[0:P, 0:1], in_=ss[0:P, 0:1], func=AF.Ln, scale=1.0 / n, bias=eps_t[0:P, 0:1]),
                 reads=[ss, eps_t], writes=[ln])
            S.op("act", lambda h: h.activation(out=rstd[0:P, 0:1], in_=ln[0:P, 0:1], func=AF.Exp, scale=-0.5),
                 reads=[ln], writes=[rstd])

        def norm_to(x_ap, P, x_reads, gain, dst):
            S.op("dve", lambda h: h.scalar_tensor_tensor(out=dst[0:P, :], in0=x_ap, scalar=st["rstd"][0:P, 0:1], in1=gain[0:P, :],
                                                         op0=ALU.mult, op1=ALU.mult),
                 reads=list(x_reads) + [st["rstd"], gain], writes=[dst])

        def transpose8(src, P, dstT, tb=0):
            def f(h):
                for kc in range(8):
                    ins = h.transpose(out=bankb(tb)[:, kc * 128:kc * 128 + P], in_=src[0:P, kc * 128:(kc + 1) * 128],
                                      identity=ident_b[0:P, 0:P])
                return ins
            S.op("pe", f, reads=[src, ident_b], writes=[bB[tb]])
            S.op("act", lambda h: h.copy(out=dstT[:, :, 0:P], in_=bankb(tb)[:, :].rearrange("p (a b) -> p a b", a=8)[:, :, 0:P]),
                 reads=[bB[tb]], writes=[dstT])

        def mm(out_ap, pairs, reads, wbank):
            def f(h):
                n = len(pairs)
                for i, (l, r) in enumerate(pairs):
                    ins = h.matmul(out_ap, lhsT=l, rhs=r, start=(i == 0), stop=(i == n - 1))
                return ins
            S.op("pe", f, reads=reads, writes=wbank)

        def post_norm_residual(mix_pair, P, gain, xtile_ap, xtile_tb):
            mix = PS[mix_pair][0:P, :]
            rms_stats(mix, P, [bB[2 * mix_pair], bB[2 * mix_pair + 1]], tt)
            S.op("dve", lambda h: h.scalar_tensor_tensor(out=tt[0:P, :], in0=mix, scalar=st["rstd"][0:P, 0:1], in1=gain[0:P, :],
                                                         op0=ALU.mult, op1=ALU.mult),
                 reads=[bB[2 * mix_pair], bB[2 * mix_pair + 1], st["rstd"], gain], writes=[tt])
            S.op("pool", lambda h: h.tensor_tensor(out=xtile_ap, in0=xtile_ap, in1=tt[0:P, :], op=ALU.add),
                 reads=[tt], writes=[xtile_tb])

        def out_proj(P, Wo, gain, xtile_ap, xtile_tb):
            transpose8(mixcat, P, mcT, tb=0)
            for half in range(2):
                mm(bank(6 + half)[0:P, :], [(mcT[:, kc, 0:P], Wo[:, kc, half * 512:(half + 1) * 512]) for kc in range(8)],
                   [mcT, Wo], [bB[6 + half]])
            post_norm_residual(3, P, gain, xtile_ap, xtile_tb)

        def mem_attend_prompt(l, qT_ap_fn, qreads, P):
            def f(h):
                for hh in range(4):
                    c, p = divmod(hh, 2)
                    for mc in range(2):
                        ins = h.matmul(bank(4 + hh // 2)[:, ((hh % 2) * 2 + mc) * 128:((hh % 2) * 2 + mc) * 128 + P],
                                       lhsT=KmT[l][:, p, c, mc * 128:(mc + 1) * 128], rhs=qT_ap_fn(c, p),
                                       start=True, stop=True)
                return ins
            S.op("pe", f, reads=[KmT[l]] + qreads, writes=[bB[4], bB[5]])
            ck(30)
            S.op("act", lambda h: h.activation(out=pT_sb[:, :, 0:P], in_=PS[2][:, :].rearrange("p (a b) -> p a b", a=8)[:, :, 0:P],
                                               func=AF.Exp, scale=0.125),
                 reads=[bB[4], bB[5]], writes=[pT_sb])

            def g(h):
                for hh in range(4):
                    for mc in range(2):
                        ins = h.matmul(bank(1)[0:P, hh * 65:(hh + 1) * 65], lhsT=pT_sb[:, hh * 2 + mc, 0:P], rhs=Vm[l][:, mc, hh, 0:65],
                                       start=(mc == 0), stop=(mc == 1))
                return ins
            ck(31)
            S.op("pe", g, reads=[pT_sb, Vm[l]], writes=[bB[1]])
            ck(32)
            finish_mem(P)

        def finish_mem(P):
            om = bank(1)[0:P, 0:260].rearrange("p (a b) -> p a b", a=4)
            S.op("dve", lambda h: h.reciprocal(out=st["rden"][0:P, 0:4], in_=om[:, :, 64]), reads=[bB[1]], writes=[st["rden"]])
            S.op("dve", lambda h: h.tensor_tensor(out=mixcat[0:P, 768:1024].rearrange("p (a b) -> p a b", a=4), in0=om[:, :, 0:64],
                                                  in1=bc(st["rden"][0:P, 0:4], 2, 64), op=ALU.mult),
                 reads=[bB[1], st["rden"]], writes=[mixcat])

        mem_sample_fn = []

        def zero_acc(regions, banks):
            def f(h):
                for r_ in regions:
                    n = r_.shape[-1]
                    ins = h.matmul(r_, lhsT=z16[0:16, 0:NS], rhs=z16[0:16, 0:n], start=True, stop=False)
                return ins
            S.op("pe", f, reads=[z16], writes=banks)

        def alloc_sample_attn(es, tag, onesel=None):
            Kc = [mk(es, tag + "Kc%d" % i, [128, 2, 256]) for i in range(2)]
            Vc = [mk(es, tag + "Vc%d" % i, [128, 2, 4, 65]) for i in range(2)]
            prod = mk(es, tag + "prod", [128, 768])
            sc8 = mk(es, tag + "sc8", [128, 12])
            pp8 = mk(es, tag + "pp8", [128, 12])
            Pm = mk(es, tag + "Pm", [128, 12, NS])
            if onesel is None:
                onesel = mk(es, tag + "onesel", [NS, NS, 128])
            S.op("pool", lambda h: h.memset(Vc[0][:], 1.0), writes=[Vc[0]])
            S.op("pool", lambda h: h.memset(Vc[1][:], 1.0), writes=[Vc[1]])
            S.op("dve", lambda h: h.tensor_copy(out=onesel[:], in_=bc(ident_f[0:NS, 0:NS], 2, 128)), reads=[ident_f], writes=[onesel])
            return Kc, Vc, prod, sc8, pp8, Pm, onesel

        with ExitStack() as es:
            eps_t = mk(ges, "eps_t", [128, 1])
            S.op("pool", lambda h: h.memset(eps_t[:], EPS), writes=[eps_t])
            S.op("pool", lambda h: h.memset(z16[:], 0.0), writes=[z16])
            S.dma("sp", ident_f[:], c_ident, writes=[ident_f])
            S.dma("sp", cmask[:], c_cmask, writes=[cmask])
            S.dma("sp", id16r[:].rearrange("p a b -> p (a b)"), c_id16r, writes=[id16r])
            S.dma("sp", flag[:], c_flag, writes=[flag])
            S.dma("sp", xs_sb[:], xs_in, writes=[xs_sb])
            S.op("dve", lambda h: h.tensor_copy(out=ident_b[:], in_=ident_f[:]), reads=[ident_f], writes=[ident_b])
            S.op("pool", lambda h: h.memset(ones_row[:], 1.0), writes=[ones_row])
            S.op("pool", lambda h: h.memset(ones96[:], 1.0), writes=[ones96])
            for l in range(2):
                S.op("pool", lambda h, l=l: h.memset(Vm[l][:], 1.0), writes=[Vm[l]])
                S.op("pool", lambda h, l=l: h.memset(KmT[l][:], 0.0), writes=[KmT[l]])
            gm = mk(es, "gm", [128, D])
            Wm = mk(es, "Wm", [128, 8, 512], BF16)
            hmT = mk(es, "hmT", [128, 8, 256], BF16)
            kvf = mk(es, "kvf", [128, 512])
            xm = [mk(es, "xm%d" % i, [128, D]) for i in range(2)]
            for mt in range(2):
                S.dma("sp", xm[mt][:], mem_in[mt * 128:(mt + 1) * 128, :], writes=[xm[mt]])
            if stop == -4:
                S.flush()
                return nc
            for l in range(2):
                load_gain(gm, g_mem[l, :])
                load_w(Wm, w_mem_kv[l], 8, 512)
                for mt in range(2):
                    rms_stats(xm[mt][:], 128, [xm[mt]], hb)
                    if stop == -3:
                        S.flush()
                        return nc
                    norm_to(xm[mt][:], 128, [xm[mt]], gm, hb)
                    if stop == -2:
                        S.flush()
                        return nc
                    transpose8(hb, 128, hT, tb=0)
                    if stop == -1:
                        S.flush()
                        return nc
                    S.op("dve", lambda h, mt=mt: h.tensor_copy(out=hmT[:, :, mt * 128:(mt + 1) * 128], in_=hT[:]),
                         reads=[hT], writes=[hmT])
                for c in range(2):
                    mm(bank(2)[:, 0:256], [(Wm[:, kc, c * 128:(c + 1) * 128], hmT[:, kc, :]) for kc in range(8)], [Wm, hmT], [bB[2]])
                    S.op("act", lambda h, c=c, l=l: h.copy(out=KmT[l][0:64, 0, c, :], in_=bank(2)[0:64, 0:256]), reads=[bB[2]], writes=[KmT[l]])
                    S.op("act", lambda h, c=c, l=l: h.copy(out=KmT[l][64:128, 1, c, :], in_=bank(2)[64:128, 0:256]), reads=[bB[2]], writes=[KmT[l]])
                if stop == -10:
                    S.flush()
                    return nc
                for mt in range(2):
                    mm(bank(3)[:, 0:512], [(hmT[:, kc, mt * 128:(mt + 1) * 128], Wm[:, kc, :]) for kc in range(8)], [Wm, hmT], [bB[3]])
                    S.op("dve", lambda h: h.tensor_copy(out=kvf[:], in_=bank(3)[:, 0:512]), reads=[bB[3]], writes=[kvf])
                    if stop == -11:
                        S.flush()
                        return nc
                    S.op("act", lambda h, l=l, mt=mt: h.copy(out=Vm[l][:, mt, :, 0:64],
                                                             in_=bank(3)[:, 256:512].rearrange("p (a b) -> p a b", a=4)),
                         reads=[bB[3]], writes=[Vm[l]])
                    if stop == -12:
                        S.flush()
                        return nc
                    S.dma("sp", mkp[l, mt * 128:(mt + 1) * 128, :], kvf[:, 0:256], reads=[kvf])
                    S.dma("sp", mvp[l, mt * 128:(mt + 1) * 128, :], kvf[:, 256:512], reads=[kvf])
            S.flush()
        if 0 <= stop <= 0:
            return nc

        with ExitStack() as es:
            Wa = mk(es, "Wa", [128, 8, 2576], BF16)
            Wo = mk(es, "Wo0", [128, 8, D], BF16)
            wgu = mk(es, "wgu", [16, 384], BF16)
            bg = mk(es, "bg", [1, 384], BF16)
            gpre = mk(es, "gpre0", [128, D])
            gpost = mk(es, "gpost0", [128, D])
            gn = mk(es, "gn", [128, DV])
            Sst = mk(es, "Sst", [96, 4, DV])
            S_bf = mk(es, "S_bf", [96, 4, DV], BF16)
            glr_sb = mk(es, "glr_sb", [16, 128], BF16)
            e1 = mk(es, "e1", [96, 4, 128])
            spl = mk(es, "spl", [96, 4, 128])
            cc = mk(es, "cc", [96, 4, 128])
            Einv = mk(es, "Einv", [96, 4, 128])
            Edec = mk(es, "Edec", [96, 4, 128])
            keT = mk(es, "keT", [96, 4, 128], BF16)
            qeT = mk(es, "qeT", [96, 4, 128], BF16)
            kdT = mk(es, "kdT", [96, 4, 128], BF16)
            kd_sb = mk(es, "kd_sb", [128, 384], BF16)
            v_sb = mk(es, "v_sb", [128, 768], BF16)
            attnT_sb = mk(es, "attnT_sb", [128, 4, 128], BF16)
            on = mk(es, "on", [128, 768])
            er = mk(es, "er", [128, 768])
            sg = mk(es, "sg", [128, 768])

            load_w(Wa, w_in_a, 8, 2576)
            load_w(Wo, w_out[0], 8, D)
            S.dma("pool", wgu[:], w_gate_up, writes=[wgu])
            S.dma("pool", bg[:], b_gate, writes=[bg])
            load_gain(gpre, g_mix_pre[0, :])
            load_gain(gpost, g_mix_post[0, :])
            S.dma("sp", gn[:], gla_norm[0, :].partition_broadcast(128), writes=[gn])
            S.op("pool", lambda h: h.memset(Sst[:], 0.0), writes=[Sst])
            S.op("pool", lambda h: h.memset(S_bf[:], 0.0), writes=[S_bf])

            elast = st["el"]

            def gate_path(P):
                mm(bank(1)[0:16, 0:P], [(Wa[:, kc, 2304:2320], hT[:, kc, 0:P]) for kc in range(8)], [Wa, hT], [bB[1]])
                S.op("dve", lambda h: h.tensor_copy(out=glr_sb[:, 0:P], in_=bank(1)[0:16, 0:P]), reads=[bB[1]], writes=[glr_sb])

            def zgate(P):
                def f(h):
                    for hh in range(4):
                        o = bank(1)[0:96, hh * 128:hh * 128 + P]
                        h.matmul(o, lhsT=wgu[0:16, 96 * hh:96 * hh + 96], rhs=glr_sb[0:16, 0:P], start=True, stop=False)
                        ins = h.matmul(o, lhsT=bg[0:1, 96 * hh:96 * hh + 96], rhs=ones_row[0:1, 0:P], start=False, stop=True)
                    return ins
                S.op("pe", f, reads=[wgu, bg, glr_sb, ones_row], writes=[bB[1]])
                zv = bank(1)[0:96, :].rearrange("p (a b) -> p a b", a=4)[:, :, 0:P]
                S.op("act", lambda h: h.activation(out=e1[:, :, 0:P], in_=zv, func=AF.Exp, scale=-1.0), reads=[bB[1]], writes=[e1])
                S.op("act", lambda h: h.activation(out=spl[:, :, 0:P], in_=e1[:, :, 0:P], func=AF.Ln, scale=1.0, bias=one_t[0:96, 0:1]),
                     reads=[e1, one_t], writes=[spl])

            one_t = mk(es, "one_t", [128, 1])
            S.op("pool", lambda h: h.memset(one_t[:], 1.0), writes=[one_t])

            def proj_fm(col0, bk, P):
                def f(h):
                    for hh in range(4):
                        for kc in range(8):
                            ins = h.matmul(bank(bk)[0:96, hh * 128:hh * 128 + P], lhsT=Wa[:, kc, col0 + 96 * hh:col0 + 96 * hh + 96],
                                           rhs=hT[:, kc, 0:P], start=(kc == 0), stop=(kc == 7))
                    return ins
                S.op("pe", f, reads=[Wa, hT], writes=[bB[bk]])

            def proj_tm(col0, pair, P):
                for half in range(2):
                    mm(bank(2 * pair + half)[0:P, 0:384],
                       [(hT[:, kc, 0:P], Wa[:, kc, col0 + 384 * half:col0 + 384 * (half + 1)]) for kc in range(8)],
                       [Wa, hT], [bB[2 * pair + half]])

            def pair768(pair, P):
                return PS[pair][0:P, :].rearrange("p (a b) -> p a b", a=2)[:, :, 0:384]

            def gla_norm_gate(P, o_pair, r_pair):
                ov = pair768(o_pair, P).rearrange("p a (i n) -> p a i n", i=2)
                for hh in range(4):
                    S.op("act", lambda h, hh=hh: h.activation(out=sg[0:P, 0:DV], in_=ov[:, hh // 2, hh % 2, :], func=AF.Square,
                                                              accum_out=st["ssg"][0:P, hh:hh + 1]),
                         reads=[bB[2 * o_pair], bB[2 * o_pair + 1]], writes=[sg, st["ssg"]])
                S.op("act", lambda h: h.activation(out=st["lng"][0:P, :], in_=st["ssg"][0:P, :], func=AF.Ln, scale=1.0 / DV,
                                                   bias=eps_t[0:P, 0:1]), reads=[st["ssg"], eps_t], writes=[st["lng"]])
                S.op("act", lambda h: h.activation(out=st["rstdg"][0:P, :], in_=st["lng"][0:P, :], func=AF.Exp, scale=-0.5),
                     reads=[st["lng"]], writes=[st["rstdg"]])
                for hh in range(4):
                    S.op("dve", lambda h, hh=hh: h.scalar_tensor_tensor(out=on[0:P, hh * DV:(hh + 1) * DV], in0=ov[:, hh // 2, hh % 2, :],
                                                                        scalar=st["rstdg"][0:P, hh:hh + 1], in1=gn[0:P, :],
                                                                        op0=ALU.mult, op1=ALU.mult),
                         reads=[bB[2 * o_pair], bB[2 * o_pair + 1], st["rstdg"], gn], writes=[on])
                rv = pair768(r_pair, P)
                er3 = er[0:P, :].rearrange("p (a b) -> p a b", a=2)
                S.op("act", lambda h: h.activation(out=er3, in_=rv, func=AF.Exp, scale=-1.0),
                     reads=[bB[2 * r_pair], bB[2 * r_pair + 1]], writes=[er])
                S.op("pool", lambda h: h.tensor_scalar(out=er[0:P, :], in0=er[0:P, :], scalar1=1.0, scalar2=None, op0=ALU.add),
                     reads=[], writes=[er])
                S.op("dve", lambda h: h.reciprocal(out=er[0:P, :], in_=er[0:P, :]), reads=[], writes=[er])
                S.op("dve", lambda h: h.tensor_tensor(out=sg[0:P, :].rearrange("p (a b) -> p a b", a=2), in0=rv, in1=er3, op=ALU.mult),
                     reads=[bB[2 * r_pair], bB[2 * r_pair + 1], er], writes=[sg])
                S.op("dve", lambda h: h.tensor_tensor(out=mixcat[0:P, 0:768], in0=on[0:P, :], in1=sg[0:P, :], op=ALU.mult),
                     reads=[on, sg], writes=[mixcat])

            def load_x0(row0):
                xt_ = next_xt()
                S.dma("sp", xt_[:], xin[row0:row0 + 128, :], writes=[xt_])
                return xt_

            def gla_tile(row0, full, out_slot, xtile=None):
                P = 128
                if xtile is None:
                    xtile = load_x0(row0)
                rms_stats(xtile[:], P, [xtile], hb)
                norm_to(xtile[:], P, [xtile], gpre, hb)
                transpose8(hb, P, hT, tb=0)
                gate_path(P)
                ck(1)
                proj_fm(384, 2, P)
                ck(2)
                if full:
                    proj_fm(0, 3, P)
                zgate(P)
                ck(3)
                proj_tm(768, 2, P)
                if full:
                    proj_tm(1536, 3, P)
                S.op("act", lambda h: h.copy(out=v_sb[:].rearrange("p (a b) -> p a b", a=2), in_=pair768(2, P)),
                     reads=[bB[4], bB[5]], writes=[v_sb])
                for hh in range(4):
                    S.op("dve", lambda h, hh=hh: h.tensor_tensor_scan(out=cc[:, hh, :], data0=ones96[:, :], data1=spl[:, hh, :], initial=0.0,
                                                                      op0=ALU.mult, op1=ALU.add),
                         reads=[spl, ones96], writes=[cc])
                S.op("act", lambda h: h.activation(out=Einv[:], in_=cc[:], func=AF.Exp, scale=1.0 / 16), reads=[cc], writes=[Einv])
                if full:
                    S.op("act", lambda h: h.activation(out=Edec[:], in_=cc[:], func=AF.Exp, scale=-1.0 / 16), reads=[cc], writes=[Edec])
                S.op("act", lambda h: h.activation(out=elast[0:96, 0:4], in_=cc[:, :, 127], func=AF.Exp, scale=-1.0 / 16),
                     reads=[cc], writes=[elast])
                ck(4)
                kT = bank(2)[0:96, :].rearrange("p (a b) -> p a b", a=4)
                qT = bank(3)[0:96, :].rearrange("p (a b) -> p a b", a=4)
                S.op("dve", lambda h: h.tensor_tensor(out=keT[:], in0=kT, in1=Einv[:], op=ALU.mult), reads=[bB[2], Einv], writes=[keT])
                if full:
                    S.op("dve", lambda h: h.scalar_tensor_tensor(out=qeT[:], in0=qT, scalar=DK ** -0.5, in1=Edec[:], op0=ALU.mult, op1=ALU.mult),
                         reads=[bB[3], Edec], writes=[qeT])
                for hh in range(4):
                    S.op("dve", lambda h, hh=hh: h.scalar_tensor_tensor(out=kdT[:, hh, :], in0=kT[:, hh, :], scalar=elast[0:96, hh:hh + 1],
                                                                        in1=Einv[:, hh, :], op0=ALU.mult, op1=ALU.mult),
                         reads=[bB[2], elast, Einv], writes=[kdT])

                ck(5)

                def trk(h):
                    for hh in range(4):
                        ins = h.transpose(out=bankb(0)[:, hh * 96:(hh + 1) * 96], in_=kdT[:, hh, :], identity=ident_b[0:96, 0:96])
                    return ins
                S.op("pe", trk, reads=[kdT, ident_b], writes=[bB[0]])
                S.op("act", lambda h: h.copy(out=kd_sb[:], in_=bankb(0)[:, 0:384]), reads=[bB[0]], writes=[kd_sb])
                ck(6)
                ck(7)
                if full:
                    def fa(h):
                        for hh in range(4):
                            ins = h.matmul(bank(1)[:, hh * 128:(hh + 1) * 128], lhsT=keT[:, hh, :], rhs=qeT[:, hh, :], start=True, stop=True)
                        return ins
                    S.op("pe", fa, reads=[keT, qeT], writes=[bB[1]])
                    S.op("dve", lambda h: h.tensor_tensor(out=attnT_sb[:], in0=bank(1)[:, :].rearrange("p (a b) -> p a b", a=4),
                                                          in1=bc(cmask[:], 1, 4), op=ALU.mult),
                         reads=[bB[1], cmask], writes=[attnT_sb])

                    def fo(h):
                        for hh in range(4):
                            o = bank(4 + hh // 2)[:, (hh % 2) * DV:(hh % 2 + 1) * DV]
                            h.matmul(o, lhsT=attnT_sb[:, hh, :], rhs=v_sb[:, hh * DV:(hh + 1) * DV], start=True, stop=False)
                            ins = h.matmul(o, lhsT=qeT[:, hh, :], rhs=S_bf[:, hh, :], start=False, stop=True)
                        return ins
                    S.op("pe", fo, reads=[attnT_sb, v_sb, qeT, S_bf], writes=[bB[4], bB[5]])

                def fs(h):
                    for hh in range(4):
                        ins = h.matmul(bank(2 + hh // 2)[0:96, (hh % 2) * DV:(hh % 2 + 1) * DV], lhsT=kd_sb[:, 96 * hh:96 * hh + 96],
                                       rhs=v_sb[:, hh * DV:(hh + 1) * DV], start=True, stop=True)
                    return ins
                S.op("pe", fs, reads=[kd_sb, v_sb], writes=[bB[2], bB[3]])
                for hh in range(4):
                    S.op("dve", lambda h, hh=hh: h.scalar_tensor_tensor(out=Sst[:, hh, :], in0=Sst[:, hh, :], scalar=elast[0:96, hh:hh + 1],
                                                                        in1=bank(2 + hh // 2)[0:96, (hh % 2) * DV:(hh % 2 + 1) * DV],
                                                                        op0=ALU.mult, op1=ALU.add),
                         reads=[elast, bB[2], bB[3]], writes=[Sst])
                S.op("pool", lambda h: h.tensor_copy(out=S_bf[:], in_=Sst[:]), reads=[Sst], writes=[S_bf])
                ck(8)
                if not full:
                    return
                for c in range(2):
                    mm(bank(1)[:, c * 128:c * 128 + P], [(Wa[:, kc, 2320 + 128 * c:2320 + 128 * (c + 1)], hT[:, kc, 0:P]) for kc in range(8)],
                       [Wa, hT], [bB[1]])
                S.op("act", lambda h: h.copy(out=qmT_sb[:, :, 0:P], in_=bank(1)[:, 0:256].rearrange("p (a b) -> p a b", a=2)[:, :, 0:P]),
                     reads=[bB[1]], writes=[qmT_sb])
                ck(9)
                gla_norm_gate(P, 2, 3)
                ck(10)
                mem_attend_prompt(0, lambda c, p: qmT_sb[:, c, 0:P], [qmT_sb], P)
                ck(11)
                out_proj(P, Wo, gpost, xtile[:], xtile)
                ck(12)
                S.dma("sp", xa[out_slot * 128:(out_slot + 1) * 128, :], xtile[:], reads=[xtile], writes=[b_xa[out_slot]])

            esp = ExitStack()
            PB = []
            for par in range(4):
                d = {}
                d["hb"] = mk(esp, "p_hb%d" % par, [128, D], BF16)
                d["hT"] = mk(esp, "p_hT%d" % par, [128, 8, 128], BF16)
                d["glr"] = mk(esp, "p_glr%d" % par, [16, 128], BF16)
                d["sp"] = mk(esp, "p_sp%d" % par, [96, 4, 128])
                d["cc"] = mk(esp, "p_cc%d" % par, [96, 4, 128])
                d["kdT"] = mk(esp, "p_kdT%d" % par, [96, 4, 128], BF16)
                d["kd"] = mk(esp, "p_kd%d" % par, [128, 384], BF16)
                d["v"] = mk(esp, "p_v%d" % par, [128, 768], BF16)
                d["ss"] = mk(esp, "p_ss%d" % par, [128, 1])
                d["ln"] = mk(esp, "p_ln%d" % par, [128, 1])
                d["rs"] = mk(esp, "p_rs%d" % par, [128, 1])
                d["el"] = mk(esp, "p_el%d" % par, [96, 4])
                PB.append(d)

            def prefix_tile(t, Q):
                def q(fn, *a, **k):
                    Q.append((fn, a, k))

                par = t % 4
                d = PB[par]
                bT, bZ, bK, bX = 2 * par, 2 * par, 2 * par + 1, 2 * par + 1
                hbP, hTP, glrP, spP, ccP, kdTP, kdP, vP = d["hb"], d["hT"], d["glr"], d["sp"], d["cc"], d["kdT"], d["kd"], d["v"]
                ssP, lnP, rsP, elP = d["ss"], d["ln"], d["rs"], d["el"]
                xtile = next_xt()
                q(S.dma, "sp", xtile[:], xin[t * 128:(t + 1) * 128, :], writes=[xtile])
                q(S.op, "act", lambda h: h.activation(out=hbP[:], in_=xtile[:], func=AF.Square, accum_out=ssP[:, 0:1]), reads=[xtile], writes=[hbP, ssP])
                q(S.op, "act", lambda h: h.activation(out=lnP[:], in_=ssP[:], func=AF.Ln, scale=1.0 / D, bias=eps_t[:, 0:1]), reads=[ssP, eps_t], writes=[lnP])
                q(S.op, "act", lambda h: h.activation(out=rsP[:], in_=lnP[:], func=AF.Exp, scale=-0.5), reads=[lnP], writes=[rsP])
                q(S.op, "dve", lambda h: h.scalar_tensor_tensor(out=hbP[:], in0=xtile[:], scalar=rsP[:, 0:1], in1=gpre[:], op0=ALU.mult, op1=ALU.mult),
                     reads=[xtile, rsP, gpre], writes=[hbP])

                def ftr(h):
                    for kc in range(8):
                        ins = h.transpose(out=bankb(bT)[:, kc * 128:(kc + 1) * 128], in_=hbP[:, kc * 128:(kc + 1) * 128], identity=ident_b[:])
                    return ins
                q(S.op, "pe", ftr, reads=[hbP, ident_b], writes=[bB[bT]])
                q(S.op, "act", lambda h: h.copy(out=hTP[:].rearrange("p a b -> p (a b)"), in_=bankb(bT)[:, :]), reads=[bB[bT]], writes=[hTP])
                q(mm, bank(bZ)[0:16, 0:128], [(Wa[:, kc, 2304:2320], hTP[:, kc, :]) for kc in range(8)], [Wa, hTP], [bB[bZ]])
                q(S.op, "dve", lambda h: h.tensor_copy(out=glrP[:], in_=bank(bZ)[0:16, 0:128]), reads=[bB[bZ]], writes=[glrP])

                def fk_(h):
                    for hh in range(4):
                        for kc in range(8):
                            ins = h.matmul(bank(bK)[0:96, hh * 128:(hh + 1) * 128], lhsT=Wa[:, kc, 384 + 96 * hh:384 + 96 * hh + 96],
                                           rhs=hTP[:, kc, :], start=(kc == 0), stop=(kc == 7))
                    return ins
                q(S.op, "pe", fk_, reads=[Wa, hTP], writes=[bB[bK]])

                def fz(h):
                    for hh in range(4):
                        o = bank(bZ)[0:96, hh * 128:(hh + 1) * 128]
                        h.matmul(o, lhsT=wgu[0:16, 96 * hh:96 * hh + 96], rhs=glrP[0:16, :], start=True, stop=False)
                        ins = h.matmul(o, lhsT=bg[0:1, 96 * hh:96 * hh + 96], rhs=ones_row[0:1, :], start=False, stop=True)
                    return ins
                q(S.op, "pe", fz, reads=[wgu, bg, glrP, ones_row], writes=[bB[bZ]])
                zv = bank(bZ)[0:96, :].rearrange("p (a b) -> p a b", a=4)
                q(S.op, "act", lambda h: h.activation(out=spP[:], in_=zv, func=AF.Exp, scale=-1.0), reads=[bB[bZ]], writes=[spP])
                q(S.op, "act", lambda h: h.activation(out=spP[:], in_=spP[:], func=AF.Ln, scale=1.0, bias=one_t[0:96, 0:1]), reads=[one_t], writes=[spP])
                for hh in range(4):
                    q(S.op, "dve", lambda h, hh=hh: h.tensor_tensor_scan(out=ccP[:, hh, :], data0=ones96[:, :], data1=spP[:, hh, :], initial=0.0,
                                                                      op0=ALU.mult, op1=ALU.add), reads=[spP, ones96], writes=[ccP])
                q(S.op, "act", lambda h: h.activation(out=elP[:], in_=ccP[:, :, 127], func=AF.Exp, scale=-1.0 / 16), reads=[ccP], writes=[elP])
                q(S.op, "act", lambda h: h.activation(out=ccP[:], in_=ccP[:], func=AF.Exp, scale=1.0 / 16), reads=[], writes=[ccP])
                kT = bank(bK)[0:96, :].rearrange("p (a b) -> p a b", a=4)
                for hh in range(4):
                    q(S.op, "dve", lambda h, hh=hh: h.scalar_tensor_tensor(out=kdTP[:, hh, :], in0=kT[:, hh, :], scalar=elP[:, hh:hh + 1], in1=ccP[:, hh, :],
                                                                        op0=ALU.mult, op1=ALU.mult), reads=[bB[bK], elP, ccP], writes=[kdTP])

                def ftk(h):
                    for hh in range(4):
                        ins = h.transpose(out=bankb(bT)[:, hh * 96:(hh + 1) * 96], in_=kdTP[:, hh, :], identity=ident_b[0:96, 0:96])
                    return ins
                q(S.op, "pe", ftk, reads=[kdTP, ident_b], writes=[bB[bT]])
                q(S.op, "act", lambda h: h.copy(out=kdP[:], in_=bankb(bT)[:, 0:384]), reads=[bB[bT]], writes=[kdP])
                for half in range(2):
                    q(mm, bank(bX)[:, 0:384], [(hTP[:, kc, :], Wa[:, kc, 768 + 384 * half:768 + 384 * (half + 1)]) for kc in range(8)], [Wa, hTP], [bB[bX]])
                    if half == 0:
                        q(S.op, "act", lambda h: h.copy(out=vP[:, 0:384], in_=bank(bX)[:, 0:384]), reads=[bB[bX]], writes=[vP])
                    else:
                        q(S.op, "dve", lambda h: h.tensor_copy(out=vP[:, 384:768], in_=bank(bX)[:, 0:384]), reads=[bB[bX]], writes=[vP])
                for pr_, bk_ in ((0, bX), (1, bK)):
                    def fs_(h, pr_=pr_, bk_=bk_):
                        for i_ in range(2):
                            hh = 2 * pr_ + i_
                            ins = h.matmul(bank(bk_)[0:96, i_ * DV:(i_ + 1) * DV], lhsT=kdP[:, 96 * hh:96 * hh + 96], rhs=vP[:, hh * DV:(hh + 1) * DV],
                                           start=True, stop=True)
                        return ins
                    q(S.op, "pe", fs_, reads=[kdP, vP], writes=[bB[bk_]])
                    for i_ in range(2):
                        hh = 2 * pr_ + i_
                        q(S.op, "dve", lambda h, hh=hh, i_=i_, bk_=bk_: h.scalar_tensor_tensor(out=Sst[:, hh, :], in0=Sst[:, hh, :], scalar=elP[:, hh:hh + 1],
                                                                                            in1=bank(bk_)[0:96, i_ * DV:(i_ + 1) * DV], op0=ALU.mult, op1=ALU.add),
                             reads=[elP, bB[bk_]], writes=[Sst])

            for t0 in range(0, NPRE, 4):
                lists = []
                for t in range(t0, t0 + 4):
                    if t < NPRE:
                        Q = []
                        prefix_tile(t, Q)
                        lists.append(Q)
                for i_ in range(max(len(L) for L in lists)):
                    for L in lists:
                        if i_ < len(L):
                            fn_, a_, k_ = L[i_]
                            fn_(*a_, **k_)
            S.op("pool", lambda h: h.tensor_copy(out=S_bf[:], in_=Sst[:]), reads=[Sst], writes=[S_bf])
            S.flush()
            ck(100)
            esp.close()
            pre_x = load_x0(NPRE * 128)
            for t in range(NT1):
                nxt_x = load_x0((NPRE + t + 1) * 128) if t + 1 < NT1 else None
                gla_tile((NPRE + t) * 128, True, t, xtile=pre_x)
                pre_x = nxt_x
            S.dma("sp", stp.rearrange("h d v -> d h v"), Sst[:], reads=[Sst])

            P = NS
            kq_s = mk(es, "kq_s", [96, 4, NS])
            eg = mk(es, "eg", [96, 4, NS])
            ktok = mk(es, "ktok", [NS, 384])
            vtok = mk(es, "vtok", [NS, 768])
            kmask = mk(es, "kmask", [NS, NS, 384])
            qmask = mk(es, "qmask", [96, 4, NS, NS])
            Sin = [mk(es, "Sin%d" % i, [96, 4, DV]) for i in range(2)]
            Snew = [mk(es, "Snew%d" % i, [96, 4, DV]) for i in range(2)]
            qm_tok = mk(es, "qm_tok", [NS, 256])
            rms_stats(xs_sb[:], P, [xs_sb], hb)
            norm_to(xs_sb[:], P, [xs_sb], gpre, hb)
            transpose8(hb, P, hT, tb=0)
            gate_path(P)
            proj_fm(384, 2, P)
            proj_fm(0, 3, P)
            zgate(P)
            S.op("act", lambda h: h.activation(out=eg[:], in_=spl[:, :, 0:P], func=AF.Exp, scale=-1.0 / 16), reads=[spl], writes=[eg])
            qTs = bank(3)[0:96, :].rearrange("p (a b) -> p a b", a=4)[:, :, 0:P]
            S.op("dve", lambda h: h.tensor_scalar(out=kq_s[:], in0=qTs, scalar1=DK ** -0.5, scalar2=None, op0=ALU.mult),
                 reads=[bB[3]], writes=[kq_s])
            S.op("dve", lambda h: h.tensor_tensor(out=qmask[:], in0=bc(kq_s[:], 3, NS), in1=bc(id16r[0:96, :, :], 1, 4), op=ALU.mult),
                 reads=[kq_s, id16r], writes=[qmask])
            mm(bank(2)[0:P, 0:384], [(hT[:, kc, 0:P], Wa[:, kc, 384:768]) for kc in range(8)], [Wa, hT], [bB[2]])
            S.op("act", lambda h: h.copy(out=ktok[:], in_=bank(2)[0:P, 0:384]), reads=[bB[2]], writes=[ktok])
            S.op("dve", lambda h: h.tensor_tensor(out=kmask[:], in0=bc(ktok[:], 1, NS), in1=bc(ident_f[0:NS, 0:NS], 2, 384), op=ALU.mult),
                 reads=[ktok, ident_f], writes=[kmask])
            proj_tm(768, 2, P)
            S.op("act", lambda h: h.copy(out=vtok[:].rearrange("p (a b) -> p a b", a=2), in_=pair768(2, P)), reads=[bB[4], bB[5]], writes=[vtok])
            mm(bank(1)[0:P, 0:256], [(hT[:, kc, 0:P], Wa[:, kc, 2320:2576]) for kc in range(8)], [Wa, hT], [bB[1]])
            S.op("act", lambda h: h.copy(out=qm_tok[:], in_=bank(1)[0:P, 0:256]), reads=[bB[1]], writes=[qm_tok])
            proj_tm(1536, 3, P)
            ck(20)
            zero_acc([bank(4)[0:NS, 0:384], bank(5)[0:NS, 0:384]], [bB[4], bB[5]])
            for i in range(NS):
                si, sn = Sin[i % 2], Snew[i % 2]
                S.dma("act", si[:], st_in[i].rearrange("h d v -> d h v"), writes=[si])

                def fk(h, i=i):
                    for hh in range(4):
                        ins = h.matmul(bank(2 + hh // 2)[0:96, (hh % 2) * DV:(hh % 2 + 1) * DV], lhsT=kmask[0:NS, i, 96 * hh:96 * hh + 96],
                                       rhs=vtok[0:NS, hh * DV:(hh + 1) * DV], start=True, stop=True)
                    return ins
                S.op("pe", fk, reads=[kmask, vtok], writes=[bB[2], bB[3]])
                for hh in range(4):
                    S.op("dve", lambda h, hh=hh, i=i, si=si, sn=sn: h.scalar_tensor_tensor(
                        out=sn[:, hh, :], in0=si[:, hh, :], scalar=eg[:, hh, i:i + 1],
                        in1=bank(2 + hh // 2)[0:96, (hh % 2) * DV:(hh % 2 + 1) * DV], op0=ALU.mult, op1=ALU.add),
                        reads=[si, eg, bB[2], bB[3]], writes=[sn])
                S.dma("sp", sts[i].rearrange("h d v -> d h v"), sn[:], reads=[sn])

                def fq(h, i=i, sn=sn):
                    for hh in range(4):
                        ins = h.matmul(bank(4 + hh // 2)[0:NS, (hh % 2) * DV:(hh % 2 + 1) * DV], lhsT=qmask[:, hh, i, :], rhs=sn[:, hh, :],
                                       start=False, stop=(i == NS - 1 and hh % 2 == 1))
                    return ins
                S.op("pe", fq, reads=[qmask, sn], writes=[bB[4], bB[5]])
            ck(21)
            gla_norm_gate(P, 2, 3)
            ck(22)
            sab = alloc_sample_attn(es, "p2", onesel=TB_sub(kmask, 128))

            def mem_sample(l, qtok, sab):
                Kc, Vc, prod, sc8, pp8, Pm, onesel = sab
                zero_acc([bank(1)[0:NS, 0:260]], [bB[1]])
                for i in range(NS):
                    kc_, vc_ = Kc[i % 2], Vc[i % 2]
                    S.dma("act", kc_[:], mk_in[l, i].rearrange("(mc p) n -> p mc n", p=128), writes=[kc_])
                    for mc in range(2):
                        S.dma("act", vc_[:, mc, :, 0:64], mv_in[l, i, mc * 128:(mc + 1) * 128, :].rearrange("p (a b) -> p a b", a=4), writes=[vc_])
                    mm(bank(0)[:, 0:256], [(onesel[0:NS, i, :], qtok[0:NS, :])], [onesel, qtok], [bB[0]])
                    S.op("dve", lambda h, kc_=kc_: h.tensor_tensor(out=prod[:, 0:512].rearrange("p (a b) -> p a b", a=2), in0=kc_[:],
                                                                   in1=bc(bank(0)[:, 0:256], 1, 2), op=ALU.mult),
                         reads=[kc_, bB[0]], writes=[prod])
                    S.op("dve", lambda h: h.tensor_reduce(out=sc8[:, 0:8], in_=prod[:, 0:512].rearrange("p (a b) -> p a b", b=64), axis=AX.X, op=ALU.add),
                         reads=[prod], writes=[sc8])
                    S.op("act", lambda h: h.activation(out=pp8[:, 0:8], in_=sc8[:, 0:8], func=AF.Exp, scale=0.125), reads=[sc8], writes=[pp8])
                    S.op("dve", lambda h, i=i: h.tensor_tensor(out=Pm[:, 0:8, :], in0=bc(pp8[:, 0:8], 2, NS), in1=bc(id16r[:, i, :], 1, 8), op=ALU.mult),
                         reads=[pp8, id16r], writes=[Pm])

                    def fm(h, i=i, vc_=vc_):
                        for hh in range(4):
                            for mc in range(2):
                                ins = h.matmul(bank(1)[0:NS, hh * 65:(hh + 1) * 65], lhsT=Pm[:, mc * 4 + hh, :], rhs=vc_[:, mc, hh, :],
                                               start=False, stop=(i == NS - 1 and mc == 1 and hh == 3))
                        return ins
                    S.op("pe", fm, reads=[Pm, vc_], writes=[bB[1]])
                finish_mem(NS)

            mem_sample(0, qm_tok, sab)
            ck(23)
            mem_sample_fn.append(mem_sample)
            out_proj(P, Wo, gpost, xs_sb[:], xs_sb)
            S.flush()
        if 0 <= stop <= 1:
            S.dma("sp", ys, xs_sb[:], reads=[xs_sb])
            S.flush()
            return nc

        def mlp_phase(l, tiles, es):
            Wu = mk(es, "Wu%d" % l, [128, 8, 4096], BF16)
            Wd = mk(es, "Wd%d" % l, [128, 32, D], BF16)
            gpre = mk(es, "gfpre%d" % l, [128, D])
            gpost = mk(es, "gfpost%d" % l, [128, D])
            hTg = mk(es, "hTg%d" % l, [128, 8, 256], BF16)
            actT = mk(es, "actT%d" % l, [128, 32, 256], BF16)
            load_w(Wu, w_up[l], 8, 4096)
            load_w(Wd, w_down[l], 32, D)
            load_gain(gpre, g_ffn_pre[l, :])
            load_gain(gpost, g_ffn_post[l, :])
            groups = []
            i = 0
            while i < len(tiles):
                if tiles[i][0] == 128 and i + 1 < len(tiles) and tiles[i + 1][0] == 128:
                    groups.append(tiles[i:i + 2])
                    i += 2
                else:
                    groups.append(tiles[i:i + 1])
                    i += 1
            xtl = {}

            def n_elem(g, only=None):
                for ti, (P, src, sbuf, dst, dbuf) in enumerate(groups[g]):
                    if only is not None and ti != only:
                        continue
                    if src is None:
                        xtile, xap = xs_sb, xs_sb[:]
                    else:
                        xtile = next_xt()
                        xap = xtile[0:P, :]
                        S.dma("sp", xap, src, reads=[sbuf], writes=[xtile])
                    xtl[(g, ti)] = (xtile, xap)

            def n_pe(g):
                off = 0
                for ti, (P, src, sbuf, dst, dbuf) in enumerate(groups[g]):
                    xtile, xap = xtl[(g, ti)]
                    rms_stats(xap, P, [xtile], hb)
                    norm_to(xap, P, [xtile], gpre, hb)
                    transpose8(hb, P, hT, tb=ti % 2)
                    S.op("act", lambda h, off=off, P=P: h.copy(out=hTg[:, :, off:off + P], in_=hT[:, :, 0:P]), reads=[hT], writes=[hTg])
                    off += P

            def up(g):
                N = sum(t[0] for t in groups[g])
                for ffc in range(32):
                    bk = (2, 3, 0, 1)[ffc % 4]
                    mm(bank(bk)[:, 0:N], [(Wu[:, kc, ffc * 128:(ffc + 1) * 128], hTg[:, kc, 0:N]) for kc in range(8)], [Wu, hTg], [bB[bk]])
                    S.op("act", lambda h, ffc=ffc, bk=bk, N=N: h.activation(out=actT[:, ffc, 0:N], in_=bank(bk)[:, 0:N], func=AF.Square),
                         reads=[bB[bk]], writes=[actT])
                    S.op("dve", lambda h, ffc=ffc, bk=bk, N=N: h.scalar_tensor_tensor(out=actT[:, ffc, 0:N], in0=bank(bk)[:, 0:N], scalar=0.0,
                                                                                      in1=actT[:, ffc, 0:N], op0=ALU.is_gt, op1=ALU.mult),
                         reads=[bB[bk]], writes=[actT])

            def down(g):
                off = 0
                for ti, (P, src, sbuf, dst, dbuf) in enumerate(groups[g]):
                    xtile, xap = xtl[(g, ti)]
                    pair = 2 + ti % 2
                    for half in range(2):
                        mm(bank(2 * pair + half)[0:P, :], [(actT[:, ffc, off:off + P], Wd[:, ffc, half * 512:(half + 1) * 512]) for ffc in range(32)],
                           [actT, Wd], [bB[2 * pair + half]])
                    post_norm_residual(pair, P, gpost, xap, xtile)
                    if dst is not None:
                        S.dma("sp", dst, xap, reads=[xtile], writes=[dbuf])
                    off += P

            n_elem(0)
            n_pe(0)
            for g in range(len(groups)):
                up(g)
                if g + 1 < len(groups):
                    n_elem(g + 1, only=0)
                down(g)
                if g + 1 < len(groups):
                    if len(groups[g + 1]) > 1:
                        n_elem(g + 1, only=1)
                    n_pe(g + 1)

        with ExitStack() as es:
            tiles = [(128, xa[t * 128:(t + 1) * 128, :], b_xa[t], xb[t * 128:(t + 1) * 128, :], b_xb[t]) for t in range(NT1)]
            tiles.append((NS, None, None, None, None))
            mlp_phase(0, tiles, es)
            S.flush()
        if 0 <= stop <= 2:
            S.dma("sp", ys, xs_sb[:], reads=[xs_sb])
            S.flush()
            return nc

        with ExitStack() as es:
            es2 = ExitStack()
            Wkv = mk(es, "Wkv", [128, 8, 512], BF16)
            Wb = mk(es, "Wb", [128, 8, 1024], BF16)
            Wo = mk(es, "Wo1", [128, 8, D], BF16)
            gkv = mk(es, "gkv", [128, D])
            gpre = mk(es, "gpre1", [128, D])
            gpost = mk(es, "gpost1", [128, D])
            EBs = mk(es, "EBs", [128, 12])
            eb0 = mk(es, "eb0", [NS, 12])
            esink = mk(es, "esink", [128, 12])
            rb_sb = mk(es, "rb_sb", [32, 12])
            rbh = mk(es, "rbh", [32, 12, 128])
            ohu = mk(es, "ohu", [32, 384])
            validu = mk(es, "validu", [128, 384])
            ebt = mk(es, "ebt", [128, 384])
            kvf = mk(es, "kvf1", [128, 512])
            KT_all = mk(es2, "KT_all", [128, 2, 2, NT1 * 128], BF16)
            V_all = mk(es2, "V_all", [128, NT1, 4, 66], BF16)
            QT_all = mk(es2, "QT_all", [128, 6, NT * 128], BF16)
            QmT_all = mk(es2, "QmT_all", [128, 2, NT * 128], BF16)
            EB = mk(es2, "EB", [128, 12, 2, 128])
            es_sb = mk(es2, "es_sb", [128, 2, 384])
            pTg = [mk(es2, "pTg%d" % g, [128, 2, 3, 128], BF16) for g in range(4)]

            load_w(Wkv, w_kv, 8, 512)
            wbv = w_in_b.rearrange("(kc p) n -> p kc n", p=128)
            for i in range(6):
                s_, j_ = divmod(i, 3)
                for half in range(2):
                    hd = 6 * s_ + 3 * half + j_
                    S.dma("pool", Wb[:, :, i * 128 + half * 64:i * 128 + half * 64 + 64], wbv[:, :, hd * 64:(hd + 1) * 64], writes=[Wb])
            S.dma("pool", Wb[:, :, 768:1024], wbv[:, :, 768:1024], writes=[Wb])
            load_w(Wo, w_out[1], 8, D)
            load_gain(gkv, g_kv[0, :])
            load_gain(gpre, g_mix_pre[1, :])
            load_gain(gpost, g_mix_post[1, :])
            S.dma("sp", rb_sb[:], rel_bias, writes=[rb_sb])
            S.dma("sp", ohu[:], c_ohu, writes=[ohu])
            S.dma("sp", validu[:], c_valid, writes=[validu])
            S.dma("sp", esink[:], sinks[0, :].partition_broadcast(128), writes=[esink])
            S.op("act", lambda h: h.activation(out=esink[:], in_=esink[:], func=AF.Exp), reads=[], writes=[esink])
            S.op("dve", lambda h: h.tensor_copy(out=rbh[:], in_=bc(rb_sb[:], 2, 128)), reads=[rb_sb], writes=[rbh])
            for hd in range(12):
                mm(bank(2)[:, 0:384], [(rbh[:, hd, :], ohu[:])], [rbh, ohu], [bB[2]])
                S.op("act", lambda h: h.activation(out=ebt[:], in_=bank(2)[:, 0:384], func=AF.Exp), reads=[bB[2]], writes=[ebt])
                S.op("dve", lambda h: h.tensor_tensor(out=ebt[:], in0=ebt[:], in1=validu[:], op=ALU.mult), reads=[validu], writes=[ebt])
                S.dma("sp", ebd.ap()[hd], ebt[:], reads=[ebt], writes=[b_ebd])
            for hd in range(12):
                for kb in range(2):
                    S.dma("sp", EB[:, hd, kb, :], bass.AP(ebd, hd * 128 * 384 + 255 - 128 * kb, [[383, 128], [1, 128]]), reads=[b_ebd], writes=[EB])
                S.dma("sp", EBs[:, hd:hd + 1], bass.AP(ebd, hd * 128 * 384 + 255, [[383, 128], [1, 1]]), reads=[b_ebd], writes=[EBs], allow_slow_non_contiguous=True)
                S.dma("sp", eb0[:, hd:hd + 1], bass.AP(ebd, hd * 128 * 384 + 127, [[384, NS], [1, 1]]), reads=[b_ebd], writes=[eb0], allow_slow_non_contiguous=True)
            bKT = [Buf() for _ in range(NT1)]
            bV = [Buf() for _ in range(NT1)]
            bQ = [[Buf() for _ in range(6)] for _ in range(NT)]
            bQm = [[Buf() for _ in range(2)] for _ in range(NT)]
            B1S = [dict(hb=hb, hb2=hb2, hT=hT, hT2=hT2, ss=st["ss"], ln=st["ln"], rs=st["rstd"])]
            B1S.append(dict(hb=mk(es2, "b1_hb", [128, D], BF16), hb2=mk(es2, "b1_hb2", [128, D], BF16),
                            hT=mk(es2, "b1_hT", [128, 8, 128], BF16), hT2=mk(es2, "b1_hT2", [128, 8, 128], BF16),
                            ss=mk(es2, "b1_ss", [128, 1]), ln=mk(es2, "b1_ln", [128, 1]), rs=mk(es2, "b1_rs", [128, 1])))

            S.op("pool", lambda h: h.memset(V_all[:], 1.0), writes=bV)
            S.op("pool", lambda h: h.memset(KT_all[:], 0.0), writes=bKT)

            def b1_tile(t, Q, par):
                def q(fn, *a, **k):
                    Q.append((fn, a, k))
                W = B1S[par]
                hbP, hb2P, hTP, hT2P, ssP, lnP, rsP = W["hb"], W["hb2"], W["hT"], W["hT2"], W["ss"], W["ln"], W["rs"]
                bT, bA, bBk, bC = 4 * par, 4 * par + 1, 4 * par + 2, 4 * par + 3
                xtile = next_xt()
                q(S.dma, "sp", xtile[:], xb[t * 128:(t + 1) * 128, :], reads=[b_xb[t]], writes=[xtile])
                q(S.op, "act", lambda h: h.activation(out=hbP[:], in_=xtile[:], func=AF.Square, accum_out=ssP[:, 0:1]), reads=[xtile], writes=[hbP, ssP])
                q(S.op, "act", lambda h: h.activation(out=lnP[:, 0:1], in_=ssP[:, 0:1], func=AF.Ln, scale=1.0 / D, bias=eps_t[:, 0:1]), reads=[ssP, eps_t], writes=[lnP])
                q(S.op, "act", lambda h: h.activation(out=rsP[:, 0:1], in_=lnP[:, 0:1], func=AF.Exp, scale=-0.5), reads=[lnP], writes=[rsP])
                q(S.op, "dve", lambda h: h.scalar_tensor_tensor(out=hbP[:], in0=xtile[:], scalar=rsP[:, 0:1], in1=gkv[:], op0=ALU.mult, op1=ALU.mult),
                  reads=[xtile, rsP, gkv], writes=[hbP])

                def ftr(h, src=hbP):
                    for kc in range(8):
                        ins = h.transpose(out=bankb(bT)[:, kc * 128:(kc + 1) * 128], in_=src[:, kc * 128:(kc + 1) * 128], identity=ident_b[:])
                    return ins
                q(S.op, "pe", ftr, reads=[hbP, ident_b], writes=[bB[bT]])
                q(S.op, "act", lambda h: h.copy(out=hTP[:].rearrange("p a b -> p (a b)"), in_=bankb(bT)[:, :]), reads=[bB[bT]], writes=[hTP])
                for s_ in range(2):
                    q(mm, bank(bA)[:, s_ * 128:(s_ + 1) * 128], [(Wkv[:, kc, s_ * 128:(s_ + 1) * 128], hTP[:, kc, :]) for kc in range(8)], [Wkv, hTP], [bB[bA]])
                q(S.op, "act", lambda h: h.copy(out=KT_all[0:64, 0, :, t * 128:(t + 1) * 128], in_=bank(bA)[0:64, 0:256].rearrange("p (a b) -> p a b", a=2)),
                  reads=[bB[bA]], writes=[bKT[t]])
                q(S.op, "act", lambda h: h.copy(out=KT_all[64:128, 1, :, t * 128:(t + 1) * 128], in_=bank(bA)[64:128, 0:256].rearrange("p (a b) -> p a b", a=2)),
                  reads=[bB[bA]], writes=[bKT[t]])
                q(mm, bank(bBk)[:, 0:512], [(hTP[:, kc, :], Wkv[:, kc, :]) for kc in range(8)], [Wkv, hTP], [bB[bBk]])
                q(S.op, "dve", lambda h: h.tensor_copy(out=V_all[:, t, :, 0:64], in_=bank(bBk)[:, 256:512].rearrange("p (a b) -> p a b", a=4)),
                  reads=[bB[bBk]], writes=[bV[t]])
                if t == NT:
                    q(S.op, "act", lambda h: h.copy(out=kvf[:], in_=bank(bBk)[:, 0:512]), reads=[bB[bBk]], writes=[kvf])
                    q(S.dma, "sp", swkp, kvf[:, 0:256], reads=[kvf])
                    q(S.dma, "sp", swvp, kvf[:, 256:512], reads=[kvf])
                if t == 0:
                    q(S.op, "dve", lambda h: h.tensor_scalar(out=V_all[:, 0, :, :], in0=V_all[:, 0, :, :], scalar1=flag[:, 0:1], scalar2=None, op0=ALU.mult),
                      reads=[flag], writes=[bV[0]])
                    return
                q(S.op, "dve", lambda h: h.scalar_tensor_tensor(out=hb2P[:], in0=xtile[:], scalar=rsP[:, 0:1], in1=gpre[:], op0=ALU.mult, op1=ALU.mult),
                  reads=[xtile, rsP, gpre], writes=[hb2P])
                q(S.op, "pe", lambda h: ftr(h, src=hb2P), reads=[hb2P, ident_b], writes=[bB[bT]])
                q(S.op, "act", lambda h: h.copy(out=hT2P[:].rearrange("p a b -> p (a b)"), in_=bankb(bT)[:, :]), reads=[bB[bT]], writes=[hT2P])
                for i in range(8):
                    bk = (bA, bBk, bC)[i % 3]
                    q(mm, bank(bk)[:, 0:128], [(Wb[:, kc, i * 128:(i + 1) * 128], hT2P[:, kc, :]) for kc in range(8)], [Wb, hT2P], [bB[bk]])
                    dst = QT_all[:, i, (t - 1) * 128:t * 128] if i < 6 else QmT_all[:, i - 6, (t - 1) * 128:t * 128]
                    tok = bQ[t - 1][i] if i < 6 else bQm[t - 1][i - 6]
                    q(S.op, "act" if i % 2 else "dve",
                      (lambda h, dst=dst, bk=bk: h.copy(out=dst, in_=bank(bk)[:, 0:128])) if i % 2 else
                      (lambda h, dst=dst, bk=bk: h.tensor_copy(out=dst, in_=bank(bk)[:, 0:128])),
                      reads=[bB[bk]], writes=[tok])

            for t0 in range(0, NT1, 2):
                lists = []
                for par, t in enumerate(range(t0, min(t0 + 2, NT1))):
                    Q = []
                    b1_tile(t, Q, par)
                    lists.append(Q)
                for i_ in range(max(len(L) for L in lists)):
                    for L in lists:
                        if i_ < len(L):
                            fn_, a_, k_ = L[i_]
                            fn_(*a_, **k_)

            def swa_finish(P, extra=None):
                o3 = [bank(6 + i)[0:P, 0:390].rearrange("p (a b) -> p a b", a=6) for i in range(2)]
                for i in range(2):
                    S.op("dve", lambda h, i=i: h.tensor_tensor(out=den12[0:P, 6 * i:6 * i + 6], in0=o3[i][:, :, 64], in1=esink[0:P, 6 * i:6 * i + 6], op=ALU.add),
                         reads=[bB[6 + i], esink], writes=[den12])
                if extra is not None:
                    S.op("dve", lambda h: h.tensor_tensor(out=den12[0:P, :], in0=den12[0:P, :], in1=extra[0:P, :], op=ALU.add),
                         reads=[extra], writes=[den12])
                S.op("dve", lambda h: h.reciprocal(out=rden12[0:P, :], in_=den12[0:P, :]), reads=[den12], writes=[rden12])

            def swa_norm(P, srcs, src_reads):
                for i in range(2):
                    S.op("dve", lambda h, i=i: h.tensor_tensor(out=mixcat[0:P, 384 * i:384 * (i + 1)].rearrange("p (a b) -> p a b", a=6), in0=srcs[i],
                                                               in1=bc(rden12[0:P, 6 * i:6 * i + 6], 2, 64), op=ALU.mult),
                         reads=list(src_reads) + [rden12], writes=[mixcat])

            def b2(t):
                P = 128
                xtile = next_xt()
                S.dma("sp", xtile[:], xb[(t + 1) * 128:(t + 2) * 128, :], reads=[b_xb[t + 1]], writes=[xtile])
                ebt_ = EB
                for g in range(4):
                    s_, p_ = divmod(g, 2)
                    pr = 1 + g % 2
                    def fsc(h, g=g, s_=s_, p_=p_, pr=pr):
                        for kb in range(2):
                            ins = h.matmul(bank(2 * pr + kb)[:, 0:384], lhsT=KT_all[:, p_, s_, (t + kb) * 128:(t + kb + 1) * 128],
                                           rhs=QT_all[:, 3 * s_:3 * s_ + 3, t * 128:(t + 1) * 128], start=True, stop=True)
                        return ins
                    S.op("pe", fsc, reads=[bKT[t], bKT[t + 1]] + bQ[t], writes=[bB[2 * pr], bB[2 * pr + 1]])
                    S.op("act", lambda h, pr=pr: h.activation(out=es_sb[:], in_=pair768(pr, 128), func=AF.Exp, scale=0.125),
                         reads=[bB[2 * pr], bB[2 * pr + 1]], writes=[es_sb])
                    S.op("dve", lambda h, g=g, ebt_=ebt_: h.tensor_tensor(out=pTg[g][:], in0=es_sb[:].rearrange("p a (j q) -> p a j q", j=3),
                                                                          in1=ebt_[:, 3 * g:3 * g + 3, :, :].rearrange("p j a q -> p a j q"), op=ALU.mult),
                         reads=[es_sb, ebt_], writes=[pTg[g]])
                    def fo(h, g=g):
                        for j in range(3):
                            hd = 3 * g + j
                            for kb in range(2):
                                ins = h.matmul(bank(6 + hd // 6)[:, (hd % 6) * 65:(hd % 6 + 1) * 65], lhsT=pTg[g][:, kb, j, :],
                                               rhs=V_all[:, t + kb, g, 0:65], start=(kb == 0), stop=(kb == 1))
                        return ins
                    S.op("pe", fo, reads=[pTg[g], bV[t], bV[t + 1]], writes=[bB[6], bB[7]])
                swa_finish(P)
                o3 = [bank(6 + i)[0:P, 0:390].rearrange("p (a b) -> p a b", a=6)[:, :, 0:64] for i in range(2)]
                swa_norm(P, o3, [bB[6], bB[7]])
                mem_attend_prompt(1, lambda c, p: QmT_all[:, c, t * 128:(t + 1) * 128], bQm[t], P)
                out_proj(P, Wo, gpost, xtile[:], xtile)
                S.dma("sp", xa[t * 128:(t + 1) * 128, :], xtile[:], reads=[xtile], writes=[b_xa[t]])

            for t in range(NT):
                b2(t)

            S.flush()
            es2.close()
            P = NS
            sab = alloc_sample_attn(es, "p4")
            Kc, Vc, prod, sc8, pp8, Pm, onesel = sab
            knew = mk(es, "knew", [NS, 512])
            qtok = mk(es, "qtok", [NS, 1024])
            Ks = [mk(es, "Ks%d" % i, [128, 4, 64]) for i in range(2)]
            Vs = [mk(es, "Vs%d" % i, [128, 4, 65]) for i in range(2)]
            sn12 = mk(es, "sn12", [NS, 12])
            pn12 = mk(es, "pn12", [NS, 12])
            tmpv = mk(es, "tmpv", [NS, 768])
            osum = mk(es, "osum", [NS, 768])
            S.op("pool", lambda h: h.memset(Vs[0][:], 1.0), writes=[Vs[0]])
            S.op("pool", lambda h: h.memset(Vs[1][:], 1.0), writes=[Vs[1]])
            rms_stats(xs_sb[:], P, [xs_sb], hb)
            norm_to(xs_sb[:], P, [xs_sb], gkv, hb)
            transpose8(hb, P, hT, tb=0)
            mm(bank(3)[0:P, 0:512], [(hT[:, kc, 0:P], Wkv[:, kc, :]) for kc in range(8)], [Wkv, hT], [bB[3]])
            S.op("act", lambda h: h.copy(out=knew[:], in_=bank(3)[0:P, 0:512]), reads=[bB[3]], writes=[knew])
            norm_to(xs_sb[:], P, [xs_sb], gpre, hb2)
            transpose8(hb2, P, hT2, tb=1)
            for half in range(2):
                mm(bank(4 + half)[0:P, :], [(hT2[:, kc, 0:P], Wb[:, kc, half * 512:(half + 1) * 512]) for kc in range(8)], [Wb, hT2], [bB[4 + half]])
            S.op("act", lambda h: h.copy(out=qtok[:], in_=PS[2][0:P, :]), reads=[bB[4], bB[5]], writes=[qtok])
            S.dma("sp", swks[:, 0:127, :], swk_in[:, 1:128, :])
            S.dma("sp", swvs[:, 0:127, :], swv_in[:, 1:128, :])
            S.dma("sp", swks[:, 127, :], knew[:, 0:256], reads=[knew])
            S.dma("sp", swvs[:, 127, :], knew[:, 256:512], reads=[knew])

            def qview(ap2d, s_):
                return ap2d[:, s_ * 384:(s_ + 1) * 384].rearrange("p (j a d) -> p a j d", j=3, a=2)

            zero_acc([bank(6)[0:NS, 0:390], bank(7)[0:NS, 0:390]], [bB[6], bB[7]])
            for i in range(NS):
                ks_, vs_ = Ks[i % 2], Vs[i % 2]
                S.dma("act", ks_[:], swk_in[i].rearrange("w (g d) -> w g d", g=4), writes=[ks_])
                S.dma("act", vs_[:, :, 0:64], swv_in[i].rearrange("w (g d) -> w g d", g=4), writes=[vs_])
                for half in range(2):
                    mm(bank(2 + half)[:, 0:384], [(onesel[0:NS, i, :], qtok[0:NS, half * 384:(half + 1) * 384])], [onesel, qtok], [bB[2 + half]])
                for s_ in range(2):
                    S.op("dve", lambda h, s_=s_, ks_=ks_: h.tensor_tensor(
                        out=prod[:, 384 * s_:384 * (s_ + 1)].rearrange("p (a j d) -> p a j d", a=2, j=3),
                        in0=bc(ks_[:, 2 * s_:2 * s_ + 2, :], 2, 3), in1=qview(bank(2 + s_)[:, 0:384], 0), op=ALU.mult),
                        reads=[ks_, bB[2 + s_]], writes=[prod])
                S.op("dve", lambda h: h.tensor_reduce(out=sc8[:, :], in_=prod[:, :].rearrange("p (a b) -> p a b", b=64), axis=AX.X, op=ALU.add),
                     reads=[prod], writes=[sc8])
                S.op("act", lambda h: h.activation(out=pp8[:, :], in_=sc8[:, :], func=AF.Exp, scale=0.125), reads=[sc8], writes=[pp8])
                S.op("dve", lambda h: h.tensor_tensor(out=pp8[:, :], in0=pp8[:, :], in1=EBs[:, :], op=ALU.mult), reads=[EBs], writes=[pp8])
                S.op("dve", lambda h, i=i: h.tensor_tensor(out=Pm[:, :, :], in0=bc(pp8[:, :], 2, NS), in1=bc(id16r[:, i, :], 1, 12), op=ALU.mult),
                     reads=[pp8, id16r], writes=[Pm])

                def fo2(h, i=i, vs_=vs_):
                    for hd in range(12):
                        ins = h.matmul(bank(6 + hd // 6)[0:NS, (hd % 6) * 65:(hd % 6 + 1) * 65], lhsT=Pm[:, hd, :], rhs=vs_[:, hd // 3, :],
                                       start=False, stop=(i == NS - 1 and hd % 6 == 5))
                    return ins
                S.op("pe", fo2, reads=[Pm, vs_], writes=[bB[6], bB[7]])
            for s_ in range(2):
                S.op("dve", lambda h, s_=s_: h.tensor_tensor(
                    out=prod[0:NS, 384 * s_:384 * (s_ + 1)].rearrange("p (a j d) -> p a j d", a=2, j=3),
                    in0=bc(knew[:, 0:256].rearrange("p (g d) -> p g d", g=4)[:, 2 * s_:2 * s_ + 2, :], 2, 3), in1=qview(qtok[:, 0:768], s_), op=ALU.mult),
                    reads=[knew, qtok], writes=[prod])
            S.op("dve", lambda h: h.tensor_reduce(out=sn12[:], in_=prod[0:NS, :].rearrange("p (a b) -> p a b", b=64), axis=AX.X, op=ALU.add),
                 reads=[prod], writes=[sn12])
            S.op("act", lambda h: h.activation(out=pn12[:], in_=sn12[:], func=AF.Exp, scale=0.125), reads=[sn12], writes=[pn12])
            S.op("dve", lambda h: h.tensor_tensor(out=pn12[:], in0=pn12[:], in1=eb0[:], op=ALU.mult), reads=[eb0], writes=[pn12])
            S.op("dve", lambda h: h.tensor_tensor(out=tmpv[:].rearrange("p (g j d) -> p g j d", g=4, j=3),
                                                  in0=bc(knew[:, 256:512].rearrange("p (g d) -> p g d", g=4), 2, 3),
                                                  in1=bc(pn12[:].rearrange("p (g j) -> p g j", g=4), 3, 64), op=ALU.mult),
                 reads=[knew, pn12], writes=[tmpv])
            swa_finish(P, extra=pn12)
            for i in range(2):
                S.op("dve", lambda h, i=i: h.tensor_tensor(out=osum[:, 384 * i:384 * (i + 1)].rearrange("p (a b) -> p a b", a=6),
                                                           in0=bank(6 + i)[0:P, 0:390].rearrange("p (a b) -> p a b", a=6)[:, :, 0:64],
                                                           in1=tmpv[:, 384 * i:384 * (i + 1)].rearrange("p (a b) -> p a b", a=6), op=ALU.add),
                     reads=[bB[6 + i], tmpv], writes=[osum])
            swa_norm(P, [osum[:, 384 * i:384 * (i + 1)].rearrange("p (a b) -> p a b", a=6) for i in range(2)], [osum])
            mem_sample_fn[0](1, TB_view(qtok, 768), sab)
            out_proj(P, Wo, gpost, xs_sb[:], xs_sb)
            S.flush()
        if 0 <= stop <= 3:
            S.dma("sp", ys, xs_sb[:], reads=[xs_sb])
            S.flush()
            return nc

        with ExitStack() as es:
            tiles = [(128, xa[t * 128:(t + 1) * 128, :], b_xa[t], y[t * 128:(t + 1) * 128, :], b_y) for t in range(NT)]
            tiles.append((NS, None, None, None, None))
            mlp_phase(1, tiles, es)
            S.dma("sp", ys, xs_sb[:], reads=[xs_sb])
            S.flush()
    return nc


class TB_sub:
    def __init__(self, tb, n):
        self.tb = tb
        self.b = tb.b
        self.n = n

    def __getitem__(self, k):
        if k == slice(None, None, None):
            return self.tb.t[:, :, 0:self.n]
        a, i, c = k
        assert c == slice(None, None, None)
        return self.tb.t[a, i, 0:self.n]


class TB_view:
    def __init__(self, tb, c0):
        self.tb = tb
        self.b = tb.b
        self.c0 = c0

    def __getitem__(self, k):
        rows, cols = k
        assert cols == slice(None, None, None)
        return self.tb.t[rows, self.c0:self.c0 + 256]


def _consts():
    c = {}
    c["c_ident"] = np.eye(128, dtype=np.float32)
    s = np.arange(128)[:, None]
    t = np.arange(128)[None, :]
    c["c_cmask"] = (s <= t).astype(np.float32)
    u = np.arange(384)
    dist = u - 127
    valid = (dist >= 0) & (dist < 128)
    n = np.maximum(dist, 0)
    nf = np.maximum(n, 1).astype(np.float32)
    large = 16 + (np.log(nf / np.float32(16)) / np.float32(math.log(128 / 16)) * np.float32(16)).astype(np.int32)
    large = np.minimum(large, 31)
    bucket = np.where(n < 16, n, large)
    oh = np.zeros((32, 384), np.float32)
    oh[bucket, u] = 1.0
    oh[:, ~valid] = 0.0
    c["c_ohu"] = oh
    c["c_valid"] = np.repeat(valid.astype(np.float32)[None, :], 128, axis=0)
    c["c_id16r"] = np.repeat(np.eye(16, dtype=np.float32).reshape(1, 256), 128, axis=0)
    return c


_NC_CACHE = {}


def kernel(**inp):
    NT, NPRE, NS = 16, 47, 16
    f = lambda a: np.ascontiguousarray(np.asarray(a, dtype=np.float32))
    xp = f(inp["x_prompt"])
    B, L, _ = xp.shape
    NT = L // (4 * 128)
    NPRE = 3 * NT - 1
    key = (NT, NPRE, NS)
    if key not in _NC_CACHE:
        import os
        _NC_CACHE[key] = build(NT, NPRE, NS, stop=int(os.environ.get('KSTOP', '99')))
    nc = _NC_CACHE[key]
    consts = _consts()
    shared = {}
    for k in ("norm_mix_pre", "norm_mix_post", "norm_ffn_pre", "norm_ffn_post", "norm_mem", "w_mem_kv", "w_gate_up", "b_gate",
              "gla_norm", "sinks", "rel_bias", "w_out", "w_ffn_up", "w_ffn_down"):
        shared[k] = f(inp[k])
    shared["w_in_a"] = f(inp["w_in_a"])[0]
    shared["w_in_b"] = f(inp["w_in_b"])[0]
    shared["w_gate_up"] = f(inp["w_gate_up"])[0]
    shared["norm_kv"] = f(inp["norm_kv"]).reshape(1, D)
    shared["w_kv"] = f(inp["w_kv"])
    shared.update(consts)
    xs = f(inp["x_sample"]).reshape(-1, D)
    st = f(inp["state_gla"])[0]
    swk = f(inp["cache_swa_k"]).reshape(-1, 128, 256)
    swv = f(inp["cache_swa_v"]).reshape(-1, 128, 256)
    mkc = f(inp["cache_mem_k"]).reshape(2, -1, 256, 256)
    mvc = f(inp["cache_mem_v"]).reshape(2, -1, 256, 256)
    mem = f(inp["mem_prompt"])
    pad = np.zeros(((NPRE + 1) * 128, D), np.float32)
    in_maps = []
    for c in range(NCORES):
        b, j = divmod(c, 4)
        xpad = np.concatenate([pad, xp[b]], axis=0)
        m = dict(shared)
        m["xin"] = np.ascontiguousarray(xpad[j * NT * 128:(j * NT + NPRE + 1 + NT) * 128])
        sl = slice(c * NS, (c + 1) * NS)
        m["xs_in"] = xs[sl]
        m["st_in"] = st[sl]
        m["swk_in"] = swk[sl]
        m["swv_in"] = swv[sl]
        m["mk_in"] = np.ascontiguousarray(mkc[:, sl])
        m["mv_in"] = np.ascontiguousarray(mvc[:, sl])
        m["mem_in"] = mem[b]
        m["c_flag"] = np.full((128, 1), 1.0 if j > 0 else 0.0, np.float32)
        in_maps.append(m)
    res = run_bass_kernel_spmd(nc, in_maps, core_ids=list(range(NCORES)))
    R = res.results
    y_prompt = np.stack([np.concatenate([R[4 * b + j]["y"] for j in range(4)], axis=0) for b in range(B)])
    y_sample = np.concatenate([R[c]["ys"] for c in range(NCORES)], axis=0).reshape(-1, 1, D)
    stp = np.stack([R[4 * b + 3]["stp"] for b in range(B)])[None]
    sts = np.concatenate([R[c]["sts"] for c in range(NCORES)], axis=0)[None]
    swkp = np.stack([R[4 * b + 3]["swkp"] for b in range(B)]).reshape(B, 128, 4, 64)
    swvp = np.stack([R[4 * b + 3]["swvp"] for b in range(B)]).reshape(B, 128, 4, 64)
    swks = np.concatenate([R[c]["swks"] for c in range(NCORES)], axis=0).reshape(-1, 128, 4, 64)
    swvs = np.concatenate([R[c]["swvs"] for c in range(NCORES)], axis=0).reshape(-1, 128, 4, 64)
    mkp = np.stack([R[4 * b]["mkp"] for b in range(B)], axis=1).reshape(2, B, 256, 4, 64)
    mvp = np.stack([R[4 * b]["mvp"] for b in range(B)], axis=1).reshape(2, B, 256, 4, 64)
    return (y_prompt, y_sample, stp, sts, swkp, swvp, swks, swvs, mkp, mvp)
```
